# Optimizing a Trainium2 kernel written in Bass

```python
import jax, jax.numpy as jnp
from jax import lax
import numpy as np

D_MODEL = 1024
BATCH = 32
SEQ = 256
DEPTH = 1
DEC_BATCH = 2
DEC_SEQ = 4096
PAST_LEN = 256

GRID_W = 64
WIDTH_A = D_MODEL // 2
HEAD_A = 128
N_HEADS_A = WIDTH_A // HEAD_A
WIDTH_B = D_MODEL - WIDTH_A
HEAD_B = 64
N_HEADS_B = WIDTH_B // HEAD_B
LORA_W = 32
LORA_A = 32
LORA_G = 96
CHUNK = 64
D_FF = 2816
CONV_W = 3
RMS_EPS = 1e-6
GN_EPS = 64e-5
DECAY_SCALE = 0.6065306597
P_A = 5 * WIDTH_A
P_B = 3 * WIDTH_B + 2 * LORA_W + LORA_A + LORA_G
P_IN = P_A + P_B

kernel_name = 'hymba_hgrn2_rwkv7_convffn_diffusion_step'


def rmsnorm(x, w):
    xf = x.astype(jnp.float32)
    y = xf * lax.rsqrt(jnp.mean(xf * xf, axis=-1, keepdims=True) + RMS_EPS)
    return (y * w.astype(jnp.float32)).astype(x.dtype)


def dwconv1d(x, w):
    C = x.shape[-1]
    return lax.conv_general_dilated(x, w[:, None, :].astype(x.dtype), (1,), 'SAME',
                                    dimension_numbers=('NWC', 'WIO', 'NWC'), feature_group_count=C)


def dwconv2d_grid(x, w):
    B, T, C = x.shape
    rows = T // GRID_W
    y = lax.conv_general_dilated(x.reshape(B, rows, GRID_W, C), w[:, :, None, :].astype(x.dtype), (1, 1), 'SAME',
                                 dimension_numbers=('NHWC', 'HWIO', 'NHWC'), feature_group_count=C)
    return y.reshape(B, T, C)


def hgrn2_chunk_scan(q, k, v, logf, s0):
    B, T, H, DK = q.shape
    DV = v.shape[-1]
    nc = T // CHUNK

    def chunks(z):
        return z.reshape(B, nc, CHUNK, H, z.shape[-1]).transpose(1, 0, 3, 2, 4)

    causal = jnp.tril(jnp.ones((CHUNK, CHUNK), dtype=bool))

    def step(S, inp):
        qc, kc, vc, gc = inp
        b = jnp.cumsum(gc, axis=-2)
        b_last = b[:, :, -1:, :]
        inter = jnp.einsum('bhtd,bhde->bhte', qc * jnp.exp(b), S)
        diff = jnp.where(causal[:, :, None], b[:, :, :, None, :] - b[:, :, None, :, :], -jnp.inf)
        scores = jnp.einsum('bhtd,bhsd,bhtsd->bhts', qc, kc, jnp.exp(diff))
        intra = jnp.einsum('bhts,bhse->bhte', scores, vc)
        S = jnp.exp(b_last)[:, :, 0, :, None] * S + jnp.einsum('bhsd,bhse->bhde', kc * jnp.exp(b_last - b), vc)
        return S, inter + intra

    S, o = lax.scan(step, s0.astype(q.dtype), tuple(chunks(z) for z in (q, k, v, logf)))
    o = o.transpose(1, 0, 3, 2, 4).reshape(B, T, H, DV)
    return o, S


def hgrn2_mixer(u_a, lb, norm_w, s0):
    B, T, _ = u_a.shape
    q, i, zf, zb, g = jnp.split(u_a, 5, axis=-1)
    heads = lambda z: z.reshape(B, T, N_HEADS_A, HEAD_A)
    flip = lambda z: jnp.flip(z, axis=1)
    q = heads(jax.nn.silu(q))
    i = heads(i)
    f_f = lb[0] + (1 - lb[0]) * jax.nn.sigmoid(zf)
    f_b = lb[1] + (1 - lb[1]) * jax.nn.sigmoid(zb)
    o_f, S_f = hgrn2_chunk_scan(q, heads(1 - f_f), i, heads(jnp.log(f_f)), s0[:, 0])
    o_b, S_b = hgrn2_chunk_scan(flip(q), flip(heads(1 - f_b)), flip(i), flip(heads(jnp.log(f_b))), s0[:, 1])
    o = o_f + flip(o_b)
    o = rmsnorm(o, norm_w.reshape(N_HEADS_A, HEAD_A)).reshape(B, T, WIDTH_A)
    return o * jax.nn.silu(g), jnp.stack([S_f, S_b], axis=1)


def rwkv7_scan(r, w, k, v, kk, a, s0):
    xs = tuple(jnp.moveaxis(z, 1, 0) for z in (r, w, k, v, kk, a))

    def step(S, inp):
        r_t, w_t, k_t, v_t, kk_t, a_t = inp
        sk = jnp.einsum('bhij,bhj->bhi', S, kk_t)
        S = (S * w_t[:, :, None, :] - sk[..., None] * (kk_t * a_t)[:, :, None, :]
             + v_t[..., None] * k_t[:, :, None, :])
        return S, jnp.einsum('bhij,bhj->bhi', S, r_t)

    S, ys = lax.scan(step, s0.astype(r.dtype), xs)
    return jnp.moveaxis(ys, 0, 1), S


def rwkv7_mixer(u_b, p, s0):
    B, T, _ = u_b.shape
    u_b = dwconv1d(u_b, p['rwkv_conv'])
    idx = [WIDTH_B, 2 * WIDTH_B, 3 * WIDTH_B, 3 * WIDTH_B + LORA_W, 3 * WIDTH_B + 2 * LORA_W,
           3 * WIDTH_B + 2 * LORA_W + LORA_A]
    r, k, v, wdf, wdb, ad, gd = jnp.split(u_b, idx, axis=-1)
    heads = lambda z: z.reshape(B, T, N_HEADS_B, HEAD_B)
    flip = lambda z: jnp.flip(z, axis=1)
    w_f = jnp.exp(-DECAY_SCALE * jax.nn.sigmoid(p['rwkv_w0'][0] + jnp.tanh(wdf) @ p['rwkv_w2'][0]))
    w_b = jnp.exp(-DECAY_SCALE * jax.nn.sigmoid(p['rwkv_w0'][1] + jnp.tanh(wdb) @ p['rwkv_w2'][1]))
    a = jax.nn.sigmoid(p['rwkv_a0'] + ad @ p['rwkv_a2'])
    g = jax.nn.sigmoid(gd) @ p['rwkv_g2']
    kk = heads(k * p['rwkv_k_k']).astype(jnp.float32)
    kk = (kk * lax.rsqrt(jnp.sum(kk * kk, axis=-1, keepdims=True) + 1e-12)).astype(u_b.dtype)
    k = k * (1 + (a - 1) * p['rwkv_k_a'])
    r, k, v, a = heads(r), heads(k), heads(v), heads(a)
    w_f, w_b = heads(w_f), heads(w_b)
    y_f, S_f = rwkv7_scan(r, w_f, k, v, kk, a, s0[:, 0])
    y_b, S_b = rwkv7_scan(flip(r), flip(w_b), flip(k), flip(v), flip(kk), flip(a), s0[:, 1])
    y = (y_f + flip(y_b)).astype(jnp.float32)
    mu = jnp.mean(y, axis=-1, keepdims=True)
    var = jnp.mean(jnp.square(y - mu), axis=-1, keepdims=True)
    yn = (y - mu) * lax.rsqrt(var + GN_EPS)
    yn = (yn * p['rwkv_ln_w'].reshape(N_HEADS_B, HEAD_B) + p['rwkv_ln_b'].reshape(N_HEADS_B, HEAD_B)).astype(u_b.dtype)
    bonus = jnp.sum(r * k * p['rwkv_r_k'].reshape(N_HEADS_B, HEAD_B), axis=-1, keepdims=True) * v
    out = (yn + bonus).reshape(B, T, WIDTH_B) * g
    return out, jnp.stack([S_f, S_b], axis=1)


def block(x, mod, s_hgrn, s_rwkv, ffn_conv_fn, p):
    shift1, scale1, gate1, shift2, scale2, gate2 = jnp.split(mod[:, None, :], 6, axis=-1)
    h = rmsnorm(x, p['norm_mix_w']) * (1 + scale1) + shift1
    u = h @ p['w_in']
    o_a, s_hgrn_new = hgrn2_mixer(u[..., :P_A], p['lb'], p['hgrn_norm_w'], s_hgrn)
    o_b, s_rwkv_new = rwkv7_mixer(u[..., P_A:], p, s_rwkv)
    x = x + gate1 * (jnp.concatenate([o_a, o_b], axis=-1) @ p['w_out'])
    h = rmsnorm(x, p['norm_ffn_w']) * (1 + scale2) + shift2
    gt = ffn_conv_fn(h @ p['ffn_w_gate']) + p['ffn_conv_b']
    x = x + gate2 * ((jax.nn.gelu(gt) * (h @ p['ffn_w_up'])) @ p['ffn_w_down'])
    return x, s_hgrn_new, s_rwkv_new


def setup_inputs(seed: int = 0) -> dict:
    key = jax.random.key(seed)
    ks = jax.random.split(key, 32)
    nrm = lambda k, shape, s: jax.random.normal(k, shape, jnp.float32) * s
    L = DEPTH
    centre = jnp.eye(CONV_W, dtype=jnp.float32)[1]
    return {
        'x_prompt': nrm(ks[0], (BATCH, SEQ, D_MODEL), 1.0),
        'x_sample': nrm(ks[1], (DEC_BATCH, DEC_SEQ, D_MODEL), 1.0),
        'state_hgrn': nrm(ks[2], (DEC_BATCH, L, 2, N_HEADS_A, HEAD_A, HEAD_A), 0.5),
        'state_rwkv': nrm(ks[3], (DEC_BATCH, L, 2, N_HEADS_B, HEAD_B, HEAD_B), 0.5),
        'c': nrm(ks[4], (DEC_BATCH, D_MODEL), 1.0),
        'c_ctx': nrm(ks[5], (D_MODEL,), 1.0),
        'ada_w': nrm(ks[6], (L, D_MODEL, 6 * D_MODEL), 0.5 * D_MODEL ** -0.5),
        'ada_b': nrm(ks[7], (L, 6 * D_MODEL), 0.02),
        'norm_mix_w': 1.0 + nrm(ks[8], (L, D_MODEL), 0.05),
        'w_in': nrm(ks[9], (L, D_MODEL, P_IN), D_MODEL ** -0.5),
        'hgrn_lb': nrm(ks[10], (L + 1, 2, WIDTH_A), 0.5),
        'hgrn_norm_w': 1.0 + nrm(ks[11], (L, WIDTH_A), 0.05),
        'rwkv_conv': nrm(ks[12], (L, CONV_W, P_B), 0.3) + centre[None, :, None],
        'rwkv_w0': nrm(ks[13], (L, 2, WIDTH_B), 1.0),
        'rwkv_w2': nrm(ks[14], (L, 2, LORA_W, WIDTH_B), 0.3 * LORA_W ** -0.5),
        'rwkv_a0': nrm(ks[15], (L, WIDTH_B), 0.5),
        'rwkv_a2': nrm(ks[16], (L, LORA_A, WIDTH_B), 0.5 * LORA_A ** -0.5),
        'rwkv_g2': nrm(ks[17], (L, LORA_G, WIDTH_B), LORA_G ** -0.5),
        'rwkv_k_k': 1.0 + nrm(ks[18], (L, WIDTH_B), 0.1),
        'rwkv_k_a': 1.0 + nrm(ks[19], (L, WIDTH_B), 0.1),
        'rwkv_r_k': nrm(ks[20], (L, WIDTH_B), 0.1),
        'rwkv_ln_w': 1.0 + nrm(ks[21], (L, WIDTH_B), 0.05),
        'rwkv_ln_b': nrm(ks[22], (L, WIDTH_B), 0.01),
        'w_out': nrm(ks[23], (L, D_MODEL, D_MODEL), D_MODEL ** -0.5),
        'norm_ffn_w': 1.0 + nrm(ks[24], (L, D_MODEL), 0.05),
        'ffn_w_gate': nrm(ks[25], (L, D_MODEL, D_FF), D_MODEL ** -0.5),
        'ffn_w_up': nrm(ks[26], (L, D_MODEL, D_FF), D_MODEL ** -0.5),
        'ffn_conv': nrm(ks[27], (L, CONV_W, CONV_W, D_FF), 1.0 / 3.0),
        'ffn_conv_b': nrm(ks[28], (L, D_FF), 0.02),
        'ffn_w_down': nrm(ks[29], (L, D_FF, D_MODEL), D_FF ** -0.5),
        'final_norm_w': 1.0 + nrm(ks[30], (D_MODEL,), 0.05),
    }


def reference(x_prompt, x_sample, state_hgrn, state_rwkv, c, c_ctx, ada_w, ada_b, norm_mix_w, w_in,
              hgrn_lb, hgrn_norm_w, rwkv_conv, rwkv_w0, rwkv_w2, rwkv_a0, rwkv_a2, rwkv_g2, rwkv_k_k,
              rwkv_k_a, rwkv_r_k, rwkv_ln_w, rwkv_ln_b, w_out, norm_ffn_w, ffn_w_gate, ffn_w_up, ffn_conv,
              ffn_conv_b, ffn_w_down, final_norm_w):
    lb_all = jnp.cumsum(jax.nn.softmax(hgrn_lb.astype(jnp.float32), axis=0), axis=0).astype(x_prompt.dtype)
    b_ctx = x_prompt.shape[0]
    zeros_h = jnp.zeros((b_ctx, 2, N_HEADS_A, HEAD_A, HEAD_A), x_prompt.dtype)
    zeros_r = jnp.zeros((b_ctx, 2, N_HEADS_B, HEAD_B, HEAD_B), x_prompt.dtype)
    xp, xs = x_prompt, x_sample
    new_h, new_r = [], []
    for l in range(DEPTH):
        p = dict(norm_mix_w=norm_mix_w[l], w_in=w_in[l], lb=lb_all[l], hgrn_norm_w=hgrn_norm_w[l],
                 rwkv_conv=rwkv_conv[l], rwkv_w0=rwkv_w0[l], rwkv_w2=rwkv_w2[l], rwkv_a0=rwkv_a0[l],
                 rwkv_a2=rwkv_a2[l], rwkv_g2=rwkv_g2[l], rwkv_k_k=rwkv_k_k[l], rwkv_k_a=rwkv_k_a[l],
                 rwkv_r_k=rwkv_r_k[l], rwkv_ln_w=rwkv_ln_w[l], rwkv_ln_b=rwkv_ln_b[l], w_out=w_out[l],
                 norm_ffn_w=norm_ffn_w[l], ffn_w_gate=ffn_w_gate[l], ffn_w_up=ffn_w_up[l],
                 ffn_conv_b=ffn_conv_b[l], ffn_w_down=ffn_w_down[l])
        conv_l = ffn_conv[l]
        mod_ctx = jax.nn.silu(c_ctx)[None, :] @ ada_w[l] + ada_b[l]
        xp, s_h, s_r = block(xp, mod_ctx, zeros_h, zeros_r, lambda z: dwconv1d(z, conv_l[1]), p)
        new_h.append(s_h)
        new_r.append(s_r)
        mod_lat = jax.nn.silu(c) @ ada_w[l] + ada_b[l]
        xs, _, _ = block(xs, mod_lat, state_hgrn[:, l], state_rwkv[:, l], lambda z: dwconv2d_grid(z, conv_l), p)
    y_prompt = rmsnorm(xp, final_norm_w)
    y_sample = rmsnorm(xs, final_norm_w)
    new_state_hgrn = jnp.stack(new_h, axis=1)
    new_state_rwkv = jnp.stack(new_r, axis=1)
    return (y_prompt, y_sample, new_state_hgrn, new_state_rwkv)
```

```python
import contextlib
import numpy as np
import concourse.bass as bass
import concourse.mybir as mybir
from concourse.bass_utils import run_bass_kernel_spmd

F32 = mybir.dt.float32
BF16 = mybir.dt.bfloat16
AF = mybir.ActivationFunctionType
ALU = mybir.AluOpType
AX = mybir.AxisListType

D = 1024
KD = 8
WA = 512
PA = 2560
PB = 1728
PIN = 4288
DFF = 2816
NF = 22
DECAY = 0.6065306597
RMS_EPS = 1e-6
GN_EPS = 64e-5
GRID_W = 64
N_CORES = 8

ENGS = ("pe", "act", "dve", "pool", "sp")


class Buf:
    __slots__ = ("name", "lw", "rd", "rdd")

    def __init__(self, name):
        self.name = name
        self.lw = None
        self.rd = {}
        self.rdd = []


class Ins:
    __slots__ = ("eng", "fn", "deps", "is_dma", "sig", "sigval", "dsem", "dval")

    def __init__(self, eng, fn, deps, is_dma):
        self.eng = eng
        self.fn = fn
        self.deps = deps
        self.is_dma = is_dma
        self.sig = False
        self.sigval = 0
        self.dsem = None
        self.dval = 0


class Sched:
    NDMA = 16

    def __init__(self, nc):
        self.nc = nc
        self.ins = []
        self.last = {}
        self.dmas = []

    def barrier(self):
        deps = sorted(set(self.last.values()) | set(self.dmas))
        if not deps:
            return
        for e in ENGS:
            self.ins.append(Ins(e, None, list(deps), False))
        self.dmas = []

    def add(self, eng, fn, reads=(), writes=(), is_dma=False):
        deps = set()
        for b in reads:
            if b.lw is not None:
                deps.add(b.lw)
        for b in writes:
            if b.lw is not None:
                deps.add(b.lw)
            deps.update(b.rd.values())
            deps.update(b.rdd)
        i = len(self.ins)
        self.ins.append(Ins(eng, fn, sorted(deps), is_dma))
        self.last[eng] = i
        if is_dma:
            self.dmas.append(i)
        for b in reads:
            if is_dma:
                b.rdd.append(i)
            else:
                b.rd[eng] = i
        for b in writes:
            b.lw = i
            b.rd = {}
            b.rdd = []
        return i

    def finalize(self, final_bufs):
        nc = self.nc
        ins = self.ins
        fdeps = set()
        for b in final_bufs:
            if b.lw is not None:
                fdeps.add(b.lw)
        ins.append(Ins("sp", None, sorted(fdeps), False))
        for it in ins:
            for d in it.deps:
                dd = ins[d]
                if dd.eng == "pe" and it.eng == "pe" and not dd.is_dma and not it.is_dma:
                    continue
                dd.sig = True
        st = contextlib.ExitStack()
        esem = {e: st.enter_context(nc.semaphore(f"s_{e}")) for e in ("pe", "act", "dve", "pool")}
        dq = [e for e in ENGS if any(it.is_dma and it.eng == e for it in ins)]
        per = max(2, self.NDMA // max(1, len(dq)))
        dsems = []
        dpool = {}
        for e in dq:
            dpool[e] = list(range(len(dsems), len(dsems) + per))
            dsems += [st.enter_context(nc.semaphore(f"s_dma_{e}{i}")) for i in range(per)]
        dcount = [0] * len(dsems)
        dlast = [None] * len(dsems)
        ecount = {e: 0 for e in esem}
        rr = {e: 0 for e in dq}
        for k, it in enumerate(ins):
            if it.is_dma:
                s = dpool[it.eng][rr[it.eng] % per]
                rr[it.eng] += 1
                if dlast[s] is not None:
                    it.deps = sorted(set(it.deps) | {dlast[s]})
                dcount[s] += 16
                it.dsem = s
                it.dval = dcount[s]
                dlast[s] = k
            elif it.sig:
                ecount[it.eng] += 1
                it.sigval = ecount[it.eng]
        progs = {e: [] for e in ENGS}
        waited = {e: {} for e in ENGS}
        for it in ins:
            w = {}
            for d in it.deps:
                dd = ins[d]
                if dd.is_dma:
                    key = ("d", dd.dsem)
                    val = dd.dval
                else:
                    if dd.eng == "pe" and it.eng == "pe" and not it.is_dma:
                        continue
                    key = ("e", dd.eng)
                    val = dd.sigval
                if w.get(key, 0) < val:
                    w[key] = val
            wl = []
            for key, val in w.items():
                if waited[it.eng].get(key, 0) >= val:
                    continue
                waited[it.eng][key] = val
                wl.append((dsems[key[1]] if key[0] == "d" else esem[key[1]], val))
            progs[it.eng].append((wl, it))
        self.counts = {e: len(progs[e]) for e in ENGS}
        with nc.Block() as block:
            def runner(name):
                def run(eng):
                    for wl, it in progs[name]:
                        for sem, val in wl:
                            eng.wait_ge(sem, val)
                        if it.fn is None:
                            continue
                        r = it.fn(eng)
                        if it.is_dma:
                            r.then_inc(dsems[it.dsem], 16)
                        elif it.sig:
                            r.then_inc(esem[it.eng], 1)
                return run
            block.tensor(runner("pe"))
            block.scalar(runner("act"))
            block.vector(runner("dve"))
            block.gpsimd(runner("pool"))
            block.sync(runner("sp"))
        st.close()


class Tl:
    __slots__ = ("t", "b")

    def __init__(self, t, name):
        self.t = t
        self.b = Buf(name)

    def __getitem__(self, k):
        return self.t[k]


def _bufs(xs):
    return [x.b if isinstance(x, Tl) else x for x in xs]


class Builder:
    def __init__(self, n_ctx, L_ctx, L_lat, debug=()):
        self.n_ctx, self.L_ctx, self.L_lat = n_ctx, L_ctx, L_lat
        self.NT = n_ctx * L_ctx + L_lat
        self.NG = self.NT // 128
        self.QL = min(1024, L_lat)
        self.NQ = L_lat // self.QL
        self.seqs = [(i * L_ctx, L_ctx, 0) for i in range(n_ctx)] + [(n_ctx * L_ctx, L_lat, 1)]
        self.debug = set(debug)
        self.nc = bass.Bass("TRN2", target_bir_lowering=False)
        self.S = Sched(self.nc)
        self.final = []
        self.uid = 0

    @contextlib.contextmanager
    def scope(self):
        with contextlib.ExitStack() as st:
            yield st
            self.S.barrier()

    def sb(self, st, name, shape, dt):
        self.uid += 1
        return Tl(st.enter_context(self.nc.sbuf_tensor(f"{name}_{self.uid}", shape, dt)), name)

    def ps(self, st, name, shape, dt):
        self.uid += 1
        return Tl(st.enter_context(self.nc.psum_tensor(f"{name}_{self.uid}", shape, dt)), name)

    def din(self, name, shape, dt=F32):
        return self.nc.dram_tensor(name, list(shape), dt, kind="ExternalInput").ap()

    def dout(self, name, shape, dt=F32):
        return self.nc.dram_tensor(name, list(shape), dt, kind="ExternalOutput").ap()

    def dscr(self, name, shape, dt):
        kind = "ExternalOutput" if name in self.debug else "Internal"
        return self.nc.dram_tensor(name, list(shape), dt, kind=kind).ap()

    def mm(self, out, lhsT, rhs, start, stop, reads, writes):
        self.S.add("pe", lambda e: e.matmul(out, lhsT=lhsT, rhs=rhs, start=start, stop=stop),
                   _bufs(reads), _bufs(writes))

    def tr(self, out, in_, ident, reads, writes):
        self.S.add("pe", lambda e: e.transpose(out=out, in_=in_, identity=ident), _bufs(reads), _bufs(writes))

    def act(self, out, in_, func, reads, writes, scale=1.0, bias=0.0):
        self.S.add("act", lambda e: e.activation(out=out, in_=in_, func=func, scale=scale, bias=bias),
                   _bufs(reads), _bufs(writes))

    def tt(self, eng, out, in0, in1, op, reads, writes):
        self.S.add(eng, lambda e: e.tensor_tensor(out=out, in0=in0, in1=in1, op=op), _bufs(reads), _bufs(writes))

    def ts(self, eng, out, in0, s1, s2, op0, op1, reads, writes):
        self.S.add(eng, lambda e: e.tensor_scalar(out=out, in0=in0, scalar1=s1, scalar2=s2, op0=op0, op1=op1),
                   _bufs(reads), _bufs(writes))

    def stt(self, out, in0, scalar, in1, op0, op1, reads, writes):
        self.S.add("dve", lambda e: e.scalar_tensor_tensor(out=out, in0=in0, scalar=scalar, in1=in1, op0=op0, op1=op1),
                   _bufs(reads), _bufs(writes))

    def cp(self, eng, out, in_, reads, writes):
        if eng == "act":
            self.act(out, in_, AF.Copy, reads, writes)
        else:
            self.S.add(eng, lambda e: e.tensor_copy(out=out, in_=in_), _bufs(reads), _bufs(writes))

    def red(self, out, in_, reads, writes):
        self.S.add("dve", lambda e: e.tensor_reduce(out=out, in_=in_, axis=AX.X, op=ALU.add), _bufs(reads), _bufs(writes))

    def recip(self, out, in_, reads, writes):
        self.S.add("dve", lambda e: e.reciprocal(out=out, in_=in_), _bufs(reads), _bufs(writes))

    def memset(self, eng, ap, val, writes):
        self.S.add(eng, lambda e: e.memset(ap, val), [], _bufs(writes))

    def asel(self, ap, pattern, cmp, fill, base, cm, tl):
        self.S.add("pool", lambda e: e.affine_select(out=ap, in_=ap, pattern=pattern, compare_op=cmp, fill=fill,
                                                     base=base, channel_multiplier=cm), [tl.b], [tl.b])

    def dma(self, eng, out, in_, reads, writes):
        nt = getattr(self, "_untracked", None)
        if nt is None:
            nt = self._untracked = set(id(b) for b in list(self.bR.values()) + list(self.bO.values()))
        writes = [w for w in _bufs(writes) if id(w) not in nt]
        reads = [r for r in _bufs(reads) if id(r) not in nt]
        self.S.add(eng, lambda e: e.dma_start(out=out, in_=in_), reads, writes, is_dma=True)

    def dbg(self, name, ap, shape, rd):
        if name in self.debug:
            d = self.dout(name, shape)
            b = Buf(name)
            self.dma("pool", d, ap, [rd], [b])
            self.final.append(b)

    def declare_io(self):
        NT, n_ctx = self.NT, self.n_ctx
        I = {}
        I["xT"] = self.din("xT", [D, NT])
        I["cvec"] = self.din("cvec", [128, 16])
        I["ada_w"] = self.din("ada_w", [D, 6 * D])
        I["adab"] = self.din("adab", [128, 48])
        I["nmw"] = self.din("nmw", [128, 8])
        I["nfw"] = self.din("nfw", [128, 8])
        I["fnw"] = self.din("fnw", [128, 8])
        I["w_in"] = self.din("w_in", [D, PIN])
        I["rwkv_conv"] = self.din("rwkv_conv", [3, PB])
        I["hgrn_lb"] = self.din("hgrn_lb", [1, 2048])
        I["prow"] = self.din("prow", [1, 9 * 512])
        I["w2x"] = self.din("w2x", [32, 3 * 512])
        I["g2"] = self.din("g2", [96, 512])
        I["w_out"] = self.din("w_out", [D, D])
        I["w_gate"] = self.din("w_gate", [D, DFF])
        I["w_up"] = self.din("w_up", [D, DFF])
        I["w_down"] = self.din("w_down", [DFF, D])
        I["convw"] = self.din("convw", [128, NF * 9])
        I["cvb"] = self.din("cvb", [128, NF])
        I["xTo"] = self.din("xTo", [D, self.QL + 128])
        I["selv"] = self.din("selv", [128, self.NQ + 2])
        I["st_h"] = self.din("st_h", [2, 128, 4 * 128])
        I["st_r"] = self.din("st_r", [2, 64, 8 * 64])
        self.I = I
        O = {}
        O["yT"] = self.dout("yT", [D, n_ctx * self.L_ctx + self.QL])
        O["nsh"] = self.dout("nsh", [n_ctx, 2, 128, 512])
        O["nsr"] = self.dout("nsr", [n_ctx, 2, 64, 512])
        self.O = O
        self.bO = {k: Buf("out_" + k) for k in O}
        R = {}
        R["OPS"] = self.dscr("OPS", [NT, 14 * 512], BF16)
        R["CX"] = self.dscr("CX", [NT, 3 * 512], BF16)
        R["GCs"] = self.dscr("GCs", [self.NG, 2, 64, 16], F32)
        R["ECs"] = self.dscr("ECs", [self.NG, 2, 128, 8], F32)
        R["YS"] = self.dscr("YS", [2, NT, 512], F32)
        R["OSC"] = self.dscr("OSC", [2, NT, 512], F32)
        R["X1T"] = self.dscr("X1T", [D, n_ctx * self.L_ctx], F32)
        R["X1W"] = self.dscr("X1W", [D, self.QL + 128], F32)
        self.R = R
        self.bR = {k: Buf("scr_" + k) for k in R}

    def phase0(self, gst0, gst):
        nc, I = self.nc, self.I
        C = {}
        C["ident"] = self.sb(gst0, "ident", [128, 128], BF16)
        C["onesD"] = self.sb(gst0, "onesD", [128, 128], BF16)
        mods = self.sb(gst0, "mods", [128, 6, 2, 8], F32)
        fnw = self.sb(gst0, "fnw", [128, 8], F32)
        C["selv"] = self.sb(gst0, "selv", [128, self.NQ + 2], F32)
        self.dma("sp", C["selv"][:], I["selv"], [], [C["selv"]])
        identf = self.sb(gst, "identf", [128, 128], F32)
        self.memset("pool", identf[:], 0.0, [identf])
        self.asel(identf[:], [[-1, 128]], ALU.not_equal, 1.0, 0, 1, identf)
        self.cp("dve", C["ident"][:], identf[:], [identf], [C["ident"]])
        self.memset("pool", C["onesD"][:], 1.0 / D, [C["onesD"]])
        trif = self.sb(gst, "trif", [128, 128], F32)
        self.memset("pool", trif[:], 1.0, [trif])
        self.asel(trif[:], [[1, 128]], ALU.is_ge, 0.0, 0, -1, trif)
        self.asel(trif[:, 64:128], [[0, 64]], ALU.is_ge, 0.0, -64, 1, trif)
        trib = self.sb(gst, "trib", [128, 128], F32)
        self.memset("pool", trib[:], 1.0, [trib])
        self.asel(trib[:], [[-1, 128]], ALU.is_ge, 0.0, 0, 1, trib)
        self.asel(trib[:, 0:64], [[0, 64]], ALU.is_ge, 0.0, 63, -1, trib)
        C["tri"] = [trif, trib]
        C["identf"] = identf
        cind = self.sb(gst, "cind", [128, 2], F32)
        self.memset("pool", cind[:], 1.0, [cind])
        self.asel(cind[:, 0:1], [[0, 1]], ALU.is_ge, 0.0, 63, -1, cind)
        self.asel(cind[:, 1:2], [[0, 1]], ALU.is_ge, 0.0, -64, 1, cind)
        C["cind"] = cind
        M1 = self.sb(gst, "M1", [128, 128], F32)
        M3 = self.sb(gst, "M3", [128, 64], F32)
        MI = self.sb(gst, "MI", [128, 64], F32)
        for m in (M1, M3, MI):
            self.memset("pool", m[:], 1.0, [m])
        self.asel(M1[0:64, 0:64], [[1, 64]], ALU.is_gt, 0.0, 0, -1, M1)
        self.asel(M1[0:64, 64:128], [[1, 64]], ALU.is_ge, 0.0, 0, -1, M1)
        self.asel(M1[64:128, 0:64], [[-1, 64]], ALU.is_gt, 0.0, 0, 1, M1)
        self.asel(M1[64:128, 64:128], [[-1, 64]], ALU.is_ge, 0.0, 0, 1, M1)
        self.asel(M3[0:64, :], [[-1, 64]], ALU.is_gt, 0.0, 0, 1, M3)
        self.asel(M3[64:128, :], [[1, 64]], ALU.is_gt, 0.0, 0, -1, M3)
        self.asel(MI[0:64, :], [[1, 64]], ALU.is_ge, 0.0, 0, -1, MI)
        self.asel(MI[64:128, :], [[-1, 64]], ALU.is_ge, 0.0, 0, 1, MI)
        C["M1"], C["M3"], C["MI"] = M1, M3, MI
        self.dbg("dbg_M1", M1[:], [128, 128], M1)
        self.dbg("dbg_trif", trif[:], [128, 128], trif)
        self.dbg("dbg_trib", trib[:], [128, 128], trib)

        modT = self.sb(gst, "modT", [128, 48, 2], F32)
        lb = self.sb(gst, "lb", [128, 1024], F32)
        omlb = self.sb(gst, "omlb", [128, 1024], F32)
        prow = self.sb(gst, "prow", [128, 9 * 512], F32)
        omka = self.sb(gst, "omka", [128, 512], F32)
        with self.scope() as st:
            cv = self.sb(st, "cv", [128, 16], F32)
            self.dma("sp", cv[:], I["cvec"], [], [cv])
            scv = self.sb(st, "scv", [128, 16], F32)
            self.act(scv[:], cv[:], AF.Silu, [cv], [scv])
            adab = self.sb(st, "adab", [128, 48], F32)
            self.dma("sp", adab[:], I["adab"], [], [adab])
            acc = self.sb(st, "modacc", [128, 96], F32)
            abuf = [self.sb(st, f"adaw{i}", [128, 3 * D], F32) for i in range(2)]
            pm = self.ps(st, "pmod", [128, 512], F32)
            adv = I["ada_w"].rearrange("(k p) c -> k p c", p=128)
            it = 0
            for k in range(KD):
                for hf in range(2):
                    ab = abuf[it % 2]
                    it += 1
                    self.dma("sp", ab[:, :], adv[k][:, hf * 3072:(hf + 1) * 3072], [], [ab])
                    for mm_ in range(24):
                        m = hf * 24 + mm_
                        self.mm(pm[:, 2 * m:2 * m + 2], ab[:, mm_ * 128:(mm_ + 1) * 128], scv[:, 2 * k:2 * k + 2], True, True,
                                [ab, scv], [pm])
                if k == 0:
                    self.cp("dve", acc[:], pm[:, 0:96], [pm], [acc])
                else:
                    self.tt("dve", acc[:], pm[:, 0:96], acc[:], ALU.add, [pm, acc], [acc])
            self.tt("dve", modT[:], acc[:].rearrange("p (m s) -> p m s", s=2),
                    adab[:].unsqueeze(2).broadcast_to([128, 48, 2]), ALU.add, [acc, adab], [modT])
            self.dbg("dbg_modT", modT[:].rearrange("p m s -> p (m s)"), [128, 96], modT)
            nmw = self.sb(st, "nmw", [128, 8], F32)
            nfw = self.sb(st, "nfw", [128, 8], F32)
            self.dma("sp", nmw[:], I["nmw"], [], [nmw])
            self.dma("sp", nfw[:], I["nfw"], [], [nfw])
            for s in range(2):
                for (wi, off, nw) in ((0, 8, nmw), (3, 32, nfw)):
                    self.stt(mods[:, wi, s, :], modT[:, off:off + 8, s], 1.0, nw[:], ALU.add, ALU.mult,
                             [modT, nw], [mods])
                for (wi, off) in ((1, 0), (2, 16), (4, 24), (5, 40)):
                    self.cp("dve", mods[:, wi, s, :], modT[:, off:off + 8, s], [modT], [mods])
            C["mods"] = mods
            self.dma("sp", fnw[:], I["fnw"], [], [fnw])
            C["fnw"] = fnw
            lbr = self.sb(st, "lbraw", [128, 2048], F32)
            self.dma("sp", lbr[:], I["hgrn_lb"].partition_broadcast(128), [], [lbr])
            lbd = self.sb(st, "lbd", [128, 1024], F32)
            self.tt("dve", lbd[:], lbr[:, 0:1024], lbr[:, 1024:2048], ALU.subtract, [lbr], [lbd])
            self.act(lb[:], lbd[:], AF.Sigmoid, [lbd], [lb])
            self.ts("dve", omlb[:], lb[:], -1.0, 1.0, ALU.mult, ALU.add, [lb], [omlb])
            C["lb"], C["omlb"] = lb, omlb
            self.dma("sp", prow[:], I["prow"].partition_broadcast(128), [], [prow])
            C["prow"] = prow
            self.ts("dve", omka[:], prow[:, 1024:1536], -1.0, 1.0, ALU.mult, ALU.add, [prow], [omka])
            C["omka"] = omka
        self.C = C

    def rms_rstd(self, st_tiles, src, n, pbank, rstd):
        sq = st_tiles["sq"]
        self.act(sq[:, 0:8 * n], src[:, 0:8 * n], AF.Square, [src], [sq])
        c0 = 0
        while c0 < n:
            cn = min(512, n - c0)
            for k in range(KD):
                self.mm(pbank[:, 0:cn], self.C["onesD"][:], sq[:, k * n + c0:k * n + c0 + cn], k == 0, k == KD - 1,
                        [self.C["onesD"], sq], [pbank])
            self.act(rstd[:, c0:c0 + cn], pbank[:, 0:cn], AF.Sqrt, [pbank], [rstd], bias=RMS_EPS)
            c0 += cn
        self.recip(rstd[:, 0:n], rstd[:, 0:n], [rstd], [rstd])

    def norm_mod(self, tiles, xt, n, pbank, A, B, hT, modtl):
        rstd = tiles["rstd"]
        self.rms_rstd(tiles, xt, n, pbank, rstd)
        for k in range(KD):
            tmp = tiles["ntmp"][k % 2]
            self.tt("dve", tmp[:, 0:n], xt[:, k * n:(k + 1) * n], rstd[:, 0:n], ALU.mult, [xt, rstd], [tmp])
            self.act(hT[:, k * n:(k + 1) * n], tmp[:, 0:n], AF.Identity, [tmp, modtl], [hT],
                     scale=A[:, k:k + 1], bias=B[:, k:k + 1])

    def phaseA(self):
        nc, I, C, R = self.nc, self.I, self.C, self.R
        mods = C["mods"]
        xTv = I["xT"].rearrange("(k p) t -> p k t", p=128)
        import os as _os
        _rnew = "1"
        for part in (("H",) if _rnew == "0" else ("H", "R")):
            halo = 0 if part == "H" else 1
            nw = 128 + 2 * halo
            with self.scope() as st:
                X = {"part": part, "nw": nw, "halo": halo}
                if part == "H":
                    X["lf"] = [self.sb(st, f"lf{i}", [128, 512], F32) for i in range(2)]
                else:
                    X["sg"] = [self.sb(st, f"sg{i}", [128, 512], F32) for i in range(2)]
                    X["lt"] = [self.sb(st, f"lt{i}", [128, 192], F32) for i in range(2)]
                if part == "H":
                    WH = self.sb(st, "WH", [128, 8 * PA], BF16)
                    wv = I["w_in"].rearrange("(k p) c -> p k c", p=128)
                    for k in range(KD):
                        self.dma("pool", WH[:, k * PA:(k + 1) * PA], wv[:, k, 0:PA], [], [WH])
                    X["WH"] = WH
                else:
                    WR = [self.sb(st, f"WR{t}", [128, 8 * PB], BF16) for t in range(3)]
                    with self.scope() as st2:
                        cvbc = self.sb(st2, "cvbc", [128, 3 * PB], F32)
                        self.dma("sp", cvbc[:], I["rwkv_conv"].rearrange("(o a) c -> o (a c)", o=1).partition_broadcast(128),
                                 [], [cvbc])
                        stg = [self.sb(st2, f"wstg{i}", [128, PB], F32) for i in range(2)]
                        wv = I["w_in"].rearrange("(k p) c -> p k c", p=128)
                        for k in range(KD):
                            sg = stg[k % 2]
                            self.dma("sp", sg[:], wv[:, k, PA:PIN], [], [sg])
                            for t in range(3):
                                self.tt("dve" if t != 1 else "pool", WR[t][:, k * PB:(k + 1) * PB], sg[:],
                                        cvbc[:, t * PB:(t + 1) * PB], ALU.mult, [sg, cvbc], [WR[t]])
                    X["WR"] = WR
                    X["W2x"] = self.sb(st, "W2x", [32, 1536], BF16)
                    self.dma("pool", X["W2x"][:], I["w2x"], [], [X["W2x"]])
                    X["G2"] = self.sb(st, "G2", [96, 512], BF16)
                    self.dma("pool", X["G2"][:], I["g2"], [], [X["G2"]])
                    X["lw"] = self.sb(st, "lw", [32, 384], BF16)
                    X["lg"] = self.sb(st, "lg", [96, 128], BF16)
                X["tiles"] = {
                    "sq": self.sb(st, "sq", [128, 8 * nw], BF16),
                    "rstd": self.sb(st, "rstd", [128, nw], F32),
                    "ntmp": [self.sb(st, f"ntmp{i}", [128, nw], F32) for i in range(2)],
                }
                X["xts"] = [self.sb(st, f"xt{i}", [128, 8 * nw], F32) for i in range(2)]
                X["hTs"] = [self.sb(st, f"hT{i}", [128, 8 * nw], BF16) for i in range(2)]
                X["OPst"] = [self.sb(st, f"OPst{i}", [128, 14 * 512], BF16) for i in range(2)]
                X["CXst"] = [self.sb(st, f"CXst{i}", [128, 3 * 512], BF16) for i in range(2)]
                X["dcs"] = [self.sb(st, f"dcs{i}", [128, 32], F32) for i in range(2)]
                X["sm"] = [self.sb(st, f"sm{i}", [128, 8], F32) for i in range(4)]
                if part == "H":
                    X["qs"] = [self.sb(st, f"qs{i}", [128, 512], F32) for i in range(2)]
                    X["sgz"] = [[self.sb(st, f"sgz{i}{d}", [128, 512], F32) for d in range(2)] for i in range(2)]
                    X["ft"] = [self.sb(st, f"ft{i}", [128, 512], F32) for i in range(6)]
                    X["pss"] = self.ps(st, "pss", [128, 512], F32)
                    X["pp"] = [self.ps(st, f"pp{i}", [128, 512], F32) for i in range(5)]
                    X["LA"] = self.ps(st, "LA", [128, 512], F32)
                    X["LB"] = self.ps(st, "LB", [128, 512], F32)
                else:
                    X["rkv"] = [[self.sb(st, f"rkv{i}{j}", [128, 512], F32) for j in range(3)] for i in range(2)]
                    X["ft"] = [self.sb(st, f"ft{i}", [128, 512], F32) for i in range(9)]
                    X["pss"] = self.ps(st, "pss", [128, 512], F32)
                    X["pp"] = [self.ps(st, f"pp{i}", [128, 512], F32) for i in range(4)]
                    X["LA"] = self.ps(st, "LA", [128, 512], F32)
                    X["LB"] = self.ps(st, "LB", [128, 512], F32)
                    X["LC"] = self.ps(st, "LC", [128, 512], F32)
                if not (part == "R" and _rnew == "2"):
                    self.A_normproj(X, 0)
                for g in range(self.NG):
                    if part == "R" and _rnew == "2":
                        self.A_normproj(X, g)
                        self.A_early(X, g)
                        self.A_late_R(X, g)
                        continue
                    self.A_early(X, g)
                    if g + 1 < self.NG:
                        self.A_normproj(X, g + 1)
                    if part == "H":
                        self.A_late_H(X, g)
                    else:
                        self.A_late_R(X, g)
        if _rnew == "0":
            self.phaseA_oldR()

    def A_normproj(self, X, g):
        I, C = self.I, self.C
        mods = C["mods"]
        xTv = I["xT"].rearrange("(k p) t -> p k t", p=128)
        halo, nw = X["halo"], X["nw"]
        t0 = g * 128
        s0, sl, kind = [s for s in self.seqs if s[0] <= t0 < s[0] + s[1]][0]
        s1 = s0 + sl
        lo, hi = max(t0 - halo, s0), min(t0 + 128 + halo, s1)
        off = lo - (t0 - halo)
        n = hi - lo
        xt, hT = X["xts"][g % 2], X["hTs"][g % 2]
        x3 = xt[:, :].rearrange("p (k t) -> p k t", k=8)
        if n < nw:
            self.memset("dve", xt[:, :], 0.0, [xt])
        self.dma("sp", x3[:, :, off:off + n], xTv[:, :, lo:hi], [], [xt])
        self.norm_mod(X["tiles"], xt, nw, X["pss"], mods[:, 0, kind, :], mods[:, 1, kind, :], hT, mods)
        h3 = hT[:, :].rearrange("p (k t) -> p k t", k=8)
        if off > 0:
            self.memset("dve", h3[:, :, 0:off], 0.0, [hT])
        if off + n < nw:
            self.memset("dve", h3[:, :, off + n:nw], 0.0, [hT])
        pp = X["pp"]
        if X["part"] == "H":
            WH = X["WH"]
            for cg in range(5):
                for k in range(KD):
                    self.mm(pp[cg][:, :], hT[:, k * nw:k * nw + 128], WH[:, k * PA + cg * 512:k * PA + (cg + 1) * 512],
                            k == 0, k == KD - 1, [hT, WH], [pp[cg]])
        else:
            WR = X["WR"]
            for cg in range(4):
                c0 = cg * 512
                cn = 512 if cg < 3 else 192
                for t in range(3):
                    for k in range(KD):
                        self.mm(pp[cg][:, 0:cn], hT[:, k * nw + t:k * nw + t + 128], WR[t][:, k * PB + c0:k * PB + c0 + cn],
                                t == 0 and k == 0, t == 2 and k == KD - 1, [hT, WR[t]], [pp[cg]])

    def A_early(self, X, g):
        pp = X["pp"]
        b = g % 2
        ops, cxs = X["OPst"][b], X["CXst"][b]
        if X["part"] == "H":
            self.act(X["qs"][b][:], pp[0][:], AF.Silu, [pp[0]], [X["qs"][b]])
            self.act(ops[:, 13 * 512:14 * 512], pp[1][:], AF.Copy, [pp[1]], [ops])
            self.act(X["sgz"][b][0][:], pp[2][:], AF.Sigmoid, [pp[2]], [X["sgz"][b][0]])
            self.act(X["sgz"][b][1][:], pp[3][:], AF.Sigmoid, [pp[3]], [X["sgz"][b][1]])
            self.act(cxs[:, 1024:1536], pp[4][:], AF.Silu, [pp[4]], [cxs])
        else:
            lt = X["lt"][b]
            self.act(lt[:, 0:64], pp[3][:, 0:64], AF.Tanh, [pp[3]], [lt])
            self.act(lt[:, 64:96], pp[3][:, 64:96], AF.Copy, [pp[3]], [lt])
            self.act(lt[:, 96:192], pp[3][:, 96:192], AF.Sigmoid, [pp[3]], [lt])
            r_, k_, v_ = X["rkv"][b]
            self.cp("dve", r_[:], pp[0][:], [pp[0]], [r_])
            self.cp("act", k_[:], pp[1][:], [pp[1]], [k_])
            self.cp("dve", v_[:], pp[2][:], [pp[2]], [v_])

    def A_late_H(self, X, g):
        C, R = self.C, self.R
        OPSd, CXd = R["OPS"], R["CX"]
        b = g % 2
        t0 = g * 128
        ops, cxs, dcs = X["OPst"][b], X["CXst"][b], X["dcs"][b]
        qs = X["qs"][b]
        ft = X["ft"]
        LA, LB = X["LA"], X["LB"]
        for d in range(2):
            sg_ = X["sgz"][b][d]
            lf = X["lf"][d]
            f_, kd, csc, e1, e2 = ft[0], ft[1 + d], ft[3], ft[4], ft[5]
            self.tt("dve", f_[:], sg_[:], C["omlb"][:, d * 512:(d + 1) * 512], ALU.mult, [sg_, C["omlb"]], [f_])
            self.tt("dve", f_[:], f_[:], C["lb"][:, d * 512:(d + 1) * 512], ALU.add, [f_, C["lb"]], [f_])
            self.act(lf[:], f_[:], AF.Ln, [f_], [lf])
            self.ts("dve", kd[:], f_[:], -1.0, 1.0, ALU.mult, ALU.add, [f_], [kd])
            self.mm(LA[:], C["tri"][d][:], lf[:], True, True, [C["tri"][d], lf], [LA])
            self.ts("dve", csc[:], LA[:], -80.0, None, ALU.max, ALU.bypass, [LA], [csc])
            self.act(e1[:], csc[:], AF.Exp, [csc], [e1])
            self.act(e2[:], csc[:], AF.Exp, [csc], [e2], scale=-1.0)
            self.tt("dve", ops[:, (6 * d + 4) * 512:(6 * d + 5) * 512], qs[:], e1[:], ALU.mult, [qs, e1], [ops])
            self.tt("dve", ops[:, (6 * d + 5) * 512:(6 * d + 6) * 512], kd[:], e2[:], ALU.mult, [kd, e2], [ops])
            for h in range(4):
                c = (d * 4 + h) * 2
                self.mm(LB[:, c:c + 2], lf[:, h * 128:(h + 1) * 128], C["cind"][:], True, True, [lf, C["cind"]], [LB])
        self.act(dcs[:, 0:16].rearrange("p (c x) -> p c x", c=2), LB[:, 0:16].rearrange("p (x c) -> p c x", c=2),
                 AF.Exp, [LB], [dcs])
        self.dma("pool", R["ECs"][g].rearrange("c p x -> p c x"), dcs[:, 0:16].rearrange("p (c x) -> p c x", c=2),
                 [dcs], [self.bR["ECs"]])
        o3 = ops[:, :].rearrange("p (s c) -> p s c", s=14)
        OP3 = OPSd[t0:t0 + 128, :].rearrange("t (s c) -> t s c", s=14)
        self.dma("pool", OP3[:, 4:6, :], o3[:, 4:6, :], [ops], [self.bR["OPS"]])
        self.dma("pool", OP3[:, 10:12, :], o3[:, 10:12, :], [ops], [self.bR["OPS"]])
        self.dma("pool", OP3[:, 13:14, :], o3[:, 13:14, :], [ops], [self.bR["OPS"]])
        self.dma("pool", CXd[t0:t0 + 128, 1024:1536], cxs[:, 1024:1536], [cxs], [self.bR["CX"]])

    def A_late_R(self, X, g):
        C, R = self.C, self.R
        OPSd, CXd = R["OPS"], R["CX"]
        prow = C["prow"]
        kk_bc, ka_bc, rk_bc = prow[:, 512:1024], prow[:, 1024:1536], prow[:, 1536:2048]
        b = g % 2
        t0 = g * 128
        ops, cxs, dcs = X["OPst"][b], X["CXst"][b], X["dcs"][b]
        r_, k_, v_ = X["rkv"][b]
        lt, lw, lg, W2x, G2 = X["lt"][b], X["lw"], X["lg"], X["W2x"], X["G2"]
        ft, sm = X["ft"], X["sm"]
        LA, LB, LC = X["LA"], X["LB"], X["LC"]
        identf = C["identf"]
        for i in range(3):
            self.tr(LC[0:32, i * 128:(i + 1) * 128], lt[:, i * 32:(i + 1) * 32], identf[:], [lt, identf], [LC])
        self.tr(LC[0:96, 384:512], lt[:, 96:192], identf[:], [lt, identf], [LC])
        self.cp("dve", lw[0:32, :], LC[0:32, 0:384], [LC], [lw])
        self.cp("dve", lg[:, :], LC[0:96, 384:512], [LC], [lg])
        a_, sgf, sgb = ft[0], X["sg"][0], X["sg"][1]
        self.mm(LA[:], lw[0:32, 256:384], W2x[0:32, 1024:1536], True, True, [lw, W2x], [LA])
        self.mm(LB[:], lw[0:32, 128:256], W2x[0:32, 512:1024], True, True, [lw, W2x], [LB])
        self.mm(LC[:], lg[:, :], G2[:, :], True, True, [lg, G2], [LC])
        self.tt("dve", a_[:], LA[:], prow[:, 4096:4608], ALU.add, [LA, prow], [a_])
        self.act(a_[:], a_[:], AF.Sigmoid, [a_], [a_])
        self.mm(LA[:], lw[0:32, 0:128], W2x[0:32, 0:512], True, True, [lw, W2x], [LA])
        self.tt("dve", sgb[:], LB[:], prow[:, 3584:4096], ALU.add, [LB, prow], [sgb])
        self.act(sgb[:], sgb[:], AF.Sigmoid, [sgb], [sgb])
        self.act(cxs[:, 0:512], LC[:], AF.Copy, [LC], [cxs])
        self.tt("dve", sgf[:], LA[:], prow[:, 3072:3584], ALU.add, [LA, prow], [sgf])
        self.act(sgf[:], sgf[:], AF.Sigmoid, [sgf], [sgf])
        kx, sq, kk, t1, kp, nb, t2 = ft[1], ft[2], ft[3], ft[4], ft[5], ft[6], ft[4]
        self.tt("dve", kx[:], k_[:], kk_bc, ALU.mult, [k_, prow], [kx])
        self.act(sq[:], kx[:], AF.Square, [kx], [sq])
        self.red(sm[0][:], sq[:].rearrange("p (h j) -> p h j", h=8), [sq], [sm[0]])
        self.act(sm[1][:], sm[0][:], AF.Sqrt, [sm[0]], [sm[1]], bias=1e-12)
        self.recip(sm[1][:], sm[1][:], [sm[1]], [sm[1]])
        self.tt("dve", kk[:].rearrange("p (h j) -> p h j", h=8), kx[:].rearrange("p (h j) -> p h j", h=8),
                sm[1][:].unsqueeze(2).broadcast_to([128, 8, 64]), ALU.mult, [kx, sm[1]], [kk])
        self.tt("dve", t1[:], a_[:], ka_bc, ALU.mult, [a_, prow], [t1])
        self.tt("dve", t1[:], t1[:], C["omka"][:], ALU.add, [t1, C["omka"]], [t1])
        self.tt("dve", kp[:], k_[:], t1[:], ALU.mult, [k_, t1], [kp])
        self.stt(nb[:], kk[:], -1.0, a_[:], ALU.mult, ALU.mult, [kk, a_], [nb])
        self.tt("dve", t2[:], r_[:], kp[:], ALU.mult, [r_, kp], [t2])
        self.tt("dve", t2[:], t2[:], rk_bc, ALU.mult, [t2, prow], [t2])
        self.red(sm[2][:], t2[:].rearrange("p (h j) -> p h j", h=8), [t2], [sm[2]])
        self.tt("dve", cxs[:, 512:1024].rearrange("p (h j) -> p h j", h=8), v_[:].rearrange("p (h j) -> p h j", h=8),
                sm[2][:].unsqueeze(2).broadcast_to([128, 8, 64]), ALU.mult, [v_, sm[2]], [cxs])
        self.cp("act", ops[:, 12 * 512:13 * 512], v_[:], [v_], [ops])
        for d in range(2):
            sg_ = (sgf, sgb)[d]
            pc = (LA, LB)[d]
            gi, ginv, tmp, ge = ft[7], ft[8], ft[2], ft[1]
            self.mm(pc[:], C["tri"][d][:], sg_[:], True, True, [C["tri"][d], sg_], [pc])
            self.act(gi[:], pc[:], AF.Exp, [pc], [gi], scale=-DECAY)
            self.act(ginv[:], pc[:], AF.Exp, [pc], [ginv], scale=DECAY)
            self.tt("dve", tmp[:], pc[:], sg_[:], ALU.subtract, [pc, sg_], [tmp])
            self.act(ge[:], tmp[:], AF.Exp, [tmp], [ge], scale=-DECAY)
            b0 = 6 * d * 512
            self.tt("dve", ops[:, b0:b0 + 512], r_[:], gi[:], ALU.mult, [r_, gi], [ops])
            self.tt("dve", ops[:, b0 + 512:b0 + 1024], kp[:], ginv[:], ALU.mult, [kp, ginv], [ops])
            self.tt("dve", ops[:, b0 + 1024:b0 + 1536], nb[:], ginv[:], ALU.mult, [nb, ginv], [ops])
            self.tt("dve", ops[:, b0 + 1536:b0 + 2048], kk[:], ge[:], ALU.mult, [kk, ge], [ops])
            for h in range(8):
                c = (d * 8 + h) * 2
                self.mm(LC[0:64, c:c + 2], sg_[:, h * 64:(h + 1) * 64], C["cind"][:], True, True, [sg_, C["cind"]], [LC])
        self.act(dcs[0:64, 0:32].rearrange("p (c x) -> p c x", c=2), LC[0:64, 0:32].rearrange("p (x c) -> p c x", c=2),
                 AF.Exp, [LC], [dcs], scale=-DECAY)
        self.dma("pool", R["GCs"][g].rearrange("c p x -> p c x"), dcs[0:64, 0:32].rearrange("p (c x) -> p c x", c=2),
                 [dcs], [self.bR["GCs"]])
        o3 = ops[:, :].rearrange("p (s c) -> p s c", s=14)
        OP3 = OPSd[t0:t0 + 128, :].rearrange("t (s c) -> t s c", s=14)
        self.dma("pool", OP3[:, 0:4, :], o3[:, 0:4, :], [ops], [self.bR["OPS"]])
        self.dma("pool", OP3[:, 6:10, :], o3[:, 6:10, :], [ops], [self.bR["OPS"]])
        self.dma("pool", OP3[:, 12:13, :], o3[:, 12:13, :], [ops], [self.bR["OPS"]])
        self.dma("pool", CXd[t0:t0 + 128, 0:1024], cxs[:, 0:1024], [cxs], [self.bR["CX"]])

    def phaseA_oldR(self):
        nc, I, C, R = self.nc, self.I, self.C, self.R
        mods = C["mods"]
        xTv = I["xT"].rearrange("(k p) t -> p k t", p=128)
        for part in ("R",):
            halo = 0 if part == "H" else 1
            nw = 128 + 2 * halo
            with self.scope() as st:
                nF = 14
                ft = [self.sb(st, f"ft{i}", [128, 512], F32) for i in range(nF)]
                if part == "H":
                    WH = self.sb(st, "WH", [128, 8 * PA], BF16)
                    wv = I["w_in"].rearrange("(k p) c -> p k c", p=128)
                    for k in range(KD):
                        self.dma("pool", WH[:, k * PA:(k + 1) * PA], wv[:, k, 0:PA], [], [WH])
                else:
                    WR = [self.sb(st, f"WR{t}", [128, 8 * PB], BF16) for t in range(3)]
                    with self.scope() as st2:
                        cvbc = self.sb(st2, "cvbc", [128, 3 * PB], F32)
                        self.dma("sp", cvbc[:], I["rwkv_conv"].rearrange("(o a) c -> o (a c)", o=1).partition_broadcast(128),
                                 [], [cvbc])
                        stg = [self.sb(st2, f"wstg{i}", [128, PB], F32) for i in range(2)]
                        wv = I["w_in"].rearrange("(k p) c -> p k c", p=128)
                        for k in range(KD):
                            sg = stg[k % 2]
                            self.dma("sp", sg[:], wv[:, k, PA:PIN], [], [sg])
                            for t in range(3):
                                self.tt("dve" if t != 1 else "pool", WR[t][:, k * PB:(k + 1) * PB], sg[:],
                                        cvbc[:, t * PB:(t + 1) * PB], ALU.mult, [sg, cvbc], [WR[t]])
                    for t in range(3):
                        self.dbg(f"dbg_WR{t}", WR[t][:, 0:2 * PB], [128, 2 * PB], WR[t])
                    W2x = self.sb(st, "W2x", [32, 1536], BF16)
                    self.dma("pool", W2x[:], I["w2x"], [], [W2x])
                    G2 = self.sb(st, "G2", [96, 512], BF16)
                    self.dma("pool", G2[:], I["g2"], [], [G2])
                    lw = self.sb(st, "lw", [32, 384], BF16)
                    lg = self.sb(st, "lg", [96, 128], BF16)
                    lt = self.sb(st, "lt", [128, 192], BF16)
                tiles = {
                    "sq": self.sb(st, "sq", [128, 8 * nw], BF16),
                    "rstd": self.sb(st, "rstd", [128, nw], F32),
                    "ntmp": [self.sb(st, f"ntmp{i}", [128, nw], F32) for i in range(2)],
                }
                xts = [self.sb(st, f"xt{i}", [128, 8 * nw], F32) for i in range(2)]
                hTs = [self.sb(st, f"hT{i}", [128, 8 * nw], BF16) for i in range(2)]
                OPst = [self.sb(st, f"OPst{i}", [128, 14 * 512], BF16) for i in range(2)]
                CXst = [self.sb(st, f"CXst{i}", [128, 3 * 512], BF16) for i in range(2)]
                sm = [self.sb(st, f"sm{i}", [128, 8], F32) for i in range(4)]
                dcs = [self.sb(st, f"dcs{i}", [128, 32], F32) for i in range(2)]
                if part == "H":
                    pb = [self.ps(st, f"pb{i}", [128, 512], F32) for i in range(8)]
                else:
                    pb = [self.ps(st, f"pb{i}", [128, 512], F32) for i in range(7)]
                    pbT = self.ps(st, "pbT", [128, 1024], BF16)
                for g in range(self.NG):
                    t0 = g * 128
                    s0, sl, kind = [s for s in self.seqs if s[0] <= t0 < s[0] + s[1]][0]
                    s1 = s0 + sl
                    lo, hi = max(t0 - halo, s0), min(t0 + 128 + halo, s1)
                    off = lo - (t0 - halo)
                    n = hi - lo
                    xt, hT = xts[g % 2], hTs[g % 2]
                    ops, cxs = OPst[g % 2], CXst[g % 2]
                    x3 = xt[:, :].rearrange("p (k t) -> p k t", k=8)
                    if n < nw:
                        self.memset("dve", xt[:, :], 0.0, [xt])
                    self.dma("sp", x3[:, :, off:off + n], xTv[:, :, lo:hi], [], [xt])
                    self.norm_mod(tiles, xt, nw, pb[0], mods[:, 0, kind, :], mods[:, 1, kind, :], hT, mods)
                    h3 = hT[:, :].rearrange("p (k t) -> p k t", k=8)
                    if off > 0:
                        self.memset("dve", h3[:, :, 0:off], 0.0, [hT])
                    if off + n < nw:
                        self.memset("dve", h3[:, :, off + n:nw], 0.0, [hT])
                    if part == "H":
                        self.phaseA_H(g, t0, hT, nw, WH, pb, ft, dcs[g % 2], ops, cxs)
                    else:
                        self.phaseA_R_old(g, t0, hT, nw, WR, W2x, G2, lw, lg, lt, pb, pbT, ft, sm, dcs[g % 2], ops, cxs)


    def phaseA_R_old(self, g, t0, hT, nw, WR, W2x, G2, lw, lg, lt, pb, pbT, ft, sm, dcs, ops, cxs):
        C, R = self.C, self.R
        OPSd, CXd = R["OPS"], R["CX"]
        prow = C["prow"]
        kk_bc, ka_bc, rk_bc = prow[:, 512:1024], prow[:, 1024:1536], prow[:, 1536:2048]
        for cg in range(4):
            c0 = cg * 512
            cn = 512 if cg < 3 else 192
            for t in range(3):
                for k in range(KD):
                    self.mm(pb[1 + cg][:, 0:cn], hT[:, k * nw + t:k * nw + t + 128], WR[t][:, k * PB + c0:k * PB + c0 + cn],
                            t == 0 and k == 0, t == 2 and k == KD - 1, [hT, WR[t]], [pb[1 + cg]])
        pr, pk, pv, pl = pb[1], pb[2], pb[3], pb[4]
        self.act(lt[:, 0:64], pl[:, 0:64], AF.Tanh, [pl], [lt])
        self.act(lt[:, 64:96], pl[:, 64:96], AF.Copy, [pl], [lt])
        self.act(lt[:, 96:192], pl[:, 96:192], AF.Sigmoid, [pl], [lt])
        ident = C["ident"]
        for i in range(3):
            self.tr(pbT[0:32, i * 128:(i + 1) * 128], lt[:, i * 32:(i + 1) * 32], ident[:], [lt, ident], [pbT])
        self.tr(pbT[0:96, 384:512], lt[:, 96:192], ident[:], [lt, ident], [pbT])
        self.cp("dve", lw[0:32, :], pbT[0:32, 0:384], [pbT], [lw])
        self.cp("dve", lg[:, :], pbT[0:96, 384:512], [pbT], [lg])
        pa, pwf, pwb, pg = pb[4], pb[5], pb[6], pb[0]
        self.mm(pa[:], lw[0:32, 256:384], W2x[0:32, 1024:1536], True, True, [lw, W2x], [pa])
        self.mm(pwf[:], lw[0:32, 0:128], W2x[0:32, 0:512], True, True, [lw, W2x], [pwf])
        self.mm(pwb[:], lw[0:32, 128:256], W2x[0:32, 512:1024], True, True, [lw, W2x], [pwb])
        self.mm(pg[:], lg[:, :], G2[:, :], True, True, [lg, G2], [pg])
        a_, sgf, sgb = ft[0], ft[1], ft[2]
        self.tt("dve", a_[:], pa[:], prow[:, 4096:4608], ALU.add, [pa, prow], [a_])
        self.act(a_[:], a_[:], AF.Sigmoid, [a_], [a_])
        self.tt("dve", sgf[:], pwf[:], prow[:, 3072:3584], ALU.add, [pwf, prow], [sgf])
        self.act(sgf[:], sgf[:], AF.Sigmoid, [sgf], [sgf])
        self.tt("dve", sgb[:], pwb[:], prow[:, 3584:4096], ALU.add, [pwb, prow], [sgb])
        self.act(sgb[:], sgb[:], AF.Sigmoid, [sgb], [sgb])
        self.act(cxs[:, 0:512], pg[:], AF.Copy, [pg], [cxs])
        kx, sq, kk, t1, kp, nb, t2 = ft[3], ft[4], ft[5], ft[6], ft[7], ft[8], ft[9]
        self.tt("dve", kx[:], pk[:], kk_bc, ALU.mult, [pk, prow], [kx])
        self.act(sq[:], kx[:], AF.Square, [kx], [sq])
        self.red(sm[0][:], sq[:].rearrange("p (h j) -> p h j", h=8), [sq], [sm[0]])
        self.act(sm[1][:], sm[0][:], AF.Sqrt, [sm[0]], [sm[1]], bias=1e-12)
        self.recip(sm[1][:], sm[1][:], [sm[1]], [sm[1]])
        self.tt("dve", kk[:].rearrange("p (h j) -> p h j", h=8), kx[:].rearrange("p (h j) -> p h j", h=8),
                sm[1][:].unsqueeze(2).broadcast_to([128, 8, 64]), ALU.mult, [kx, sm[1]], [kk])
        self.tt("dve", t1[:], a_[:], ka_bc, ALU.mult, [a_, prow], [t1])
        self.tt("dve", t1[:], t1[:], C["omka"][:], ALU.add, [t1, C["omka"]], [t1])
        self.tt("dve", kp[:], pk[:], t1[:], ALU.mult, [pk, t1], [kp])
        self.stt(nb[:], kk[:], -1.0, a_[:], ALU.mult, ALU.mult, [kk, a_], [nb])
        self.tt("dve", t2[:], pr[:], kp[:], ALU.mult, [pr, kp], [t2])
        self.tt("dve", t2[:], t2[:], rk_bc, ALU.mult, [t2, prow], [t2])
        self.red(sm[2][:], t2[:].rearrange("p (h j) -> p h j", h=8), [t2], [sm[2]])
        self.tt("dve", cxs[:, 512:1024].rearrange("p (h j) -> p h j", h=8), pv[:].rearrange("p (h j) -> p h j", h=8),
                sm[2][:].unsqueeze(2).broadcast_to([128, 8, 64]), ALU.mult, [pv, sm[2]], [cxs])
        self.act(ops[:, 12 * 512:13 * 512], pv[:], AF.Copy, [pv], [ops])
        pgc = pb[4]
        for d in range(2):
            sg_ = (sgf, sgb)[d]
            pc = pb[5 + d]
            gi, ginv, tmp, ge = ft[10], ft[11], ft[12], ft[13]
            self.mm(pc[:], C["tri"][d][:], sg_[:], True, True, [C["tri"][d], sg_], [pc])
            self.act(gi[:], pc[:], AF.Exp, [pc], [gi], scale=-DECAY)
            self.act(ginv[:], pc[:], AF.Exp, [pc], [ginv], scale=DECAY)
            self.tt("dve", tmp[:], pc[:], sg_[:], ALU.subtract, [pc, sg_], [tmp])
            self.act(ge[:], tmp[:], AF.Exp, [tmp], [ge], scale=-DECAY)
            b0 = 6 * d * 512
            self.tt("dve", ops[:, b0:b0 + 512], pr[:], gi[:], ALU.mult, [pr, gi], [ops])
            self.tt("dve", ops[:, b0 + 512:b0 + 1024], kp[:], ginv[:], ALU.mult, [kp, ginv], [ops])
            self.tt("dve", ops[:, b0 + 1024:b0 + 1536], nb[:], ginv[:], ALU.mult, [nb, ginv], [ops])
            self.tt("dve", ops[:, b0 + 1536:b0 + 2048], kk[:], ge[:], ALU.mult, [kk, ge], [ops])
            for h in range(8):
                c = (d * 8 + h) * 2
                self.mm(pgc[0:64, c:c + 2], sg_[:, h * 64:(h + 1) * 64], C["cind"][:], True, True, [sg_, C["cind"]], [pgc])
        self.act(dcs[0:64, 0:32].rearrange("p (c x) -> p c x", c=2), pgc[0:64, 0:32].rearrange("p (x c) -> p c x", c=2),
                 AF.Exp, [pgc], [dcs], scale=-DECAY)
        self.dma("pool", R["GCs"][g].rearrange("c p x -> p c x"), dcs[0:64, 0:32].rearrange("p (c x) -> p c x", c=2),
                 [dcs], [self.bR["GCs"]])
        o3 = ops[:, :].rearrange("p (s c) -> p s c", s=14)
        OP3 = OPSd[t0:t0 + 128, :].rearrange("t (s c) -> t s c", s=14)
        self.dma("pool", OP3[:, 0:4, :], o3[:, 0:4, :], [ops], [self.bR["OPS"]])
        self.dma("pool", OP3[:, 6:10, :], o3[:, 6:10, :], [ops], [self.bR["OPS"]])
        self.dma("pool", OP3[:, 12:13, :], o3[:, 12:13, :], [ops], [self.bR["OPS"]])
        self.dma("pool", CXd[t0:t0 + 128, 0:1024], cxs[:, 0:1024], [cxs], [self.bR["CX"]])


    def scan(self):
        C, R, I, O = self.C, self.R, self.I, self.O
        ident, M1, M3, MI = C["ident"], C["M1"], C["M3"], C["MI"]
        OPSd = R["OPS"]
        with self.scope() as st:
            OP = [self.sb(st, f"OP{i}", [128, 8 * 512], BF16) for i in range(2)]
            gCt = [self.sb(st, f"gCt{i}", [128, 8], F32) for i in range(2)]
            eCt = [self.sb(st, f"eCt{i}", [128, 8], F32) for i in range(2)]
            XT = self.sb(st, "XT", [128, 8 * 256], BF16)
            GS = self.sb(st, "GS", [128, 8 * 320], BF16)
            XA = [self.sb(st, f"XA{i}", [128, 8 * 128], BF16) for i in range(2)]
            NN = [self.sb(st, f"NN{i}", [128, 8 * 128], BF16) for i in range(2)]
            GT = self.sb(st, "GT", [128, 512], BF16)
            Pb = self.sb(st, "Pb", [128, 512], BF16)
            T32 = self.sb(st, "T32", [128, 512], F32)
            Tb = [self.sb(st, f"Tb{i}", [128, 512], BF16) for i in range(2)]
            Ttmp = self.sb(st, "Ttmp", [128, 512], F32)
            Ysb = [self.sb(st, f"Ysb{i}", [128, 512], F32) for i in range(2)]
            HT = self.sb(st, "HT", [128, 1024], BF16)
            SCz = self.sb(st, "SCz", [128, 4 * 2 * 64], BF16)
            Osb = [self.sb(st, f"Osb{i}", [128, 512], F32) for i in range(2)]
            S32 = self.sb(st, "S32", [128, 1024], F32)
            Sb = [self.sb(st, f"Sb{i}", [128, 1024], BF16) for i in range(2)]
            Stmp = self.sb(st, "Stmp", [128, 512], F32)
            BT0 = self.ps(st, "BT0", [128, 1024], BF16)
            BT1 = self.ps(st, "BT1", [128, 1024], BF16)
            PX = self.ps(st, "PX", [128, 1024], F32)
            PN = self.ps(st, "PN", [128, 1024], F32)
            F4 = self.ps(st, "F4", [128, 512], F32)
            F5 = self.ps(st, "F5", [128, 512], F32)
            self.memset("pool", SCz[:], 0.0, [SCz])
            M1b = M1[:].unsqueeze(1).broadcast_to([128, 8, 128])
            M3b = M3[:].unsqueeze(1).broadcast_to([128, 8, 64])
            stepno = 0
            for si, (s0, sl, kind) in enumerate(self.seqs):
                nch = sl // 64
                if kind == 0:
                    self.memset("pool", T32[:], 0.0, [T32])
                    self.memset("pool", S32[:], 0.0, [S32])
                else:
                    for l in range(2):
                        self.dma("sp", T32[64 * l:64 * l + 64, :], I["st_r"][l], [], [T32])
                        self.dma("sp", S32[:, l * 512:(l + 1) * 512], I["st_h"][l], [], [S32])
                cur = 0
                self.cp("act", Tb[cur][:], T32[:], [T32], [Tb[cur]])
                self.cp("act", Sb[cur][:], S32[:], [S32], [Sb[cur]])
                for i in range(nch):
                    op, gc, ec = OP[stepno % 2], gCt[stepno % 2], eCt[stepno % 2]
                    ysb, osb = Ysb[stepno % 2], Osb[stepno % 2]
                    stepno += 1
                    tok = [s0 + 64 * i, s0 + 64 * (nch - 1 - i)]
                    o3 = op[:, :].rearrange("p (s c) -> p s c", s=8)
                    for l in range(2):
                        src = OPSd[tok[l]:tok[l] + 64, :].rearrange("t (s c) -> t s c", s=14)
                        self.dma("sp", o3[64 * l:64 * l + 64, 0:6, :], src[:, 6 * l:6 * l + 6, :], [self.bR["OPS"]], [op])
                        self.dma("sp", o3[64 * l:64 * l + 64, 6:8, :], src[:, 12:14, :], [self.bR["OPS"]], [op])
                        gg, cc = tok[l] // 128, (tok[l] % 128) // 64
                        self.dma("sp", gc[64 * l:64 * l + 64, :], R["GCs"][gg, cc][:, 8 * l:8 * l + 8], [self.bR["GCs"]], [gc])
                        self.dma("sp", ec[:, 4 * l:4 * l + 4], R["ECs"][gg, cc][:, 4 * l:4 * l + 4], [self.bR["ECs"]], [ec])
                    for h in range(8):
                        bt = BT0 if h < 4 else BT1
                        for X, slot in enumerate((3, 0, 2, 1)):
                            for l in range(2):
                                r0 = 64 * l
                                self.tr(bt[r0:r0 + 64, (h % 4) * 256 + X * 64:(h % 4) * 256 + X * 64 + 64],
                                        op[r0:r0 + 64, slot * 512 + h * 64:slot * 512 + h * 64 + 64],
                                        ident[r0:r0 + 64, r0:r0 + 64], [op, ident], [bt])
                    self.cp("dve", XT[:, 0:1024], BT0[:, :], [BT0], [XT])
                    self.cp("act", XT[:, 1024:2048], BT1[:, :], [BT1], [XT])
                    for h in range(8):
                        for l in range(2):
                            r0 = 64 * l
                            xb = h * 256
                            self.mm(PX[r0:r0 + 64, h * 128:h * 128 + 128], XT[r0:r0 + 64, xb + 128:xb + 192],
                                    XT[r0:r0 + 64, xb:xb + 128], True, True, [XT], [PX])
                            self.mm(PN[r0:r0 + 64, h * 128:h * 128 + 128], XT[r0:r0 + 64, xb + 192:xb + 256],
                                    XT[r0:r0 + 64, xb:xb + 128], True, True, [XT], [PN])
                            self.mm(F4[r0:r0 + 64, h * 64:h * 64 + 64], XT[r0:r0 + 64, xb:xb + 64],
                                    XT[r0:r0 + 64, xb + 128:xb + 192], True, True, [XT], [F4])
                    G3 = GS[:, :].rearrange("p (h c) -> p h c", h=8)
                    self.tt("dve", G3[:, :, 0:128], PX[:, :].rearrange("p (h c) -> p h c", h=8), M1b, ALU.mult, [PX, M1], [GS])
                    self.tt("dve", G3[:, :, 128:256], PN[:, :].rearrange("p (h c) -> p h c", h=8), M1b, ALU.mult, [PN, M1], [GS])
                    self.tt("dve", G3[:, :, 256:320], F4[:, :].rearrange("p (h c) -> p h c", h=8), M3b, ALU.mult, [F4, M3], [GS])
                    xa = XA[0]
                    xa3 = xa[:, :].rearrange("p (h c) -> p h c", h=8)
                    self.cp("dve", xa3[:, :, 0:64], op[:, 3 * 512:4 * 512].rearrange("p (h c) -> p h c", h=8), [op], [xa])
                    for h in range(8):
                        for l in range(2):
                            r0 = 64 * l
                            self.mm(F5[r0:r0 + 64, h * 64:h * 64 + 64], GS[r0:r0 + 64, h * 320 + 128:h * 320 + 192],
                                    op[r0:r0 + 64, 6 * 512 + h * 64:6 * 512 + h * 64 + 64], True, True, [GS, op], [F5])
                    self.cp("act", xa3[:, :, 64:128], F5[:, :].rearrange("p (h c) -> p h c", h=8), [F5], [xa])
                    xc = 0
                    for k in range(6):
                        xcur, xnext = XA[xc], XA[1 - xc]
                        if k == 0:
                            nsrc = GS
                            noff = lambda h: h * 320 + 256
                            ntoff = lambda h: h * 320
                        else:
                            nsrc = NN[(k - 1) % 2]
                            noff = lambda h: h * 128
                            ntoff = lambda h: h * 128 + 64
                        for h in range(8):
                            for l in range(2):
                                r0 = 64 * l
                                self.mm(PX[r0:r0 + 64, h * 128:h * 128 + 128], nsrc[r0:r0 + 64, ntoff(h):ntoff(h) + 64],
                                        xcur[r0:r0 + 64, h * 128:h * 128 + 128], True, True, [nsrc, xcur], [PX])
                        if k < 5:
                            nn = NN[k % 2]
                            for h in range(8):
                                for l in range(2):
                                    r0 = 64 * l
                                    self.mm(PN[r0:r0 + 64, h * 128:h * 128 + 64], nsrc[r0:r0 + 64, ntoff(h):ntoff(h) + 64],
                                            nsrc[r0:r0 + 64, noff(h):noff(h) + 64], True, True, [nsrc], [PN])
                                    self.mm(PN[r0:r0 + 64, h * 128 + 64:h * 128 + 128], nsrc[r0:r0 + 64, noff(h):noff(h) + 64],
                                            nsrc[r0:r0 + 64, ntoff(h):ntoff(h) + 64], True, True, [nsrc], [PN])
                        self.tt("dve", xnext[:, :], PX[:, :], xcur[:, :], ALU.add, [PX, xcur], [xnext])
                        if k < 5:
                            self.cp("act", nn[:, :], PN[:, :], [PN], [nn])
                        xc = 1 - xc
                    x6 = XA[xc]
                    x63 = x6[:, :].rearrange("p (h c) -> p h c", h=8)
                    for h in range(8):
                        for l in range(2):
                            r0 = 64 * l
                            self.tr(BT1[r0:r0 + 64, h * 64:h * 64 + 64], x6[r0:r0 + 64, h * 128:h * 128 + 64],
                                    ident[r0:r0 + 64, r0:r0 + 64], [x6, ident], [BT1])
                    self.cp("dve", GT[:, :], BT1[:, 0:512], [BT1], [GT])
                    tbc, tbn = Tb[cur], Tb[1 - cur]
                    for h in range(8):
                        for l in range(2):
                            r0 = 64 * l
                            self.mm(F4[r0:r0 + 64, h * 64:h * 64 + 64], GT[r0:r0 + 64, h * 64:h * 64 + 64],
                                    tbc[r0:r0 + 64, h * 64:h * 64 + 64], True, True, [GT, tbc], [F4])
                    self.tt("dve", Pb[:, :].rearrange("p (h c) -> p h c", h=8), F4[:, :].rearrange("p (h c) -> p h c", h=8),
                            x63[:, :, 64:128], ALU.add, [F4, x6], [Pb])
                    for h in range(8):
                        for l in range(2):
                            r0 = 64 * l
                            yo = PN[r0:r0 + 64, h * 64:h * 64 + 64]
                            self.mm(yo, XT[r0:r0 + 64, h * 256 + 64:h * 256 + 128], tbc[r0:r0 + 64, h * 64:h * 64 + 64],
                                    True, False, [XT, tbc], [PN])
                            self.mm(yo, GS[r0:r0 + 64, h * 320 + 192:h * 320 + 256],
                                    op[r0:r0 + 64, 6 * 512 + h * 64:6 * 512 + h * 64 + 64], False, False, [GS, op], [PN])
                            self.mm(yo, GS[r0:r0 + 64, h * 320 + 64:h * 320 + 128], Pb[r0:r0 + 64, h * 64:h * 64 + 64],
                                    False, True, [GS, Pb], [PN])
                    self.cp("act", ysb[:, :], PN[:, 0:512], [PN], [ysb])
                    for l in range(2):
                        self.dma("pool", R["YS"][l, tok[l]:tok[l] + 64, :], ysb[64 * l:64 * l + 64, :], [ysb], [self.bR["YS"]])
                    for h in range(8):
                        for l in range(2):
                            r0 = 64 * l
                            to = F5[r0:r0 + 64, h * 64:h * 64 + 64]
                            self.mm(to, op[r0:r0 + 64, 1 * 512 + h * 64:1 * 512 + h * 64 + 64],
                                    op[r0:r0 + 64, 6 * 512 + h * 64:6 * 512 + h * 64 + 64], True, False, [op], [F5])
                            self.mm(to, op[r0:r0 + 64, 2 * 512 + h * 64:2 * 512 + h * 64 + 64],
                                    Pb[r0:r0 + 64, h * 64:h * 64 + 64], False, True, [op, Pb], [F5])
                    self.tt("dve", Ttmp[:, :], F5[:, :], T32[:, :], ALU.add, [F5, T32], [Ttmp])
                    self.tt("dve", T32[:, :].rearrange("p (h c) -> p h c", h=8), Ttmp[:, :].rearrange("p (h c) -> p h c", h=8),
                            gc[:, :].unsqueeze(2).broadcast_to([128, 8, 64]), ALU.mult, [Ttmp, gc], [T32])
                    self.cp("act", tbn[:, :], T32[:, :], [T32], [tbn])
                    sbc, sbn = Sb[cur], Sb[1 - cur]
                    for X, slot in enumerate((4, 5)):
                        for h in range(4):
                            c0 = (X * 4 + h) * 128
                            self.tr(BT0[:, c0:c0 + 128], op[:, slot * 512 + h * 128:slot * 512 + h * 128 + 128], ident[:, :],
                                    [op, ident], [BT0])
                    self.cp("dve", HT[:, :], BT0[:, :], [BT0], [HT])
                    for h in range(4):
                        for l in range(2):
                            r0 = 64 * l
                            self.mm(PX[r0:r0 + 64, h * 64:h * 64 + 64], HT[:, (4 + h) * 128 + r0:(4 + h) * 128 + r0 + 64],
                                    HT[:, h * 128 + r0:h * 128 + r0 + 64], True, True, [HT], [PX])
                    for l in range(2):
                        r0 = 64 * l
                        sc4 = SCz[r0:r0 + 64, :].rearrange("p (h l t) -> p h l t", h=4, l=2)
                        self.tt("dve", sc4[:, :, l, :], PX[r0:r0 + 64, 0:256].rearrange("p (h t) -> p h t", h=4),
                                MI[r0:r0 + 64, :].unsqueeze(1).broadcast_to([64, 4, 64]), ALU.mult, [PX, MI], [SCz])
                    for h in range(4):
                        for l in range(2):
                            r0 = 64 * l
                            oo = PX[r0:r0 + 64, 512 + h * 128:512 + h * 128 + 128]
                            self.mm(oo, SCz[:, (h * 2 + l) * 64:(h * 2 + l) * 64 + 64], op[:, 7 * 512 + h * 128:7 * 512 + h * 128 + 128],
                                    True, False, [SCz, op], [PX])
                            self.mm(oo, HT[:, h * 128 + r0:h * 128 + r0 + 64], sbc[:, (l * 4 + h) * 128:(l * 4 + h) * 128 + 128],
                                    False, True, [HT, sbc], [PX])
                    self.cp("act", osb[:, :], PX[:, 512:1024], [PX], [osb])
                    for l in range(2):
                        self.dma("pool", R["OSC"][l, tok[l]:tok[l] + 64, :], osb[64 * l:64 * l + 64, :], [osb], [self.bR["OSC"]])
                    for l in range(2):
                        r0 = 64 * l
                        pu = PN if l == 0 else F4
                        pu0 = 512 if l == 0 else 0
                        for h in range(4):
                            self.mm(pu[:, pu0 + h * 128:pu0 + h * 128 + 128], op[r0:r0 + 64, 5 * 512 + h * 128:5 * 512 + h * 128 + 128],
                                    op[r0:r0 + 64, 7 * 512 + h * 128:7 * 512 + h * 128 + 128], True, True, [op], [pu])
                        self.tt("dve", Stmp[:, :], pu[:, pu0:pu0 + 512], S32[:, l * 512:(l + 1) * 512], ALU.add, [pu, S32], [Stmp])
                        self.tt("dve", S32[:, l * 512:(l + 1) * 512].rearrange("p (h c) -> p h c", h=4),
                                Stmp[:, :].rearrange("p (h c) -> p h c", h=4),
                                ec[:, 4 * l:4 * l + 4].unsqueeze(2).broadcast_to([128, 4, 128]), ALU.mult, [Stmp, ec], [S32])
                    self.cp("act", sbn[:, :], S32[:, :], [S32], [sbn])
                    cur = 1 - cur
                if kind == 0:
                    for l in range(2):
                        self.dma("pool", O["nsr"][si, l], T32[64 * l:64 * l + 64, :], [T32], [self.bO["nsr"]])
                        self.dma("pool", O["nsh"][si, l], S32[:, l * 512:(l + 1) * 512], [S32], [self.bO["nsh"]])

    def ctx_tiles(self):
        out = []
        per = max(1, 512 // self.L_ctx)
        si = 0
        while si < self.n_ctx:
            ns = min(per, self.n_ctx - si)
            out.append((si * self.L_ctx, ns * self.L_ctx))
            si += ns
        return out

    def phaseC1(self):
        C, R, I = self.C, self.R, self.I
        ident, prow, mods, selv = C["ident"], C["prow"], C["mods"], C["selv"]
        hnw_bc, lnw_bc, lnb_bc = prow[:, 0:512], prow[:, 2048:2560], prow[:, 2560:3072]
        xTv = I["xT"].rearrange("(k p) t -> p k t", p=128)
        xov = I["xTo"].rearrange("(k p) t -> p k t", p=128)
        x1v = R["X1T"].rearrange("(k p) t -> p k t", p=128)
        x1wv = R["X1W"].rearrange("(k p) t -> p k t", p=128)
        NCT = self.n_ctx * self.L_ctx
        QL, NQ = self.QL, self.NQ
        WL = QL + 128
        with self.scope() as st:
            WO = self.sb(st, "WO", [128, 8 * D], BF16)
            wv = I["w_out"].rearrange("(k p) c -> p k c", p=128)
            for k in range(KD):
                self.dma("pool", WO[:, k * D:(k + 1) * D], wv[:, k, :], [], [WO])
            yf = [self.sb(st, f"yf{i}", [128, 512], F32) for i in range(2)]
            yb = [self.sb(st, f"yb{i}", [128, 512], F32) for i in range(2)]
            of = [self.sb(st, f"of{i}", [128, 512], F32) for i in range(2)]
            ob = [self.sb(st, f"ob{i}", [128, 512], F32) for i in range(2)]
            cx = [self.sb(st, f"cx{i}", [128, 1536], BF16) for i in range(2)]
            cxa = self.sb(st, "cxa", [128, 1536], F32)
            ft = [self.sb(st, f"c1f{i}", [128, 512], F32) for i in range(4)]
            sm = [self.sb(st, f"c1s{i}", [128, 8], F32) for i in range(6)]
            MIX = self.sb(st, "MIX", [128, 1024], BF16)
            MIXT = self.sb(st, "MIXT", [128, 8 * 512], BF16)
            xt = self.sb(st, "c1xt", [128, 8 * 512], F32)
            x1 = self.sb(st, "c1x1", [128, 8 * 512], F32)
            PT = self.ps(st, "c1PT", [128, 1024], BF16)
            PO = [self.ps(st, f"c1PO{i}", [128, 512], F32) for i in range(2)]
            self._c1i = 0

            def post(y, o, cxt, gq):
                sq, osq = ft[1], ft[3]
                self.act(sq[:, :], y[:, :], AF.Square, [y], [sq])
                y3 = y[:, :].rearrange("p (h j) -> p h j", h=8)
                self.red(sm[0][:, :], y3, [y], [sm[0]])
                self.red(sm[1][:, :], sq[:, :].rearrange("p (h j) -> p h j", h=8), [sq], [sm[1]])
                self.ts("dve", sm[2][:, :], sm[0][:, :], 1.0 / 64, None, ALU.mult, ALU.bypass, [sm[0]], [sm[2]])
                self.tt("dve", sm[3][:, :], sm[2][:, :], sm[2][:, :], ALU.mult, [sm[2]], [sm[3]])
                self.stt(sm[4][:, :], sm[1][:, :], 1.0 / 64, sm[3][:, :], ALU.mult, ALU.subtract, [sm[1], sm[3]], [sm[4]])
                self.act(sm[5][:, :], sm[4][:, :], AF.Sqrt, [sm[4]], [sm[5]], bias=GN_EPS)
                self.recip(sm[5][:, :], sm[5][:, :], [sm[5]], [sm[5]])
                self.tt("dve", y3, y3, sm[2][:, :].unsqueeze(2).broadcast_to([128, 8, 64]), ALU.subtract, [y, sm[2]], [y])
                self.tt("dve", y3, y3, sm[5][:, :].unsqueeze(2).broadcast_to([128, 8, 64]), ALU.mult, [y, sm[5]], [y])
                self.tt("dve", y[:, :], y[:, :], lnw_bc, ALU.mult, [y, prow], [y])
                self.tt("dve", y[:, :], y[:, :], lnb_bc, ALU.add, [y, prow], [y])
                self.tt("dve", y[:, :], y[:, :], cxt[:, 512:1024], ALU.add, [y, cxt], [y])
                self.tt("dve", MIX[:, 512:1024], y[:, :], cxt[:, 0:512], ALU.mult, [y, cxt], [MIX])
                self.act(osq[:, :], o[:, :], AF.Square, [o], [osq])
                self.red(sm[0][:, 0:4], osq[:, :].rearrange("p (h j) -> p h j", h=4), [osq], [sm[0]])
                self.act(sm[1][:, 0:4], sm[0][:, 0:4], AF.Sqrt, [sm[0]], [sm[1]], scale=1.0 / 128, bias=RMS_EPS)
                self.recip(sm[1][:, 0:4], sm[1][:, 0:4], [sm[1]], [sm[1]])
                o3 = o[:, :].rearrange("p (h j) -> p h j", h=4)
                self.tt("dve", o3, o3, sm[1][:, 0:4].unsqueeze(2).broadcast_to([128, 4, 128]), ALU.mult, [o, sm[1]], [o])
                self.tt("dve", o[:, :], o[:, :], hnw_bc, ALU.mult, [o, prow], [o])
                self.tt("dve", MIX[:, 0:512], o[:, :], cxt[:, 1024:1536], ALU.mult, [o, cxt], [MIX])
                for k in range(KD):
                    self.tr(PT[:, k * 128:(k + 1) * 128], MIX[:, k * 128:(k + 1) * 128], ident[:, :], [MIX, ident], [PT])
                self.cp("act", MIXT[:, :].rearrange("p (k t) -> p k t", k=8)[:, :, gq * 128:(gq + 1) * 128],
                        PT[:, :].rearrange("p (k t) -> p k t", k=8), [PT], [MIXT])

            def dense(n, kind, dst):
                for m in range(KD):
                    po = PO[m % 2]
                    for k in range(KD):
                        self.mm(po[:, 0:n], WO[:, k * D + m * 128:k * D + (m + 1) * 128], MIXT[:, k * 512:k * 512 + n],
                                k == 0, k == KD - 1, [WO, MIXT], [po])
                    self.stt(x1[:, m * n:(m + 1) * n], po[:, 0:n], mods[:, 2, kind, m:m + 1], xt[:, m * n:(m + 1) * n],
                             ALU.mult, ALU.add, [po, mods, xt], [x1])
                self.dma("pool", dst, x1[:, 0:8 * n].rearrange("p (k t) -> p k t", k=8), [x1], [])

            gi = 0
            for (t0, n) in self.ctx_tiles():
                self.dma("sp", xt[:, 0:8 * n].rearrange("p (k t) -> p k t", k=8), xTv[:, :, t0:t0 + n], [], [xt])
                for gq in range(n // 128):
                    ta = t0 + gq * 128
                    b = gi % 2
                    gi += 1
                    self.dma("sp", yf[b][:, :], R["YS"][0, ta:ta + 128, :], [], [yf[b]])
                    self.dma("sp", yb[b][:, :], R["YS"][1, ta:ta + 128, :], [], [yb[b]])
                    self.dma("sp", of[b][:, :], R["OSC"][0, ta:ta + 128, :], [], [of[b]])
                    self.dma("sp", ob[b][:, :], R["OSC"][1, ta:ta + 128, :], [], [ob[b]])
                    self.dma("sp", cx[b][:, :], R["CX"][ta:ta + 128, :], [], [cx[b]])
                    y, o = ft[0], ft[2]
                    self.tt("dve", y[:, :], yf[b][:, :], yb[b][:, :], ALU.add, [yf[b], yb[b]], [y])
                    self.tt("dve", o[:, :], of[b][:, :], ob[b][:, :], ALU.add, [of[b], ob[b]], [o])
                    post(y, o, cx[b], gq)
                dense(n, 0, x1v[:, :, t0:t0 + n])
            w0 = 0
            while w0 < WL:
                n = min(512, WL - w0)
                self.dma("sp", xt[:, 0:8 * n].rearrange("p (k t) -> p k t", k=8), xov[:, :, w0:w0 + n], [], [xt])
                for gq in range(n // 128):
                    wj = w0 + gq * 128
                    y, o = ft[0], ft[2]
                    for q in range(NQ):
                        b = gi % 2
                        gi += 1
                        ta = NCT + q * QL - 64 + wj
                        lo, hi = max(ta, NCT), min(ta + 128, NCT + self.L_lat)
                        p0, pn = lo - ta, hi - lo
                        if pn < 128:
                            for tl in (yf[b], yb[b], of[b], ob[b], cx[b]):
                                self.memset("dve", tl[:, :], 0.0, [tl])
                        if pn > 0:
                            self.dma("sp", yf[b][p0:p0 + pn, :], R["YS"][0, lo:hi, :], [], [yf[b]])
                            self.dma("sp", yb[b][p0:p0 + pn, :], R["YS"][1, lo:hi, :], [], [yb[b]])
                            self.dma("sp", of[b][p0:p0 + pn, :], R["OSC"][0, lo:hi, :], [], [of[b]])
                            self.dma("sp", ob[b][p0:p0 + pn, :], R["OSC"][1, lo:hi, :], [], [ob[b]])
                            self.dma("sp", cx[b][p0:p0 + pn, :], R["CX"][lo:hi, :], [], [cx[b]])
                        sc = selv[:, q:q + 1]
                        t_y, t_o = ft[1], ft[3]
                        self.tt("dve", t_y[:, :], yf[b][:, :], yb[b][:, :], ALU.add, [yf[b], yb[b]], [t_y])
                        self.tt("dve", t_o[:, :], of[b][:, :], ob[b][:, :], ALU.add, [of[b], ob[b]], [t_o])
                        if q == 0:
                            self.ts("dve", y[:, :], t_y[:, :], sc, None, ALU.mult, ALU.bypass, [t_y, selv], [y])
                            self.ts("dve", o[:, :], t_o[:, :], sc, None, ALU.mult, ALU.bypass, [t_o, selv], [o])
                            self.ts("dve", cxa[:, :], cx[b][:, :], sc, None, ALU.mult, ALU.bypass, [cx[b], selv], [cxa])
                        else:
                            self.stt(y[:, :], t_y[:, :], sc, y[:, :], ALU.mult, ALU.add, [t_y, selv, y], [y])
                            self.stt(o[:, :], t_o[:, :], sc, o[:, :], ALU.mult, ALU.add, [t_o, selv, o], [o])
                            self.stt(cxa[:, :], cx[b][:, :], sc, cxa[:, :], ALU.mult, ALU.add, [cx[b], selv, cxa], [cxa])
                    post(y, o, cxa, gq)
                dense(n, 1, x1wv[:, :, w0:w0 + n])
                w0 += n

    def phaseC2(self):
        C, R, I, O = self.C, self.R, self.I, self.O
        mods, fnw, selv = C["mods"], C["fnw"], C["selv"]
        x1v = R["X1T"].rearrange("(k p) t -> p k t", p=128)
        x1wv = R["X1W"].rearrange("(k p) t -> p k t", p=128)
        yTv = O["yT"].rearrange("(k p) t -> p k t", p=128)
        NWM = 640
        NCT = self.n_ctx * self.L_ctx
        QL, NQ = self.QL, self.NQ
        with self.scope() as st:
            WG = self.sb(st, "WG", [128, 8 * DFF], BF16)
            WU = self.sb(st, "WU", [128, 8 * DFF], BF16)
            for (W, nm) in ((WG, "w_gate"), (WU, "w_up")):
                wv = I[nm].rearrange("(k p) c -> p k c", p=128)
                for k in range(KD):
                    self.dma("pool", W[:, k * DFF:(k + 1) * DFF], wv[:, k, :], [], [W])
            cw = self.sb(st, "cw", [128, NF * 9], F32)
            self.dma("sp", cw[:, :], I["convw"], [], [cw])
            cvb = self.sb(st, "cvb", [128, NF], F32)
            self.dma("sp", cvb[:, :], I["cvb"], [], [cvb])
            wd = [self.sb(st, f"wd{i}", [128, NF * 128], BF16) for i in range(2)]
            wdv = I["w_down"].rearrange("(f p) c -> p f c", p=128)
            x1w = self.sb(st, "x1w", [128, 8 * NWM], F32)
            tiles = {
                "sq": self.sb(st, "c2sq", [128, 8 * NWM], BF16),
                "rstd": self.sb(st, "c2rstd", [128, NWM], F32),
                "ntmp": [self.sb(st, f"c2ntmp{i}", [128, NWM], F32) for i in range(2)],
            }
            h2T = self.sb(st, "h2T", [128, 8 * NWM], BF16)
            HM = self.sb(st, "HM", [128, NF * 512], BF16)
            GP = [self.sb(st, f"GP{i}", [128, 10 * 66], F32) for i in range(2)]
            GPc = [self.sb(st, f"GPc{i}", [128, 2 * 258], F32) for i in range(2)]
            acc = [self.sb(st, f"c2acc{i}", [128, 512], F32) for i in range(2)]
            gl = [self.sb(st, f"c2gl{i}", [128, 512], F32) for i in range(2)]
            x2 = self.sb(st, "x2", [128, 8 * 512], F32)
            PG = [self.ps(st, f"c2PG{i}", [128, 1024], F32) for i in range(2)]
            PU = [self.ps(st, f"c2PU{i}", [128, 512], F32) for i in range(2)]
            PD = [self.ps(st, f"c2PD{i}", [128, 512], F32) for i in range(2)]
            fi = 0
            wdi = 0
            work = [(0, t0, n, None) for (t0, n) in self.ctx_tiles()]
            for r0 in range(0, QL // 64, 8):
                work.append((1, r0, min(8, QL // 64 - r0) * 64, None))
            for (kind, t0, n, _) in work:
                if kind == 1:
                    r0 = t0
                    nwv = n + 128
                    coff = 64
                    self.dma("sp", x1w[:, 0:8 * nwv].rearrange("p (k t) -> p k t", k=8),
                             x1wv[:, :, 64 * r0:64 * r0 + nwv], [], [x1w])
                else:
                    nwv, coff = n, 0
                    self.dma("sp", x1w[:, 0:8 * nwv].rearrange("p (k t) -> p k t", k=8), x1v[:, :, t0:t0 + n], [], [x1w])
                self.norm_mod(tiles, x1w, nwv, PU[0], mods[:, 3, kind, :], mods[:, 4, kind, :], h2T, mods)
                nrow_t = n // 64
                for gpz in (GP if kind == 1 else GPc):
                    self.memset("dve", gpz[:, :], 0.0, [gpz])
                for f in range(NF):
                    pg, pu = PG[fi % 2], PU[fi % 2]
                    ac, g_ = acc[fi % 2], gl[fi % 2]
                    c0 = 0
                    while c0 < nwv:
                        cn = min(512, nwv - c0)
                        for k in range(KD):
                            self.mm(pg[:, c0:c0 + cn], WG[:, k * DFF + f * 128:k * DFF + (f + 1) * 128],
                                    h2T[:, k * nwv + c0:k * nwv + c0 + cn], k == 0, k == KD - 1, [WG, h2T], [pg])
                        c0 += cn
                    for k in range(KD):
                        self.mm(pu[:, 0:n], WU[:, k * DFF + f * 128:k * DFF + (f + 1) * 128],
                                h2T[:, k * nwv + coff:k * nwv + coff + n], k == 0, k == KD - 1, [WU, h2T], [pu])
                    if kind == 1:
                        gp = GP[fi % 2]
                        gp3 = gp[:, :].rearrange("p (r c) -> p r c", c=66)
                        nrw = nwv // 64
                        self.cp("act", gp3[:, 0:nrw, 1:65], pg[:, 0:nwv].rearrange("p (r c) -> p r c", c=64), [pg], [gp])
                        if r0 == 0:
                            self.ts("dve", gp3[:, 0, 1:65], gp3[:, 0, 1:65], selv[:, NQ:NQ + 1], None, ALU.mult, ALU.bypass,
                                    [gp, selv], [gp])
                        if r0 + nrow_t == QL // 64:
                            self.ts("dve", gp3[:, nrw - 1, 1:65], gp3[:, nrw - 1, 1:65], selv[:, NQ + 1:NQ + 2], None,
                                    ALU.mult, ALU.bypass, [gp, selv], [gp])
                        a3 = ac[:, 0:n].rearrange("p (r c) -> p r c", c=64)
                        first = True
                        for dy in range(3):
                            for dx in range(3):
                                src = gp3[:, dy:dy + nrow_t, dx:dx + 64]
                                wcol = cw[:, f * 9 + dy * 3 + dx:f * 9 + dy * 3 + dx + 1]
                                if first:
                                    self.ts("dve", a3, src, wcol, cvb[:, f:f + 1], ALU.mult, ALU.add, [gp, cw, cvb], [ac])
                                    first = False
                                else:
                                    self.stt(a3, src, wcol, a3, ALU.mult, ALU.add, [gp, cw, ac], [ac])
                    else:
                        gp = GPc[fi % 2]
                        nsq = n // self.L_ctx
                        Lc = self.L_ctx
                        gp3 = gp[:, 0:nsq * (Lc + 2)].rearrange("p (r c) -> p r c", c=Lc + 2)
                        self.cp("act", gp3[:, :, 1:Lc + 1], pg[:, 0:n].rearrange("p (r c) -> p r c", c=Lc), [pg], [gp])
                        a3 = ac[:, 0:n].rearrange("p (r c) -> p r c", c=Lc)
                        for dx in range(3):
                            src = gp3[:, :, dx:dx + Lc]
                            wcol = cw[:, f * 9 + 3 + dx:f * 9 + 3 + dx + 1]
                            if dx == 0:
                                self.ts("dve", a3, src, wcol, cvb[:, f:f + 1], ALU.mult, ALU.add, [gp, cw, cvb], [ac])
                            else:
                                self.stt(a3, src, wcol, a3, ALU.mult, ALU.add, [gp, cw, ac], [ac])
                    self.act(g_[:, 0:n], ac[:, 0:n], AF.Gelu_apprx_tanh, [ac], [g_])
                    self.tt("dve", HM[:, f * 512:f * 512 + n], pu[:, 0:n], g_[:, 0:n], ALU.mult, [pu, g_], [HM])
                    fi += 1
                for m in range(KD):
                    w_ = wd[wdi % 2]
                    pd = PD[wdi % 2]
                    wdi += 1
                    self.dma("pool", w_[:, :].rearrange("p (f c) -> p f c", c=128), wdv[:, :, m * 128:(m + 1) * 128], [], [w_])
                    for f in range(NF):
                        self.mm(pd[:, 0:n], w_[:, f * 128:(f + 1) * 128], HM[:, f * 512:f * 512 + n], f == 0, f == NF - 1,
                                [w_, HM], [pd])
                    self.stt(x2[:, m * n:(m + 1) * n], pd[:, 0:n], mods[:, 5, kind, m:m + 1],
                             x1w[:, m * nwv + coff:m * nwv + coff + n], ALU.mult, ALU.add, [pd, mods, x1w], [x2])
                rstd = tiles["rstd"]
                self.rms_rstd(tiles, x2, n, PU[1], rstd)
                for k in range(KD):
                    self.stt(x2[:, k * n:(k + 1) * n], x2[:, k * n:(k + 1) * n], fnw[:, k:k + 1], rstd[:, 0:n],
                             ALU.mult, ALU.mult, [x2, fnw, rstd], [x2])
                od = NCT + 64 * t0 if kind == 1 else t0
                self.dma("pool", yTv[:, :, od:od + n], x2[:, 0:8 * n].rearrange("p (k t) -> p k t", k=8), [x2], [])

    def build(self, phases=("0", "A", "S", "C1", "C2")):
        self.declare_io()
        with contextlib.ExitStack() as gst:
            mst = contextlib.ExitStack()
            self.phase0(gst, mst)
            if "A" in phases:
                self.phaseA()
            if "S" in phases:
                self.scan()
            if "C1" in phases:
                self.phaseC1()
            self.S.barrier()
            mst.close()
            if "C2" in phases:
                self.phaseC2()
            self.S.barrier()
            fin = list(self.final)
            self.S.finalize(fin)
        return self.nc


def fm(v):
    return np.ascontiguousarray(np.asarray(v, np.float32).reshape(-1, 128).T)


def shared_inputs(inp):
    f = lambda a: np.ascontiguousarray(np.asarray(a, np.float32))
    S = {}
    S["ada_w"] = f(inp["ada_w"][0])
    S["adab"] = fm(inp["ada_b"][0])
    S["nmw"] = fm(inp["norm_mix_w"][0])
    S["nfw"] = fm(inp["norm_ffn_w"][0])
    S["fnw"] = fm(inp["final_norm_w"])
    S["w_in"] = f(inp["w_in"][0])
    S["rwkv_conv"] = f(inp["rwkv_conv"][0])
    S["hgrn_lb"] = f(inp["hgrn_lb"]).reshape(1, 2048)
    S["prow"] = np.concatenate([f(inp[k][0]).reshape(-1) for k in
                                ("hgrn_norm_w", "rwkv_k_k", "rwkv_k_a", "rwkv_r_k", "rwkv_ln_w", "rwkv_ln_b")]
                               + [f(inp["rwkv_w0"][0, 0]), f(inp["rwkv_w0"][0, 1]), f(inp["rwkv_a0"][0])]).reshape(1, 4608)
    w2x = np.zeros((32, 1536), np.float32)
    w2x[0:32, 0:512] = inp["rwkv_w2"][0, 0]
    w2x[0:32, 512:1024] = inp["rwkv_w2"][0, 1]
    w2x[0:32, 1024:1536] = inp["rwkv_a2"][0]
    S["w2x"] = w2x
    S["g2"] = f(inp["rwkv_g2"][0])
    S["w_out"] = f(inp["w_out"][0])
    S["w_gate"] = f(inp["ffn_w_gate"][0])
    S["w_up"] = f(inp["ffn_w_up"][0])
    S["w_down"] = f(inp["ffn_w_down"][0])
    cw = f(inp["ffn_conv"][0]).reshape(9, NF, 128)
    S["convw"] = np.ascontiguousarray(cw.transpose(2, 1, 0).reshape(128, NF * 9))
    S["cvb"] = fm(inp["ffn_conv_b"][0])
    return S


def core_inputs(inp, S, ctx_ids, lat_b, q):
    f = lambda a: np.ascontiguousarray(np.asarray(a, np.float32))
    L_lat = inp["x_sample"].shape[1]
    QL = min(1024, L_lat)
    NQ = L_lat // QL
    xl = f(inp["x_sample"][lat_b])
    xs = [f(inp["x_prompt"][i]) for i in ctx_ids] + [xl]
    x = np.concatenate(xs, axis=0)
    m = dict(S)
    m["xT"] = np.ascontiguousarray(x.T)
    xw = np.zeros((QL + 128, D), np.float32)
    lo, hi = q * QL - 64, q * QL + QL + 64
    a, b = max(lo, 0), min(hi, L_lat)
    xw[a - lo:b - lo] = xl[a:b]
    m["xTo"] = np.ascontiguousarray(xw.T)
    sel = np.zeros((128, NQ + 2), np.float32)
    sel[:, q] = 1.0
    sel[:, NQ] = 0.0 if q == 0 else 1.0
    sel[:, NQ + 1] = 0.0 if q == NQ - 1 else 1.0
    m["selv"] = sel
    cv = np.stack([f(inp["c_ctx"]), f(inp["c"][lat_b])], axis=1)
    m["cvec"] = np.ascontiguousarray(cv.reshape(8, 128, 2).transpose(1, 0, 2).reshape(128, 16))
    sh = f(inp["state_hgrn"][lat_b, 0])
    m["st_h"] = np.ascontiguousarray(sh.transpose(0, 2, 1, 3).reshape(2, 128, 512))
    sr = f(inp["state_rwkv"][lat_b, 0])
    m["st_r"] = np.ascontiguousarray(sr.transpose(0, 3, 1, 2).reshape(2, 64, 512))
    return m


_PROG_CACHE = {}


def get_prog(n_ctx, L_ctx, L_lat, debug=(), phases=("0", "A", "S", "C1", "C2")):
    key = (n_ctx, L_ctx, L_lat, tuple(sorted(debug)), tuple(phases))
    if key not in _PROG_CACHE:
        b = Builder(n_ctx, L_ctx, L_lat, debug)
        nc = b.build(phases)
        _PROG_CACHE[key] = (nc, b)
    return _PROG_CACHE[key]


def kernel(**inp):
    B, L_ctx = inp["x_prompt"].shape[0], inp["x_prompt"].shape[1]
    DB, L_lat = inp["x_sample"].shape[0], inp["x_sample"].shape[1]
    n_ctx = B // N_CORES
    QL = min(1024, L_lat)
    NQ = L_lat // QL
    assert DB * NQ == N_CORES
    nc, _ = get_prog(n_ctx, L_ctx, L_lat)
    S = shared_inputs(inp)
    in_maps = []
    for c in range(N_CORES):
        ctx_ids = list(range(c * n_ctx, (c + 1) * n_ctx))
        in_maps.append(core_inputs(inp, S, ctx_ids, c % DB, c // DB))
    res = run_bass_kernel_spmd(nc, in_maps, core_ids=list(range(N_CORES)))
    y_prompt = np.zeros((B, L_ctx, D), np.float32)
    y_sample = np.zeros((DB, L_lat, D), np.float32)
    nsh = np.zeros((B, 1, 2, 4, 128, 128), np.float32)
    nsr = np.zeros((B, 1, 2, 8, 64, 64), np.float32)
    for c in range(N_CORES):
        r = res.results[c]
        y = np.asarray(r["yT"]).T
        for i in range(n_ctx):
            y_prompt[c * n_ctx + i] = y[i * L_ctx:(i + 1) * L_ctx]
        b, q = c % DB, c // DB
        y_sample[b, q * QL:(q + 1) * QL] = y[n_ctx * L_ctx:]
        h = np.asarray(r["nsh"]).reshape(n_ctx, 2, 128, 4, 128).transpose(0, 1, 3, 2, 4)
        nsh[c * n_ctx:(c + 1) * n_ctx, 0] = h
        rr = np.asarray(r["nsr"]).reshape(n_ctx, 2, 64, 8, 64).transpose(0, 1, 3, 4, 2)
        nsr[c * n_ctx:(c + 1) * n_ctx, 0] = rr
    return (y_prompt, y_sample, nsh, nsr)
```

```python
import contextlib
import numpy as np
import concourse.bass as bass
import concourse.mybir as mybir
from concourse.bass_utils import run_bass_kernel_spmd

F32 = mybir.dt.float32
BF16 = mybir.dt.bfloat16
AF = mybir.ActivationFunctionType
ALU = mybir.AluOpType
AX = mybir.AxisListType

D = 1024
KD = 8
WA = 512
PA = 2560
PB = 1728
PIN = 4288
DFF = 2816
NF = 22
DECAY = 0.6065306597
RMS_EPS = 1e-6
GN_EPS = 64e-5
GRID_W = 64
N_CORES = 8

ENGS = ("pe", "act", "dve", "pool", "sp")


class Buf:
    __slots__ = ("name", "lw", "rd", "rdd")

    def __init__(self, name):
        self.name = name
        self.lw = None
        self.rd = {}
        self.rdd = []


class Ins:
    __slots__ = ("eng", "fn", "deps", "is_dma", "sig", "sigval", "dsem", "dval")

    def __init__(self, eng, fn, deps, is_dma):
        self.eng = eng
        self.fn = fn
        self.deps = deps
        self.is_dma = is_dma
        self.sig = False
        self.sigval = 0
        self.dsem = None
        self.dval = 0


class Sched:
    NDMA = 16

    def __init__(self, nc):
        self.nc = nc
        self.ins = []
        self.last = {}
        self.dmas = []

    def barrier(self):
        deps = sorted(set(self.last.values()) | set(self.dmas))
        if not deps:
            return
        for e in ENGS:
            self.ins.append(Ins(e, None, list(deps), False))
        self.dmas = []

    def add(self, eng, fn, reads=(), writes=(), is_dma=False):
        deps = set()
        for b in reads:
            if b.lw is not None:
                deps.add(b.lw)
        for b in writes:
            if b.lw is not None:
                deps.add(b.lw)
            deps.update(b.rd.values())
            deps.update(b.rdd)
        i = len(self.ins)
        self.ins.append(Ins(eng, fn, sorted(deps), is_dma))
        self.last[eng] = i
        if is_dma:
            self.dmas.append(i)
        for b in reads:
            if is_dma:
                b.rdd.append(i)
            else:
                b.rd[eng] = i
        for b in writes:
            b.lw = i
            b.rd = {}
            b.rdd = []
        return i

    def finalize(self, final_bufs):
        nc = self.nc
        ins = self.ins
        fdeps = set()
        for b in final_bufs:
            if b.lw is not None:
                fdeps.add(b.lw)
        ins.append(Ins("sp", None, sorted(fdeps), False))
        for it in ins:
            for d in it.deps:
                dd = ins[d]
                if dd.eng == "pe" and it.eng == "pe" and not dd.is_dma and not it.is_dma:
                    continue
                dd.sig = True
        st = contextlib.ExitStack()
        esem = {e: st.enter_context(nc.semaphore(f"s_{e}")) for e in ("pe", "act", "dve", "pool")}
        dq = [e for e in ENGS if any(it.is_dma and it.eng == e for it in ins)]
        per = max(2, self.NDMA // max(1, len(dq)))
        dsems = []
        dpool = {}
        for e in dq:
            dpool[e] = list(range(len(dsems), len(dsems) + per))
            dsems += [st.enter_context(nc.semaphore(f"s_dma_{e}{i}")) for i in range(per)]
        dcount = [0] * len(dsems)
        dlast = [None] * len(dsems)
        ecount = {e: 0 for e in esem}
        rr = {e: 0 for e in dq}
        for k, it in enumerate(ins):
            if it.is_dma:
                s = dpool[it.eng][rr[it.eng] % per]
                rr[it.eng] += 1
                if dlast[s] is not None:
                    it.deps = sorted(set(it.deps) | {dlast[s]})
                dcount[s] += 16
                it.dsem = s
                it.dval = dcount[s]
                dlast[s] = k
            elif it.sig:
                ecount[it.eng] += 1
                it.sigval = ecount[it.eng]
        progs = {e: [] for e in ENGS}
        waited = {e: {} for e in ENGS}
        for it in ins:
            w = {}
            for d in it.deps:
                dd = ins[d]
                if dd.is_dma:
                    key = ("d", dd.dsem)
                    val = dd.dval
                else:
                    if dd.eng == "pe" and it.eng == "pe" and not it.is_dma:
                        continue
                    key = ("e", dd.eng)
                    val = dd.sigval
                if w.get(key, 0) < val:
                    w[key] = val
            wl = []
            for key, val in w.items():
                if waited[it.eng].get(key, 0) >= val:
                    continue
                waited[it.eng][key] = val
                wl.append((dsems[key[1]] if key[0] == "d" else esem[key[1]], val))
            progs[it.eng].append((wl, it))
        self.counts = {e: len(progs[e]) for e in ENGS}
        with nc.Block() as block:
            def runner(name):
                def run(eng):
                    for wl, it in progs[name]:
                        for sem, val in wl:
                            eng.wait_ge(sem, val)
                        if it.fn is None:
                            continue
                        r = it.fn(eng)
                        if it.is_dma:
                            r.then_inc(dsems[it.dsem], 16)
                        elif it.sig:
                            r.then_inc(esem[it.eng], 1)
                return run
            block.tensor(runner("pe"))
            block.scalar(runner("act"))
            block.vector(runner("dve"))
            block.gpsimd(runner("pool"))
            block.sync(runner("sp"))
        st.close()


class Tl:
    __slots__ = ("t", "b")

    def __init__(self, t, name):
        self.t = t
        self.b = Buf(name)

    def __getitem__(self, k):
        return self.t[k]


def _bufs(xs):
    return [x.b if isinstance(x, Tl) else x for x in xs]


class Builder:
    def __init__(self, n_ctx, L_ctx, L_lat, debug=()):
        self.n_ctx, self.L_ctx, self.L_lat = n_ctx, L_ctx, L_lat
        self.NT = n_ctx * L_ctx + L_lat
        self.NG = self.NT // 128
        self.QL = min(1024, L_lat)
        self.NQ = L_lat // self.QL
        self.seqs = [(i * L_ctx, L_ctx, 0) for i in range(n_ctx)] + [(n_ctx * L_ctx, L_lat, 1)]
        self.debug = set(debug)
        self.nc = bass.Bass("TRN2", target_bir_lowering=False)
        self.S = Sched(self.nc)
        self.final = []
        self.uid = 0

    @contextlib.contextmanager
    def scope(self):
        with contextlib.ExitStack() as st:
            yield st
            self.S.barrier()

    def sb(self, st, name, shape, dt):
        self.uid += 1
        return Tl(st.enter_context(self.nc.sbuf_tensor(f"{name}_{self.uid}", shape, dt)), name)

    def ps(self, st, name, shape, dt):
        self.uid += 1
        return Tl(st.enter_context(self.nc.psum_tensor(f"{name}_{self.uid}", shape, dt)), name)

    def din(self, name, shape, dt=F32):
        return self.nc.dram_tensor(name, list(shape), dt, kind="ExternalInput").ap()

    def dout(self, name, shape, dt=F32):
        return self.nc.dram_tensor(name, list(shape), dt, kind="ExternalOutput").ap()

    def dscr(self, name, shape, dt):
        kind = "ExternalOutput" if name in self.debug else "Internal"
        return self.nc.dram_tensor(name, list(shape), dt, kind=kind).ap()

    def mm(self, out, lhsT, rhs, start, stop, reads, writes):
        self.S.add("pe", lambda e: e.matmul(out, lhsT=lhsT, rhs=rhs, start=start, stop=stop),
                   _bufs(reads), _bufs(writes))

    def tr(self, out, in_, ident, reads, writes):
        self.S.add("pe", lambda e: e.transpose(out=out, in_=in_, identity=ident), _bufs(reads), _bufs(writes))

    def act(self, out, in_, func, reads, writes, scale=1.0, bias=0.0):
        self.S.add("act", lambda e: e.activation(out=out, in_=in_, func=func, scale=scale, bias=bias),
                   _bufs(reads), _bufs(writes))

    def tt(self, eng, out, in0, in1, op, reads, writes):
        self.S.add(eng, lambda e: e.tensor_tensor(out=out, in0=in0, in1=in1, op=op), _bufs(reads), _bufs(writes))

    def ts(self, eng, out, in0, s1, s2, op0, op1, reads, writes):
        self.S.add(eng, lambda e: e.tensor_scalar(out=out, in0=in0, scalar1=s1, scalar2=s2, op0=op0, op1=op1),
                   _bufs(reads), _bufs(writes))

    def stt(self, out, in0, scalar, in1, op0, op1, reads, writes):
        self.S.add("dve", lambda e: e.scalar_tensor_tensor(out=out, in0=in0, scalar=scalar, in1=in1, op0=op0, op1=op1),
                   _bufs(reads), _bufs(writes))

    def cp(self, eng, out, in_, reads, writes):
        if eng == "act":
            self.act(out, in_, AF.Copy, reads, writes)
        else:
            self.S.add(eng, lambda e: e.tensor_copy(out=out, in_=in_), _bufs(reads), _bufs(writes))

    def red(self, out, in_, reads, writes):
        self.S.add("dve", lambda e: e.tensor_reduce(out=out, in_=in_, axis=AX.X, op=ALU.add), _bufs(reads), _bufs(writes))

    def recip(self, out, in_, reads, writes):
        self.S.add("dve", lambda e: e.reciprocal(out=out, in_=in_), _bufs(reads), _bufs(writes))

    def memset(self, eng, ap, val, writes):
        self.S.add(eng, lambda e: e.memset(ap, val), [], _bufs(writes))

    def asel(self, ap, pattern, cmp, fill, base, cm, tl):
        self.S.add("pool", lambda e: e.affine_select(out=ap, in_=ap, pattern=pattern, compare_op=cmp, fill=fill,
                                                     base=base, channel_multiplier=cm), [tl.b], [tl.b])

    def dma(self, eng, out, in_, reads, writes):
        nt = getattr(self, "_untracked", None)
        if nt is None:
            nt = self._untracked = set(id(b) for b in list(self.bR.values()) + list(self.bO.values()))
        writes = [w for w in _bufs(writes) if id(w) not in nt]
        reads = [r for r in _bufs(reads) if id(r) not in nt]
        self.S.add(eng, lambda e: e.dma_start(out=out, in_=in_), reads, writes, is_dma=True)

    def dbg(self, name, ap, shape, rd):
        if name in self.debug:
            d = self.dout(name, shape)
            b = Buf(name)
            self.dma("pool", d, ap, [rd], [b])
            self.final.append(b)

    def declare_io(self):
        NT, n_ctx = self.NT, self.n_ctx
        I = {}
        I["xT"] = self.din("xT", [D, NT])
        I["cvec"] = self.din("cvec", [128, 16])
        I["ada_w"] = self.din("ada_w", [D, 6 * D])
        I["adab"] = self.din("adab", [128, 48])
        I["nmw"] = self.din("nmw", [128, 8])
        I["nfw"] = self.din("nfw", [128, 8])
        I["fnw"] = self.din("fnw", [128, 8])
        I["w_in"] = self.din("w_in", [D, PIN])
        I["rwkv_conv"] = self.din("rwkv_conv", [3, PB])
        I["hgrn_lb"] = self.din("hgrn_lb", [1, 2048])
        I["prow"] = self.din("prow", [1, 9 * 512])
        I["w2x"] = self.din("w2x", [32, 3 * 512])
        I["g2"] = self.din("g2", [96, 512])
        I["w_out"] = self.din("w_out", [D, D])
        I["w_gate"] = self.din("w_gate", [D, DFF])
        I["w_up"] = self.din("w_up", [D, DFF])
        I["w_down"] = self.din("w_down", [DFF, D])
        I["convw"] = self.din("convw", [128, NF * 9])
        I["cvb"] = self.din("cvb", [128, NF])
        I["xTo"] = self.din("xTo", [D, self.QL + 128])
        I["selv"] = self.din("selv", [128, self.NQ + 2])
        I["st_h"] = self.din("st_h", [2, 128, 4 * 128])
        I["st_r"] = self.din("st_r", [2, 64, 8 * 64])
        self.I = I
        O = {}
        O["yT"] = self.dout("yT", [D, n_ctx * self.L_ctx + self.QL])
        O["nsh"] = self.dout("nsh", [n_ctx, 2, 128, 512])
        O["nsr"] = self.dout("nsr", [n_ctx, 2, 64, 512])
        self.O = O
        self.bO = {k: Buf("out_" + k) for k in O}
        R = {}
        R["OPS"] = self.dscr("OPS", [NT, 14 * 512], BF16)
        R["CX"] = self.dscr("CX", [NT, 3 * 512], BF16)
        R["GCs"] = self.dscr("GCs", [self.NG, 2, 64, 16], F32)
        R["ECs"] = self.dscr("ECs", [self.NG, 2, 128, 8], F32)
        R["YS"] = self.dscr("YS", [2, NT, 512], F32)
        R["OSC"] = self.dscr("OSC", [2, NT, 512], F32)
        R["X1T"] = self.dscr("X1T", [D, n_ctx * self.L_ctx], F32)
        R["X1W"] = self.dscr("X1W", [D, self.QL + 128], F32)
        self.R = R
        self.bR = {k: Buf("scr_" + k) for k in R}

    def phase0(self, gst0, gst):
        nc, I = self.nc, self.I
        C = {}
        C["ident"] = self.sb(gst0, "ident", [128, 128], BF16)
        C["onesD"] = self.sb(gst0, "onesD", [128, 128], BF16)
        mods = self.sb(gst0, "mods", [128, 6, 2, 8], F32)
        fnw = self.sb(gst0, "fnw", [128, 8], F32)
        C["selv"] = self.sb(gst0, "selv", [128, self.NQ + 2], F32)
        self.dma("sp", C["selv"][:], I["selv"], [], [C["selv"]])
        identf = self.sb(gst, "identf", [128, 128], F32)
        self.memset("pool", identf[:], 0.0, [identf])
        self.asel(identf[:], [[-1, 128]], ALU.not_equal, 1.0, 0, 1, identf)
        self.cp("dve", C["ident"][:], identf[:], [identf], [C["ident"]])
        self.memset("pool", C["onesD"][:], 1.0 / D, [C["onesD"]])
        trif = self.sb(gst, "trif", [128, 128], F32)
        self.memset("pool", trif[:], 1.0, [trif])
        self.asel(trif[:], [[1, 128]], ALU.is_ge, 0.0, 0, -1, trif)
        self.asel(trif[:, 64:128], [[0, 64]], ALU.is_ge, 0.0, -64, 1, trif)
        trib = self.sb(gst, "trib", [128, 128], F32)
        self.memset("pool", trib[:], 1.0, [trib])
        self.asel(trib[:], [[-1, 128]], ALU.is_ge, 0.0, 0, 1, trib)
        self.asel(trib[:, 0:64], [[0, 64]], ALU.is_ge, 0.0, 63, -1, trib)
        C["tri"] = [trif, trib]

        C["identf"] = identf
        cind = self.sb(gst, "cind", [128, 2], F32)
        self.memset("pool", cind[:], 1.0, [cind])
        self.asel(cind[:, 0:1], [[0, 1]], ALU.is_ge, 0.0, 63, -1, cind)
        self.asel(cind[:, 1:2], [[0, 1]], ALU.is_ge, 0.0, -64, 1, cind)
        C["cind"] = cind
        M1 = self.sb(gst, "M1", [128, 128], F32)
        M3 = self.sb(gst, "M3", [128, 64], F32)
        MI = self.sb(gst, "MI", [128, 64], F32)
        for m in (M1, M3, MI):
            self.memset("pool", m[:], 1.0, [m])
        self.asel(M1[0:64, 0:64], [[1, 64]], ALU.is_gt, 0.0, 0, -1, M1)
        self.asel(M1[0:64, 64:128], [[1, 64]], ALU.is_ge, 0.0, 0, -1, M1)
        self.asel(M1[64:128, 0:64], [[-1, 64]], ALU.is_gt, 0.0, 0, 1, M1)
        self.asel(M1[64:128, 64:128], [[-1, 64]], ALU.is_ge, 0.0, 0, 1, M1)
        self.asel(M3[0:64, :], [[-1, 64]], ALU.is_gt, 0.0, 0, 1, M3)
        self.asel(M3[64:128, :], [[1, 64]], ALU.is_gt, 0.0, 0, -1, M3)
        self.asel(MI[0:64, :], [[1, 64]], ALU.is_ge, 0.0, 0, -1, MI)
        self.asel(MI[64:128, :], [[-1, 64]], ALU.is_ge, 0.0, 0, 1, MI)
        C["M1"], C["M3"], C["MI"] = M1, M3, MI
        self.dbg("dbg_M1", M1[:], [128, 128], M1)
        self.dbg("dbg_trif", trif[:], [128, 128], trif)
        self.dbg("dbg_trib", trib[:], [128, 128], trib)

        modT = self.sb(gst, "modT", [128, 48, 2], F32)
        lb = self.sb(gst, "lb", [128, 1024], F32)
        omlb = self.sb(gst, "omlb", [128, 1024], F32)
        prow = self.sb(gst, "prow", [128, 9 * 512], F32)
        omka = self.sb(gst, "omka", [128, 512], F32)
        with self.scope() as st:
            cv = self.sb(st, "cv", [128, 16], F32)
            self.dma("sp", cv[:], I["cvec"], [], [cv])
            scv = self.sb(st, "scv", [128, 16], F32)
            self.act(scv[:], cv[:], AF.Silu, [cv], [scv])
            adab = self.sb(st, "adab", [128, 48], F32)
            self.dma("sp", adab[:], I["adab"], [], [adab])
            acc = self.sb(st, "modacc", [128, 96], F32)
            abuf = [self.sb(st, f"adaw{i}", [128, 3 * D], F32) for i in range(2)]
            pm = self.ps(st, "pmod", [128, 512], F32)
            adv = I["ada_w"].rearrange("(k p) c -> k p c", p=128)
            it = 0
            for k in range(KD):
                for hf in range(2):
                    ab = abuf[it % 2]
                    it += 1
                    self.dma("sp", ab[:, :], adv[k][:, hf * 3072:(hf + 1) * 3072], [], [ab])
                    for mm_ in range(24):
                        m = hf * 24 + mm_
                        self.mm(pm[:, 2 * m:2 * m + 2], ab[:, mm_ * 128:(mm_ + 1) * 128], scv[:, 2 * k:2 * k + 2], True, True,
                                [ab, scv], [pm])
                if k == 0:
                    self.cp("dve", acc[:], pm[:, 0:96], [pm], [acc])
                else:
                    self.tt("dve", acc[:], pm[:, 0:96], acc[:], ALU.add, [pm, acc], [acc])
            self.tt("dve", modT[:], acc[:].rearrange("p (m s) -> p m s", s=2),
                    adab[:].unsqueeze(2).broadcast_to([128, 48, 2]), ALU.add, [acc, adab], [modT])
            self.dbg("dbg_modT", modT[:].rearrange("p m s -> p (m s)"), [128, 96], modT)
            nmw = self.sb(st, "nmw", [128, 8], F32)
            nfw = self.sb(st, "nfw", [128, 8], F32)
            self.dma("sp", nmw[:], I["nmw"], [], [nmw])
            self.dma("sp", nfw[:], I["nfw"], [], [nfw])
            for s in range(2):
                for (wi, off, nw) in ((0, 8, nmw), (3, 32, nfw)):
                    self.stt(mods[:, wi, s, :], modT[:, off:off + 8, s], 1.0, nw[:], ALU.add, ALU.mult,
                             [modT, nw], [mods])
                for (wi, off) in ((1, 0), (2, 16), (4, 24), (5, 40)):
                    self.cp("dve", mods[:, wi, s, :], modT[:, off:off + 8, s], [modT], [mods])
            C["mods"] = mods
            self.dma("sp", fnw[:], I["fnw"], [], [fnw])
            C["fnw"] = fnw
            lbr = self.sb(st, "lbraw", [128, 2048], F32)
            self.dma("sp", lbr[:], I["hgrn_lb"].partition_broadcast(128), [], [lbr])
            lbd = self.sb(st, "lbd", [128, 1024], F32)
            self.tt("dve", lbd[:], lbr[:, 0:1024], lbr[:, 1024:2048], ALU.subtract, [lbr], [lbd])
            self.act(lb[:], lbd[:], AF.Sigmoid, [lbd], [lb])
            self.ts("dve", omlb[:], lb[:], -1.0, 1.0, ALU.mult, ALU.add, [lb], [omlb])
            C["lb"], C["omlb"] = lb, omlb
            self.dma("sp", prow[:], I["prow"].partition_broadcast(128), [], [prow])
            C["prow"] = prow
            self.ts("dve", omka[:], prow[:, 1024:1536], -1.0, 1.0, ALU.mult, ALU.add, [prow], [omka])
            C["omka"] = omka
        self.C = C

    def rms_rstd(self, st_tiles, src, n, pbank, rstd):
        sq = st_tiles["sq"]
        self.act(sq[:, 0:8 * n], src[:, 0:8 * n], AF.Square, [src], [sq])
        c0 = 0
        while c0 < n:
            cn = min(512, n - c0)
            for k in range(KD):
                self.mm(pbank[:, 0:cn], self.C["onesD"][:], sq[:, k * n + c0:k * n + c0 + cn], k == 0, k == KD - 1,
                        [self.C["onesD"], sq], [pbank])
            self.act(rstd[:, c0:c0 + cn], pbank[:, 0:cn], AF.Sqrt, [pbank], [rstd], bias=RMS_EPS)
            c0 += cn
        self.recip(rstd[:, 0:n], rstd[:, 0:n], [rstd], [rstd])

    def norm_mod(self, tiles, xt, n, pbank, A, B, hT, modtl):
        rstd = tiles["rstd"]
        self.rms_rstd(tiles, xt, n, pbank, rstd)
        for k in range(KD):
            tmp = tiles["ntmp"][k % 2]
            self.tt("dve", tmp[:, 0:n], xt[:, k * n:(k + 1) * n], rstd[:, 0:n], ALU.mult, [xt, rstd], [tmp])
            self.act(hT[:, k * n:(k + 1) * n], tmp[:, 0:n], AF.Identity, [tmp, modtl], [hT],
                     scale=A[:, k:k + 1], bias=B[:, k:k + 1])

    def phaseA(self):
        nc, I, C, R = self.nc, self.I, self.C, self.R
        mods = C["mods"]
        xTv = I["xT"].rearrange("(k p) t -> p k t", p=128)
        import os as _os
        _rnew = "1"
        for part in (("H",) if _rnew == "0" else ("H", "R")):
            halo = 0 if part == "H" else 1
            nw = 128 + 2 * halo
            with self.scope() as st:
                X = {"part": part, "nw": nw, "halo": halo}
                if part == "H":
                    X["lf"] = [self.sb(st, f"lf{i}", [128, 512], F32) for i in range(2)]
                else:
                    X["sg"] = [self.sb(st, f"sg{i}", [128, 512], F32) for i in range(2)]
                    X["lt"] = [self.sb(st, f"lt{i}", [128, 192], F32) for i in range(2)]
                if part == "H":
                    WH = self.sb(st, "WH", [128, 8 * PA], BF16)
                    wv = I["w_in"].rearrange("(k p) c -> p k c", p=128)
                    for k in range(KD):
                        self.dma("pool", WH[:, k * PA:(k + 1) * PA], wv[:, k, 0:PA], [], [WH])
                    X["WH"] = WH
                else:
                    WR = [self.sb(st, f"WR{t}", [128, 8 * PB], BF16) for t in range(3)]
                    with self.scope() as st2:
                        cvbc = self.sb(st2, "cvbc", [128, 3 * PB], F32)
                        self.dma("sp", cvbc[:], I["rwkv_conv"].rearrange("(o a) c -> o (a c)", o=1).partition_broadcast(128),
                                 [], [cvbc])
                        stg = [self.sb(st2, f"wstg{i}", [128, PB], F32) for i in range(2)]
                        wv = I["w_in"].rearrange("(k p) c -> p k c", p=128)
                        for k in range(KD):
                            sg = stg[k % 2]
                            self.dma("sp", sg[:], wv[:, k, PA:PIN], [], [sg])
                            for t in range(3):
                                self.tt("dve" if t != 1 else "pool", WR[t][:, k * PB:(k + 1) * PB], sg[:],
                                        cvbc[:, t * PB:(t + 1) * PB], ALU.mult, [sg, cvbc], [WR[t]])
                    X["WR"] = WR
                    X["W2x"] = self.sb(st, "W2x", [32, 1536], BF16)
                    self.dma("pool", X["W2x"][:], I["w2x"], [], [X["W2x"]])
                    X["G2"] = self.sb(st, "G2", [96, 512], BF16)
                    self.dma("pool", X["G2"][:], I["g2"], [], [X["G2"]])
                    X["lw"] = self.sb(st, "lw", [32, 384], BF16)
                    X["lg"] = self.sb(st, "lg", [96, 128], BF16)
                X["tiles"] = {
                    "sq": self.sb(st, "sq", [128, 8 * nw], BF16),
                    "rstd": self.sb(st, "rstd", [128, nw], F32),
                    "ntmp": [self.sb(st, f"ntmp{i}", [128, nw], F32) for i in range(2)],
                }
                X["xts"] = [self.sb(st, f"xt{i}", [128, 8 * nw], F32) for i in range(2)]
                X["hTs"] = [self.sb(st, f"hT{i}", [128, 8 * nw], BF16) for i in range(2)]
                X["OPst"] = [self.sb(st, f"OPst{i}", [128, 14 * 512], BF16) for i in range(2)]
                X["CXst"] = [self.sb(st, f"CXst{i}", [128, 3 * 512], BF16) for i in range(2)]
                X["dcs"] = [self.sb(st, f"dcs{i}", [128, 32], F32) for i in range(2)]
                X["sm"] = [self.sb(st, f"sm{i}", [128, 8], F32) for i in range(4)]
                if part == "H":
                    X["qs"] = [self.sb(st, f"qs{i}", [128, 512], F32) for i in range(2)]
                    X["sgz"] = [[self.sb(st, f"sgz{i}{d}", [128, 512], F32) for d in range(2)] for i in range(2)]
                    X["ft"] = [self.sb(st, f"ft{i}", [128, 512], F32) for i in range(6)]
                    X["pss"] = self.ps(st, "pss", [128, 512], F32)
                    X["pp"] = [self.ps(st, f"pp{i}", [128, 512], F32) for i in range(5)]
                    X["LA"] = self.ps(st, "LA", [128, 512], F32)
                    X["LB"] = self.ps(st, "LB", [128, 512], F32)
                else:
                    X["rkv"] = [[self.sb(st, f"rkv{i}{j}", [128, 512], F32) for j in range(3)] for i in range(2)]
                    X["ft"] = [self.sb(st, f"ft{i}", [128, 512], F32) for i in range(9)]
                    X["pss"] = self.ps(st, "pss", [128, 512], F32)
                    X["pp"] = [self.ps(st, f"pp{i}", [128, 512], F32) for i in range(4)]
                    X["LA"] = self.ps(st, "LA", [128, 512], F32)
                    X["LB"] = self.ps(st, "LB", [128, 512], F32)
                    X["LC"] = self.ps(st, "LC", [128, 512], F32)
                if not (part == "R" and _rnew == "2"):
                    self.A_normproj(X, 0)
                for g in range(self.NG):
                    if part == "R" and _rnew == "2":
                        self.A_normproj(X, g)
                        self.A_early(X, g)
                        self.A_late_R(X, g)
                        continue
                    self.A_early(X, g)
                    if g + 1 < self.NG:
                        self.A_normproj(X, g + 1)
                    if part == "H":
                        self.A_late_H(X, g)
                    else:
                        self.A_late_R(X, g)
        if _rnew == "0":
            self.phaseA_oldR()

    def A_normproj(self, X, g):
        I, C = self.I, self.C
        mods = C["mods"]
        xTv = I["xT"].rearrange("(k p) t -> p k t", p=128)
        halo, nw = X["halo"], X["nw"]
        t0 = g * 128
        s0, sl, kind = [s for s in self.seqs if s[0] <= t0 < s[0] + s[1]][0]
        s1 = s0 + sl
        lo, hi = max(t0 - halo, s0), min(t0 + 128 + halo, s1)
        off = lo - (t0 - halo)
        n = hi - lo
        xt, hT = X["xts"][g % 2], X["hTs"][g % 2]
        x3 = xt[:, :].rearrange("p (k t) -> p k t", k=8)
        if n < nw:
            self.memset("dve", xt[:, :], 0.0, [xt])
        self.dma("sp", x3[:, :, off:off + n], xTv[:, :, lo:hi], [], [xt])
        self.norm_mod(X["tiles"], xt, nw, X["pss"], mods[:, 0, kind, :], mods[:, 1, kind, :], hT, mods)
        h3 = hT[:, :].rearrange("p (k t) -> p k t", k=8)
        if off > 0:
            self.memset("dve", h3[:, :, 0:off], 0.0, [hT])
        if off + n < nw:
            self.memset("dve", h3[:, :, off + n:nw], 0.0, [hT])
        pp = X["pp"]
        if X["part"] == "H":
            WH = X["WH"]
            for cg in range(5):
                for k in range(KD):
                    self.mm(pp[cg][:, :], hT[:, k * nw:k * nw + 128], WH[:, k * PA + cg * 512:k * PA + (cg + 1) * 512],
                            k == 0, k == KD - 1, [hT, WH], [pp[cg]])
        else:
            WR = X["WR"]
            for cg in range(4):
                c0 = cg * 512
                cn = 512 if cg < 3 else 192
                for t in range(3):
                    for k in range(KD):
                        self.mm(pp[cg][:, 0:cn], hT[:, k * nw + t:k * nw + t + 128], WR[t][:, k * PB + c0:k * PB + c0 + cn],
                                t == 0 and k == 0, t == 2 and k == KD - 1, [hT, WR[t]], [pp[cg]])

    def A_early(self, X, g):
        pp = X["pp"]
        b = g % 2
        ops, cxs = X["OPst"][b], X["CXst"][b]
        if X["part"] == "H":
            self.act(X["qs"][b][:], pp[0][:], AF.Silu, [pp[0]], [X["qs"][b]])
            self.act(ops[:, 13 * 512:14 * 512], pp[1][:], AF.Copy, [pp[1]], [ops])
            self.act(X["sgz"][b][0][:], pp[2][:], AF.Sigmoid, [pp[2]], [X["sgz"][b][0]])
            self.act(X["sgz"][b][1][:], pp[3][:], AF.Sigmoid, [pp[3]], [X["sgz"][b][1]])
            self.act(cxs[:, 1024:1536], pp[4][:], AF.Silu, [pp[4]], [cxs])
        else:
            lt = X["lt"][b]
            self.act(lt[:, 0:64], pp[3][:, 0:64], AF.Tanh, [pp[3]], [lt])
            self.act(lt[:, 64:96], pp[3][:, 64:96], AF.Copy, [pp[3]], [lt])
            self.act(lt[:, 96:192], pp[3][:, 96:192], AF.Sigmoid, [pp[3]], [lt])
            r_, k_, v_ = X["rkv"][b]
            self.cp("dve", r_[:], pp[0][:], [pp[0]], [r_])
            self.cp("act", k_[:], pp[1][:], [pp[1]], [k_])
            self.cp("dve", v_[:], pp[2][:], [pp[2]], [v_])

    def A_late_H(self, X, g):
        C, R = self.C, self.R
        OPSd, CXd = R["OPS"], R["CX"]
        b = g % 2
        t0 = g * 128
        ops, cxs, dcs = X["OPst"][b], X["CXst"][b], X["dcs"][b]
        qs = X["qs"][b]
        ft = X["ft"]
        LA, LB = X["LA"], X["LB"]
        for d in range(2):
            sg_ = X["sgz"][b][d]
            lf = X["lf"][d]
            f_, kd, csc, e1, e2 = ft[0], ft[1 + d], ft[3], ft[4], ft[5]
            self.tt("dve", f_[:], sg_[:], C["omlb"][:, d * 512:(d + 1) * 512], ALU.mult, [sg_, C["omlb"]], [f_])
            self.tt("dve", f_[:], f_[:], C["lb"][:, d * 512:(d + 1) * 512], ALU.add, [f_, C["lb"]], [f_])
            self.act(lf[:], f_[:], AF.Ln, [f_], [lf])
            self.ts("dve", kd[:], f_[:], -1.0, 1.0, ALU.mult, ALU.add, [f_], [kd])
            self.mm(LA[:], C["tri"][d][:], lf[:], True, True, [C["tri"][d], lf], [LA])
            self.ts("dve", csc[:], LA[:], -80.0, None, ALU.max, ALU.bypass, [LA], [csc])
            self.act(e1[:], csc[:], AF.Exp, [csc], [e1])
            self.act(e2[:], csc[:], AF.Exp, [csc], [e2], scale=-1.0)
            self.tt("dve", ops[:, (6 * d + 4) * 512:(6 * d + 5) * 512], qs[:], e1[:], ALU.mult, [qs, e1], [ops])
            self.tt("dve", ops[:, (6 * d + 5) * 512:(6 * d + 6) * 512], kd[:], e2[:], ALU.mult, [kd, e2], [ops])
            for h in range(4):
                c = (d * 4 + h) * 2
                self.mm(LB[:, c:c + 2], lf[:, h * 128:(h + 1) * 128], C["cind"][:], True, True, [lf, C["cind"]], [LB])
        self.act(dcs[:, 0:16].rearrange("p (c x) -> p c x", c=2), LB[:, 0:16].rearrange("p (x c) -> p c x", c=2),
                 AF.Exp, [LB], [dcs])
        self.dma("pool", R["ECs"][g].rearrange("c p x -> p c x"), dcs[:, 0:16].rearrange("p (c x) -> p c x", c=2),
                 [dcs], [self.bR["ECs"]])
        o3 = ops[:, :].rearrange("p (s c) -> p s c", s=14)
        OP3 = OPSd[t0:t0 + 128, :].rearrange("t (s c) -> t s c", s=14)
        self.dma("pool", OP3[:, 4:6, :], o3[:, 4:6, :], [ops], [self.bR["OPS"]])
        self.dma("pool", OP3[:, 10:12, :], o3[:, 10:12, :], [ops], [self.bR["OPS"]])
        self.dma("pool", OP3[:, 13:14, :], o3[:, 13:14, :], [ops], [self.bR["OPS"]])
        self.dma("pool", CXd[t0:t0 + 128, 1024:1536], cxs[:, 1024:1536], [cxs], [self.bR["CX"]])

    def A_late_R(self, X, g):
        C, R = self.C, self.R
        OPSd, CXd = R["OPS"], R["CX"]
        prow = C["prow"]
        kk_bc, ka_bc, rk_bc = prow[:, 512:1024], prow[:, 1024:1536], prow[:, 1536:2048]
        b = g % 2
        t0 = g * 128
        ops, cxs, dcs = X["OPst"][b], X["CXst"][b], X["dcs"][b]
        r_, k_, v_ = X["rkv"][b]
        lt, lw, lg, W2x, G2 = X["lt"][b], X["lw"], X["lg"], X["W2x"], X["G2"]
        ft, sm = X["ft"], X["sm"]
        LA, LB, LC = X["LA"], X["LB"], X["LC"]
        identf = C["identf"]
        for i in range(3):
            self.tr(LC[0:32, i * 128:(i + 1) * 128], lt[:, i * 32:(i + 1) * 32], identf[:], [lt, identf], [LC])
        self.tr(LC[0:96, 384:512], lt[:, 96:192], identf[:], [lt, identf], [LC])
        self.cp("dve", lw[0:32, :], LC[0:32, 0:384], [LC], [lw])
        self.cp("dve", lg[:, :], LC[0:96, 384:512], [LC], [lg])
        a_, sgf, sgb = ft[0], X["sg"][0], X["sg"][1]
        self.mm(LA[:], lw[0:32, 256:384], W2x[0:32, 1024:1536], True, True, [lw, W2x], [LA])
        self.mm(LB[:], lw[0:32, 128:256], W2x[0:32, 512:1024], True, True, [lw, W2x], [LB])
        self.mm(LC[:], lg[:, :], G2[:, :], True, True, [lg, G2], [LC])
        self.tt("dve", a_[:], LA[:], prow[:, 4096:4608], ALU.add, [LA, prow], [a_])
        self.act(a_[:], a_[:], AF.Sigmoid, [a_], [a_])
        self.mm(LA[:], lw[0:32, 0:128], W2x[0:32, 0:512], True, True, [lw, W2x], [LA])
        self.tt("dve", sgb[:], LB[:], prow[:, 3584:4096], ALU.add, [LB, prow], [sgb])
        self.act(sgb[:], sgb[:], AF.Sigmoid, [sgb], [sgb])
        self.act(cxs[:, 0:512], LC[:], AF.Copy, [LC], [cxs])
        self.tt("dve", sgf[:], LA[:], prow[:, 3072:3584], ALU.add, [LA, prow], [sgf])
        self.act(sgf[:], sgf[:], AF.Sigmoid, [sgf], [sgf])
        kx, sq, kk, t1, kp, nb, t2 = ft[1], ft[2], ft[3], ft[4], ft[5], ft[6], ft[4]
        self.tt("dve", kx[:], k_[:], kk_bc, ALU.mult, [k_, prow], [kx])
        self.act(sq[:], kx[:], AF.Square, [kx], [sq])
        self.red(sm[0][:], sq[:].rearrange("p (h j) -> p h j", h=8), [sq], [sm[0]])
        self.act(sm[1][:], sm[0][:], AF.Sqrt, [sm[0]], [sm[1]], bias=1e-12)
        self.recip(sm[1][:], sm[1][:], [sm[1]], [sm[1]])
        self.tt("dve", kk[:].rearrange("p (h j) -> p h j", h=8), kx[:].rearrange("p (h j) -> p h j", h=8),
                sm[1][:].unsqueeze(2).broadcast_to([128, 8, 64]), ALU.mult, [kx, sm[1]], [kk])
        self.tt("dve", t1[:], a_[:], ka_bc, ALU.mult, [a_, prow], [t1])
        self.tt("dve", t1[:], t1[:], C["omka"][:], ALU.add, [t1, C["omka"]], [t1])
        self.tt("dve", kp[:], k_[:], t1[:], ALU.mult, [k_, t1], [kp])
        self.stt(nb[:], kk[:], -1.0, a_[:], ALU.mult, ALU.mult, [kk, a_], [nb])
        self.tt("dve", t2[:], r_[:], kp[:], ALU.mult, [r_, kp], [t2])
        self.tt("dve", t2[:], t2[:], rk_bc, ALU.mult, [t2, prow], [t2])
        self.red(sm[2][:], t2[:].rearrange("p (h j) -> p h j", h=8), [t2], [sm[2]])
        self.tt("dve", cxs[:, 512:1024].rearrange("p (h j) -> p h j", h=8), v_[:].rearrange("p (h j) -> p h j", h=8),
                sm[2][:].unsqueeze(2).broadcast_to([128, 8, 64]), ALU.mult, [v_, sm[2]], [cxs])
        self.cp("act", ops[:, 12 * 512:13 * 512], v_[:], [v_], [ops])
        for d in range(2):
            sg_ = (sgf, sgb)[d]
            pc = (LA, LB)[d]
            gi, ginv, tmp, ge = ft[7], ft[8], ft[2], ft[1]
            self.mm(pc[:], C["tri"][d][:], sg_[:], True, True, [C["tri"][d], sg_], [pc])
            self.act(gi[:], pc[:], AF.Exp, [pc], [gi], scale=-DECAY)
            self.act(ginv[:], pc[:], AF.Exp, [pc], [ginv], scale=DECAY)
            self.tt("dve", tmp[:], pc[:], sg_[:], ALU.subtract, [pc, sg_], [tmp])
            self.act(ge[:], tmp[:], AF.Exp, [tmp], [ge], scale=-DECAY)
            b0 = 6 * d * 512
            self.tt("dve", ops[:, b0:b0 + 512], r_[:], gi[:], ALU.mult, [r_, gi], [ops])
            self.tt("dve", ops[:, b0 + 512:b0 + 1024], kp[:], ginv[:], ALU.mult, [kp, ginv], [ops])
            self.tt("dve", ops[:, b0 + 1024:b0 + 1536], nb[:], ginv[:], ALU.mult, [nb, ginv], [ops])
            self.tt("dve", ops[:, b0 + 1536:b0 + 2048], kk[:], ge[:], ALU.mult, [kk, ge], [ops])
            for h in range(8):
                c = (d * 8 + h) * 2
                self.mm(LC[0:64, c:c + 2], sg_[:, h * 64:(h + 1) * 64], C["cind"][:], True, True, [sg_, C["cind"]], [LC])
        self.act(dcs[0:64, 0:32].rearrange("p (c x) -> p c x", c=2), LC[0:64, 0:32].rearrange("p (x c) -> p c x", c=2),
                 AF.Exp, [LC], [dcs], scale=-DECAY)
        self.dma("pool", R["GCs"][g].rearrange("c p x -> p c x"), dcs[0:64, 0:32].rearrange("p (c x) -> p c x", c=2),
                 [dcs], [self.bR["GCs"]])
        o3 = ops[:, :].rearrange("p (s c) -> p s c", s=14)
        OP3 = OPSd[t0:t0 + 128, :].rearrange("t (s c) -> t s c", s=14)
        self.dma("pool", OP3[:, 0:4, :], o3[:, 0:4, :], [ops], [self.bR["OPS"]])
        self.dma("pool", OP3[:, 6:10, :], o3[:, 6:10, :], [ops], [self.bR["OPS"]])
        self.dma("pool", OP3[:, 12:13, :], o3[:, 12:13, :], [ops], [self.bR["OPS"]])
        self.dma("pool", CXd[t0:t0 + 128, 0:1024], cxs[:, 0:1024], [cxs], [self.bR["CX"]])

    def phaseA_oldR(self):
        nc, I, C, R = self.nc, self.I, self.C, self.R
        mods = C["mods"]
        xTv = I["xT"].rearrange("(k p) t -> p k t", p=128)
        for part in ("R",):
            halo = 0 if part == "H" else 1
            nw = 128 + 2 * halo
            with self.scope() as st:
                nF = 14
                ft = [self.sb(st, f"ft{i}", [128, 512], F32) for i in range(nF)]
                if part == "H":
                    WH = self.sb(st, "WH", [128, 8 * PA], BF16)
                    wv = I["w_in"].rearrange("(k p) c -> p k c", p=128)
                    for k in range(KD):
                        self.dma("pool", WH[:, k * PA:(k + 1) * PA], wv[:, k, 0:PA], [], [WH])
                else:
                    WR = [self.sb(st, f"WR{t}", [128, 8 * PB], BF16) for t in range(3)]
                    with self.scope() as st2:
                        cvbc = self.sb(st2, "cvbc", [128, 3 * PB], F32)
                        self.dma("sp", cvbc[:], I["rwkv_conv"].rearrange("(o a) c -> o (a c)", o=1).partition_broadcast(128),
                                 [], [cvbc])
                        stg = [self.sb(st2, f"wstg{i}", [128, PB], F32) for i in range(2)]
                        wv = I["w_in"].rearrange("(k p) c -> p k c", p=128)
                        for k in range(KD):
                            sg = stg[k % 2]
                            self.dma("sp", sg[:], wv[:, k, PA:PIN], [], [sg])
                            for t in range(3):
                                self.tt("dve" if t != 1 else "pool", WR[t][:, k * PB:(k + 1) * PB], sg[:],
                                        cvbc[:, t * PB:(t + 1) * PB], ALU.mult, [sg, cvbc], [WR[t]])
                    for t in range(3):
                        self.dbg(f"dbg_WR{t}", WR[t][:, 0:2 * PB], [128, 2 * PB], WR[t])
                    W2x = self.sb(st, "W2x", [32, 1536], BF16)
                    self.dma("pool", W2x[:], I["w2x"], [], [W2x])
                    G2 = self.sb(st, "G2", [96, 512], BF16)
                    self.dma("pool", G2[:], I["g2"], [], [G2])
                    lw = self.sb(st, "lw", [32, 384], BF16)
                    lg = self.sb(st, "lg", [96, 128], BF16)
                    lt = self.sb(st, "lt", [128, 192], BF16)
                tiles = {
                    "sq": self.sb(st, "sq", [128, 8 * nw], BF16),
                    "rstd": self.sb(st, "rstd", [128, nw], F32),
                    "ntmp": [self.sb(st, f"ntmp{i}", [128, nw], F32) for i in range(2)],
                }
                xts = [self.sb(st, f"xt{i}", [128, 8 * nw], F32) for i in range(2)]
                hTs = [self.sb(st, f"hT{i}", [128, 8 * nw], BF16) for i in range(2)]
                OPst = [self.sb(st, f"OPst{i}", [128, 14 * 512], BF16) for i in range(2)]
                CXst = [self.sb(st, f"CXst{i}", [128, 3 * 512], BF16) for i in range(2)]
                sm = [self.sb(st, f"sm{i}", [128, 8], F32) for i in range(4)]
                dcs = [self.sb(st, f"dcs{i}", [128, 32], F32) for i in range(2)]
                if part == "H":
                    pb = [self.ps(st, f"pb{i}", [128, 512], F32) for i in range(8)]
                else:
                    pb = [self.ps(st, f"pb{i}", [128, 512], F32) for i in range(7)]
                    pbT = self.ps(st, "pbT", [128, 1024], BF16)
                for g in range(self.NG):
                    t0 = g * 128
                    s0, sl, kind = [s for s in self.seqs if s[0] <= t0 < s[0] + s[1]][0]
                    s1 = s0 + sl
                    lo, hi = max(t0 - halo, s0), min(t0 + 128 + halo, s1)
                    off = lo - (t0 - halo)
                    n = hi - lo
                    xt, hT = xts[g % 2], hTs[g % 2]
                    ops, cxs = OPst[g % 2], CXst[g % 2]
                    x3 = xt[:, :].rearrange("p (k t) -> p k t", k=8)
                    if n < nw:
                        self.memset("dve", xt[:, :], 0.0, [xt])
                    self.dma("sp", x3[:, :, off:off + n], xTv[:, :, lo:hi], [], [xt])
                    self.norm_mod(tiles, xt, nw, pb[0], mods[:, 0, kind, :], mods[:, 1, kind, :], hT, mods)
                    h3 = hT[:, :].rearrange("p (k t) -> p k t", k=8)
                    if off > 0:
                        self.memset("dve", h3[:, :, 0:off], 0.0, [hT])
                    if off + n < nw:
                        self.memset("dve", h3[:, :, off + n:nw], 0.0, [hT])
                    if part == "H":
                        self.phaseA_H(g, t0, hT, nw, WH, pb, ft, dcs[g % 2], ops, cxs)
                    else:
                        self.phaseA_R_old(g, t0, hT, nw, WR, W2x, G2, lw, lg, lt, pb, pbT, ft, sm, dcs[g % 2], ops, cxs)


    def phaseA_R_old(self, g, t0, hT, nw, WR, W2x, G2, lw, lg, lt, pb, pbT, ft, sm, dcs, ops, cxs):
        C, R = self.C, self.R
        OPSd, CXd = R["OPS"], R["CX"]
        prow = C["prow"]
        kk_bc, ka_bc, rk_bc = prow[:, 512:1024], prow[:, 1024:1536], prow[:, 1536:2048]
        for cg in range(4):
            c0 = cg * 512
            cn = 512 if cg < 3 else 192
            for t in range(3):
                for k in range(KD):
                    self.mm(pb[1 + cg][:, 0:cn], hT[:, k * nw + t:k * nw + t + 128], WR[t][:, k * PB + c0:k * PB + c0 + cn],
                            t == 0 and k == 0, t == 2 and k == KD - 1, [hT, WR[t]], [pb[1 + cg]])
        pr, pk, pv, pl = pb[1], pb[2], pb[3], pb[4]
        self.act(lt[:, 0:64], pl[:, 0:64], AF.Tanh, [pl], [lt])
        self.act(lt[:, 64:96], pl[:, 64:96], AF.Copy, [pl], [lt])
        self.act(lt[:, 96:192], pl[:, 96:192], AF.Sigmoid, [pl], [lt])
        ident = C["ident"]
        for i in range(3):
            self.tr(pbT[0:32, i * 128:(i + 1) * 128], lt[:, i * 32:(i + 1) * 32], ident[:], [lt, ident], [pbT])
        self.tr(pbT[0:96, 384:512], lt[:, 96:192], ident[:], [lt, ident], [pbT])
        self.cp("dve", lw[0:32, :], pbT[0:32, 0:384], [pbT], [lw])
        self.cp("dve", lg[:, :], pbT[0:96, 384:512], [pbT], [lg])
        pa, pwf, pwb, pg = pb[4], pb[5], pb[6], pb[0]
        self.mm(pa[:], lw[0:32, 256:384], W2x[0:32, 1024:1536], True, True, [lw, W2x], [pa])
        self.mm(pwf[:], lw[0:32, 0:128], W2x[0:32, 0:512], True, True, [lw, W2x], [pwf])
        self.mm(pwb[:], lw[0:32, 128:256], W2x[0:32, 512:1024], True, True, [lw, W2x], [pwb])
        self.mm(pg[:], lg[:, :], G2[:, :], True, True, [lg, G2], [pg])
        a_, sgf, sgb = ft[0], ft[1], ft[2]
        self.tt("dve", a_[:], pa[:], prow[:, 4096:4608], ALU.add, [pa, prow], [a_])
        self.act(a_[:], a_[:], AF.Sigmoid, [a_], [a_])
        self.tt("dve", sgf[:], pwf[:], prow[:, 3072:3584], ALU.add, [pwf, prow], [sgf])
        self.act(sgf[:], sgf[:], AF.Sigmoid, [sgf], [sgf])
        self.tt("dve", sgb[:], pwb[:], prow[:, 3584:4096], ALU.add, [pwb, prow], [sgb])
        self.act(sgb[:], sgb[:], AF.Sigmoid, [sgb], [sgb])
        self.act(cxs[:, 0:512], pg[:], AF.Copy, [pg], [cxs])
        kx, sq, kk, t1, kp, nb, t2 = ft[3], ft[4], ft[5], ft[6], ft[7], ft[8], ft[9]
        self.tt("dve", kx[:], pk[:], kk_bc, ALU.mult, [pk, prow], [kx])
        self.act(sq[:], kx[:], AF.Square, [kx], [sq])
        self.red(sm[0][:], sq[:].rearrange("p (h j) -> p h j", h=8), [sq], [sm[0]])
        self.act(sm[1][:], sm[0][:], AF.Sqrt, [sm[0]], [sm[1]], bias=1e-12)
        self.recip(sm[1][:], sm[1][:], [sm[1]], [sm[1]])
        self.tt("dve", kk[:].rearrange("p (h j) -> p h j", h=8), kx[:].rearrange("p (h j) -> p h j", h=8),
                sm[1][:].unsqueeze(2).broadcast_to([128, 8, 64]), ALU.mult, [kx, sm[1]], [kk])
        self.tt("dve", t1[:], a_[:], ka_bc, ALU.mult, [a_, prow], [t1])
        self.tt("dve", t1[:], t1[:], C["omka"][:], ALU.add, [t1, C["omka"]], [t1])
        self.tt("dve", kp[:], pk[:], t1[:], ALU.mult, [pk, t1], [kp])
        self.stt(nb[:], kk[:], -1.0, a_[:], ALU.mult, ALU.mult, [kk, a_], [nb])
        self.tt("dve", t2[:], pr[:], kp[:], ALU.mult, [pr, kp], [t2])
        self.tt("dve", t2[:], t2[:], rk_bc, ALU.mult, [t2, prow], [t2])
        self.red(sm[2][:], t2[:].rearrange("p (h j) -> p h j", h=8), [t2], [sm[2]])
        self.tt("dve", cxs[:, 512:1024].rearrange("p (h j) -> p h j", h=8), pv[:].rearrange("p (h j) -> p h j", h=8),
                sm[2][:].unsqueeze(2).broadcast_to([128, 8, 64]), ALU.mult, [pv, sm[2]], [cxs])
        self.act(ops[:, 12 * 512:13 * 512], pv[:], AF.Copy, [pv], [ops])
        pgc = pb[4]
        for d in range(2):
            sg_ = (sgf, sgb)[d]
            pc = pb[5 + d]
            gi, ginv, tmp, ge = ft[10], ft[11], ft[12], ft[13]
            self.mm(pc[:], C["tri"][d][:], sg_[:], True, True, [C["tri"][d], sg_], [pc])
            self.act(gi[:], pc[:], AF.Exp, [pc], [gi], scale=-DECAY)
            self.act(ginv[:], pc[:], AF.Exp, [pc], [ginv], scale=DECAY)
            self.tt("dve", tmp[:], pc[:], sg_[:], ALU.subtract, [pc, sg_], [tmp])
            self.act(ge[:], tmp[:], AF.Exp, [tmp], [ge], scale=-DECAY)
            b0 = 6 * d * 512
            self.tt("dve", ops[:, b0:b0 + 512], pr[:], gi[:], ALU.mult, [pr, gi], [ops])
            self.tt("dve", ops[:, b0 + 512:b0 + 1024], kp[:], ginv[:], ALU.mult, [kp, ginv], [ops])
            self.tt("dve", ops[:, b0 + 1024:b0 + 1536], nb[:], ginv[:], ALU.mult, [nb, ginv], [ops])
            self.tt("dve", ops[:, b0 + 1536:b0 + 2048], kk[:], ge[:], ALU.mult, [kk, ge], [ops])
            for h in range(8):
                c = (d * 8 + h) * 2
                self.mm(pgc[0:64, c:c + 2], sg_[:, h * 64:(h + 1) * 64], C["cind"][:], True, True, [sg_, C["cind"]], [pgc])
        self.act(dcs[0:64, 0:32].rearrange("p (c x) -> p c x", c=2), pgc[0:64, 0:32].rearrange("p (x c) -> p c x", c=2),
                 AF.Exp, [pgc], [dcs], scale=-DECAY)
        self.dma("pool", R["GCs"][g].rearrange("c p x -> p c x"), dcs[0:64, 0:32].rearrange("p (c x) -> p c x", c=2),
                 [dcs], [self.bR["GCs"]])
        o3 = ops[:, :].rearrange("p (s c) -> p s c", s=14)
        OP3 = OPSd[t0:t0 + 128, :].rearrange("t (s c) -> t s c", s=14)
        self.dma("pool", OP3[:, 0:4, :], o3[:, 0:4, :], [ops], [self.bR["OPS"]])
        self.dma("pool", OP3[:, 6:10, :], o3[:, 6:10, :], [ops], [self.bR["OPS"]])
        self.dma("pool", OP3[:, 12:13, :], o3[:, 12:13, :], [ops], [self.bR["OPS"]])
        self.dma("pool", CXd[t0:t0 + 128, 0:1024], cxs[:, 0:1024], [cxs], [self.bR["CX"]])


    def scan(self):
        C, R, I, O = self.C, self.R, self.I, self.O
        ident, M1, M3, MI = C["ident"], C["M1"], C["M3"], C["MI"]
        OPSd = R["OPS"]
        with self.scope() as st:
            OP = [self.sb(st, f"OP{i}", [128, 8 * 512], BF16) for i in range(2)]
            gCt = [self.sb(st, f"gCt{i}", [128, 8], F32) for i in range(2)]
            eCt = [self.sb(st, f"eCt{i}", [128, 8], F32) for i in range(2)]
            XT = self.sb(st, "XT", [128, 8 * 256], BF16)
            GS = self.sb(st, "GS", [128, 8 * 320], BF16)
            XA = [self.sb(st, f"XA{i}", [128, 8 * 128], BF16) for i in range(2)]
            NN = [self.sb(st, f"NN{i}", [128, 8 * 128], BF16) for i in range(2)]
            GT = self.sb(st, "GT", [128, 512], BF16)
            Pb = self.sb(st, "Pb", [128, 512], BF16)
            T32 = self.sb(st, "T32", [128, 512], F32)
            Tb = [self.sb(st, f"Tb{i}", [128, 512], BF16) for i in range(2)]
            Ttmp = self.sb(st, "Ttmp", [128, 512], F32)
            Ysb = [self.sb(st, f"Ysb{i}", [128, 512], F32) for i in range(2)]
            HT = self.sb(st, "HT", [128, 1024], BF16)
            SCz = self.sb(st, "SCz", [128, 4 * 2 * 64], BF16)
            Osb = [self.sb(st, f"Osb{i}", [128, 512], F32) for i in range(2)]
            S32 = self.sb(st, "S32", [128, 1024], F32)
            Sb = [self.sb(st, f"Sb{i}", [128, 1024], BF16) for i in range(2)]
            Stmp = self.sb(st, "Stmp", [128, 512], F32)
            BT0 = self.ps(st, "BT0", [128, 1024], BF16)
            BT1 = self.ps(st, "BT1", [128, 1024], BF16)
            PX = self.ps(st, "PX", [128, 1024], F32)
            PN = self.ps(st, "PN", [128, 1024], F32)
            F4 = self.ps(st, "F4", [128, 512], F32)
            F5 = self.ps(st, "F5", [128, 512], F32)
            self.memset("pool", SCz[:], 0.0, [SCz])
            M1b = M1[:].unsqueeze(1).broadcast_to([128, 8, 128])
            M3b = M3[:].unsqueeze(1).broadcast_to([128, 8, 64])
            stepno = 0
            for si, (s0, sl, kind) in enumerate(self.seqs):
                nch = sl // 64
                if kind == 0:
                    self.memset("pool", T32[:], 0.0, [T32])
                    self.memset("pool", S32[:], 0.0, [S32])
                else:
                    for l in range(2):
                        self.dma("sp", T32[64 * l:64 * l + 64, :], I["st_r"][l], [], [T32])
                        self.dma("sp", S32[:, l * 512:(l + 1) * 512], I["st_h"][l], [], [S32])
                cur = 0
                self.cp("act", Tb[cur][:], T32[:], [T32], [Tb[cur]])
                self.cp("act", Sb[cur][:], S32[:], [S32], [Sb[cur]])
                for i in range(nch):
                    op, gc, ec = OP[stepno % 2], gCt[stepno % 2], eCt[stepno % 2]
                    ysb, osb = Ysb[stepno % 2], Osb[stepno % 2]
                    stepno += 1
                    tok = [s0 + 64 * i, s0 + 64 * (nch - 1 - i)]
                    o3 = op[:, :].rearrange("p (s c) -> p s c", s=8)
                    for l in range(2):
                        src = OPSd[tok[l]:tok[l] + 64, :].rearrange("t (s c) -> t s c", s=14)
                        self.dma("sp", o3[64 * l:64 * l + 64, 0:6, :], src[:, 6 * l:6 * l + 6, :], [self.bR["OPS"]], [op])
                        self.dma("sp", o3[64 * l:64 * l + 64, 6:8, :], src[:, 12:14, :], [self.bR["OPS"]], [op])
                        gg, cc = tok[l] // 128, (tok[l] % 128) // 64
                        self.dma("sp", gc[64 * l:64 * l + 64, :], R["GCs"][gg, cc][:, 8 * l:8 * l + 8], [self.bR["GCs"]], [gc])
                        self.dma("sp", ec[:, 4 * l:4 * l + 4], R["ECs"][gg, cc][:, 4 * l:4 * l + 4], [self.bR["ECs"]], [ec])
                    for h in range(8):
                        bt = BT0 if h < 4 else BT1
                        for X, slot in enumerate((3, 0, 2, 1)):
                            for l in range(2):
                                r0 = 64 * l
                                self.tr(bt[r0:r0 + 64, (h % 4) * 256 + X * 64:(h % 4) * 256 + X * 64 + 64],
                                        op[r0:r0 + 64, slot * 512 + h * 64:slot * 512 + h * 64 + 64],
                                        ident[r0:r0 + 64, r0:r0 + 64], [op, ident], [bt])
                    self.cp("dve", XT[:, 0:1024], BT0[:, :], [BT0], [XT])
                    self.cp("act", XT[:, 1024:2048], BT1[:, :], [BT1], [XT])
                    for h in range(8):
                        for l in range(2):
                            r0 = 64 * l
                            xb = h * 256
                            self.mm(PX[r0:r0 + 64, h * 128:h * 128 + 128], XT[r0:r0 + 64, xb + 128:xb + 192],
                                    XT[r0:r0 + 64, xb:xb + 128], True, True, [XT], [PX])
                            self.mm(PN[r0:r0 + 64, h * 128:h * 128 + 128], XT[r0:r0 + 64, xb + 192:xb + 256],
                                    XT[r0:r0 + 64, xb:xb + 128], True, True, [XT], [PN])
                            self.mm(F4[r0:r0 + 64, h * 64:h * 64 + 64], XT[r0:r0 + 64, xb:xb + 64],
                                    XT[r0:r0 + 64, xb + 128:xb + 192], True, True, [XT], [F4])
                    G3 = GS[:, :].rearrange("p (h c) -> p h c", h=8)
                    self.tt("dve", G3[:, :, 0:128], PX[:, :].rearrange("p (h c) -> p h c", h=8), M1b, ALU.mult, [PX, M1], [GS])
                    self.tt("dve", G3[:, :, 128:256], PN[:, :].rearrange("p (h c) -> p h c", h=8), M1b, ALU.mult, [PN, M1], [GS])
                    self.tt("dve", G3[:, :, 256:320], F4[:, :].rearrange("p (h c) -> p h c", h=8), M3b, ALU.mult, [F4, M3], [GS])
                    xa = XA[0]
                    xa3 = xa[:, :].rearrange("p (h c) -> p h c", h=8)
                    self.cp("dve", xa3[:, :, 0:64], op[:, 3 * 512:4 * 512].rearrange("p (h c) -> p h c", h=8), [op], [xa])
                    for h in range(8):
                        for l in range(2):
                            r0 = 64 * l
                            self.mm(F5[r0:r0 + 64, h * 64:h * 64 + 64], GS[r0:r0 + 64, h * 320 + 128:h * 320 + 192],
                                    op[r0:r0 + 64, 6 * 512 + h * 64:6 * 512 + h * 64 + 64], True, True, [GS, op], [F5])
                    self.cp("act", xa3[:, :, 64:128], F5[:, :].rearrange("p (h c) -> p h c", h=8), [F5], [xa])
                    xc = 0
                    for k in range(6):
                        xcur, xnext = XA[xc], XA[1 - xc]
                        if k == 0:
                            nsrc = GS
                            noff = lambda h: h * 320 + 256
                            ntoff = lambda h: h * 320
                        else:
                            nsrc = NN[(k - 1) % 2]
                            noff = lambda h: h * 128
                            ntoff = lambda h: h * 128 + 64
                        if k < 5:
                            nn = NN[k % 2]
                            for h in range(8):
                                for l in range(2):
                                    r0 = 64 * l
                                    self.mm(PN[r0:r0 + 64, h * 128:h * 128 + 64], nsrc[r0:r0 + 64, ntoff(h):ntoff(h) + 64],
                                            nsrc[r0:r0 + 64, noff(h):noff(h) + 64], True, True, [nsrc], [PN])
                                    self.mm(PN[r0:r0 + 64, h * 128 + 64:h * 128 + 128], nsrc[r0:r0 + 64, noff(h):noff(h) + 64],
                                            nsrc[r0:r0 + 64, ntoff(h):ntoff(h) + 64], True, True, [nsrc], [PN])
                        if k < 5:
                            self.cp("act", nn[:, :], PN[:, :], [PN], [nn])
                        for h in range(8):
                            for l in range(2):
                                r0 = 64 * l
                                self.mm(PX[r0:r0 + 64, h * 128:h * 128 + 128], nsrc[r0:r0 + 64, ntoff(h):ntoff(h) + 64],
                                        xcur[r0:r0 + 64, h * 128:h * 128 + 128], True, True, [nsrc, xcur], [PX])
                        self.tt("dve", xnext[:, :], PX[:, :], xcur[:, :], ALU.add, [PX, xcur], [xnext])
                        xc = 1 - xc
                    x6 = XA[xc]
                    x63 = x6[:, :].rearrange("p (h c) -> p h c", h=8)
                    for h in range(8):
                        for l in range(2):
                            r0 = 64 * l
                            self.tr(BT1[r0:r0 + 64, h * 64:h * 64 + 64], x6[r0:r0 + 64, h * 128:h * 128 + 64],
                                    ident[r0:r0 + 64, r0:r0 + 64], [x6, ident], [BT1])
                    self.cp("dve", GT[:, :], BT1[:, 0:512], [BT1], [GT])
                    tbc, tbn = Tb[cur], Tb[1 - cur]
                    for h in range(8):
                        for l in range(2):
                            r0 = 64 * l
                            self.mm(F4[r0:r0 + 64, h * 64:h * 64 + 64], GT[r0:r0 + 64, h * 64:h * 64 + 64],
                                    tbc[r0:r0 + 64, h * 64:h * 64 + 64], True, True, [GT, tbc], [F4])
                    self.tt("dve", Pb[:, :].rearrange("p (h c) -> p h c", h=8), F4[:, :].rearrange("p (h c) -> p h c", h=8),
                            x63[:, :, 64:128], ALU.add, [F4, x6], [Pb])
                    for h in range(8):
                        for l in range(2):
                            r0 = 64 * l
                            yo = PN[r0:r0 + 64, h * 64:h * 64 + 64]
                            self.mm(yo, XT[r0:r0 + 64, h * 256 + 64:h * 256 + 128], tbc[r0:r0 + 64, h * 64:h * 64 + 64],
                                    True, False, [XT, tbc], [PN])
                            self.mm(yo, GS[r0:r0 + 64, h * 320 + 192:h * 320 + 256],
                                    op[r0:r0 + 64, 6 * 512 + h * 64:6 * 512 + h * 64 + 64], False, False, [GS, op], [PN])
                            self.mm(yo, GS[r0:r0 + 64, h * 320 + 64:h * 320 + 128], Pb[r0:r0 + 64, h * 64:h * 64 + 64],
                                    False, True, [GS, Pb], [PN])
                    self.cp("act", ysb[:, :], PN[:, 0:512], [PN], [ysb])
                    for l in range(2):
                        self.dma("pool", R["YS"][l, tok[l]:tok[l] + 64, :], ysb[64 * l:64 * l + 64, :], [ysb], [self.bR["YS"]])
                    for h in range(8):
                        for l in range(2):
                            r0 = 64 * l
                            to = F5[r0:r0 + 64, h * 64:h * 64 + 64]
                            self.mm(to, op[r0:r0 + 64, 1 * 512 + h * 64:1 * 512 + h * 64 + 64],
                                    op[r0:r0 + 64, 6 * 512 + h * 64:6 * 512 + h * 64 + 64], True, False, [op], [F5])
                            self.mm(to, op[r0:r0 + 64, 2 * 512 + h * 64:2 * 512 + h * 64 + 64],
                                    Pb[r0:r0 + 64, h * 64:h * 64 + 64], False, True, [op, Pb], [F5])
                    self.tt("dve", Ttmp[:, :], F5[:, :], T32[:, :], ALU.add, [F5, T32], [Ttmp])
                    self.tt("dve", T32[:, :].rearrange("p (h c) -> p h c", h=8), Ttmp[:, :].rearrange("p (h c) -> p h c", h=8),
                            gc[:, :].unsqueeze(2).broadcast_to([128, 8, 64]), ALU.mult, [Ttmp, gc], [T32])
                    self.cp("act", tbn[:, :], T32[:, :], [T32], [tbn])
                    sbc, sbn = Sb[cur], Sb[1 - cur]
                    for X, slot in enumerate((4, 5)):
                        for h in range(4):
                            c0 = (X * 4 + h) * 128
                            self.tr(BT0[:, c0:c0 + 128], op[:, slot * 512 + h * 128:slot * 512 + h * 128 + 128], ident[:, :],
                                    [op, ident], [BT0])
                    self.cp("dve", HT[:, :], BT0[:, :], [BT0], [HT])
                    for h in range(4):
                        for l in range(2):
                            r0 = 64 * l
                            self.mm(PX[r0:r0 + 64, h * 64:h * 64 + 64], HT[:, (4 + h) * 128 + r0:(4 + h) * 128 + r0 + 64],
                                    HT[:, h * 128 + r0:h * 128 + r0 + 64], True, True, [HT], [PX])
                    for l in range(2):
                        r0 = 64 * l
                        sc4 = SCz[r0:r0 + 64, :].rearrange("p (h l t) -> p h l t", h=4, l=2)
                        self.tt("dve", sc4[:, :, l, :], PX[r0:r0 + 64, 0:256].rearrange("p (h t) -> p h t", h=4),
                                MI[r0:r0 + 64, :].unsqueeze(1).broadcast_to([64, 4, 64]), ALU.mult, [PX, MI], [SCz])
                    for h in range(4):
                        for l in range(2):
                            r0 = 64 * l
                            oo = PX[r0:r0 + 64, 512 + h * 128:512 + h * 128 + 128]
                            self.mm(oo, SCz[:, (h * 2 + l) * 64:(h * 2 + l) * 64 + 64], op[:, 7 * 512 + h * 128:7 * 512 + h * 128 + 128],
                                    True, False, [SCz, op], [PX])
                            self.mm(oo, HT[:, h * 128 + r0:h * 128 + r0 + 64], sbc[:, (l * 4 + h) * 128:(l * 4 + h) * 128 + 128],
                                    False, True, [HT, sbc], [PX])
                    self.cp("act", osb[:, :], PX[:, 512:1024], [PX], [osb])
                    for l in range(2):
                        self.dma("pool", R["OSC"][l, tok[l]:tok[l] + 64, :], osb[64 * l:64 * l + 64, :], [osb], [self.bR["OSC"]])
                    for l in range(2):
                        r0 = 64 * l
                        pu = PN if l == 0 else F4
                        pu0 = 512 if l == 0 else 0
                        for h in range(4):
                            self.mm(pu[:, pu0 + h * 128:pu0 + h * 128 + 128], op[r0:r0 + 64, 5 * 512 + h * 128:5 * 512 + h * 128 + 128],
                                    op[r0:r0 + 64, 7 * 512 + h * 128:7 * 512 + h * 128 + 128], True, True, [op], [pu])
                        self.tt("dve", Stmp[:, :], pu[:, pu0:pu0 + 512], S32[:, l * 512:(l + 1) * 512], ALU.add, [pu, S32], [Stmp])
                        self.tt("dve", S32[:, l * 512:(l + 1) * 512].rearrange("p (h c) -> p h c", h=4),
                                Stmp[:, :].rearrange("p (h c) -> p h c", h=4),
                                ec[:, 4 * l:4 * l + 4].unsqueeze(2).broadcast_to([128, 4, 128]), ALU.mult, [Stmp, ec], [S32])
                    self.cp("act", sbn[:, :], S32[:, :], [S32], [sbn])
                    cur = 1 - cur
                if kind == 0:
                    for l in range(2):
                        self.dma("pool", O["nsr"][si, l], T32[64 * l:64 * l + 64, :], [T32], [self.bO["nsr"]])
                        self.dma("pool", O["nsh"][si, l], S32[:, l * 512:(l + 1) * 512], [S32], [self.bO["nsh"]])

    def ctx_tiles(self):
        out = []
        per = max(1, 512 // self.L_ctx)
        si = 0
        while si < self.n_ctx:
            ns = min(per, self.n_ctx - si)
            out.append((si * self.L_ctx, ns * self.L_ctx))
            si += ns
        return out

    def phaseC1(self):
        C, R, I = self.C, self.R, self.I
        ident, prow, mods, selv = C["ident"], C["prow"], C["mods"], C["selv"]
        hnw_bc, lnw_bc, lnb_bc = prow[:, 0:512], prow[:, 2048:2560], prow[:, 2560:3072]
        xTv = I["xT"].rearrange("(k p) t -> p k t", p=128)
        xov = I["xTo"].rearrange("(k p) t -> p k t", p=128)
        x1v = R["X1T"].rearrange("(k p) t -> p k t", p=128)
        x1wv = R["X1W"].rearrange("(k p) t -> p k t", p=128)
        NCT = self.n_ctx * self.L_ctx
        QL, NQ = self.QL, self.NQ
        WL = QL + 128
        with self.scope() as st:
            WO = self.sb(st, "WO", [128, 8 * D], BF16)
            wv = I["w_out"].rearrange("(k p) c -> p k c", p=128)
            for k in range(KD):
                self.dma("pool", WO[:, k * D:(k + 1) * D], wv[:, k, :], [], [WO])
            yf = [self.sb(st, f"yf{i}", [128, 512], F32) for i in range(2)]
            yb = [self.sb(st, f"yb{i}", [128, 512], F32) for i in range(2)]
            of = [self.sb(st, f"of{i}", [128, 512], F32) for i in range(2)]
            ob = [self.sb(st, f"ob{i}", [128, 512], F32) for i in range(2)]
            cx = [self.sb(st, f"cx{i}", [128, 1536], BF16) for i in range(2)]
            cxa = self.sb(st, "cxa", [128, 1536], F32)
            ft = [self.sb(st, f"c1f{i}", [128, 512], F32) for i in range(4)]
            sm = [self.sb(st, f"c1s{i}", [128, 8], F32) for i in range(6)]
            MIX = self.sb(st, "MIX", [128, 1024], BF16)
            MIXT = self.sb(st, "MIXT", [128, 8 * 512], BF16)
            xt = self.sb(st, "c1xt", [128, 8 * 512], F32)
            x1 = self.sb(st, "c1x1", [128, 8 * 512], F32)
            PT = self.ps(st, "c1PT", [128, 1024], BF16)
            PO = [self.ps(st, f"c1PO{i}", [128, 512], F32) for i in range(2)]
            self._c1i = 0

            def post(y, o, cxt, gq):
                sq, osq = ft[1], ft[3]
                self.act(sq[:, :], y[:, :], AF.Square, [y], [sq])
                y3 = y[:, :].rearrange("p (h j) -> p h j", h=8)
                self.red(sm[0][:, :], y3, [y], [sm[0]])
                self.red(sm[1][:, :], sq[:, :].rearrange("p (h j) -> p h j", h=8), [sq], [sm[1]])
                self.ts("dve", sm[2][:, :], sm[0][:, :], 1.0 / 64, None, ALU.mult, ALU.bypass, [sm[0]], [sm[2]])
                self.tt("dve", sm[3][:, :], sm[2][:, :], sm[2][:, :], ALU.mult, [sm[2]], [sm[3]])
                self.stt(sm[4][:, :], sm[1][:, :], 1.0 / 64, sm[3][:, :], ALU.mult, ALU.subtract, [sm[1], sm[3]], [sm[4]])
                self.act(sm[5][:, :], sm[4][:, :], AF.Sqrt, [sm[4]], [sm[5]], bias=GN_EPS)
                self.recip(sm[5][:, :], sm[5][:, :], [sm[5]], [sm[5]])
                self.tt("dve", y3, y3, sm[2][:, :].unsqueeze(2).broadcast_to([128, 8, 64]), ALU.subtract, [y, sm[2]], [y])
                self.tt("dve", y3, y3, sm[5][:, :].unsqueeze(2).broadcast_to([128, 8, 64]), ALU.mult, [y, sm[5]], [y])
                self.tt("dve", y[:, :], y[:, :], lnw_bc, ALU.mult, [y, prow], [y])
                self.tt("dve", y[:, :], y[:, :], lnb_bc, ALU.add, [y, prow], [y])
                self.tt("dve", y[:, :], y[:, :], cxt[:, 512:1024], ALU.add, [y, cxt], [y])
                self.tt("dve", MIX[:, 512:1024], y[:, :], cxt[:, 0:512], ALU.mult, [y, cxt], [MIX])
                self.act(osq[:, :], o[:, :], AF.Square, [o], [osq])
                self.red(sm[0][:, 0:4], osq[:, :].rearrange("p (h j) -> p h j", h=4), [osq], [sm[0]])
                self.act(sm[1][:, 0:4], sm[0][:, 0:4], AF.Sqrt, [sm[0]], [sm[1]], scale=1.0 / 128, bias=RMS_EPS)
                self.recip(sm[1][:, 0:4], sm[1][:, 0:4], [sm[1]], [sm[1]])
                o3 = o[:, :].rearrange("p (h j) -> p h j", h=4)
                self.tt("dve", o3, o3, sm[1][:, 0:4].unsqueeze(2).broadcast_to([128, 4, 128]), ALU.mult, [o, sm[1]], [o])
                self.tt("dve", o[:, :], o[:, :], hnw_bc, ALU.mult, [o, prow], [o])
                self.tt("dve", MIX[:, 0:512], o[:, :], cxt[:, 1024:1536], ALU.mult, [o, cxt], [MIX])
                for k in range(KD):
                    self.tr(PT[:, k * 128:(k + 1) * 128], MIX[:, k * 128:(k + 1) * 128], ident[:, :], [MIX, ident], [PT])
                self.cp("act", MIXT[:, :].rearrange("p (k t) -> p k t", k=8)[:, :, gq * 128:(gq + 1) * 128],
                        PT[:, :].rearrange("p (k t) -> p k t", k=8), [PT], [MIXT])

            def dense(n, kind, dst):
                for m in range(KD):
                    po = PO[m % 2]
                    for k in range(KD):
                        self.mm(po[:, 0:n], WO[:, k * D + m * 128:k * D + (m + 1) * 128], MIXT[:, k * 512:k * 512 + n],
                                k == 0, k == KD - 1, [WO, MIXT], [po])
                    self.stt(x1[:, m * n:(m + 1) * n], po[:, 0:n], mods[:, 2, kind, m:m + 1], xt[:, m * n:(m + 1) * n],
                             ALU.mult, ALU.add, [po, mods, xt], [x1])
                self.dma("pool", dst, x1[:, 0:8 * n].rearrange("p (k t) -> p k t", k=8), [x1], [])

            gi = 0
            for (t0, n) in self.ctx_tiles():
                self.dma("sp", xt[:, 0:8 * n].rearrange("p (k t) -> p k t", k=8), xTv[:, :, t0:t0 + n], [], [xt])
                for gq in range(n // 128):
                    ta = t0 + gq * 128
                    b = gi % 2
                    gi += 1
                    self.dma("sp", yf[b][:, :], R["YS"][0, ta:ta + 128, :], [], [yf[b]])
                    self.dma("sp", yb[b][:, :], R["YS"][1, ta:ta + 128, :], [], [yb[b]])
                    self.dma("sp", of[b][:, :], R["OSC"][0, ta:ta + 128, :], [], [of[b]])
                    self.dma("sp", ob[b][:, :], R["OSC"][1, ta:ta + 128, :], [], [ob[b]])
                    self.dma("sp", cx[b][:, :], R["CX"][ta:ta + 128, :], [], [cx[b]])
                    y, o = ft[0], ft[2]
                    self.tt("dve", y[:, :], yf[b][:, :], yb[b][:, :], ALU.add, [yf[b], yb[b]], [y])
                    self.tt("dve", o[:, :], of[b][:, :], ob[b][:, :], ALU.add, [of[b], ob[b]], [o])
                    post(y, o, cx[b], gq)
                dense(n, 0, x1v[:, :, t0:t0 + n])
            w0 = 0
            while w0 < WL:
                n = min(512, WL - w0)
                self.dma("sp", xt[:, 0:8 * n].rearrange("p (k t) -> p k t", k=8), xov[:, :, w0:w0 + n], [], [xt])
                for gq in range(n // 128):
                    wj = w0 + gq * 128
                    y, o = ft[0], ft[2]
                    for q in range(NQ):
                        b = gi % 2
                        gi += 1
                        ta = NCT + q * QL - 64 + wj
                        lo, hi = max(ta, NCT), min(ta + 128, NCT + self.L_lat)
                        p0, pn = lo - ta, hi - lo
                        if pn < 128:
                            for tl in (yf[b], yb[b], of[b], ob[b], cx[b]):
                                self.memset("dve", tl[:, :], 0.0, [tl])
                        if pn > 0:
                            self.dma("sp", yf[b][p0:p0 + pn, :], R["YS"][0, lo:hi, :], [], [yf[b]])
                            self.dma("sp", yb[b][p0:p0 + pn, :], R["YS"][1, lo:hi, :], [], [yb[b]])
                            self.dma("sp", of[b][p0:p0 + pn, :], R["OSC"][0, lo:hi, :], [], [of[b]])
                            self.dma("sp", ob[b][p0:p0 + pn, :], R["OSC"][1, lo:hi, :], [], [ob[b]])
                            self.dma("sp", cx[b][p0:p0 + pn, :], R["CX"][lo:hi, :], [], [cx[b]])
                        sc = selv[:, q:q + 1]
                        t_y, t_o = ft[1], ft[3]
                        self.tt("dve", t_y[:, :], yf[b][:, :], yb[b][:, :], ALU.add, [yf[b], yb[b]], [t_y])
                        self.tt("dve", t_o[:, :], of[b][:, :], ob[b][:, :], ALU.add, [of[b], ob[b]], [t_o])
                        if q == 0:
                            self.ts("dve", y[:, :], t_y[:, :], sc, None, ALU.mult, ALU.bypass, [t_y, selv], [y])
                            self.ts("dve", o[:, :], t_o[:, :], sc, None, ALU.mult, ALU.bypass, [t_o, selv], [o])
                            self.ts("dve", cxa[:, :], cx[b][:, :], sc, None, ALU.mult, ALU.bypass, [cx[b], selv], [cxa])
                        else:
                            self.stt(y[:, :], t_y[:, :], sc, y[:, :], ALU.mult, ALU.add, [t_y, selv, y], [y])
                            self.stt(o[:, :], t_o[:, :], sc, o[:, :], ALU.mult, ALU.add, [t_o, selv, o], [o])
                            self.stt(cxa[:, :], cx[b][:, :], sc, cxa[:, :], ALU.mult, ALU.add, [cx[b], selv, cxa], [cxa])
                    post(y, o, cxa, gq)
                dense(n, 1, x1wv[:, :, w0:w0 + n])
                w0 += n

    def phaseC2(self):
        C, R, I, O = self.C, self.R, self.I, self.O
        mods, fnw, selv = C["mods"], C["fnw"], C["selv"]
        x1v = R["X1T"].rearrange("(k p) t -> p k t", p=128)
        x1wv = R["X1W"].rearrange("(k p) t -> p k t", p=128)
        yTv = O["yT"].rearrange("(k p) t -> p k t", p=128)
        NWM = 640
        NCT = self.n_ctx * self.L_ctx
        QL, NQ = self.QL, self.NQ
        with self.scope() as st:
            WG = self.sb(st, "WG", [128, 8 * DFF], BF16)
            WU = self.sb(st, "WU", [128, 8 * DFF], BF16)
            for (W, nm) in ((WG, "w_gate"), (WU, "w_up")):
                wv = I[nm].rearrange("(k p) c -> p k c", p=128)
                for k in range(KD):
                    self.dma("pool", W[:, k * DFF:(k + 1) * DFF], wv[:, k, :], [], [W])
            cw = self.sb(st, "cw", [128, NF * 9], F32)
            self.dma("sp", cw[:, :], I["convw"], [], [cw])
            cvb = self.sb(st, "cvb", [128, NF], F32)
            self.dma("sp", cvb[:, :], I["cvb"], [], [cvb])
            wd = [self.sb(st, f"wd{i}", [128, NF * 128], BF16) for i in range(2)]
            wdv = I["w_down"].rearrange("(f p) c -> p f c", p=128)
            x1w = self.sb(st, "x1w", [128, 8 * NWM], F32)
            tiles = {
                "sq": self.sb(st, "c2sq", [128, 8 * NWM], BF16),
                "rstd": self.sb(st, "c2rstd", [128, NWM], F32),
                "ntmp": [self.sb(st, f"c2ntmp{i}", [128, NWM], F32) for i in range(2)],
            }
            h2T = self.sb(st, "h2T", [128, 8 * NWM], BF16)
            HM = self.sb(st, "HM", [128, NF * 512], BF16)
            GP = [self.sb(st, f"GP{i}", [128, 10 * 66], F32) for i in range(2)]
            GPc = [self.sb(st, f"GPc{i}", [128, 2 * 258], F32) for i in range(2)]
            acc = [self.sb(st, f"c2acc{i}", [128, 512], F32) for i in range(2)]
            gl = [self.sb(st, f"c2gl{i}", [128, 512], F32) for i in range(2)]
            x2 = self.sb(st, "x2", [128, 8 * 512], F32)
            PG = [self.ps(st, f"c2PG{i}", [128, 1024], F32) for i in range(2)]
            PU = [self.ps(st, f"c2PU{i}", [128, 512], F32) for i in range(2)]
            PD = [self.ps(st, f"c2PD{i}", [128, 512], F32) for i in range(2)]
            fi = 0
            wdi = 0
            work = [(0, t0, n, None) for (t0, n) in self.ctx_tiles()]
            for r0 in range(0, QL // 64, 8):
                work.append((1, r0, min(8, QL // 64 - r0) * 64, None))
            for (kind, t0, n, _) in work:
                if kind == 1:
                    r0 = t0
                    nwv = n + 128
                    coff = 64
                    self.dma("sp", x1w[:, 0:8 * nwv].rearrange("p (k t) -> p k t", k=8),
                             x1wv[:, :, 64 * r0:64 * r0 + nwv], [], [x1w])
                else:
                    nwv, coff = n, 0
                    self.dma("sp", x1w[:, 0:8 * nwv].rearrange("p (k t) -> p k t", k=8), x1v[:, :, t0:t0 + n], [], [x1w])
                self.norm_mod(tiles, x1w, nwv, PU[0], mods[:, 3, kind, :], mods[:, 4, kind, :], h2T, mods)
                nrow_t = n // 64
                for gpz in (GP if kind == 1 else GPc):
                    self.memset("dve", gpz[:, :], 0.0, [gpz])
                for f in range(NF):
                    pg, pu = PG[fi % 2], PU[fi % 2]
                    ac, g_ = acc[fi % 2], gl[fi % 2]
                    c0 = 0
                    while c0 < nwv:
                        cn = min(512, nwv - c0)
                        for k in range(KD):
                            self.mm(pg[:, c0:c0 + cn], WG[:, k * DFF + f * 128:k * DFF + (f + 1) * 128],
                                    h2T[:, k * nwv + c0:k * nwv + c0 + cn], k == 0, k == KD - 1, [WG, h2T], [pg])
                        c0 += cn
                    for k in range(KD):
                        self.mm(pu[:, 0:n], WU[:, k * DFF + f * 128:k * DFF + (f + 1) * 128],
                                h2T[:, k * nwv + coff:k * nwv + coff + n], k == 0, k == KD - 1, [WU, h2T], [pu])
                    if kind == 1:
                        gp = GP[fi % 2]
                        gp3 = gp[:, :].rearrange("p (r c) -> p r c", c=66)
                        nrw = nwv // 64
                        self.cp("act", gp3[:, 0:nrw, 1:65], pg[:, 0:nwv].rearrange("p (r c) -> p r c", c=64), [pg], [gp])
                        if r0 == 0:
                            self.ts("dve", gp3[:, 0, 1:65], gp3[:, 0, 1:65], selv[:, NQ:NQ + 1], None, ALU.mult, ALU.bypass,
                                    [gp, selv], [gp])
                        if r0 + nrow_t == QL // 64:
                            self.ts("dve", gp3[:, nrw - 1, 1:65], gp3[:, nrw - 1, 1:65], selv[:, NQ + 1:NQ + 2], None,
                                    ALU.mult, ALU.bypass, [gp, selv], [gp])
                        a3 = ac[:, 0:n].rearrange("p (r c) -> p r c", c=64)
                        first = True
                        for dy in range(3):
                            for dx in range(3):
                                src = gp3[:, dy:dy + nrow_t, dx:dx + 64]
                                wcol = cw[:, f * 9 + dy * 3 + dx:f * 9 + dy * 3 + dx + 1]
                                if first:
                                    self.ts("dve", a3, src, wcol, cvb[:, f:f + 1], ALU.mult, ALU.add, [gp, cw, cvb], [ac])
                                    first = False
                                else:
                                    self.stt(a3, src, wcol, a3, ALU.mult, ALU.add, [gp, cw, ac], [ac])
                    else:
                        gp = GPc[fi % 2]
                        nsq = n // self.L_ctx
                        Lc = self.L_ctx
                        gp3 = gp[:, 0:nsq * (Lc + 2)].rearrange("p (r c) -> p r c", c=Lc + 2)
                        self.cp("act", gp3[:, :, 1:Lc + 1], pg[:, 0:n].rearrange("p (r c) -> p r c", c=Lc), [pg], [gp])
                        a3 = ac[:, 0:n].rearrange("p (r c) -> p r c", c=Lc)
                        for dx in range(3):
                            src = gp3[:, :, dx:dx + Lc]
                            wcol = cw[:, f * 9 + 3 + dx:f * 9 + 3 + dx + 1]
                            if dx == 0:
                                self.ts("dve", a3, src, wcol, cvb[:, f:f + 1], ALU.mult, ALU.add, [gp, cw, cvb], [ac])
                            else:
                                self.stt(a3, src, wcol, a3, ALU.mult, ALU.add, [gp, cw, ac], [ac])
                    self.act(g_[:, 0:n], ac[:, 0:n], AF.Gelu_apprx_tanh, [ac], [g_])
                    self.tt("dve", HM[:, f * 512:f * 512 + n], pu[:, 0:n], g_[:, 0:n], ALU.mult, [pu, g_], [HM])
                    fi += 1
                for m in range(KD):
                    w_ = wd[wdi % 2]
                    pd = PD[wdi % 2]
                    wdi += 1
                    self.dma("pool", w_[:, :].rearrange("p (f c) -> p f c", c=128), wdv[:, :, m * 128:(m + 1) * 128], [], [w_])
                    for f in range(NF):
                        self.mm(pd[:, 0:n], w_[:, f * 128:(f + 1) * 128], HM[:, f * 512:f * 512 + n], f == 0, f == NF - 1,
                                [w_, HM], [pd])
                    self.stt(x2[:, m * n:(m + 1) * n], pd[:, 0:n], mods[:, 5, kind, m:m + 1],
                             x1w[:, m * nwv + coff:m * nwv + coff + n], ALU.mult, ALU.add, [pd, mods, x1w], [x2])
                rstd = tiles["rstd"]
                self.rms_rstd(tiles, x2, n, PU[1], rstd)
                for k in range(KD):
                    self.stt(x2[:, k * n:(k + 1) * n], x2[:, k * n:(k + 1) * n], fnw[:, k:k + 1], rstd[:, 0:n],
                             ALU.mult, ALU.mult, [x2, fnw, rstd], [x2])
                od = NCT + 64 * t0 if kind == 1 else t0
                self.dma("pool", yTv[:, :, od:od + n], x2[:, 0:8 * n].rearrange("p (k t) -> p k t", k=8), [x2], [])

    def build(self, phases=("0", "A", "S", "C1", "C2")):
        self.declare_io()
        with contextlib.ExitStack() as gst:
            mst = contextlib.ExitStack()
            self.phase0(gst, mst)
            if "A" in phases:
                self.phaseA()
            if "S" in phases:
                self.scan()
            if "C1" in phases:
                self.phaseC1()
            self.S.barrier()
            mst.close()
            if "C2" in phases:
                self.phaseC2()
            self.S.barrier()
            fin = list(self.final)
            self.S.finalize(fin)
        return self.nc


def fm(v):
    return np.ascontiguousarray(np.asarray(v, np.float32).reshape(-1, 128).T)


def shared_inputs(inp):
    f = lambda a: np.ascontiguousarray(np.asarray(a, np.float32))
    S = {}
    S["ada_w"] = f(inp["ada_w"][0])
    S["adab"] = fm(inp["ada_b"][0])
    S["nmw"] = fm(inp["norm_mix_w"][0])
    S["nfw"] = fm(inp["norm_ffn_w"][0])
    S["fnw"] = fm(inp["final_norm_w"])
    S["w_in"] = f(inp["w_in"][0])
    S["rwkv_conv"] = f(inp["rwkv_conv"][0])
    S["hgrn_lb"] = f(inp["hgrn_lb"]).reshape(1, 2048)
    S["prow"] = np.concatenate([f(inp[k][0]).reshape(-1) for k in
                                ("hgrn_norm_w", "rwkv_k_k", "rwkv_k_a", "rwkv_r_k", "rwkv_ln_w", "rwkv_ln_b")]
                               + [f(inp["rwkv_w0"][0, 0]), f(inp["rwkv_w0"][0, 1]), f(inp["rwkv_a0"][0])]).reshape(1, 4608)
    w2x = np.zeros((32, 1536), np.float32)
    w2x[0:32, 0:512] = inp["rwkv_w2"][0, 0]
    w2x[0:32, 512:1024] = inp["rwkv_w2"][0, 1]
    w2x[0:32, 1024:1536] = inp["rwkv_a2"][0]
    S["w2x"] = w2x
    S["g2"] = f(inp["rwkv_g2"][0])
    S["w_out"] = f(inp["w_out"][0])
    S["w_gate"] = f(inp["ffn_w_gate"][0])
    S["w_up"] = f(inp["ffn_w_up"][0])
    S["w_down"] = f(inp["ffn_w_down"][0])
    cw = f(inp["ffn_conv"][0]).reshape(9, NF, 128)
    S["convw"] = np.ascontiguousarray(cw.transpose(2, 1, 0).reshape(128, NF * 9))
    S["cvb"] = fm(inp["ffn_conv_b"][0])
    return S


def core_inputs(inp, S, ctx_ids, lat_b, q):
    f = lambda a: np.ascontiguousarray(np.asarray(a, np.float32))
    L_lat = inp["x_sample"].shape[1]
    QL = min(1024, L_lat)
    NQ = L_lat // QL
    xl = f(inp["x_sample"][lat_b])
    xs = [f(inp["x_prompt"][i]) for i in ctx_ids] + [xl]
    x = np.concatenate(xs, axis=0)
    m = dict(S)
    m["xT"] = np.ascontiguousarray(x.T)
    xw = np.zeros((QL + 128, D), np.float32)
    lo, hi = q * QL - 64, q * QL + QL + 64
    a, b = max(lo, 0), min(hi, L_lat)
    xw[a - lo:b - lo] = xl[a:b]
    m["xTo"] = np.ascontiguousarray(xw.T)
    sel = np.zeros((128, NQ + 2), np.float32)
    sel[:, q] = 1.0
    sel[:, NQ] = 0.0 if q == 0 else 1.0
    sel[:, NQ + 1] = 0.0 if q == NQ - 1 else 1.0
    m["selv"] = sel
    cv = np.stack([f(inp["c_ctx"]), f(inp["c"][lat_b])], axis=1)
    m["cvec"] = np.ascontiguousarray(cv.reshape(8, 128, 2).transpose(1, 0, 2).reshape(128, 16))
    sh = f(inp["state_hgrn"][lat_b, 0])
    m["st_h"] = np.ascontiguousarray(sh.transpose(0, 2, 1, 3).reshape(2, 128, 512))
    sr = f(inp["state_rwkv"][lat_b, 0])
    m["st_r"] = np.ascontiguousarray(sr.transpose(0, 3, 1, 2).reshape(2, 64, 512))
    return m


_PROG_CACHE = {}


def get_prog(n_ctx, L_ctx, L_lat, debug=(), phases=("0", "A", "S", "C1", "C2")):
    key = (n_ctx, L_ctx, L_lat, tuple(sorted(debug)), tuple(phases))
    if key not in _PROG_CACHE:
        b = Builder(n_ctx, L_ctx, L_lat, debug)
        nc = b.build(phases)
        _PROG_CACHE[key] = (nc, b)
    return _PROG_CACHE[key]


def kernel(**inp):
    B, L_ctx = inp["x_prompt"].shape[0], inp["x_prompt"].shape[1]
    DB, L_lat = inp["x_sample"].shape[0], inp["x_sample"].shape[1]
    n_ctx = B // N_CORES
    QL = min(1024, L_lat)
    NQ = L_lat // QL
    assert DB * NQ == N_CORES
    nc, _ = get_prog(n_ctx, L_ctx, L_lat)
    S = shared_inputs(inp)
    in_maps = []
    for c in range(N_CORES):
        ctx_ids = list(range(c * n_ctx, (c + 1) * n_ctx))
        in_maps.append(core_inputs(inp, S, ctx_ids, c % DB, c // DB))
    res = run_bass_kernel_spmd(nc, in_maps, core_ids=list(range(N_CORES)))
    y_prompt = np.zeros((B, L_ctx, D), np.float32)
    y_sample = np.zeros((DB, L_lat, D), np.float32)
    nsh = np.zeros((B, 1, 2, 4, 128, 128), np.float32)
    nsr = np.zeros((B, 1, 2, 8, 64, 64), np.float32)
    for c in range(N_CORES):
        r = res.results[c]
        y = np.asarray(r["yT"]).T
        for i in range(n_ctx):
            y_prompt[c * n_ctx + i] = y[i * L_ctx:(i + 1) * L_ctx]
        b, q = c % DB, c // DB
        y_sample[b, q * QL:(q + 1) * QL] = y[n_ctx * L_ctx:]
        h = np.asarray(r["nsh"]).reshape(n_ctx, 2, 128, 4, 128).transpose(0, 1, 3, 2, 4)
        nsh[c * n_ctx:(c + 1) * n_ctx, 0] = h
        rr = np.asarray(r["nsr"]).reshape(n_ctx, 2, 64, 8, 64).transpose(0, 1, 3, 4, 2)
        nsr[c * n_ctx:(c + 1) * n_ctx, 0] = rr
    return (y_prompt, y_sample, nsh, nsr)
```

```python
import contextlib
import numpy as np
import concourse.bass as bass
import concourse.mybir as mybir
from concourse.bass_utils import run_bass_kernel_spmd

F32 = mybir.dt.float32
BF16 = mybir.dt.bfloat16
AF = mybir.ActivationFunctionType
ALU = mybir.AluOpType
AX = mybir.AxisListType

D = 1024
KD = 8
WA = 512
PA = 2560
PB = 1728
PIN = 4288
DFF = 2816
NF = 22
DECAY = 0.6065306597
RMS_EPS = 1e-6
GN_EPS = 64e-5
GRID_W = 64
N_CORES = 8

ENGS = ("pe", "act", "dve", "pool", "sp")


class Buf:
    __slots__ = ("name", "lw", "rd", "rdd")

    def __init__(self, name):
        self.name = name
        self.lw = None
        self.rd = {}
        self.rdd = []


class Ins:
    __slots__ = ("eng", "fn", "deps", "is_dma", "sig", "sigval", "dsem", "dval")

    def __init__(self, eng, fn, deps, is_dma):
        self.eng = eng
        self.fn = fn
        self.deps = deps
        self.is_dma = is_dma
        self.sig = False
        self.sigval = 0
        self.dsem = None
        self.dval = 0


class Sched:
    NDMA = 16

    def __init__(self, nc):
        self.nc = nc
        self.ins = []
        self.last = {}
        self.dmas = []

    def barrier(self):
        deps = sorted(set(self.last.values()) | set(self.dmas))
        if not deps:
            return
        for e in ENGS:
            self.ins.append(Ins(e, None, list(deps), False))
        self.dmas = []

    def add(self, eng, fn, reads=(), writes=(), is_dma=False):
        deps = set()
        for b in reads:
            if b.lw is not None:
                deps.add(b.lw)
        for b in writes:
            if b.lw is not None:
                deps.add(b.lw)
            deps.update(b.rd.values())
            deps.update(b.rdd)
        i = len(self.ins)
        self.ins.append(Ins(eng, fn, sorted(deps), is_dma))
        self.last[eng] = i
        if is_dma:
            self.dmas.append(i)
        for b in reads:
            if is_dma:
                b.rdd.append(i)
            else:
                b.rd[eng] = i
        for b in writes:
            b.lw = i
            b.rd = {}
            b.rdd = []
        return i

    def finalize(self, final_bufs):
        nc = self.nc
        ins = self.ins
        fdeps = set()
        for b in final_bufs:
            if b.lw is not None:
                fdeps.add(b.lw)
        ins.append(Ins("sp", None, sorted(fdeps), False))
        for it in ins:
            for d in it.deps:
                dd = ins[d]
                if dd.eng == "pe" and it.eng == "pe" and not dd.is_dma and not it.is_dma:
                    continue
                dd.sig = True
        st = contextlib.ExitStack()
        esem = {e: st.enter_context(nc.semaphore(f"s_{e}")) for e in ("pe", "act", "dve", "pool")}
        dq = [e for e in ENGS if any(it.is_dma and it.eng == e for it in ins)]
        per = max(2, self.NDMA // max(1, len(dq)))
        dsems = []
        dpool = {}
        for e in dq:
            dpool[e] = list(range(len(dsems), len(dsems) + per))
            dsems += [st.enter_context(nc.semaphore(f"s_dma_{e}{i}")) for i in range(per)]
        dcount = [0] * len(dsems)
        dlast = [None] * len(dsems)
        ecount = {e: 0 for e in esem}
        rr = {e: 0 for e in dq}
        for k, it in enumerate(ins):
            if it.is_dma:
                s = dpool[it.eng][rr[it.eng] % per]
                rr[it.eng] += 1
                if dlast[s] is not None:
                    it.deps = sorted(set(it.deps) | {dlast[s]})
                dcount[s] += 16
                it.dsem = s
                it.dval = dcount[s]
                dlast[s] = k
            elif it.sig:
                ecount[it.eng] += 1
                it.sigval = ecount[it.eng]
        progs = {e: [] for e in ENGS}
        waited = {e: {} for e in ENGS}
        for it in ins:
            w = {}
            for d in it.deps:
                dd = ins[d]
                if dd.is_dma:
                    key = ("d", dd.dsem)
                    val = dd.dval
                else:
                    if dd.eng == "pe" and it.eng == "pe" and not it.is_dma:
                        continue
                    key = ("e", dd.eng)
                    val = dd.sigval
                if w.get(key, 0) < val:
                    w[key] = val
            wl = []
            for key, val in w.items():
                if waited[it.eng].get(key, 0) >= val:
                    continue
                waited[it.eng][key] = val
                wl.append((dsems[key[1]] if key[0] == "d" else esem[key[1]], val))
            progs[it.eng].append((wl, it))
        self.counts = {e: len(progs[e]) for e in ENGS}
        with nc.Block() as block:
            def runner(name):
                def run(eng):
                    for wl, it in progs[name]:
                        for sem, val in wl:
                            eng.wait_ge(sem, val)
                        if it.fn is None:
                            continue
                        r = it.fn(eng)
                        if it.is_dma:
                            r.then_inc(dsems[it.dsem], 16)
                        elif it.sig:
                            r.then_inc(esem[it.eng], 1)
                return run
            block.tensor(runner("pe"))
            block.scalar(runner("act"))
            block.vector(runner("dve"))
            block.gpsimd(runner("pool"))
            block.sync(runner("sp"))
        st.close()


class Tl:
    __slots__ = ("t", "b")

    def __init__(self, t, name):
        self.t = t
        self.b = Buf(name)

    def __getitem__(self, k):
        return self.t[k]


def _bufs(xs):
    return [x.b if isinstance(x, Tl) else x for x in xs]


class Builder:
    def __init__(self, n_ctx, L_ctx, L_lat, debug=()):
        self.n_ctx, self.L_ctx, self.L_lat = n_ctx, L_ctx, L_lat
        self.NT = n_ctx * L_ctx + L_lat
        self.NG = self.NT // 128
        self.QL = min(1024, L_lat)
        self.NQ = L_lat // self.QL
        self.seqs = [(i * L_ctx, L_ctx, 0) for i in range(n_ctx)] + [(n_ctx * L_ctx, L_lat, 1)]
        self.debug = set(debug)
        self.nc = bass.Bass("TRN2", target_bir_lowering=False)
        self.S = Sched(self.nc)
        self.final = []
        self.uid = 0

    @contextlib.contextmanager
    def scope(self):
        with contextlib.ExitStack() as st:
            yield st
            self.S.barrier()

    def sb(self, st, name, shape, dt):
        self.uid += 1
        return Tl(st.enter_context(self.nc.sbuf_tensor(f"{name}_{self.uid}", shape, dt)), name)

    def ps(self, st, name, shape, dt):
        self.uid += 1
        return Tl(st.enter_context(self.nc.psum_tensor(f"{name}_{self.uid}", shape, dt)), name)

    def din(self, name, shape, dt=F32):
        return self.nc.dram_tensor(name, list(shape), dt, kind="ExternalInput").ap()

    def dout(self, name, shape, dt=F32):
        return self.nc.dram_tensor(name, list(shape), dt, kind="ExternalOutput").ap()

    def dscr(self, name, shape, dt):
        kind = "ExternalOutput" if name in self.debug else "Internal"
        return self.nc.dram_tensor(name, list(shape), dt, kind=kind).ap()

    def mm(self, out, lhsT, rhs, start, stop, reads, writes):
        self.S.add("pe", lambda e: e.matmul(out, lhsT=lhsT, rhs=rhs, start=start, stop=stop),
                   _bufs(reads), _bufs(writes))

    def tr(self, out, in_, ident, reads, writes):
        self.S.add("pe", lambda e: e.transpose(out=out, in_=in_, identity=ident), _bufs(reads), _bufs(writes))

    def act(self, out, in_, func, reads, writes, scale=1.0, bias=0.0):
        self.S.add("act", lambda e: e.activation(out=out, in_=in_, func=func, scale=scale, bias=bias),
                   _bufs(reads), _bufs(writes))

    def tt(self, eng, out, in0, in1, op, reads, writes):
        self.S.add(eng, lambda e: e.tensor_tensor(out=out, in0=in0, in1=in1, op=op), _bufs(reads), _bufs(writes))

    def ts(self, eng, out, in0, s1, s2, op0, op1, reads, writes):
        self.S.add(eng, lambda e: e.tensor_scalar(out=out, in0=in0, scalar1=s1, scalar2=s2, op0=op0, op1=op1),
                   _bufs(reads), _bufs(writes))

    def stt(self, out, in0, scalar, in1, op0, op1, reads, writes):
        self.S.add("dve", lambda e: e.scalar_tensor_tensor(out=out, in0=in0, scalar=scalar, in1=in1, op0=op0, op1=op1),
                   _bufs(reads), _bufs(writes))

    def cp(self, eng, out, in_, reads, writes):
        if eng == "act":
            self.act(out, in_, AF.Copy, reads, writes)
        else:
            self.S.add(eng, lambda e: e.tensor_copy(out=out, in_=in_), _bufs(reads), _bufs(writes))

    def red(self, out, in_, reads, writes):
        self.S.add("dve", lambda e: e.tensor_reduce(out=out, in_=in_, axis=AX.X, op=ALU.add), _bufs(reads), _bufs(writes))

    def recip(self, out, in_, reads, writes):
        self.S.add("dve", lambda e: e.reciprocal(out=out, in_=in_), _bufs(reads), _bufs(writes))

    def memset(self, eng, ap, val, writes):
        self.S.add(eng, lambda e: e.memset(ap, val), [], _bufs(writes))

    def asel(self, ap, pattern, cmp, fill, base, cm, tl):
        self.S.add("pool", lambda e: e.affine_select(out=ap, in_=ap, pattern=pattern, compare_op=cmp, fill=fill,
                                                     base=base, channel_multiplier=cm), [tl.b], [tl.b])

    def dma(self, eng, out, in_, reads, writes):
        nt = getattr(self, "_untracked", None)
        if nt is None:
            nt = self._untracked = set(id(b) for b in list(self.bR.values()) + list(self.bO.values()))
        writes = [w for w in _bufs(writes) if id(w) not in nt]
        reads = [r for r in _bufs(reads) if id(r) not in nt]
        self.S.add(eng, lambda e: e.dma_start(out=out, in_=in_), reads, writes, is_dma=True)

    def dbg(self, name, ap, shape, rd):
        if name in self.debug:
            d = self.dout(name, shape)
            b = Buf(name)
            self.dma("pool", d, ap, [rd], [b])
            self.final.append(b)

    def declare_io(self):
        NT, n_ctx = self.NT, self.n_ctx
        I = {}
        I["xT"] = self.din("xT", [D, NT])
        I["cvec"] = self.din("cvec", [128, 16])
        I["ada_w"] = self.din("ada_w", [D, 6 * D])
        I["adab"] = self.din("adab", [128, 48])
        I["nmw"] = self.din("nmw", [128, 8])
        I["nfw"] = self.din("nfw", [128, 8])
        I["fnw"] = self.din("fnw", [128, 8])
        I["w_in"] = self.din("w_in", [D, PIN])
        I["rwkv_conv"] = self.din("rwkv_conv", [3, PB])
        I["hgrn_lb"] = self.din("hgrn_lb", [1, 2048])
        I["prow"] = self.din("prow", [1, 9 * 512])
        I["w2x"] = self.din("w2x", [32, 3 * 512])
        I["g2"] = self.din("g2", [96, 512])
        I["w_out"] = self.din("w_out", [D, D])
        I["w_gate"] = self.din("w_gate", [D, DFF])
        I["w_up"] = self.din("w_up", [D, DFF])
        I["w_down"] = self.din("w_down", [DFF, D])
        I["convw"] = self.din("convw", [128, NF * 9])
        I["cvb"] = self.din("cvb", [128, NF])
        I["xTo"] = self.din("xTo", [D, self.QL + 128])
        I["selv"] = self.din("selv", [128, self.NQ + 2])
        I["st_h"] = self.din("st_h", [2, 128, 4 * 128])
        I["st_r"] = self.din("st_r", [2, 64, 8 * 64])
        self.I = I
        O = {}
        O["yT"] = self.dout("yT", [D, n_ctx * self.L_ctx + self.QL])
        O["nsh"] = self.dout("nsh", [n_ctx, 2, 128, 512])
        O["nsr"] = self.dout("nsr", [n_ctx, 2, 64, 512])
        self.O = O
        self.bO = {k: Buf("out_" + k) for k in O}
        R = {}
        R["OPS"] = self.dscr("OPS", [NT, 14 * 512], BF16)
        R["CX"] = self.dscr("CX", [NT, 3 * 512], BF16)
        R["GCs"] = self.dscr("GCs", [self.NG, 2, 64, 16], F32)
        R["ECs"] = self.dscr("ECs", [self.NG, 2, 128, 8], F32)
        R["YS"] = self.dscr("YS", [2, NT, 512], F32)
        R["OSC"] = self.dscr("OSC", [2, NT, 512], F32)
        R["X1T"] = self.dscr("X1T", [D, n_ctx * self.L_ctx], F32)
        R["X1W"] = self.dscr("X1W", [D, self.QL + 128], F32)
        self.R = R
        self.bR = {k: Buf("scr_" + k) for k in R}

    def phase0(self, gst0, gst):
        nc, I = self.nc, self.I
        C = {}
        C["ident"] = self.sb(gst0, "ident", [128, 128], BF16)
        C["onesD"] = self.sb(gst0, "onesD", [128, 128], BF16)
        mods = self.sb(gst0, "mods", [128, 6, 2, 8], F32)
        fnw = self.sb(gst0, "fnw", [128, 8], F32)
        C["selv"] = self.sb(gst0, "selv", [128, self.NQ + 2], F32)
        self.dma("sp", C["selv"][:], I["selv"], [], [C["selv"]])
        identf = self.sb(gst, "identf", [128, 128], F32)
        self.memset("pool", identf[:], 0.0, [identf])
        self.asel(identf[:], [[-1, 128]], ALU.not_equal, 1.0, 0, 1, identf)
        self.cp("dve", C["ident"][:], identf[:], [identf], [C["ident"]])
        self.memset("pool", C["onesD"][:], 1.0 / D, [C["onesD"]])
        trif = self.sb(gst, "trif", [128, 128], F32)
        self.memset("pool", trif[:], 1.0, [trif])
        self.asel(trif[:], [[1, 128]], ALU.is_ge, 0.0, 0, -1, trif)
        self.asel(trif[:, 64:128], [[0, 64]], ALU.is_ge, 0.0, -64, 1, trif)
        trib = self.sb(gst, "trib", [128, 128], F32)
        self.memset("pool", trib[:], 1.0, [trib])
        self.asel(trib[:], [[-1, 128]], ALU.is_ge, 0.0, 0, 1, trib)
        self.asel(trib[:, 0:64], [[0, 64]], ALU.is_ge, 0.0, 63, -1, trib)
        C["tri"] = [trif, trib]
        t16f = self.sb(gst, "tri16f", [128, 128], BF16)
        self.memset("pool", t16f[:], 1.0, [t16f])
        self.asel(t16f[:], [[1, 128]], ALU.is_ge, 0.0, 0, -1, t16f)
        self.asel(t16f[:, 64:128], [[0, 64]], ALU.is_ge, 0.0, -64, 1, t16f)
        t16b = self.sb(gst, "tri16b", [128, 128], BF16)
        self.memset("pool", t16b[:], 1.0, [t16b])
        self.asel(t16b[:], [[-1, 128]], ALU.is_ge, 0.0, 0, 1, t16b)
        self.asel(t16b[:, 0:64], [[0, 64]], ALU.is_ge, 0.0, 63, -1, t16b)
        C["tri16"] = [t16f, t16b]

        C["identf"] = identf
        cind = self.sb(gst, "cind", [128, 2], F32)
        self.memset("pool", cind[:], 1.0, [cind])
        self.asel(cind[:, 0:1], [[0, 1]], ALU.is_ge, 0.0, 63, -1, cind)
        self.asel(cind[:, 1:2], [[0, 1]], ALU.is_ge, 0.0, -64, 1, cind)
        C["cind"] = cind
        M1 = self.sb(gst, "M1", [128, 128], F32)
        M3 = self.sb(gst, "M3", [128, 64], F32)
        MI = self.sb(gst, "MI", [128, 64], F32)
        for m in (M1, M3, MI):
            self.memset("pool", m[:], 1.0, [m])
        self.asel(M1[0:64, 0:64], [[1, 64]], ALU.is_gt, 0.0, 0, -1, M1)
        self.asel(M1[0:64, 64:128], [[1, 64]], ALU.is_ge, 0.0, 0, -1, M1)
        self.asel(M1[64:128, 0:64], [[-1, 64]], ALU.is_gt, 0.0, 0, 1, M1)
        self.asel(M1[64:128, 64:128], [[-1, 64]], ALU.is_ge, 0.0, 0, 1, M1)
        self.asel(M3[0:64, :], [[-1, 64]], ALU.is_gt, 0.0, 0, 1, M3)
        self.asel(M3[64:128, :], [[1, 64]], ALU.is_gt, 0.0, 0, -1, M3)
        self.asel(MI[0:64, :], [[1, 64]], ALU.is_ge, 0.0, 0, -1, MI)
        self.asel(MI[64:128, :], [[-1, 64]], ALU.is_ge, 0.0, 0, 1, MI)
        C["M1"], C["M3"], C["MI"] = M1, M3, MI
        self.dbg("dbg_M1", M1[:], [128, 128], M1)
        self.dbg("dbg_trif", trif[:], [128, 128], trif)
        self.dbg("dbg_trib", trib[:], [128, 128], trib)

        modT = self.sb(gst, "modT", [128, 48, 2], F32)
        lb = self.sb(gst, "lb", [128, 1024], F32)
        omlb = self.sb(gst, "omlb", [128, 1024], F32)
        prow = self.sb(gst, "prow", [128, 9 * 512], F32)
        omka = self.sb(gst, "omka", [128, 512], F32)
        with self.scope() as st:
            cv = self.sb(st, "cv", [128, 16], F32)
            self.dma("sp", cv[:], I["cvec"], [], [cv])
            scv = self.sb(st, "scv", [128, 16], F32)
            self.act(scv[:], cv[:], AF.Silu, [cv], [scv])
            adab = self.sb(st, "adab", [128, 48], F32)
            self.dma("sp", adab[:], I["adab"], [], [adab])
            acc = self.sb(st, "modacc", [128, 96], F32)
            abuf = [self.sb(st, f"adaw{i}", [128, 3 * D], F32) for i in range(2)]
            pm = self.ps(st, "pmod", [128, 512], F32)
            adv = I["ada_w"].rearrange("(k p) c -> k p c", p=128)
            it = 0
            for k in range(KD):
                for hf in range(2):
                    ab = abuf[it % 2]
                    it += 1
                    self.dma("sp", ab[:, :], adv[k][:, hf * 3072:(hf + 1) * 3072], [], [ab])
                    for mm_ in range(24):
                        m = hf * 24 + mm_
                        self.mm(pm[:, 2 * m:2 * m + 2], ab[:, mm_ * 128:(mm_ + 1) * 128], scv[:, 2 * k:2 * k + 2], True, True,
                                [ab, scv], [pm])
                if k == 0:
                    self.cp("dve", acc[:], pm[:, 0:96], [pm], [acc])
                else:
                    self.tt("dve", acc[:], pm[:, 0:96], acc[:], ALU.add, [pm, acc], [acc])
            self.tt("dve", modT[:], acc[:].rearrange("p (m s) -> p m s", s=2),
                    adab[:].unsqueeze(2).broadcast_to([128, 48, 2]), ALU.add, [acc, adab], [modT])
            self.dbg("dbg_modT", modT[:].rearrange("p m s -> p (m s)"), [128, 96], modT)
            nmw = self.sb(st, "nmw", [128, 8], F32)
            nfw = self.sb(st, "nfw", [128, 8], F32)
            self.dma("sp", nmw[:], I["nmw"], [], [nmw])
            self.dma("sp", nfw[:], I["nfw"], [], [nfw])
            for s in range(2):
                for (wi, off, nw) in ((0, 8, nmw), (3, 32, nfw)):
                    self.stt(mods[:, wi, s, :], modT[:, off:off + 8, s], 1.0, nw[:], ALU.add, ALU.mult,
                             [modT, nw], [mods])
                for (wi, off) in ((1, 0), (2, 16), (4, 24), (5, 40)):
                    self.cp("dve", mods[:, wi, s, :], modT[:, off:off + 8, s], [modT], [mods])
            C["mods"] = mods
            self.dma("sp", fnw[:], I["fnw"], [], [fnw])
            C["fnw"] = fnw
            lbr = self.sb(st, "lbraw", [128, 2048], F32)
            self.dma("sp", lbr[:], I["hgrn_lb"].partition_broadcast(128), [], [lbr])
            lbd = self.sb(st, "lbd", [128, 1024], F32)
            self.tt("dve", lbd[:], lbr[:, 0:1024], lbr[:, 1024:2048], ALU.subtract, [lbr], [lbd])
            self.act(lb[:], lbd[:], AF.Sigmoid, [lbd], [lb])
            self.ts("dve", omlb[:], lb[:], -1.0, 1.0, ALU.mult, ALU.add, [lb], [omlb])
            C["lb"], C["omlb"] = lb, omlb
            self.dma("sp", prow[:], I["prow"].partition_broadcast(128), [], [prow])
            C["prow"] = prow
            self.ts("dve", omka[:], prow[:, 1024:1536], -1.0, 1.0, ALU.mult, ALU.add, [prow], [omka])
            C["omka"] = omka
        self.C = C

    def rms_rstd(self, st_tiles, src, n, pbank, rstd):
        sq = st_tiles["sq"]
        self.act(sq[:, 0:8 * n], src[:, 0:8 * n], AF.Square, [src], [sq])
        c0 = 0
        while c0 < n:
            cn = min(512, n - c0)
            for k in range(KD):
                self.mm(pbank[:, 0:cn], self.C["onesD"][:], sq[:, k * n + c0:k * n + c0 + cn], k == 0, k == KD - 1,
                        [self.C["onesD"], sq], [pbank])
            self.act(rstd[:, c0:c0 + cn], pbank[:, 0:cn], AF.Sqrt, [pbank], [rstd], bias=RMS_EPS)
            c0 += cn
        self.recip(rstd[:, 0:n], rstd[:, 0:n], [rstd], [rstd])

    def norm_mod(self, tiles, xt, n, pbank, A, B, hT, modtl):
        rstd = tiles["rstd"]
        self.rms_rstd(tiles, xt, n, pbank, rstd)
        for k in range(KD):
            tmp = tiles["ntmp"][k % 2]
            self.tt("dve", tmp[:, 0:n], xt[:, k * n:(k + 1) * n], rstd[:, 0:n], ALU.mult, [xt, rstd], [tmp])
            self.act(hT[:, k * n:(k + 1) * n], tmp[:, 0:n], AF.Identity, [tmp, modtl], [hT],
                     scale=A[:, k:k + 1], bias=B[:, k:k + 1])

    def phaseA(self):
        nc, I, C, R = self.nc, self.I, self.C, self.R
        mods = C["mods"]
        xTv = I["xT"].rearrange("(k p) t -> p k t", p=128)
        import os as _os
        _rnew = "1"
        for part in (("H",) if _rnew == "0" else ("H", "R")):
            halo = 0 if part == "H" else 1
            nw = 128 + 2 * halo
            with self.scope() as st:
                X = {"part": part, "nw": nw, "halo": halo}
                if part == "H":
                    X["lf"] = [self.sb(st, f"lf{i}", [128, 512], F32) for i in range(2)]
                else:
                    X["sg"] = [self.sb(st, f"sg{i}", [128, 512], F32) for i in range(2)]
                    X["lt"] = [self.sb(st, f"lt{i}", [128, 192], F32) for i in range(2)]
                if part == "H":
                    WH = self.sb(st, "WH", [128, 8 * PA], BF16)
                    wv = I["w_in"].rearrange("(k p) c -> p k c", p=128)
                    for k in range(KD):
                        self.dma("pool", WH[:, k * PA:(k + 1) * PA], wv[:, k, 0:PA], [], [WH])
                    X["WH"] = WH
                else:
                    WR = [self.sb(st, f"WR{t}", [128, 8 * PB], BF16) for t in range(3)]
                    with self.scope() as st2:
                        cvbc = self.sb(st2, "cvbc", [128, 3 * PB], F32)
                        self.dma("sp", cvbc[:], I["rwkv_conv"].rearrange("(o a) c -> o (a c)", o=1).partition_broadcast(128),
                                 [], [cvbc])
                        stg = [self.sb(st2, f"wstg{i}", [128, PB], F32) for i in range(2)]
                        wv = I["w_in"].rearrange("(k p) c -> p k c", p=128)
                        for k in range(KD):
                            sg = stg[k % 2]
                            self.dma("sp", sg[:], wv[:, k, PA:PIN], [], [sg])
                            for t in range(3):
                                self.tt("dve" if t != 1 else "pool", WR[t][:, k * PB:(k + 1) * PB], sg[:],
                                        cvbc[:, t * PB:(t + 1) * PB], ALU.mult, [sg, cvbc], [WR[t]])
                    X["WR"] = WR
                    X["W2x"] = self.sb(st, "W2x", [32, 1536], BF16)
                    self.dma("pool", X["W2x"][:], I["w2x"], [], [X["W2x"]])
                    X["G2"] = self.sb(st, "G2", [96, 512], BF16)
                    self.dma("pool", X["G2"][:], I["g2"], [], [X["G2"]])
                    X["lw"] = self.sb(st, "lw", [32, 384], BF16)
                    X["lg"] = self.sb(st, "lg", [96, 128], BF16)
                X["tiles"] = {
                    "sq": self.sb(st, "sq", [128, 8 * nw], BF16),
                    "rstd": self.sb(st, "rstd", [128, nw], F32),
                    "ntmp": [self.sb(st, f"ntmp{i}", [128, nw], F32) for i in range(2)],
                }
                X["xts"] = [self.sb(st, f"xt{i}", [128, 8 * nw], F32) for i in range(2)]
                X["hTs"] = [self.sb(st, f"hT{i}", [128, 8 * nw], BF16) for i in range(2)]
                X["OPst"] = [self.sb(st, f"OPst{i}", [128, 14 * 512], BF16) for i in range(2)]
                X["CXst"] = [self.sb(st, f"CXst{i}", [128, 3 * 512], BF16) for i in range(2)]
                X["dcs"] = [self.sb(st, f"dcs{i}", [128, 32], F32) for i in range(2)]
                X["sm"] = [self.sb(st, f"sm{i}", [128, 8], F32) for i in range(4)]
                X["hilo"] = [self.sb(st, f"hilo{i}", [128, 512], BF16) for i in range(2)]
                if part == "H":
                    X["qs"] = [self.sb(st, f"qs{i}", [128, 512], F32) for i in range(2)]
                    X["sgz"] = [[self.sb(st, f"sgz{i}{d}", [128, 512], F32) for d in range(2)] for i in range(2)]
                    X["ft"] = [self.sb(st, f"ft{i}", [128, 512], F32) for i in range(6)]
                    X["pss"] = self.ps(st, "pss", [128, 512], F32)
                    X["pp"] = [self.ps(st, f"pp{i}", [128, 512], F32) for i in range(5)]
                    X["LA"] = self.ps(st, "LA", [128, 512], F32)
                    X["LB"] = self.ps(st, "LB", [128, 512], F32)
                else:
                    X["rkv"] = [[self.sb(st, f"rkv{i}{j}", [128, 512], F32) for j in range(3)] for i in range(2)]
                    X["ft"] = [self.sb(st, f"ft{i}", [128, 512], F32) for i in range(9)]
                    X["pss"] = self.ps(st, "pss", [128, 512], F32)
                    X["pp"] = [self.ps(st, f"pp{i}", [128, 512], F32) for i in range(4)]
                    X["LA"] = self.ps(st, "LA", [128, 512], F32)
                    X["LB"] = self.ps(st, "LB", [128, 512], F32)
                    X["LC"] = self.ps(st, "LC", [128, 512], F32)
                if not (part == "R" and _rnew == "2"):
                    self.A_normproj(X, 0)
                for g in range(self.NG):
                    if part == "R" and _rnew == "2":
                        self.A_normproj(X, g)
                        self.A_early(X, g)
                        self.A_late_R(X, g)
                        continue
                    self.A_early(X, g)
                    if g + 1 < self.NG:
                        self.A_normproj(X, g + 1)
                    if part == "H":
                        self.A_late_H(X, g)
                    else:
                        self.A_late_R(X, g)
        if _rnew == "0":
            self.phaseA_oldR()

    def A_normproj(self, X, g):
        I, C = self.I, self.C
        mods = C["mods"]
        xTv = I["xT"].rearrange("(k p) t -> p k t", p=128)
        halo, nw = X["halo"], X["nw"]
        t0 = g * 128
        s0, sl, kind = [s for s in self.seqs if s[0] <= t0 < s[0] + s[1]][0]
        s1 = s0 + sl
        lo, hi = max(t0 - halo, s0), min(t0 + 128 + halo, s1)
        off = lo - (t0 - halo)
        n = hi - lo
        xt, hT = X["xts"][g % 2], X["hTs"][g % 2]
        x3 = xt[:, :].rearrange("p (k t) -> p k t", k=8)
        if n < nw:
            self.memset("dve", xt[:, :], 0.0, [xt])
        self.dma("sp", x3[:, :, off:off + n], xTv[:, :, lo:hi], [], [xt])
        self.norm_mod(X["tiles"], xt, nw, X["pss"], mods[:, 0, kind, :], mods[:, 1, kind, :], hT, mods)
        h3 = hT[:, :].rearrange("p (k t) -> p k t", k=8)
        if off > 0:
            self.memset("dve", h3[:, :, 0:off], 0.0, [hT])
        if off + n < nw:
            self.memset("dve", h3[:, :, off + n:nw], 0.0, [hT])
        pp = X["pp"]
        if X["part"] == "H":
            WH = X["WH"]
            for cg in range(5):
                for k in range(KD):
                    self.mm(pp[cg][:, :], hT[:, k * nw:k * nw + 128], WH[:, k * PA + cg * 512:k * PA + (cg + 1) * 512],
                            k == 0, k == KD - 1, [hT, WH], [pp[cg]])
        else:
            WR = X["WR"]
            for cg in range(4):
                c0 = cg * 512
                cn = 512 if cg < 3 else 192
                for t in range(3):
                    for k in range(KD):
                        self.mm(pp[cg][:, 0:cn], hT[:, k * nw + t:k * nw + t + 128], WR[t][:, k * PB + c0:k * PB + c0 + cn],
                                t == 0 and k == 0, t == 2 and k == KD - 1, [hT, WR[t]], [pp[cg]])

    def A_early(self, X, g):
        pp = X["pp"]
        b = g % 2
        ops, cxs = X["OPst"][b], X["CXst"][b]
        if X["part"] == "H":
            self.act(X["qs"][b][:], pp[0][:], AF.Silu, [pp[0]], [X["qs"][b]])
            self.act(ops[:, 13 * 512:14 * 512], pp[1][:], AF.Copy, [pp[1]], [ops])
            self.act(X["sgz"][b][0][:], pp[2][:], AF.Sigmoid, [pp[2]], [X["sgz"][b][0]])
            self.act(X["sgz"][b][1][:], pp[3][:], AF.Sigmoid, [pp[3]], [X["sgz"][b][1]])
            self.act(cxs[:, 1024:1536], pp[4][:], AF.Silu, [pp[4]], [cxs])
        else:
            lt = X["lt"][b]
            self.act(lt[:, 0:64], pp[3][:, 0:64], AF.Tanh, [pp[3]], [lt])
            self.act(lt[:, 64:96], pp[3][:, 64:96], AF.Copy, [pp[3]], [lt])
            self.act(lt[:, 96:192], pp[3][:, 96:192], AF.Sigmoid, [pp[3]], [lt])
            r_, k_, v_ = X["rkv"][b]
            self.cp("dve", r_[:], pp[0][:], [pp[0]], [r_])
            self.cp("act", k_[:], pp[1][:], [pp[1]], [k_])
            self.cp("dve", v_[:], pp[2][:], [pp[2]], [v_])

    def A_late_H(self, X, g):
        C, R = self.C, self.R
        OPSd, CXd = R["OPS"], R["CX"]
        b = g % 2
        t0 = g * 128
        ops, cxs, dcs = X["OPst"][b], X["CXst"][b], X["dcs"][b]
        qs = X["qs"][b]
        ft = X["ft"]
        LA, LB = X["LA"], X["LB"]
        for d in range(2):
            sg_ = X["sgz"][b][d]
            lf = X["lf"][d]
            f_, kd, csc, e1, e2 = ft[0], ft[1 + d], ft[3], ft[4], ft[5]
            self.tt("dve", f_[:], sg_[:], C["omlb"][:, d * 512:(d + 1) * 512], ALU.mult, [sg_, C["omlb"]], [f_])
            self.tt("dve", f_[:], f_[:], C["lb"][:, d * 512:(d + 1) * 512], ALU.add, [f_, C["lb"]], [f_])
            self.act(lf[:], f_[:], AF.Ln, [f_], [lf])
            self.ts("dve", kd[:], f_[:], -1.0, 1.0, ALU.mult, ALU.add, [f_], [kd])
            hi, lo = X["hilo"]
            self.cp("dve", hi[:], lf[:], [lf], [hi])
            self.tt("dve", lo[:], lf[:], hi[:], ALU.subtract, [lf, hi], [lo])
            self.mm(LA[:], C["tri16"][d][:], hi[:], True, False, [C["tri16"][d], hi], [LA])
            self.mm(LA[:], C["tri16"][d][:], lo[:], False, True, [C["tri16"][d], lo], [LA])
            self.ts("dve", csc[:], LA[:], -80.0, None, ALU.max, ALU.bypass, [LA], [csc])
            self.act(e1[:], csc[:], AF.Exp, [csc], [e1])
            self.act(e2[:], csc[:], AF.Exp, [csc], [e2], scale=-1.0)
            self.tt("dve", ops[:, (6 * d + 4) * 512:(6 * d + 5) * 512], qs[:], e1[:], ALU.mult, [qs, e1], [ops])
            self.tt("dve", ops[:, (6 * d + 5) * 512:(6 * d + 6) * 512], kd[:], e2[:], ALU.mult, [kd, e2], [ops])
            for h in range(4):
                c = (d * 4 + h) * 2
                self.mm(LB[:, c:c + 2], lf[:, h * 128:(h + 1) * 128], C["cind"][:], True, True, [lf, C["cind"]], [LB])
        self.act(dcs[:, 0:16].rearrange("p (c x) -> p c x", c=2), LB[:, 0:16].rearrange("p (x c) -> p c x", c=2),
                 AF.Exp, [LB], [dcs])
        self.dma("pool", R["ECs"][g].rearrange("c p x -> p c x"), dcs[:, 0:16].rearrange("p (c x) -> p c x", c=2),
                 [dcs], [self.bR["ECs"]])
        o3 = ops[:, :].rearrange("p (s c) -> p s c", s=14)
        OP3 = OPSd[t0:t0 + 128, :].rearrange("t (s c) -> t s c", s=14)
        self.dma("pool", OP3[:, 4:6, :], o3[:, 4:6, :], [ops], [self.bR["OPS"]])
        self.dma("pool", OP3[:, 10:12, :], o3[:, 10:12, :], [ops], [self.bR["OPS"]])
        self.dma("pool", OP3[:, 13:14, :], o3[:, 13:14, :], [ops], [self.bR["OPS"]])
        self.dma("pool", CXd[t0:t0 + 128, 1024:1536], cxs[:, 1024:1536], [cxs], [self.bR["CX"]])

    def A_late_R(self, X, g):
        C, R = self.C, self.R
        OPSd, CXd = R["OPS"], R["CX"]
        prow = C["prow"]
        kk_bc, ka_bc, rk_bc = prow[:, 512:1024], prow[:, 1024:1536], prow[:, 1536:2048]
        b = g % 2
        t0 = g * 128
        ops, cxs, dcs = X["OPst"][b], X["CXst"][b], X["dcs"][b]
        r_, k_, v_ = X["rkv"][b]
        lt, lw, lg, W2x, G2 = X["lt"][b], X["lw"], X["lg"], X["W2x"], X["G2"]
        ft, sm = X["ft"], X["sm"]
        LA, LB, LC = X["LA"], X["LB"], X["LC"]
        identf = C["identf"]
        for i in range(3):
            self.tr(LC[0:32, i * 128:(i + 1) * 128], lt[:, i * 32:(i + 1) * 32], identf[:], [lt, identf], [LC])
        self.tr(LC[0:96, 384:512], lt[:, 96:192], identf[:], [lt, identf], [LC])
        self.cp("dve", lw[0:32, :], LC[0:32, 0:384], [LC], [lw])
        self.cp("dve", lg[:, :], LC[0:96, 384:512], [LC], [lg])
        a_, sgf, sgb = ft[0], X["sg"][0], X["sg"][1]
        self.mm(LA[:], lw[0:32, 256:384], W2x[0:32, 1024:1536], True, True, [lw, W2x], [LA])
        self.mm(LB[:], lw[0:32, 128:256], W2x[0:32, 512:1024], True, True, [lw, W2x], [LB])
        self.mm(LC[:], lg[:, :], G2[:, :], True, True, [lg, G2], [LC])
        self.tt("dve", a_[:], LA[:], prow[:, 4096:4608], ALU.add, [LA, prow], [a_])
        self.act(a_[:], a_[:], AF.Sigmoid, [a_], [a_])
        self.mm(LA[:], lw[0:32, 0:128], W2x[0:32, 0:512], True, True, [lw, W2x], [LA])
        self.tt("dve", sgb[:], LB[:], prow[:, 3584:4096], ALU.add, [LB, prow], [sgb])
        self.act(sgb[:], sgb[:], AF.Sigmoid, [sgb], [sgb])
        self.act(cxs[:, 0:512], LC[:], AF.Copy, [LC], [cxs])
        self.tt("dve", sgf[:], LA[:], prow[:, 3072:3584], ALU.add, [LA, prow], [sgf])
        self.act(sgf[:], sgf[:], AF.Sigmoid, [sgf], [sgf])
        kx, sq, kk, t1, kp, nb, t2 = ft[1], ft[2], ft[3], ft[4], ft[5], ft[6], ft[4]
        self.tt("dve", kx[:], k_[:], kk_bc, ALU.mult, [k_, prow], [kx])
        self.act(sq[:], kx[:], AF.Square, [kx], [sq])
        self.red(sm[0][:], sq[:].rearrange("p (h j) -> p h j", h=8), [sq], [sm[0]])
        self.act(sm[1][:], sm[0][:], AF.Sqrt, [sm[0]], [sm[1]], bias=1e-12)
        self.recip(sm[1][:], sm[1][:], [sm[1]], [sm[1]])
        self.tt("dve", kk[:].rearrange("p (h j) -> p h j", h=8), kx[:].rearrange("p (h j) -> p h j", h=8),
                sm[1][:].unsqueeze(2).broadcast_to([128, 8, 64]), ALU.mult, [kx, sm[1]], [kk])
        self.tt("dve", t1[:], a_[:], ka_bc, ALU.mult, [a_, prow], [t1])
        self.tt("dve", t1[:], t1[:], C["omka"][:], ALU.add, [t1, C["omka"]], [t1])
        self.tt("dve", kp[:], k_[:], t1[:], ALU.mult, [k_, t1], [kp])
        self.stt(nb[:], kk[:], -1.0, a_[:], ALU.mult, ALU.mult, [kk, a_], [nb])
        self.tt("dve", t2[:], r_[:], kp[:], ALU.mult, [r_, kp], [t2])
        self.tt("dve", t2[:], t2[:], rk_bc, ALU.mult, [t2, prow], [t2])
        self.red(sm[2][:], t2[:].rearrange("p (h j) -> p h j", h=8), [t2], [sm[2]])
        self.tt("dve", cxs[:, 512:1024].rearrange("p (h j) -> p h j", h=8), v_[:].rearrange("p (h j) -> p h j", h=8),
                sm[2][:].unsqueeze(2).broadcast_to([128, 8, 64]), ALU.mult, [v_, sm[2]], [cxs])
        self.cp("act", ops[:, 12 * 512:13 * 512], v_[:], [v_], [ops])
        for d in range(2):
            sg_ = (sgf, sgb)[d]
            pc = (LA, LB)[d]
            gi, ginv, tmp, ge = ft[7], ft[8], ft[2], ft[1]
            self.mm(pc[:], C["tri"][d][:], sg_[:], True, True, [C["tri"][d], sg_], [pc])
            self.act(gi[:], pc[:], AF.Exp, [pc], [gi], scale=-DECAY)
            self.act(ginv[:], pc[:], AF.Exp, [pc], [ginv], scale=DECAY)
            self.tt("dve", tmp[:], pc[:], sg_[:], ALU.subtract, [pc, sg_], [tmp])
            self.act(ge[:], tmp[:], AF.Exp, [tmp], [ge], scale=-DECAY)
            b0 = 6 * d * 512
            self.tt("dve", ops[:, b0:b0 + 512], r_[:], gi[:], ALU.mult, [r_, gi], [ops])
            self.tt("dve", ops[:, b0 + 512:b0 + 1024], kp[:], ginv[:], ALU.mult, [kp, ginv], [ops])
            self.tt("dve", ops[:, b0 + 1024:b0 + 1536], nb[:], ginv[:], ALU.mult, [nb, ginv], [ops])
            self.tt("dve", ops[:, b0 + 1536:b0 + 2048], kk[:], ge[:], ALU.mult, [kk, ge], [ops])
            for h in range(8):
                c = (d * 8 + h) * 2
                self.mm(LC[0:64, c:c + 2], sg_[:, h * 64:(h + 1) * 64], C["cind"][:], True, True, [sg_, C["cind"]], [LC])
        self.act(dcs[0:64, 0:32].rearrange("p (c x) -> p c x", c=2), LC[0:64, 0:32].rearrange("p (x c) -> p c x", c=2),
                 AF.Exp, [LC], [dcs], scale=-DECAY)
        self.dma("pool", R["GCs"][g].rearrange("c p x -> p c x"), dcs[0:64, 0:32].rearrange("p (c x) -> p c x", c=2),
                 [dcs], [self.bR["GCs"]])
        o3 = ops[:, :].rearrange("p (s c) -> p s c", s=14)
        OP3 = OPSd[t0:t0 + 128, :].rearrange("t (s c) -> t s c", s=14)
        self.dma("pool", OP3[:, 0:4, :], o3[:, 0:4, :], [ops], [self.bR["OPS"]])
        self.dma("pool", OP3[:, 6:10, :], o3[:, 6:10, :], [ops], [self.bR["OPS"]])
        self.dma("pool", OP3[:, 12:13, :], o3[:, 12:13, :], [ops], [self.bR["OPS"]])
        self.dma("pool", CXd[t0:t0 + 128, 0:1024], cxs[:, 0:1024], [cxs], [self.bR["CX"]])

    def phaseA_oldR(self):
        nc, I, C, R = self.nc, self.I, self.C, self.R
        mods = C["mods"]
        xTv = I["xT"].rearrange("(k p) t -> p k t", p=128)
        for part in ("R",):
            halo = 0 if part == "H" else 1
            nw = 128 + 2 * halo
            with self.scope() as st:
                nF = 14
                ft = [self.sb(st, f"ft{i}", [128, 512], F32) for i in range(nF)]
                if part == "H":
                    WH = self.sb(st, "WH", [128, 8 * PA], BF16)
                    wv = I["w_in"].rearrange("(k p) c -> p k c", p=128)
                    for k in range(KD):
                        self.dma("pool", WH[:, k * PA:(k + 1) * PA], wv[:, k, 0:PA], [], [WH])
                else:
                    WR = [self.sb(st, f"WR{t}", [128, 8 * PB], BF16) for t in range(3)]
                    with self.scope() as st2:
                        cvbc = self.sb(st2, "cvbc", [128, 3 * PB], F32)
                        self.dma("sp", cvbc[:], I["rwkv_conv"].rearrange("(o a) c -> o (a c)", o=1).partition_broadcast(128),
                                 [], [cvbc])
                        stg = [self.sb(st2, f"wstg{i}", [128, PB], F32) for i in range(2)]
                        wv = I["w_in"].rearrange("(k p) c -> p k c", p=128)
                        for k in range(KD):
                            sg = stg[k % 2]
                            self.dma("sp", sg[:], wv[:, k, PA:PIN], [], [sg])
                            for t in range(3):
                                self.tt("dve" if t != 1 else "pool", WR[t][:, k * PB:(k + 1) * PB], sg[:],
                                        cvbc[:, t * PB:(t + 1) * PB], ALU.mult, [sg, cvbc], [WR[t]])
                    for t in range(3):
                        self.dbg(f"dbg_WR{t}", WR[t][:, 0:2 * PB], [128, 2 * PB], WR[t])
                    W2x = self.sb(st, "W2x", [32, 1536], BF16)
                    self.dma("pool", W2x[:], I["w2x"], [], [W2x])
                    G2 = self.sb(st, "G2", [96, 512], BF16)
                    self.dma("pool", G2[:], I["g2"], [], [G2])
                    lw = self.sb(st, "lw", [32, 384], BF16)
                    lg = self.sb(st, "lg", [96, 128], BF16)
                    lt = self.sb(st, "lt", [128, 192], BF16)
                tiles = {
                    "sq": self.sb(st, "sq", [128, 8 * nw], BF16),
                    "rstd": self.sb(st, "rstd", [128, nw], F32),
                    "ntmp": [self.sb(st, f"ntmp{i}", [128, nw], F32) for i in range(2)],
                }
                xts = [self.sb(st, f"xt{i}", [128, 8 * nw], F32) for i in range(2)]
                hTs = [self.sb(st, f"hT{i}", [128, 8 * nw], BF16) for i in range(2)]
                OPst = [self.sb(st, f"OPst{i}", [128, 14 * 512], BF16) for i in range(2)]
                CXst = [self.sb(st, f"CXst{i}", [128, 3 * 512], BF16) for i in range(2)]
                sm = [self.sb(st, f"sm{i}", [128, 8], F32) for i in range(4)]
                dcs = [self.sb(st, f"dcs{i}", [128, 32], F32) for i in range(2)]
                if part == "H":
                    pb = [self.ps(st, f"pb{i}", [128, 512], F32) for i in range(8)]
                else:
                    pb = [self.ps(st, f"pb{i}", [128, 512], F32) for i in range(7)]
                    pbT = self.ps(st, "pbT", [128, 1024], BF16)
                for g in range(self.NG):
                    t0 = g * 128
                    s0, sl, kind = [s for s in self.seqs if s[0] <= t0 < s[0] + s[1]][0]
                    s1 = s0 + sl
                    lo, hi = max(t0 - halo, s0), min(t0 + 128 + halo, s1)
                    off = lo - (t0 - halo)
                    n = hi - lo
                    xt, hT = xts[g % 2], hTs[g % 2]
                    ops, cxs = OPst[g % 2], CXst[g % 2]
                    x3 = xt[:, :].rearrange("p (k t) -> p k t", k=8)
                    if n < nw:
                        self.memset("dve", xt[:, :], 0.0, [xt])
                    self.dma("sp", x3[:, :, off:off + n], xTv[:, :, lo:hi], [], [xt])
                    self.norm_mod(tiles, xt, nw, pb[0], mods[:, 0, kind, :], mods[:, 1, kind, :], hT, mods)
                    h3 = hT[:, :].rearrange("p (k t) -> p k t", k=8)
                    if off > 0:
                        self.memset("dve", h3[:, :, 0:off], 0.0, [hT])
                    if off + n < nw:
                        self.memset("dve", h3[:, :, off + n:nw], 0.0, [hT])
                    if part == "H":
                        self.phaseA_H(g, t0, hT, nw, WH, pb, ft, dcs[g % 2], ops, cxs)
                    else:
                        self.phaseA_R_old(g, t0, hT, nw, WR, W2x, G2, lw, lg, lt, pb, pbT, ft, sm, dcs[g % 2], ops, cxs)


    def phaseA_R_old(self, g, t0, hT, nw, WR, W2x, G2, lw, lg, lt, pb, pbT, ft, sm, dcs, ops, cxs):
        C, R = self.C, self.R
        OPSd, CXd = R["OPS"], R["CX"]
        prow = C["prow"]
        kk_bc, ka_bc, rk_bc = prow[:, 512:1024], prow[:, 1024:1536], prow[:, 1536:2048]
        for cg in range(4):
            c0 = cg * 512
            cn = 512 if cg < 3 else 192
            for t in range(3):
                for k in range(KD):
                    self.mm(pb[1 + cg][:, 0:cn], hT[:, k * nw + t:k * nw + t + 128], WR[t][:, k * PB + c0:k * PB + c0 + cn],
                            t == 0 and k == 0, t == 2 and k == KD - 1, [hT, WR[t]], [pb[1 + cg]])
        pr, pk, pv, pl = pb[1], pb[2], pb[3], pb[4]
        self.act(lt[:, 0:64], pl[:, 0:64], AF.Tanh, [pl], [lt])
        self.act(lt[:, 64:96], pl[:, 64:96], AF.Copy, [pl], [lt])
        self.act(lt[:, 96:192], pl[:, 96:192], AF.Sigmoid, [pl], [lt])
        ident = C["ident"]
        for i in range(3):
            self.tr(pbT[0:32, i * 128:(i + 1) * 128], lt[:, i * 32:(i + 1) * 32], ident[:], [lt, ident], [pbT])
        self.tr(pbT[0:96, 384:512], lt[:, 96:192], ident[:], [lt, ident], [pbT])
        self.cp("dve", lw[0:32, :], pbT[0:32, 0:384], [pbT], [lw])
        self.cp("dve", lg[:, :], pbT[0:96, 384:512], [pbT], [lg])
        pa, pwf, pwb, pg = pb[4], pb[5], pb[6], pb[0]
        self.mm(pa[:], lw[0:32, 256:384], W2x[0:32, 1024:1536], True, True, [lw, W2x], [pa])
        self.mm(pwf[:], lw[0:32, 0:128], W2x[0:32, 0:512], True, True, [lw, W2x], [pwf])
        self.mm(pwb[:], lw[0:32, 128:256], W2x[0:32, 512:1024], True, True, [lw, W2x], [pwb])
        self.mm(pg[:], lg[:, :], G2[:, :], True, True, [lg, G2], [pg])
        a_, sgf, sgb = ft[0], ft[1], ft[2]
        self.tt("dve", a_[:], pa[:], prow[:, 4096:4608], ALU.add, [pa, prow], [a_])
        self.act(a_[:], a_[:], AF.Sigmoid, [a_], [a_])
        self.tt("dve", sgf[:], pwf[:], prow[:, 3072:3584], ALU.add, [pwf, prow], [sgf])
        self.act(sgf[:], sgf[:], AF.Sigmoid, [sgf], [sgf])
        self.tt("dve", sgb[:], pwb[:], prow[:, 3584:4096], ALU.add, [pwb, prow], [sgb])
        self.act(sgb[:], sgb[:], AF.Sigmoid, [sgb], [sgb])
        self.act(cxs[:, 0:512], pg[:], AF.Copy, [pg], [cxs])
        kx, sq, kk, t1, kp, nb, t2 = ft[3], ft[4], ft[5], ft[6], ft[7], ft[8], ft[9]
        self.tt("dve", kx[:], pk[:], kk_bc, ALU.mult, [pk, prow], [kx])
        self.act(sq[:], kx[:], AF.Square, [kx], [sq])
        self.red(sm[0][:], sq[:].rearrange("p (h j) -> p h j", h=8), [sq], [sm[0]])
        self.act(sm[1][:], sm[0][:], AF.Sqrt, [sm[0]], [sm[1]], bias=1e-12)
        self.recip(sm[1][:], sm[1][:], [sm[1]], [sm[1]])
        self.tt("dve", kk[:].rearrange("p (h j) -> p h j", h=8), kx[:].rearrange("p (h j) -> p h j", h=8),
                sm[1][:].unsqueeze(2).broadcast_to([128, 8, 64]), ALU.mult, [kx, sm[1]], [kk])
        self.tt("dve", t1[:], a_[:], ka_bc, ALU.mult, [a_, prow], [t1])
        self.tt("dve", t1[:], t1[:], C["omka"][:], ALU.add, [t1, C["omka"]], [t1])
        self.tt("dve", kp[:], pk[:], t1[:], ALU.mult, [pk, t1], [kp])
        self.stt(nb[:], kk[:], -1.0, a_[:], ALU.mult, ALU.mult, [kk, a_], [nb])
        self.tt("dve", t2[:], pr[:], kp[:], ALU.mult, [pr, kp], [t2])
        self.tt("dve", t2[:], t2[:], rk_bc, ALU.mult, [t2, prow], [t2])
        self.red(sm[2][:], t2[:].rearrange("p (h j) -> p h j", h=8), [t2], [sm[2]])
        self.tt("dve", cxs[:, 512:1024].rearrange("p (h j) -> p h j", h=8), pv[:].rearrange("p (h j) -> p h j", h=8),
                sm[2][:].unsqueeze(2).broadcast_to([128, 8, 64]), ALU.mult, [pv, sm[2]], [cxs])
        self.act(ops[:, 12 * 512:13 * 512], pv[:], AF.Copy, [pv], [ops])
        pgc = pb[4]
        for d in range(2):
            sg_ = (sgf, sgb)[d]
            pc = pb[5 + d]
            gi, ginv, tmp, ge = ft[10], ft[11], ft[12], ft[13]
            self.mm(pc[:], C["tri"][d][:], sg_[:], True, True, [C["tri"][d], sg_], [pc])
            self.act(gi[:], pc[:], AF.Exp, [pc], [gi], scale=-DECAY)
            self.act(ginv[:], pc[:], AF.Exp, [pc], [ginv], scale=DECAY)
            self.tt("dve", tmp[:], pc[:], sg_[:], ALU.subtract, [pc, sg_], [tmp])
            self.act(ge[:], tmp[:], AF.Exp, [tmp], [ge], scale=-DECAY)
            b0 = 6 * d * 512
            self.tt("dve", ops[:, b0:b0 + 512], pr[:], gi[:], ALU.mult, [pr, gi], [ops])
            self.tt("dve", ops[:, b0 + 512:b0 + 1024], kp[:], ginv[:], ALU.mult, [kp, ginv], [ops])
            self.tt("dve", ops[:, b0 + 1024:b0 + 1536], nb[:], ginv[:], ALU.mult, [nb, ginv], [ops])
            self.tt("dve", ops[:, b0 + 1536:b0 + 2048], kk[:], ge[:], ALU.mult, [kk, ge], [ops])
            for h in range(8):
                c = (d * 8 + h) * 2
                self.mm(pgc[0:64, c:c + 2], sg_[:, h * 64:(h + 1) * 64], C["cind"][:], True, True, [sg_, C["cind"]], [pgc])
        self.act(dcs[0:64, 0:32].rearrange("p (c x) -> p c x", c=2), pgc[0:64, 0:32].rearrange("p (x c) -> p c x", c=2),
                 AF.Exp, [pgc], [dcs], scale=-DECAY)
        self.dma("pool", R["GCs"][g].rearrange("c p x -> p c x"), dcs[0:64, 0:32].rearrange("p (c x) -> p c x", c=2),
                 [dcs], [self.bR["GCs"]])
        o3 = ops[:, :].rearrange("p (s c) -> p s c", s=14)
        OP3 = OPSd[t0:t0 + 128, :].rearrange("t (s c) -> t s c", s=14)
        self.dma("pool", OP3[:, 0:4, :], o3[:, 0:4, :], [ops], [self.bR["OPS"]])
        self.dma("pool", OP3[:, 6:10, :], o3[:, 6:10, :], [ops], [self.bR["OPS"]])
        self.dma("pool", OP3[:, 12:13, :], o3[:, 12:13, :], [ops], [self.bR["OPS"]])
        self.dma("pool", CXd[t0:t0 + 128, 0:1024], cxs[:, 0:1024], [cxs], [self.bR["CX"]])


    def scan(self):
        C, R, I, O = self.C, self.R, self.I, self.O
        ident, M1, M3, MI = C["ident"], C["M1"], C["M3"], C["MI"]
        OPSd = R["OPS"]
        with self.scope() as st:
            OP = [self.sb(st, f"OP{i}", [128, 8 * 512], BF16) for i in range(2)]
            gCt = [self.sb(st, f"gCt{i}", [128, 8], F32) for i in range(2)]
            eCt = [self.sb(st, f"eCt{i}", [128, 8], F32) for i in range(2)]
            XT = self.sb(st, "XT", [128, 8 * 256], BF16)
            GS = self.sb(st, "GS", [128, 8 * 320], BF16)
            XA = [self.sb(st, f"XA{i}", [128, 8 * 128], BF16) for i in range(2)]
            NN = [self.sb(st, f"NN{i}", [128, 8 * 128], BF16) for i in range(2)]
            GT = self.sb(st, "GT", [128, 512], BF16)
            Pb = self.sb(st, "Pb", [128, 512], BF16)
            T32 = self.sb(st, "T32", [128, 512], F32)
            Tb = [self.sb(st, f"Tb{i}", [128, 512], BF16) for i in range(2)]
            Ttmp = self.sb(st, "Ttmp", [128, 512], F32)
            Ysb = [self.sb(st, f"Ysb{i}", [128, 512], F32) for i in range(2)]
            HT = self.sb(st, "HT", [128, 1024], BF16)
            SCz = self.sb(st, "SCz", [128, 4 * 2 * 64], BF16)
            Osb = [self.sb(st, f"Osb{i}", [128, 512], F32) for i in range(2)]
            S32 = self.sb(st, "S32", [128, 1024], F32)
            Sb = [self.sb(st, f"Sb{i}", [128, 1024], BF16) for i in range(2)]
            Stmp = self.sb(st, "Stmp", [128, 512], F32)
            BT0 = self.ps(st, "BT0", [128, 1024], BF16)
            BT1 = self.ps(st, "BT1", [128, 1024], BF16)
            PX = self.ps(st, "PX", [128, 1024], F32)
            PN = self.ps(st, "PN", [128, 1024], F32)
            F4 = self.ps(st, "F4", [128, 512], F32)
            F5 = self.ps(st, "F5", [128, 512], F32)
            self.memset("pool", SCz[:], 0.0, [SCz])
            M1b = M1[:].unsqueeze(1).broadcast_to([128, 8, 128])
            M3b = M3[:].unsqueeze(1).broadcast_to([128, 8, 64])
            stepno = 0
            for si, (s0, sl, kind) in enumerate(self.seqs):
                nch = sl // 64
                if kind == 0:
                    self.memset("pool", T32[:], 0.0, [T32])
                    self.memset("pool", S32[:], 0.0, [S32])
                else:
                    for l in range(2):
                        self.dma("sp", T32[64 * l:64 * l + 64, :], I["st_r"][l], [], [T32])
                        self.dma("sp", S32[:, l * 512:(l + 1) * 512], I["st_h"][l], [], [S32])
                cur = 0
                self.cp("act", Tb[cur][:], T32[:], [T32], [Tb[cur]])
                self.cp("act", Sb[cur][:], S32[:], [S32], [Sb[cur]])
                for i in range(nch):
                    op, gc, ec = OP[stepno % 2], gCt[stepno % 2], eCt[stepno % 2]
                    ysb, osb = Ysb[stepno % 2], Osb[stepno % 2]
                    stepno += 1
                    tok = [s0 + 64 * i, s0 + 64 * (nch - 1 - i)]
                    o3 = op[:, :].rearrange("p (s c) -> p s c", s=8)
                    for l in range(2):
                        src = OPSd[tok[l]:tok[l] + 64, :].rearrange("t (s c) -> t s c", s=14)
                        self.dma("sp", o3[64 * l:64 * l + 64, 0:6, :], src[:, 6 * l:6 * l + 6, :], [self.bR["OPS"]], [op])
                        self.dma("sp", o3[64 * l:64 * l + 64, 6:8, :], src[:, 12:14, :], [self.bR["OPS"]], [op])
                        gg, cc = tok[l] // 128, (tok[l] % 128) // 64
                        self.dma("sp", gc[64 * l:64 * l + 64, :], R["GCs"][gg, cc][:, 8 * l:8 * l + 8], [self.bR["GCs"]], [gc])
                        self.dma("sp", ec[:, 4 * l:4 * l + 4], R["ECs"][gg, cc][:, 4 * l:4 * l + 4], [self.bR["ECs"]], [ec])
                    for h in range(8):
                        bt = BT0 if h < 4 else BT1
                        for X, slot in enumerate((3, 0, 2, 1)):
                            for l in range(2):
                                r0 = 64 * l
                                self.tr(bt[r0:r0 + 64, (h % 4) * 256 + X * 64:(h % 4) * 256 + X * 64 + 64],
                                        op[r0:r0 + 64, slot * 512 + h * 64:slot * 512 + h * 64 + 64],
                                        ident[r0:r0 + 64, r0:r0 + 64], [op, ident], [bt])
                    self.cp("dve", XT[:, 0:1024], BT0[:, :], [BT0], [XT])
                    self.cp("act", XT[:, 1024:2048], BT1[:, :], [BT1], [XT])
                    for h in range(8):
                        for l in range(2):
                            r0 = 64 * l
                            xb = h * 256
                            self.mm(PX[r0:r0 + 64, h * 128:h * 128 + 128], XT[r0:r0 + 64, xb + 128:xb + 192],
                                    XT[r0:r0 + 64, xb:xb + 128], True, True, [XT], [PX])
                            self.mm(PN[r0:r0 + 64, h * 128:h * 128 + 128], XT[r0:r0 + 64, xb + 192:xb + 256],
                                    XT[r0:r0 + 64, xb:xb + 128], True, True, [XT], [PN])
                            self.mm(F4[r0:r0 + 64, h * 64:h * 64 + 64], XT[r0:r0 + 64, xb:xb + 64],
                                    XT[r0:r0 + 64, xb + 128:xb + 192], True, True, [XT], [F4])
                    G3 = GS[:, :].rearrange("p (h c) -> p h c", h=8)
                    self.tt("dve", G3[:, :, 0:128], PX[:, :].rearrange("p (h c) -> p h c", h=8), M1b, ALU.mult, [PX, M1], [GS])
                    self.tt("dve", G3[:, :, 128:256], PN[:, :].rearrange("p (h c) -> p h c", h=8), M1b, ALU.mult, [PN, M1], [GS])
                    self.tt("dve", G3[:, :, 256:320], F4[:, :].rearrange("p (h c) -> p h c", h=8), M3b, ALU.mult, [F4, M3], [GS])
                    xa = XA[0]
                    xa3 = xa[:, :].rearrange("p (h c) -> p h c", h=8)
                    self.cp("dve", xa3[:, :, 0:64], op[:, 3 * 512:4 * 512].rearrange("p (h c) -> p h c", h=8), [op], [xa])
                    for h in range(8):
                        for l in range(2):
                            r0 = 64 * l
                            self.mm(F5[r0:r0 + 64, h * 64:h * 64 + 64], GS[r0:r0 + 64, h * 320 + 128:h * 320 + 192],
                                    op[r0:r0 + 64, 6 * 512 + h * 64:6 * 512 + h * 64 + 64], True, True, [GS, op], [F5])
                    self.cp("act", xa3[:, :, 64:128], F5[:, :].rearrange("p (h c) -> p h c", h=8), [F5], [xa])
                    xc = 0
                    for k in range(6):
                        xcur, xnext = XA[xc], XA[1 - xc]
                        if k == 0:
                            nsrc = GS
                            noff = lambda h: h * 320 + 256
                            ntoff = lambda h: h * 320
                        else:
                            nsrc = NN[(k - 1) % 2]
                            noff = lambda h: h * 128
                            ntoff = lambda h: h * 128 + 64
                        if k < 5:
                            nn = NN[k % 2]
                            for h in range(8):
                                for l in range(2):
                                    r0 = 64 * l
                                    self.mm(PN[r0:r0 + 64, h * 128:h * 128 + 64], nsrc[r0:r0 + 64, ntoff(h):ntoff(h) + 64],
                                            nsrc[r0:r0 + 64, noff(h):noff(h) + 64], True, True, [nsrc], [PN])
                                    self.mm(PN[r0:r0 + 64, h * 128 + 64:h * 128 + 128], nsrc[r0:r0 + 64, noff(h):noff(h) + 64],
                                            nsrc[r0:r0 + 64, ntoff(h):ntoff(h) + 64], True, True, [nsrc], [PN])
                        if k < 5:
                            self.cp("act", nn[:, :], PN[:, :], [PN], [nn])
                        for h in range(8):
                            for l in range(2):
                                r0 = 64 * l
                                self.mm(PX[r0:r0 + 64, h * 128:h * 128 + 128], nsrc[r0:r0 + 64, ntoff(h):ntoff(h) + 64],
                                        xcur[r0:r0 + 64, h * 128:h * 128 + 128], True, True, [nsrc, xcur], [PX])
                        self.tt("dve", xnext[:, :], PX[:, :], xcur[:, :], ALU.add, [PX, xcur], [xnext])
                        xc = 1 - xc
                    x6 = XA[xc]
                    x63 = x6[:, :].rearrange("p (h c) -> p h c", h=8)
                    for h in range(8):
                        for l in range(2):
                            r0 = 64 * l
                            self.tr(BT1[r0:r0 + 64, h * 64:h * 64 + 64], x6[r0:r0 + 64, h * 128:h * 128 + 64],
                                    ident[r0:r0 + 64, r0:r0 + 64], [x6, ident], [BT1])
                    self.cp("dve", GT[:, :], BT1[:, 0:512], [BT1], [GT])
                    tbc, tbn = Tb[cur], Tb[1 - cur]
                    for h in range(8):
                        for l in range(2):
                            r0 = 64 * l
                            self.mm(F4[r0:r0 + 64, h * 64:h * 64 + 64], GT[r0:r0 + 64, h * 64:h * 64 + 64],
                                    tbc[r0:r0 + 64, h * 64:h * 64 + 64], True, True, [GT, tbc], [F4])
                    self.tt("dve", Pb[:, :].rearrange("p (h c) -> p h c", h=8), F4[:, :].rearrange("p (h c) -> p h c", h=8),
                            x63[:, :, 64:128], ALU.add, [F4, x6], [Pb])
                    for h in range(8):
                        for l in range(2):
                            r0 = 64 * l
                            yo = PN[r0:r0 + 64, h * 64:h * 64 + 64]
                            self.mm(yo, XT[r0:r0 + 64, h * 256 + 64:h * 256 + 128], tbc[r0:r0 + 64, h * 64:h * 64 + 64],
                                    True, False, [XT, tbc], [PN])
                            self.mm(yo, GS[r0:r0 + 64, h * 320 + 192:h * 320 + 256],
                                    op[r0:r0 + 64, 6 * 512 + h * 64:6 * 512 + h * 64 + 64], False, False, [GS, op], [PN])
                            self.mm(yo, GS[r0:r0 + 64, h * 320 + 64:h * 320 + 128], Pb[r0:r0 + 64, h * 64:h * 64 + 64],
                                    False, True, [GS, Pb], [PN])
                    self.cp("act", ysb[:, :], PN[:, 0:512], [PN], [ysb])
                    for l in range(2):
                        self.dma("pool", R["YS"][l, tok[l]:tok[l] + 64, :], ysb[64 * l:64 * l + 64, :], [ysb], [self.bR["YS"]])
                    for h in range(8):
                        for l in range(2):
                            r0 = 64 * l
                            to = F5[r0:r0 + 64, h * 64:h * 64 + 64]
                            self.mm(to, op[r0:r0 + 64, 1 * 512 + h * 64:1 * 512 + h * 64 + 64],
                                    op[r0:r0 + 64, 6 * 512 + h * 64:6 * 512 + h * 64 + 64], True, False, [op], [F5])
                            self.mm(to, op[r0:r0 + 64, 2 * 512 + h * 64:2 * 512 + h * 64 + 64],
                                    Pb[r0:r0 + 64, h * 64:h * 64 + 64], False, True, [op, Pb], [F5])
                    self.tt("dve", Ttmp[:, :], F5[:, :], T32[:, :], ALU.add, [F5, T32], [Ttmp])
                    self.tt("dve", T32[:, :].rearrange("p (h c) -> p h c", h=8), Ttmp[:, :].rearrange("p (h c) -> p h c", h=8),
                            gc[:, :].unsqueeze(2).broadcast_to([128, 8, 64]), ALU.mult, [Ttmp, gc], [T32])
                    self.cp("act", tbn[:, :], T32[:, :], [T32], [tbn])
                    sbc, sbn = Sb[cur], Sb[1 - cur]
                    for X, slot in enumerate((4, 5)):
                        for h in range(4):
                            c0 = (X * 4 + h) * 128
                            self.tr(BT0[:, c0:c0 + 128], op[:, slot * 512 + h * 128:slot * 512 + h * 128 + 128], ident[:, :],
                                    [op, ident], [BT0])
                    self.cp("dve", HT[:, :], BT0[:, :], [BT0], [HT])
                    for h in range(4):
                        for l in range(2):
                            r0 = 64 * l
                            self.mm(PX[r0:r0 + 64, h * 64:h * 64 + 64], HT[:, (4 + h) * 128 + r0:(4 + h) * 128 + r0 + 64],
                                    HT[:, h * 128 + r0:h * 128 + r0 + 64], True, True, [HT], [PX])
                    for l in range(2):
                        r0 = 64 * l
                        sc4 = SCz[r0:r0 + 64, :].rearrange("p (h l t) -> p h l t", h=4, l=2)
                        self.tt("dve", sc4[:, :, l, :], PX[r0:r0 + 64, 0:256].rearrange("p (h t) -> p h t", h=4),
                                MI[r0:r0 + 64, :].unsqueeze(1).broadcast_to([64, 4, 64]), ALU.mult, [PX, MI], [SCz])
                    for h in range(4):
                        for l in range(2):
                            r0 = 64 * l
                            oo = PX[r0:r0 + 64, 512 + h * 128:512 + h * 128 + 128]
                            self.mm(oo, SCz[:, (h * 2 + l) * 64:(h * 2 + l) * 64 + 64], op[:, 7 * 512 + h * 128:7 * 512 + h * 128 + 128],
                                    True, False, [SCz, op], [PX])
                            self.mm(oo, HT[:, h * 128 + r0:h * 128 + r0 + 64], sbc[:, (l * 4 + h) * 128:(l * 4 + h) * 128 + 128],
                                    False, True, [HT, sbc], [PX])
                    self.cp("act", osb[:, :], PX[:, 512:1024], [PX], [osb])
                    for l in range(2):
                        self.dma("pool", R["OSC"][l, tok[l]:tok[l] + 64, :], osb[64 * l:64 * l + 64, :], [osb], [self.bR["OSC"]])
                    for l in range(2):
                        r0 = 64 * l
                        pu = PN if l == 0 else F4
                        pu0 = 512 if l == 0 else 0
                        for h in range(4):
                            self.mm(pu[:, pu0 + h * 128:pu0 + h * 128 + 128], op[r0:r0 + 64, 5 * 512 + h * 128:5 * 512 + h * 128 + 128],
                                    op[r0:r0 + 64, 7 * 512 + h * 128:7 * 512 + h * 128 + 128], True, True, [op], [pu])
                        self.tt("dve", Stmp[:, :], pu[:, pu0:pu0 + 512], S32[:, l * 512:(l + 1) * 512], ALU.add, [pu, S32], [Stmp])
                        self.tt("dve", S32[:, l * 512:(l + 1) * 512].rearrange("p (h c) -> p h c", h=4),
                                Stmp[:, :].rearrange("p (h c) -> p h c", h=4),
                                ec[:, 4 * l:4 * l + 4].unsqueeze(2).broadcast_to([128, 4, 128]), ALU.mult, [Stmp, ec], [S32])
                    self.cp("act", sbn[:, :], S32[:, :], [S32], [sbn])
                    cur = 1 - cur
                if kind == 0:
                    for l in range(2):
                        self.dma("pool", O["nsr"][si, l], T32[64 * l:64 * l + 64, :], [T32], [self.bO["nsr"]])
                        self.dma("pool", O["nsh"][si, l], S32[:, l * 512:(l + 1) * 512], [S32], [self.bO["nsh"]])

    def ctx_tiles(self):
        out = []
        per = max(1, 512 // self.L_ctx)
        si = 0
        while si < self.n_ctx:
            ns = min(per, self.n_ctx - si)
            out.append((si * self.L_ctx, ns * self.L_ctx))
            si += ns
        return out

    def phaseC1(self):
        C, R, I = self.C, self.R, self.I
        ident, prow, mods, selv = C["ident"], C["prow"], C["mods"], C["selv"]
        hnw_bc, lnw_bc, lnb_bc = prow[:, 0:512], prow[:, 2048:2560], prow[:, 2560:3072]
        xTv = I["xT"].rearrange("(k p) t -> p k t", p=128)
        xov = I["xTo"].rearrange("(k p) t -> p k t", p=128)
        x1v = R["X1T"].rearrange("(k p) t -> p k t", p=128)
        x1wv = R["X1W"].rearrange("(k p) t -> p k t", p=128)
        NCT = self.n_ctx * self.L_ctx
        QL, NQ = self.QL, self.NQ
        WL = QL + 128
        with self.scope() as st:
            WO = self.sb(st, "WO", [128, 8 * D], BF16)
            wv = I["w_out"].rearrange("(k p) c -> p k c", p=128)
            for k in range(KD):
                self.dma("pool", WO[:, k * D:(k + 1) * D], wv[:, k, :], [], [WO])
            yf = [self.sb(st, f"yf{i}", [128, 512], F32) for i in range(2)]
            yb = [self.sb(st, f"yb{i}", [128, 512], F32) for i in range(2)]
            of = [self.sb(st, f"of{i}", [128, 512], F32) for i in range(2)]
            ob = [self.sb(st, f"ob{i}", [128, 512], F32) for i in range(2)]
            cx = [self.sb(st, f"cx{i}", [128, 1536], BF16) for i in range(2)]
            cxa = self.sb(st, "cxa", [128, 1536], F32)
            ft = [self.sb(st, f"c1f{i}", [128, 512], F32) for i in range(4)]
            sm = [self.sb(st, f"c1s{i}", [128, 8], F32) for i in range(6)]
            MIX = self.sb(st, "MIX", [128, 1024], BF16)
            MIXT = self.sb(st, "MIXT", [128, 8 * 512], BF16)
            xt = self.sb(st, "c1xt", [128, 8 * 512], F32)
            x1 = self.sb(st, "c1x1", [128, 8 * 512], F32)
            PT = self.ps(st, "c1PT", [128, 1024], BF16)
            PO = [self.ps(st, f"c1PO{i}", [128, 512], F32) for i in range(2)]
            self._c1i = 0

            def post(y, o, cxt, gq):
                sq, osq = ft[1], ft[3]
                self.act(sq[:, :], y[:, :], AF.Square, [y], [sq])
                y3 = y[:, :].rearrange("p (h j) -> p h j", h=8)
                self.red(sm[0][:, :], y3, [y], [sm[0]])
                self.red(sm[1][:, :], sq[:, :].rearrange("p (h j) -> p h j", h=8), [sq], [sm[1]])
                self.ts("dve", sm[2][:, :], sm[0][:, :], 1.0 / 64, None, ALU.mult, ALU.bypass, [sm[0]], [sm[2]])
                self.tt("dve", sm[3][:, :], sm[2][:, :], sm[2][:, :], ALU.mult, [sm[2]], [sm[3]])
                self.stt(sm[4][:, :], sm[1][:, :], 1.0 / 64, sm[3][:, :], ALU.mult, ALU.subtract, [sm[1], sm[3]], [sm[4]])
                self.act(sm[5][:, :], sm[4][:, :], AF.Sqrt, [sm[4]], [sm[5]], bias=GN_EPS)
                self.recip(sm[5][:, :], sm[5][:, :], [sm[5]], [sm[5]])
                self.tt("dve", y3, y3, sm[2][:, :].unsqueeze(2).broadcast_to([128, 8, 64]), ALU.subtract, [y, sm[2]], [y])
                self.tt("dve", y3, y3, sm[5][:, :].unsqueeze(2).broadcast_to([128, 8, 64]), ALU.mult, [y, sm[5]], [y])
                self.tt("dve", y[:, :], y[:, :], lnw_bc, ALU.mult, [y, prow], [y])
                self.tt("dve", y[:, :], y[:, :], lnb_bc, ALU.add, [y, prow], [y])
                self.tt("dve", y[:, :], y[:, :], cxt[:, 512:1024], ALU.add, [y, cxt], [y])
                self.tt("dve", MIX[:, 512:1024], y[:, :], cxt[:, 0:512], ALU.mult, [y, cxt], [MIX])
                self.act(osq[:, :], o[:, :], AF.Square, [o], [osq])
                self.red(sm[0][:, 0:4], osq[:, :].rearrange("p (h j) -> p h j", h=4), [osq], [sm[0]])
                self.act(sm[1][:, 0:4], sm[0][:, 0:4], AF.Sqrt, [sm[0]], [sm[1]], scale=1.0 / 128, bias=RMS_EPS)
                self.recip(sm[1][:, 0:4], sm[1][:, 0:4], [sm[1]], [sm[1]])
                o3 = o[:, :].rearrange("p (h j) -> p h j", h=4)
                self.tt("dve", o3, o3, sm[1][:, 0:4].unsqueeze(2).broadcast_to([128, 4, 128]), ALU.mult, [o, sm[1]], [o])
                self.tt("dve", o[:, :], o[:, :], hnw_bc, ALU.mult, [o, prow], [o])
                self.tt("dve", MIX[:, 0:512], o[:, :], cxt[:, 1024:1536], ALU.mult, [o, cxt], [MIX])
                for k in range(KD):
                    self.tr(PT[:, k * 128:(k + 1) * 128], MIX[:, k * 128:(k + 1) * 128], ident[:, :], [MIX, ident], [PT])
                self.cp("act", MIXT[:, :].rearrange("p (k t) -> p k t", k=8)[:, :, gq * 128:(gq + 1) * 128],
                        PT[:, :].rearrange("p (k t) -> p k t", k=8), [PT], [MIXT])

            def dense(n, kind, dst):
                for m in range(KD):
                    po = PO[m % 2]
                    for k in range(KD):
                        self.mm(po[:, 0:n], WO[:, k * D + m * 128:k * D + (m + 1) * 128], MIXT[:, k * 512:k * 512 + n],
                                k == 0, k == KD - 1, [WO, MIXT], [po])
                    self.stt(x1[:, m * n:(m + 1) * n], po[:, 0:n], mods[:, 2, kind, m:m + 1], xt[:, m * n:(m + 1) * n],
                             ALU.mult, ALU.add, [po, mods, xt], [x1])
                self.dma("pool", dst, x1[:, 0:8 * n].rearrange("p (k t) -> p k t", k=8), [x1], [])

            gi = 0
            for (t0, n) in self.ctx_tiles():
                self.dma("sp", xt[:, 0:8 * n].rearrange("p (k t) -> p k t", k=8), xTv[:, :, t0:t0 + n], [], [xt])
                for gq in range(n // 128):
                    ta = t0 + gq * 128
                    b = gi % 2
                    gi += 1
                    self.dma("sp", yf[b][:, :], R["YS"][0, ta:ta + 128, :], [], [yf[b]])
                    self.dma("sp", yb[b][:, :], R["YS"][1, ta:ta + 128, :], [], [yb[b]])
                    self.dma("sp", of[b][:, :], R["OSC"][0, ta:ta + 128, :], [], [of[b]])
                    self.dma("sp", ob[b][:, :], R["OSC"][1, ta:ta + 128, :], [], [ob[b]])
                    self.dma("sp", cx[b][:, :], R["CX"][ta:ta + 128, :], [], [cx[b]])
                    y, o = ft[0], ft[2]
                    self.tt("dve", y[:, :], yf[b][:, :], yb[b][:, :], ALU.add, [yf[b], yb[b]], [y])
                    self.tt("dve", o[:, :], of[b][:, :], ob[b][:, :], ALU.add, [of[b], ob[b]], [o])
                    post(y, o, cx[b], gq)
                dense(n, 0, x1v[:, :, t0:t0 + n])
            w0 = 0
            while w0 < WL:
                n = min(512, WL - w0)
                self.dma("sp", xt[:, 0:8 * n].rearrange("p (k t) -> p k t", k=8), xov[:, :, w0:w0 + n], [], [xt])
                for gq in range(n // 128):
                    wj = w0 + gq * 128
                    y, o = ft[0], ft[2]
                    for q in range(NQ):
                        b = gi % 2
                        gi += 1
                        ta = NCT + q * QL - 64 + wj
                        lo, hi = max(ta, NCT), min(ta + 128, NCT + self.L_lat)
                        p0, pn = lo - ta, hi - lo
                        if pn < 128:
                            for tl in (yf[b], yb[b], of[b], ob[b], cx[b]):
                                self.memset("dve", tl[:, :], 0.0, [tl])
                        if pn > 0:
                            self.dma("sp", yf[b][p0:p0 + pn, :], R["YS"][0, lo:hi, :], [], [yf[b]])
                            self.dma("sp", yb[b][p0:p0 + pn, :], R["YS"][1, lo:hi, :], [], [yb[b]])
                            self.dma("sp", of[b][p0:p0 + pn, :], R["OSC"][0, lo:hi, :], [], [of[b]])
                            self.dma("sp", ob[b][p0:p0 + pn, :], R["OSC"][1, lo:hi, :], [], [ob[b]])
                            self.dma("sp", cx[b][p0:p0 + pn, :], R["CX"][lo:hi, :], [], [cx[b]])
                        sc = selv[:, q:q + 1]
                        t_y, t_o = ft[1], ft[3]
                        self.tt("dve", t_y[:, :], yf[b][:, :], yb[b][:, :], ALU.add, [yf[b], yb[b]], [t_y])
                        self.tt("dve", t_o[:, :], of[b][:, :], ob[b][:, :], ALU.add, [of[b], ob[b]], [t_o])
                        if q == 0:
                            self.ts("dve", y[:, :], t_y[:, :], sc, None, ALU.mult, ALU.bypass, [t_y, selv], [y])
                            self.ts("dve", o[:, :], t_o[:, :], sc, None, ALU.mult, ALU.bypass, [t_o, selv], [o])
                            self.ts("dve", cxa[:, :], cx[b][:, :], sc, None, ALU.mult, ALU.bypass, [cx[b], selv], [cxa])
                        else:
                            self.stt(y[:, :], t_y[:, :], sc, y[:, :], ALU.mult, ALU.add, [t_y, selv, y], [y])
                            self.stt(o[:, :], t_o[:, :], sc, o[:, :], ALU.mult, ALU.add, [t_o, selv, o], [o])
                            self.stt(cxa[:, :], cx[b][:, :], sc, cxa[:, :], ALU.mult, ALU.add, [cx[b], selv, cxa], [cxa])
                    post(y, o, cxa, gq)
                dense(n, 1, x1wv[:, :, w0:w0 + n])
                w0 += n

    def phaseC2(self):
        C, R, I, O = self.C, self.R, self.I, self.O
        mods, fnw, selv = C["mods"], C["fnw"], C["selv"]
        x1v = R["X1T"].rearrange("(k p) t -> p k t", p=128)
        x1wv = R["X1W"].rearrange("(k p) t -> p k t", p=128)
        yTv = O["yT"].rearrange("(k p) t -> p k t", p=128)
        NWM = 640
        NCT = self.n_ctx * self.L_ctx
        QL, NQ = self.QL, self.NQ
        with self.scope() as st:
            WG = self.sb(st, "WG", [128, 8 * DFF], BF16)
            WU = self.sb(st, "WU", [128, 8 * DFF], BF16)
            for (W, nm) in ((WG, "w_gate"), (WU, "w_up")):
                wv = I[nm].rearrange("(k p) c -> p k c", p=128)
                for k in range(KD):
                    self.dma("pool", W[:, k * DFF:(k + 1) * DFF], wv[:, k, :], [], [W])
            cw = self.sb(st, "cw", [128, NF * 9], F32)
            self.dma("sp", cw[:, :], I["convw"], [], [cw])
            cvb = self.sb(st, "cvb", [128, NF], F32)
            self.dma("sp", cvb[:, :], I["cvb"], [], [cvb])
            wd = [self.sb(st, f"wd{i}", [128, NF * 128], BF16) for i in range(2)]
            wdv = I["w_down"].rearrange("(f p) c -> p f c", p=128)
            x1w = self.sb(st, "x1w", [128, 8 * NWM], F32)
            tiles = {
                "sq": self.sb(st, "c2sq", [128, 8 * NWM], BF16),
                "rstd": self.sb(st, "c2rstd", [128, NWM], F32),
                "ntmp": [self.sb(st, f"c2ntmp{i}", [128, NWM], F32) for i in range(2)],
            }
            h2T = self.sb(st, "h2T", [128, 8 * NWM], BF16)
            HM = self.sb(st, "HM", [128, NF * 512], BF16)
            GP = [self.sb(st, f"GP{i}", [128, 10 * 66], F32) for i in range(2)]
            GPc = [self.sb(st, f"GPc{i}", [128, 2 * 258], F32) for i in range(2)]
            acc = [self.sb(st, f"c2acc{i}", [128, 512], F32) for i in range(2)]
            gl = [self.sb(st, f"c2gl{i}", [128, 512], F32) for i in range(2)]
            x2 = self.sb(st, "x2", [128, 8 * 512], F32)
            PG = [self.ps(st, f"c2PG{i}", [128, 1024], F32) for i in range(2)]
            PU = [self.ps(st, f"c2PU{i}", [128, 512], F32) for i in range(2)]
            PD = [self.ps(st, f"c2PD{i}", [128, 512], F32) for i in range(2)]
            fi = 0
            wdi = 0
            work = [(0, t0, n, None) for (t0, n) in self.ctx_tiles()]
            for r0 in range(0, QL // 64, 8):
                work.append((1, r0, min(8, QL // 64 - r0) * 64, None))
            for (kind, t0, n, _) in work:
                if kind == 1:
                    r0 = t0
                    nwv = n + 128
                    coff = 64
                    self.dma("sp", x1w[:, 0:8 * nwv].rearrange("p (k t) -> p k t", k=8),
                             x1wv[:, :, 64 * r0:64 * r0 + nwv], [], [x1w])
                else:
                    nwv, coff = n, 0
                    self.dma("sp", x1w[:, 0:8 * nwv].rearrange("p (k t) -> p k t", k=8), x1v[:, :, t0:t0 + n], [], [x1w])
                self.norm_mod(tiles, x1w, nwv, PU[0], mods[:, 3, kind, :], mods[:, 4, kind, :], h2T, mods)
                nrow_t = n // 64
                for gpz in (GP if kind == 1 else GPc):
                    self.memset("dve", gpz[:, :], 0.0, [gpz])
                for f in range(NF):
                    pg, pu = PG[fi % 2], PU[fi % 2]
                    ac, g_ = acc[fi % 2], gl[fi % 2]
                    c0 = 0
                    while c0 < nwv:
                        cn = min(512, nwv - c0)
                        for k in range(KD):
                            self.mm(pg[:, c0:c0 + cn], WG[:, k * DFF + f * 128:k * DFF + (f + 1) * 128],
                                    h2T[:, k * nwv + c0:k * nwv + c0 + cn], k == 0, k == KD - 1, [WG, h2T], [pg])
                        c0 += cn
                    for k in range(KD):
                        self.mm(pu[:, 0:n], WU[:, k * DFF + f * 128:k * DFF + (f + 1) * 128],
                                h2T[:, k * nwv + coff:k * nwv + coff + n], k == 0, k == KD - 1, [WU, h2T], [pu])
                    if kind == 1:
                        gp = GP[fi % 2]
                        gp3 = gp[:, :].rearrange("p (r c) -> p r c", c=66)
                        nrw = nwv // 64
                        self.cp("act", gp3[:, 0:nrw, 1:65], pg[:, 0:nwv].rearrange("p (r c) -> p r c", c=64), [pg], [gp])
                        if r0 == 0:
                            self.ts("dve", gp3[:, 0, 1:65], gp3[:, 0, 1:65], selv[:, NQ:NQ + 1], None, ALU.mult, ALU.bypass,
                                    [gp, selv], [gp])
                        if r0 + nrow_t == QL // 64:
                            self.ts("dve", gp3[:, nrw - 1, 1:65], gp3[:, nrw - 1, 1:65], selv[:, NQ + 1:NQ + 2], None,
                                    ALU.mult, ALU.bypass, [gp, selv], [gp])
                        a3 = ac[:, 0:n].rearrange("p (r c) -> p r c", c=64)
                        first = True
                        for dy in range(3):
                            for dx in range(3):
                                src = gp3[:, dy:dy + nrow_t, dx:dx + 64]
                                wcol = cw[:, f * 9 + dy * 3 + dx:f * 9 + dy * 3 + dx + 1]
                                if first:
                                    self.ts("dve", a3, src, wcol, cvb[:, f:f + 1], ALU.mult, ALU.add, [gp, cw, cvb], [ac])
                                    first = False
                                else:
                                    self.stt(a3, src, wcol, a3, ALU.mult, ALU.add, [gp, cw, ac], [ac])
                    else:
                        gp = GPc[fi % 2]
                        nsq = n // self.L_ctx
                        Lc = self.L_ctx
                        gp3 = gp[:, 0:nsq * (Lc + 2)].rearrange("p (r c) -> p r c", c=Lc + 2)
                        self.cp("act", gp3[:, :, 1:Lc + 1], pg[:, 0:n].rearrange("p (r c) -> p r c", c=Lc), [pg], [gp])
                        a3 = ac[:, 0:n].rearrange("p (r c) -> p r c", c=Lc)
                        for dx in range(3):
                            src = gp3[:, :, dx:dx + Lc]
                            wcol = cw[:, f * 9 + 3 + dx:f * 9 + 3 + dx + 1]
                            if dx == 0:
                                self.ts("dve", a3, src, wcol, cvb[:, f:f + 1], ALU.mult, ALU.add, [gp, cw, cvb], [ac])
                            else:
                                self.stt(a3, src, wcol, a3, ALU.mult, ALU.add, [gp, cw, ac], [ac])
                    self.act(g_[:, 0:n], ac[:, 0:n], AF.Gelu_apprx_tanh, [ac], [g_])
                    self.tt("dve", HM[:, f * 512:f * 512 + n], pu[:, 0:n], g_[:, 0:n], ALU.mult, [pu, g_], [HM])
                    fi += 1
                for m in range(KD):
                    w_ = wd[wdi % 2]
                    pd = PD[wdi % 2]
                    wdi += 1
                    self.dma("pool", w_[:, :].rearrange("p (f c) -> p f c", c=128), wdv[:, :, m * 128:(m + 1) * 128], [], [w_])
                    for f in range(NF):
                        self.mm(pd[:, 0:n], w_[:, f * 128:(f + 1) * 128], HM[:, f * 512:f * 512 + n], f == 0, f == NF - 1,
                                [w_, HM], [pd])
                    self.stt(x2[:, m * n:(m + 1) * n], pd[:, 0:n], mods[:, 5, kind, m:m + 1],
                             x1w[:, m * nwv + coff:m * nwv + coff + n], ALU.mult, ALU.add, [pd, mods, x1w], [x2])
                rstd = tiles["rstd"]
                self.rms_rstd(tiles, x2, n, PU[1], rstd)
                for k in range(KD):
                    self.stt(x2[:, k * n:(k + 1) * n], x2[:, k * n:(k + 1) * n], fnw[:, k:k + 1], rstd[:, 0:n],
                             ALU.mult, ALU.mult, [x2, fnw, rstd], [x2])
                od = NCT + 64 * t0 if kind == 1 else t0
                self.dma("pool", yTv[:, :, od:od + n], x2[:, 0:8 * n].rearrange("p (k t) -> p k t", k=8), [x2], [])

    def build(self, phases=("0", "A", "S", "C1", "C2")):
        self.declare_io()
        with contextlib.ExitStack() as gst:
            mst = contextlib.ExitStack()
            self.phase0(gst, mst)
            if "A" in phases:
                self.phaseA()
            if "S" in phases:
                self.scan()
            if "C1" in phases:
                self.phaseC1()
            self.S.barrier()
            mst.close()
            if "C2" in phases:
                self.phaseC2()
            self.S.barrier()
            fin = list(self.final)
            self.S.finalize(fin)
        return self.nc


def fm(v):
    return np.ascontiguousarray(np.asarray(v, np.float32).reshape(-1, 128).T)


def shared_inputs(inp):
    f = lambda a: np.ascontiguousarray(np.asarray(a, np.float32))
    S = {}
    S["ada_w"] = f(inp["ada_w"][0])
    S["adab"] = fm(inp["ada_b"][0])
    S["nmw"] = fm(inp["norm_mix_w"][0])
    S["nfw"] = fm(inp["norm_ffn_w"][0])
    S["fnw"] = fm(inp["final_norm_w"])
    S["w_in"] = f(inp["w_in"][0])
    S["rwkv_conv"] = f(inp["rwkv_conv"][0])
    S["hgrn_lb"] = f(inp["hgrn_lb"]).reshape(1, 2048)
    S["prow"] = np.concatenate([f(inp[k][0]).reshape(-1) for k in
                                ("hgrn_norm_w", "rwkv_k_k", "rwkv_k_a", "rwkv_r_k", "rwkv_ln_w", "rwkv_ln_b")]
                               + [f(inp["rwkv_w0"][0, 0]), f(inp["rwkv_w0"][0, 1]), f(inp["rwkv_a0"][0])]).reshape(1, 4608)
    w2x = np.zeros((32, 1536), np.float32)
    w2x[0:32, 0:512] = inp["rwkv_w2"][0, 0]
    w2x[0:32, 512:1024] = inp["rwkv_w2"][0, 1]
    w2x[0:32, 1024:1536] = inp["rwkv_a2"][0]
    S["w2x"] = w2x
    S["g2"] = f(inp["rwkv_g2"][0])
    S["w_out"] = f(inp["w_out"][0])
    S["w_gate"] = f(inp["ffn_w_gate"][0])
    S["w_up"] = f(inp["ffn_w_up"][0])
    S["w_down"] = f(inp["ffn_w_down"][0])
    cw = f(inp["ffn_conv"][0]).reshape(9, NF, 128)
    S["convw"] = np.ascontiguousarray(cw.transpose(2, 1, 0).reshape(128, NF * 9))
    S["cvb"] = fm(inp["ffn_conv_b"][0])
    return S


def core_inputs(inp, S, ctx_ids, lat_b, q):
    f = lambda a: np.ascontiguousarray(np.asarray(a, np.float32))
    L_lat = inp["x_sample"].shape[1]
    QL = min(1024, L_lat)
    NQ = L_lat // QL
    xl = f(inp["x_sample"][lat_b])
    xs = [f(inp["x_prompt"][i]) for i in ctx_ids] + [xl]
    x = np.concatenate(xs, axis=0)
    m = dict(S)
    m["xT"] = np.ascontiguousarray(x.T)
    xw = np.zeros((QL + 128, D), np.float32)
    lo, hi = q * QL - 64, q * QL + QL + 64
    a, b = max(lo, 0), min(hi, L_lat)
    xw[a - lo:b - lo] = xl[a:b]
    m["xTo"] = np.ascontiguousarray(xw.T)
    sel = np.zeros((128, NQ + 2), np.float32)
    sel[:, q] = 1.0
    sel[:, NQ] = 0.0 if q == 0 else 1.0
    sel[:, NQ + 1] = 0.0 if q == NQ - 1 else 1.0
    m["selv"] = sel
    cv = np.stack([f(inp["c_ctx"]), f(inp["c"][lat_b])], axis=1)
    m["cvec"] = np.ascontiguousarray(cv.reshape(8, 128, 2).transpose(1, 0, 2).reshape(128, 16))
    sh = f(inp["state_hgrn"][lat_b, 0])
    m["st_h"] = np.ascontiguousarray(sh.transpose(0, 2, 1, 3).reshape(2, 128, 512))
    sr = f(inp["state_rwkv"][lat_b, 0])
    m["st_r"] = np.ascontiguousarray(sr.transpose(0, 3, 1, 2).reshape(2, 64, 512))
    return m


_PROG_CACHE = {}


def get_prog(n_ctx, L_ctx, L_lat, debug=(), phases=("0", "A", "S", "C1", "C2")):
    key = (n_ctx, L_ctx, L_lat, tuple(sorted(debug)), tuple(phases))
    if key not in _PROG_CACHE:
        b = Builder(n_ctx, L_ctx, L_lat, debug)
        nc = b.build(phases)
        _PROG_CACHE[key] = (nc, b)
    return _PROG_CACHE[key]


def kernel(**inp):
    B, L_ctx = inp["x_prompt"].shape[0], inp["x_prompt"].shape[1]
    DB, L_lat = inp["x_sample"].shape[0], inp["x_sample"].shape[1]
    n_ctx = B // N_CORES
    QL = min(1024, L_lat)
    NQ = L_lat // QL
    assert DB * NQ == N_CORES
    nc, _ = get_prog(n_ctx, L_ctx, L_lat)
    S = shared_inputs(inp)
    in_maps = []
    for c in range(N_CORES):
        ctx_ids = list(range(c * n_ctx, (c + 1) * n_ctx))
        in_maps.append(core_inputs(inp, S, ctx_ids, c % DB, c // DB))
    res = run_bass_kernel_spmd(nc, in_maps, core_ids=list(range(N_CORES)))
    y_prompt = np.zeros((B, L_ctx, D), np.float32)
    y_sample = np.zeros((DB, L_lat, D), np.float32)
    nsh = np.zeros((B, 1, 2, 4, 128, 128), np.float32)
    nsr = np.zeros((B, 1, 2, 8, 64, 64), np.float32)
    for c in range(N_CORES):
        r = res.results[c]
        y = np.asarray(r["yT"]).T
        for i in range(n_ctx):
            y_prompt[c * n_ctx + i] = y[i * L_ctx:(i + 1) * L_ctx]
        b, q = c % DB, c // DB
        y_sample[b, q * QL:(q + 1) * QL] = y[n_ctx * L_ctx:]
        h = np.asarray(r["nsh"]).reshape(n_ctx, 2, 128, 4, 128).transpose(0, 1, 3, 2, 4)
        nsh[c * n_ctx:(c + 1) * n_ctx, 0] = h
        rr = np.asarray(r["nsr"]).reshape(n_ctx, 2, 64, 8, 64).transpose(0, 1, 3, 4, 2)
        nsr[c * n_ctx:(c + 1) * n_ctx, 0] = rr
    return (y_prompt, y_sample, nsh, nsr)
```

```python
import contextlib
import numpy as np
import concourse.bass as bass
import concourse.mybir as mybir
from concourse.bass_utils import run_bass_kernel_spmd

F32 = mybir.dt.float32
BF16 = mybir.dt.bfloat16
AF = mybir.ActivationFunctionType
ALU = mybir.AluOpType
AX = mybir.AxisListType

D = 1024
KD = 8
WA = 512
PA = 2560
PB = 1728
PIN = 4288
DFF = 2816
NF = 22
DECAY = 0.6065306597
RMS_EPS = 1e-6
GN_EPS = 64e-5
GRID_W = 64
N_CORES = 8

ENGS = ("pe", "act", "dve", "pool", "sp")


class Buf:
    __slots__ = ("name", "lw", "rd", "rdd")

    def __init__(self, name):
        self.name = name
        self.lw = None
        self.rd = {}
        self.rdd = []


class Ins:
    __slots__ = ("eng", "fn", "deps", "is_dma", "sig", "sigval", "dsem", "dval")

    def __init__(self, eng, fn, deps, is_dma):
        self.eng = eng
        self.fn = fn
        self.deps = deps
        self.is_dma = is_dma
        self.sig = False
        self.sigval = 0
        self.dsem = None
        self.dval = 0


class Sched:
    NDMA = 16

    def __init__(self, nc):
        self.nc = nc
        self.ins = []
        self.last = {}
        self.dmas = []

    def barrier(self):
        deps = sorted(set(self.last.values()) | set(self.dmas))
        if not deps:
            return
        for e in ENGS:
            self.ins.append(Ins(e, None, list(deps), False))
        self.dmas = []

    def add(self, eng, fn, reads=(), writes=(), is_dma=False):
        deps = set()
        for b in reads:
            if b.lw is not None:
                deps.add(b.lw)
        for b in writes:
            if b.lw is not None:
                deps.add(b.lw)
            deps.update(b.rd.values())
            deps.update(b.rdd)
        i = len(self.ins)
        self.ins.append(Ins(eng, fn, sorted(deps), is_dma))
        self.last[eng] = i
        if is_dma:
            self.dmas.append(i)
        for b in reads:
            if is_dma:
                b.rdd.append(i)
            else:
                b.rd[eng] = i
        for b in writes:
            b.lw = i
            b.rd = {}
            b.rdd = []
        return i

    def finalize(self, final_bufs):
        nc = self.nc
        ins = self.ins
        fdeps = set()
        for b in final_bufs:
            if b.lw is not None:
                fdeps.add(b.lw)
        ins.append(Ins("sp", None, sorted(fdeps), False))
        for it in ins:
            for d in it.deps:
                dd = ins[d]
                if dd.eng == "pe" and it.eng == "pe" and not dd.is_dma and not it.is_dma:
                    continue
                dd.sig = True
        st = contextlib.ExitStack()
        esem = {e: st.enter_context(nc.semaphore(f"s_{e}")) for e in ("pe", "act", "dve", "pool")}
        dq = [e for e in ENGS if any(it.is_dma and it.eng == e for it in ins)]
        per = max(2, self.NDMA // max(1, len(dq)))
        dsems = []
        dpool = {}
        for e in dq:
            dpool[e] = list(range(len(dsems), len(dsems) + per))
            dsems += [st.enter_context(nc.semaphore(f"s_dma_{e}{i}")) for i in range(per)]
        dcount = [0] * len(dsems)
        dlast = [None] * len(dsems)
        ecount = {e: 0 for e in esem}
        rr = {e: 0 for e in dq}
        for k, it in enumerate(ins):
            if it.is_dma:
                s = dpool[it.eng][rr[it.eng] % per]
                rr[it.eng] += 1
                if dlast[s] is not None:
                    it.deps = sorted(set(it.deps) | {dlast[s]})
                dcount[s] += 16
                it.dsem = s
                it.dval = dcount[s]
                dlast[s] = k
            elif it.sig:
                ecount[it.eng] += 1
                it.sigval = ecount[it.eng]
        progs = {e: [] for e in ENGS}
        waited = {e: {} for e in ENGS}
        for it in ins:
            w = {}
            for d in it.deps:
                dd = ins[d]
                if dd.is_dma:
                    key = ("d", dd.dsem)
                    val = dd.dval
                else:
                    if dd.eng == "pe" and it.eng == "pe" and not it.is_dma:
                        continue
                    key = ("e", dd.eng)
                    val = dd.sigval
                if w.get(key, 0) < val:
                    w[key] = val
            wl = []
            for key, val in w.items():
                if waited[it.eng].get(key, 0) >= val:
                    continue
                waited[it.eng][key] = val
                wl.append((dsems[key[1]] if key[0] == "d" else esem[key[1]], val))
            progs[it.eng].append((wl, it))
        self.counts = {e: len(progs[e]) for e in ENGS}
        with nc.Block() as block:
            def runner(name):
                def run(eng):
                    for wl, it in progs[name]:
                        for sem, val in wl:
                            eng.wait_ge(sem, val)
                        if it.fn is None:
                            continue
                        r = it.fn(eng)
                        if it.is_dma:
                            r.then_inc(dsems[it.dsem], 16)
                        elif it.sig:
                            r.then_inc(esem[it.eng], 1)
                return run
            block.tensor(runner("pe"))
            block.scalar(runner("act"))
            block.vector(runner("dve"))
            block.gpsimd(runner("pool"))
            block.sync(runner("sp"))
        st.close()


class Tl:
    __slots__ = ("t", "b")

    def __init__(self, t, name):
        self.t = t
        self.b = Buf(name)

    def __getitem__(self, k):
        return self.t[k]


def _bufs(xs):
    return [x.b if isinstance(x, Tl) else x for x in xs]


class Builder:
    def __init__(self, n_ctx, L_ctx, L_lat, debug=()):
        self.n_ctx, self.L_ctx, self.L_lat = n_ctx, L_ctx, L_lat
        self.NT = n_ctx * L_ctx + L_lat
        self.NG = self.NT // 128
        self.QL = min(1024, L_lat)
        self.NQ = L_lat // self.QL
        self.seqs = [(i * L_ctx, L_ctx, 0) for i in range(n_ctx)] + [(n_ctx * L_ctx, L_lat, 1)]
        self.debug = set(debug)
        self.nc = bass.Bass("TRN2", target_bir_lowering=False)
        self.S = Sched(self.nc)
        self.final = []
        self.uid = 0

    @contextlib.contextmanager
    def scope(self):
        with contextlib.ExitStack() as st:
            yield st
            self.S.barrier()

    def sb(self, st, name, shape, dt):
        self.uid += 1
        return Tl(st.enter_context(self.nc.sbuf_tensor(f"{name}_{self.uid}", shape, dt)), name)

    def ps(self, st, name, shape, dt):
        self.uid += 1
        return Tl(st.enter_context(self.nc.psum_tensor(f"{name}_{self.uid}", shape, dt)), name)

    def din(self, name, shape, dt=F32):
        return self.nc.dram_tensor(name, list(shape), dt, kind="ExternalInput").ap()

    def dout(self, name, shape, dt=F32):
        return self.nc.dram_tensor(name, list(shape), dt, kind="ExternalOutput").ap()

    def dscr(self, name, shape, dt):
        kind = "ExternalOutput" if name in self.debug else "Internal"
        return self.nc.dram_tensor(name, list(shape), dt, kind=kind).ap()

    def mm(self, out, lhsT, rhs, start, stop, reads, writes):
        self.S.add("pe", lambda e: e.matmul(out, lhsT=lhsT, rhs=rhs, start=start, stop=stop),
                   _bufs(reads), _bufs(writes))

    def tr(self, out, in_, ident, reads, writes):
        self.S.add("pe", lambda e: e.transpose(out=out, in_=in_, identity=ident), _bufs(reads), _bufs(writes))

    def act(self, out, in_, func, reads, writes, scale=1.0, bias=0.0):
        self.S.add("act", lambda e: e.activation(out=out, in_=in_, func=func, scale=scale, bias=bias),
                   _bufs(reads), _bufs(writes))

    def tt(self, eng, out, in0, in1, op, reads, writes):
        self.S.add(eng, lambda e: e.tensor_tensor(out=out, in0=in0, in1=in1, op=op), _bufs(reads), _bufs(writes))

    def ts(self, eng, out, in0, s1, s2, op0, op1, reads, writes):
        self.S.add(eng, lambda e: e.tensor_scalar(out=out, in0=in0, scalar1=s1, scalar2=s2, op0=op0, op1=op1),
                   _bufs(reads), _bufs(writes))

    def stt(self, out, in0, scalar, in1, op0, op1, reads, writes):
        self.S.add("dve", lambda e: e.scalar_tensor_tensor(out=out, in0=in0, scalar=scalar, in1=in1, op0=op0, op1=op1),
                   _bufs(reads), _bufs(writes))

    def cp(self, eng, out, in_, reads, writes):
        if eng == "act":
            self.act(out, in_, AF.Copy, reads, writes)
        else:
            self.S.add(eng, lambda e: e.tensor_copy(out=out, in_=in_), _bufs(reads), _bufs(writes))

    def red(self, out, in_, reads, writes):
        self.S.add("dve", lambda e: e.tensor_reduce(out=out, in_=in_, axis=AX.X, op=ALU.add), _bufs(reads), _bufs(writes))

    def recip(self, out, in_, reads, writes):
        self.S.add("dve", lambda e: e.reciprocal(out=out, in_=in_), _bufs(reads), _bufs(writes))

    def memset(self, eng, ap, val, writes):
        self.S.add(eng, lambda e: e.memset(ap, val), [], _bufs(writes))

    def asel(self, ap, pattern, cmp, fill, base, cm, tl):
        self.S.add("pool", lambda e: e.affine_select(out=ap, in_=ap, pattern=pattern, compare_op=cmp, fill=fill,
                                                     base=base, channel_multiplier=cm), [tl.b], [tl.b])

    def dma(self, eng, out, in_, reads, writes):
        nt = getattr(self, "_untracked", None)
        if nt is None:
            nt = self._untracked = set(id(b) for b in list(self.bR.values()) + list(self.bO.values()))
        writes = [w for w in _bufs(writes) if id(w) not in nt]
        reads = [r for r in _bufs(reads) if id(r) not in nt]
        self.S.add(eng, lambda e: e.dma_start(out=out, in_=in_), reads, writes, is_dma=True)

    def dbg(self, name, ap, shape, rd):
        if name in self.debug:
            d = self.dout(name, shape)
            b = Buf(name)
            self.dma("pool", d, ap, [rd], [b])
            self.final.append(b)

    def declare_io(self):
        NT, n_ctx = self.NT, self.n_ctx
        I = {}
        I["xT"] = self.din("xT", [D, NT])
        I["cvec"] = self.din("cvec", [128, 16])
        I["ada_w"] = self.din("ada_w", [D, 6 * D])
        I["adab"] = self.din("adab", [128, 48])
        I["nmw"] = self.din("nmw", [128, 8])
        I["nfw"] = self.din("nfw", [128, 8])
        I["fnw"] = self.din("fnw", [128, 8])
        I["w_in"] = self.din("w_in", [D, PIN])
        I["rwkv_conv"] = self.din("rwkv_conv", [3, PB])
        I["hgrn_lb"] = self.din("hgrn_lb", [1, 2048])
        I["prow"] = self.din("prow", [1, 9 * 512])
        I["w2x"] = self.din("w2x", [32, 3 * 512])
        I["g2"] = self.din("g2", [96, 512])
        I["w_out"] = self.din("w_out", [D, D])
        I["w_gate"] = self.din("w_gate", [D, DFF])
        I["w_up"] = self.din("w_up", [D, DFF])
        I["w_down"] = self.din("w_down", [DFF, D])
        I["convw"] = self.din("convw", [128, NF * 9])
        I["cvb"] = self.din("cvb", [128, NF])
        I["xTo"] = self.din("xTo", [D, self.QL + 128])
        I["selv"] = self.din("selv", [128, self.NQ + 2])
        I["st_h"] = self.din("st_h", [2, 128, 4 * 128])
        I["st_r"] = self.din("st_r", [2, 64, 8 * 64])
        self.I = I
        O = {}
        O["yT"] = self.dout("yT", [D, n_ctx * self.L_ctx + self.QL])
        O["nsh"] = self.dout("nsh", [n_ctx, 2, 128, 512])
        O["nsr"] = self.dout("nsr", [n_ctx, 2, 64, 512])
        self.O = O
        self.bO = {k: Buf("out_" + k) for k in O}
        R = {}
        R["OPS"] = self.dscr("OPS", [NT, 14 * 512], BF16)
        R["CX"] = self.dscr("CX", [NT, 3 * 512], BF16)
        R["GCs"] = self.dscr("GCs", [self.NG, 2, 64, 16], F32)
        R["ECs"] = self.dscr("ECs", [self.NG, 2, 128, 8], F32)
        R["YS"] = self.dscr("YS", [2, NT, 512], F32)
        R["OSC"] = self.dscr("OSC", [2, NT, 512], F32)
        R["X1T"] = self.dscr("X1T", [D, n_ctx * self.L_ctx], F32)
        R["X1W"] = self.dscr("X1W", [D, self.QL + 128], F32)
        self.R = R
        self.bR = {k: Buf("scr_" + k) for k in R}

    def phase0(self, gst0, gst):
        nc, I = self.nc, self.I
        C = {}
        C["ident"] = self.sb(gst0, "ident", [128, 128], BF16)
        C["onesD"] = self.sb(gst0, "onesD", [128, 128], BF16)
        mods = self.sb(gst0, "mods", [128, 6, 2, 8], F32)
        fnw = self.sb(gst0, "fnw", [128, 8], F32)
        C["selv"] = self.sb(gst0, "selv", [128, self.NQ + 2], F32)
        self.dma("sp", C["selv"][:], I["selv"], [], [C["selv"]])
        identf = self.sb(gst, "identf", [128, 128], F32)
        self.memset("pool", identf[:], 0.0, [identf])
        self.asel(identf[:], [[-1, 128]], ALU.not_equal, 1.0, 0, 1, identf)
        self.cp("dve", C["ident"][:], identf[:], [identf], [C["ident"]])
        self.memset("pool", C["onesD"][:], 1.0 / D, [C["onesD"]])
        trif = self.sb(gst, "trif", [128, 128], F32)
        self.memset("pool", trif[:], 1.0, [trif])
        self.asel(trif[:], [[1, 128]], ALU.is_ge, 0.0, 0, -1, trif)
        self.asel(trif[:, 64:128], [[0, 64]], ALU.is_ge, 0.0, -64, 1, trif)
        trib = self.sb(gst, "trib", [128, 128], F32)
        self.memset("pool", trib[:], 1.0, [trib])
        self.asel(trib[:], [[-1, 128]], ALU.is_ge, 0.0, 0, 1, trib)
        self.asel(trib[:, 0:64], [[0, 64]], ALU.is_ge, 0.0, 63, -1, trib)
        C["tri"] = [trif, trib]
        t16f = self.sb(gst, "tri16f", [128, 128], BF16)
        self.memset("pool", t16f[:], 1.0, [t16f])
        self.asel(t16f[:], [[1, 128]], ALU.is_ge, 0.0, 0, -1, t16f)
        self.asel(t16f[:, 64:128], [[0, 64]], ALU.is_ge, 0.0, -64, 1, t16f)
        t16b = self.sb(gst, "tri16b", [128, 128], BF16)
        self.memset("pool", t16b[:], 1.0, [t16b])
        self.asel(t16b[:], [[-1, 128]], ALU.is_ge, 0.0, 0, 1, t16b)
        self.asel(t16b[:, 0:64], [[0, 64]], ALU.is_ge, 0.0, 63, -1, t16b)
        C["tri16"] = [t16f, t16b]

        C["identf"] = identf
        cind = self.sb(gst, "cind", [128, 2], F32)
        self.memset("pool", cind[:], 1.0, [cind])
        self.asel(cind[:, 0:1], [[0, 1]], ALU.is_ge, 0.0, 63, -1, cind)
        self.asel(cind[:, 1:2], [[0, 1]], ALU.is_ge, 0.0, -64, 1, cind)
        C["cind"] = cind
        M1 = self.sb(gst, "M1", [128, 128], F32)
        M3 = self.sb(gst, "M3", [128, 64], F32)
        MI = self.sb(gst, "MI", [128, 64], F32)
        for m in (M1, M3, MI):
            self.memset("pool", m[:], 1.0, [m])
        self.asel(M1[0:64, 0:64], [[1, 64]], ALU.is_gt, 0.0, 0, -1, M1)
        self.asel(M1[0:64, 64:128], [[1, 64]], ALU.is_ge, 0.0, 0, -1, M1)
        self.asel(M1[64:128, 0:64], [[-1, 64]], ALU.is_gt, 0.0, 0, 1, M1)
        self.asel(M1[64:128, 64:128], [[-1, 64]], ALU.is_ge, 0.0, 0, 1, M1)
        self.asel(M3[0:64, :], [[-1, 64]], ALU.is_gt, 0.0, 0, 1, M3)
        self.asel(M3[64:128, :], [[1, 64]], ALU.is_gt, 0.0, 0, -1, M3)
        self.asel(MI[0:64, :], [[1, 64]], ALU.is_ge, 0.0, 0, -1, MI)
        self.asel(MI[64:128, :], [[-1, 64]], ALU.is_ge, 0.0, 0, 1, MI)
        C["M1"], C["M3"], C["MI"] = M1, M3, MI
        self.dbg("dbg_M1", M1[:], [128, 128], M1)
        self.dbg("dbg_trif", trif[:], [128, 128], trif)
        self.dbg("dbg_trib", trib[:], [128, 128], trib)

        modT = self.sb(gst, "modT", [128, 48, 2], F32)
        lb = self.sb(gst, "lb", [128, 1024], F32)
        omlb = self.sb(gst, "omlb", [128, 1024], F32)
        prow = self.sb(gst, "prow", [128, 9 * 512], F32)
        omka = self.sb(gst, "omka", [128, 512], F32)
        with self.scope() as st:
            cv = self.sb(st, "cv", [128, 16], F32)
            self.dma("sp", cv[:], I["cvec"], [], [cv])
            scv = self.sb(st, "scv", [128, 16], F32)
            self.act(scv[:], cv[:], AF.Silu, [cv], [scv])
            adab = self.sb(st, "adab", [128, 48], F32)
            self.dma("sp", adab[:], I["adab"], [], [adab])
            acc = self.sb(st, "modacc", [128, 96], F32)
            abuf = [self.sb(st, f"adaw{i}", [128, 3 * D], F32) for i in range(2)]
            pm = self.ps(st, "pmod", [128, 512], F32)
            adv = I["ada_w"].rearrange("(k p) c -> k p c", p=128)
            it = 0
            for k in range(KD):
                for hf in range(2):
                    ab = abuf[it % 2]
                    it += 1
                    self.dma("sp", ab[:, :], adv[k][:, hf * 3072:(hf + 1) * 3072], [], [ab])
                    for mm_ in range(24):
                        m = hf * 24 + mm_
                        self.mm(pm[:, 2 * m:2 * m + 2], ab[:, mm_ * 128:(mm_ + 1) * 128], scv[:, 2 * k:2 * k + 2], True, True,
                                [ab, scv], [pm])
                if k == 0:
                    self.cp("dve", acc[:], pm[:, 0:96], [pm], [acc])
                else:
                    self.tt("dve", acc[:], pm[:, 0:96], acc[:], ALU.add, [pm, acc], [acc])
            self.tt("dve", modT[:], acc[:].rearrange("p (m s) -> p m s", s=2),
                    adab[:].unsqueeze(2).broadcast_to([128, 48, 2]), ALU.add, [acc, adab], [modT])
            self.dbg("dbg_modT", modT[:].rearrange("p m s -> p (m s)"), [128, 96], modT)
            nmw = self.sb(st, "nmw", [128, 8], F32)
            nfw = self.sb(st, "nfw", [128, 8], F32)
            self.dma("sp", nmw[:], I["nmw"], [], [nmw])
            self.dma("sp", nfw[:], I["nfw"], [], [nfw])
            for s in range(2):
                for (wi, off, nw) in ((0, 8, nmw), (3, 32, nfw)):
                    self.stt(mods[:, wi, s, :], modT[:, off:off + 8, s], 1.0, nw[:], ALU.add, ALU.mult,
                             [modT, nw], [mods])
                for (wi, off) in ((1, 0), (2, 16), (4, 24), (5, 40)):
                    self.cp("dve", mods[:, wi, s, :], modT[:, off:off + 8, s], [modT], [mods])
            C["mods"] = mods
            self.dma("sp", fnw[:], I["fnw"], [], [fnw])
            C["fnw"] = fnw
            lbr = self.sb(st, "lbraw", [128, 2048], F32)
            self.dma("sp", lbr[:], I["hgrn_lb"].partition_broadcast(128), [], [lbr])
            lbd = self.sb(st, "lbd", [128, 1024], F32)
            self.tt("dve", lbd[:], lbr[:, 0:1024], lbr[:, 1024:2048], ALU.subtract, [lbr], [lbd])
            self.act(lb[:], lbd[:], AF.Sigmoid, [lbd], [lb])
            self.ts("dve", omlb[:], lb[:], -0.5, 0.5, ALU.mult, ALU.add, [lb], [omlb])
            self.ts("dve", lb[:], lb[:], 0.5, 0.5, ALU.mult, ALU.add, [lb], [lb])
            C["lb"], C["omlb"] = lb, omlb
            self.dma("sp", prow[:], I["prow"].partition_broadcast(128), [], [prow])
            C["prow"] = prow
            self.ts("dve", omka[:], prow[:, 1024:1536], -1.0, 1.0, ALU.mult, ALU.add, [prow], [omka])
            C["omka"] = omka
        self.C = C

    def rms_rstd(self, st_tiles, src, n, pbank, rstd, lnexp=True):
        sq = st_tiles["sq"]
        self.tt("dve", sq[:, 0:8 * n], src[:, 0:8 * n], src[:, 0:8 * n], ALU.mult, [src], [sq])
        c0 = 0
        while c0 < n:
            cn = min(512, n - c0)
            for k in range(KD):
                self.mm(pbank[:, 0:cn], self.C["onesD"][:], sq[:, k * n + c0:k * n + c0 + cn], k == 0, k == KD - 1,
                        [self.C["onesD"], sq], [pbank])
            if lnexp:
                self.act(rstd[:, c0:c0 + cn], pbank[:, 0:cn], AF.Ln, [pbank], [rstd], bias=RMS_EPS)
            else:
                self.act(rstd[:, c0:c0 + cn], pbank[:, 0:cn], AF.Sqrt, [pbank], [rstd], bias=RMS_EPS)
            c0 += cn
        if lnexp:
            self.act(rstd[:, 0:n], rstd[:, 0:n], AF.Exp, [rstd], [rstd], scale=-0.5)
        else:
            self.recip(rstd[:, 0:n], rstd[:, 0:n], [rstd], [rstd])

    def norm_mod(self, tiles, xt, n, pbank, A, B, hT, modtl, lnexp=True):
        rstd = tiles["rstd"]
        self.rms_rstd(tiles, xt, n, pbank, rstd, lnexp)
        for k in range(KD):
            tmp = tiles["ntmp"][k % 2]
            self.tt("dve", tmp[:, 0:n], xt[:, k * n:(k + 1) * n], rstd[:, 0:n], ALU.mult, [xt, rstd], [tmp])
            self.ts("dve", hT[:, k * n:(k + 1) * n], tmp[:, 0:n], A[:, k:k + 1], B[:, k:k + 1], ALU.mult, ALU.add,
                    [tmp, modtl], [hT])

    def phaseA(self):
        nc, I, C, R = self.nc, self.I, self.C, self.R
        mods = C["mods"]
        xTv = I["xT"].rearrange("(k p) t -> p k t", p=128)
        import os as _os
        _rnew = "1"
        for part in (("H",) if _rnew == "0" else ("H", "R")):
            halo = 0 if part == "H" else 1
            nw = 128 + 2 * halo
            with self.scope() as st:
                X = {"part": part, "nw": nw, "halo": halo}
                if part == "H":
                    X["lf"] = [self.sb(st, f"lf{i}", [128, 512], F32) for i in range(2)]
                else:
                    X["sg"] = [self.sb(st, f"sg{i}", [128, 512], F32) for i in range(2)]
                    X["lt"] = [self.sb(st, f"lt{i}", [128, 192], F32) for i in range(2)]
                if part == "H":
                    WH = self.sb(st, "WH", [128, 8 * PA], BF16)
                    wv = I["w_in"].rearrange("(k p) c -> p k c", p=128)
                    for k in range(KD):
                        self.dma("pool", WH[:, k * PA:(k + 1) * PA], wv[:, k, 0:PA], [], [WH])
                    X["WH"] = WH
                else:
                    WR = [self.sb(st, f"WR{t}", [128, 8 * PB], BF16) for t in range(3)]
                    with self.scope() as st2:
                        cvbc = self.sb(st2, "cvbc", [128, 3 * PB], F32)
                        self.dma("sp", cvbc[:], I["rwkv_conv"].rearrange("(o a) c -> o (a c)", o=1).partition_broadcast(128),
                                 [], [cvbc])
                        stg = [self.sb(st2, f"wstg{i}", [128, PB], F32) for i in range(2)]
                        wv = I["w_in"].rearrange("(k p) c -> p k c", p=128)
                        for k in range(KD):
                            sg = stg[k % 2]
                            self.dma("sp", sg[:], wv[:, k, PA:PIN], [], [sg])
                            for t in range(3):
                                self.tt("dve" if t != 1 else "pool", WR[t][:, k * PB:(k + 1) * PB], sg[:],
                                        cvbc[:, t * PB:(t + 1) * PB], ALU.mult, [sg, cvbc], [WR[t]])
                    X["WR"] = WR
                    X["W2x"] = self.sb(st, "W2x", [32, 1536], BF16)
                    self.dma("pool", X["W2x"][:], I["w2x"], [], [X["W2x"]])
                    X["G2"] = self.sb(st, "G2", [96, 512], BF16)
                    self.dma("pool", X["G2"][:], I["g2"], [], [X["G2"]])
                    X["lw"] = self.sb(st, "lw", [32, 384], BF16)
                    X["lg"] = self.sb(st, "lg", [96, 128], BF16)
                X["tiles"] = {
                    "sq": self.sb(st, "sq", [128, 8 * nw], BF16),
                    "rstd": self.sb(st, "rstd", [128, nw], F32),
                    "ntmp": [self.sb(st, f"ntmp{i}", [128, nw], F32) for i in range(2)],
                }
                X["xts"] = [self.sb(st, f"xt{i}", [128, 8 * nw], F32) for i in range(2)]
                X["hTs"] = [self.sb(st, f"hT{i}", [128, 8 * nw], BF16) for i in range(2)]
                X["OPst"] = [self.sb(st, f"OPst{i}", [128, 14 * 512], BF16) for i in range(2)]
                X["CXst"] = [self.sb(st, f"CXst{i}", [128, 3 * 512], BF16) for i in range(2)]
                X["dcs"] = [self.sb(st, f"dcs{i}", [128, 32], F32) for i in range(2)]
                X["sm"] = [self.sb(st, f"sm{i}", [128, 8], F32) for i in range(4)]
                X["hilo"] = [self.sb(st, f"hilo{i}", [128, 512], BF16) for i in range(2)]
                if part == "H":
                    X["qs"] = [self.sb(st, f"qs{i}", [128, 512], F32) for i in range(2)]
                    X["sgz"] = [[self.sb(st, f"sgz{i}{d}", [128, 512], F32) for d in range(2)] for i in range(2)]
                    X["ft"] = [self.sb(st, f"ft{i}", [128, 512], F32) for i in range(6)]
                    X["pss"] = self.ps(st, "pss", [128, 512], F32)
                    X["pp"] = [self.ps(st, f"pp{i}", [128, 512], F32) for i in range(5)]
                    X["LA"] = self.ps(st, "LA", [128, 512], F32)
                    X["LB"] = self.ps(st, "LB", [128, 512], F32)
                else:
                    X["rkv"] = [[self.sb(st, f"rkv{i}{j}", [128, 512], F32) for j in range(3)] for i in range(2)]
                    X["ft"] = [self.sb(st, f"ft{i}", [128, 512], F32) for i in range(9)]
                    X["pss"] = self.ps(st, "pss", [128, 512], F32)
                    X["pp"] = [self.ps(st, f"pp{i}", [128, 512], F32) for i in range(4)]
                    X["LA"] = self.ps(st, "LA", [128, 512], F32)
                    X["LB"] = self.ps(st, "LB", [128, 512], F32)
                    X["LC"] = self.ps(st, "LC", [128, 512], F32)
                if not (part == "R" and _rnew == "2"):
                    self.A_normproj(X, 0)
                for g in range(self.NG):
                    if part == "R" and _rnew == "2":
                        self.A_normproj(X, g)
                        self.A_early(X, g)
                        self.A_late_R(X, g)
                        continue
                    self.A_early(X, g)
                    if g + 1 < self.NG:
                        self.A_normproj(X, g + 1)
                    if part == "H":
                        self.A_late_H(X, g)
                    else:
                        self.A_late_R(X, g)

    def A_normproj(self, X, g):
        I, C = self.I, self.C
        mods = C["mods"]
        xTv = I["xT"].rearrange("(k p) t -> p k t", p=128)
        halo, nw = X["halo"], X["nw"]
        t0 = g * 128
        s0, sl, kind = [s for s in self.seqs if s[0] <= t0 < s[0] + s[1]][0]
        s1 = s0 + sl
        lo, hi = max(t0 - halo, s0), min(t0 + 128 + halo, s1)
        off = lo - (t0 - halo)
        n = hi - lo
        xt, hT = X["xts"][g % 2], X["hTs"][g % 2]
        x3 = xt[:, :].rearrange("p (k t) -> p k t", k=8)
        if n < nw:
            self.memset("dve", xt[:, :], 0.0, [xt])
        self.dma("sp", x3[:, :, off:off + n], xTv[:, :, lo:hi], [], [xt])
        self.norm_mod(X["tiles"], xt, nw, X["pss"], mods[:, 0, kind, :], mods[:, 1, kind, :], hT, mods)
        h3 = hT[:, :].rearrange("p (k t) -> p k t", k=8)
        if off > 0:
            self.memset("dve", h3[:, :, 0:off], 0.0, [hT])
        if off + n < nw:
            self.memset("dve", h3[:, :, off + n:nw], 0.0, [hT])
        pp = X["pp"]
        if X["part"] == "H":
            WH = X["WH"]
            for cg in range(5):
                for k in range(KD):
                    self.mm(pp[cg][:, :], hT[:, k * nw:k * nw + 128], WH[:, k * PA + cg * 512:k * PA + (cg + 1) * 512],
                            k == 0, k == KD - 1, [hT, WH], [pp[cg]])
        else:
            WR = X["WR"]
            for cg in range(4):
                c0 = cg * 512
                cn = 512 if cg < 3 else 192
                for t in range(3):
                    for k in range(KD):
                        self.mm(pp[cg][:, 0:cn], hT[:, k * nw + t:k * nw + t + 128], WR[t][:, k * PB + c0:k * PB + c0 + cn],
                                t == 0 and k == 0, t == 2 and k == KD - 1, [hT, WR[t]], [pp[cg]])

    def A_early(self, X, g):
        pp = X["pp"]
        b = g % 2
        ops, cxs = X["OPst"][b], X["CXst"][b]
        if X["part"] == "H":
            self.act(X["qs"][b][:], pp[0][:], AF.Silu, [pp[0]], [X["qs"][b]])
            self.act(cxs[:, 1024:1536], pp[4][:], AF.Silu, [pp[4]], [cxs])
            self.act(ops[:, 13 * 512:14 * 512], pp[1][:], AF.Copy, [pp[1]], [ops])
            self.act(X["sgz"][b][0][:], pp[2][:], AF.Tanh, [pp[2]], [X["sgz"][b][0]], scale=0.5)
            self.act(X["sgz"][b][1][:], pp[3][:], AF.Tanh, [pp[3]], [X["sgz"][b][1]], scale=0.5)
        else:
            lt = X["lt"][b]
            self.act(lt[:, 0:64], pp[3][:, 0:64], AF.Tanh, [pp[3]], [lt])
            self.act(lt[:, 64:96], pp[3][:, 64:96], AF.Copy, [pp[3]], [lt])
            self.act(lt[:, 96:192], pp[3][:, 96:192], AF.Sigmoid, [pp[3]], [lt])
            r_, k_, v_ = X["rkv"][b]
            self.cp("dve", r_[:], pp[0][:], [pp[0]], [r_])
            self.cp("act", k_[:], pp[1][:], [pp[1]], [k_])
            self.cp("dve", v_[:], pp[2][:], [pp[2]], [v_])

    def A_late_H(self, X, g):
        C, R = self.C, self.R
        OPSd, CXd = R["OPS"], R["CX"]
        b = g % 2
        t0 = g * 128
        ops, cxs, dcs = X["OPst"][b], X["CXst"][b], X["dcs"][b]
        qs = X["qs"][b]
        ft = X["ft"]
        LA, LB = X["LA"], X["LB"]
        for d in range(2):
            sg_ = X["sgz"][b][d]
            lf = X["lf"][d]
            f_, kd, csc, e1, e2 = ft[0], ft[1 + d], ft[3], ft[4], ft[5]
            self.tt("dve", f_[:], sg_[:], C["omlb"][:, d * 512:(d + 1) * 512], ALU.mult, [sg_, C["omlb"]], [f_])
            self.tt("pool", f_[:], f_[:], C["lb"][:, d * 512:(d + 1) * 512], ALU.add, [f_, C["lb"]], [f_])
            self.act(lf[:], f_[:], AF.Ln, [f_], [lf])
            self.ts("pool", kd[:], f_[:], -1.0, 1.0, ALU.mult, ALU.add, [f_], [kd])
            hi, lo = X["hilo"]
            self.cp("dve", hi[:], lf[:], [lf], [hi])
            self.tt("dve", lo[:], lf[:], hi[:], ALU.subtract, [lf, hi], [lo])
            self.mm(LA[:], C["tri16"][d][:], hi[:], True, False, [C["tri16"][d], hi], [LA])
            self.mm(LA[:], C["tri16"][d][:], lo[:], False, True, [C["tri16"][d], lo], [LA])
            self.ts("dve", csc[:], LA[:], -80.0, None, ALU.max, ALU.bypass, [LA], [csc])
            self.act(e1[:], csc[:], AF.Exp, [csc], [e1])
            self.act(e2[:], csc[:], AF.Exp, [csc], [e2], scale=-1.0)
            self.tt("dve", ops[:, (6 * d + 4) * 512:(6 * d + 5) * 512], qs[:], e1[:], ALU.mult, [qs, e1], [ops])
            self.tt("dve", ops[:, (6 * d + 5) * 512:(6 * d + 6) * 512], kd[:], e2[:], ALU.mult, [kd, e2], [ops])
            for h in range(4):
                c = (d * 4 + h) * 2
                self.mm(LB[:, c:c + 2], lf[:, h * 128:(h + 1) * 128], C["cind"][:], True, True, [lf, C["cind"]], [LB])
        self.act(dcs[:, 0:16].rearrange("p (c x) -> p c x", c=2), LB[:, 0:16].rearrange("p (x c) -> p c x", c=2),
                 AF.Exp, [LB], [dcs])
        self.dma("pool", R["ECs"][g].rearrange("c p x -> p c x"), dcs[:, 0:16].rearrange("p (c x) -> p c x", c=2),
                 [dcs], [self.bR["ECs"]])
        o3 = ops[:, :].rearrange("p (s c) -> p s c", s=14)
        OP3 = OPSd[t0:t0 + 128, :].rearrange("t (s c) -> t s c", s=14)
        self.dma("pool", OP3[:, 4:6, :], o3[:, 4:6, :], [ops], [self.bR["OPS"]])
        self.dma("pool", OP3[:, 10:12, :], o3[:, 10:12, :], [ops], [self.bR["OPS"]])
        self.dma("pool", OP3[:, 13:14, :], o3[:, 13:14, :], [ops], [self.bR["OPS"]])
        self.dma("pool", CXd[t0:t0 + 128, 1024:1536], cxs[:, 1024:1536], [cxs], [self.bR["CX"]])

    def A_late_R(self, X, g):
        C, R = self.C, self.R
        OPSd, CXd = R["OPS"], R["CX"]
        prow = C["prow"]
        kk_bc, ka_bc, rk_bc = prow[:, 512:1024], prow[:, 1024:1536], prow[:, 1536:2048]
        b = g % 2
        t0 = g * 128
        ops, cxs, dcs = X["OPst"][b], X["CXst"][b], X["dcs"][b]
        r_, k_, v_ = X["rkv"][b]
        lt, lw, lg, W2x, G2 = X["lt"][b], X["lw"], X["lg"], X["W2x"], X["G2"]
        ft, sm = X["ft"], X["sm"]
        LA, LB, LC = X["LA"], X["LB"], X["LC"]
        identf = C["identf"]
        for i in range(3):
            self.tr(LC[0:32, i * 128:(i + 1) * 128], lt[:, i * 32:(i + 1) * 32], identf[:], [lt, identf], [LC])
        self.tr(LC[0:96, 384:512], lt[:, 96:192], identf[:], [lt, identf], [LC])
        self.cp("dve", lw[0:32, :], LC[0:32, 0:384], [LC], [lw])
        self.cp("dve", lg[:, :], LC[0:96, 384:512], [LC], [lg])
        a_, sgf, sgb = ft[0], X["sg"][0], X["sg"][1]
        self.mm(LA[:], lw[0:32, 256:384], W2x[0:32, 1024:1536], True, True, [lw, W2x], [LA])
        self.mm(LB[:], lw[0:32, 128:256], W2x[0:32, 512:1024], True, True, [lw, W2x], [LB])
        self.mm(LC[:], lg[:, :], G2[:, :], True, True, [lg, G2], [LC])
        self.tt("dve", a_[:], LA[:], prow[:, 4096:4608], ALU.add, [LA, prow], [a_])
        self.act(a_[:], a_[:], AF.Sigmoid, [a_], [a_])
        self.mm(LA[:], lw[0:32, 0:128], W2x[0:32, 0:512], True, True, [lw, W2x], [LA])
        self.tt("dve", sgb[:], LB[:], prow[:, 3584:4096], ALU.add, [LB, prow], [sgb])
        self.act(sgb[:], sgb[:], AF.Sigmoid, [sgb], [sgb])
        self.act(cxs[:, 0:512], LC[:], AF.Copy, [LC], [cxs])
        self.tt("dve", sgf[:], LA[:], prow[:, 3072:3584], ALU.add, [LA, prow], [sgf])
        self.act(sgf[:], sgf[:], AF.Sigmoid, [sgf], [sgf])
        kx, sq, kk, t1, kp, nb, t2 = ft[1], ft[2], ft[3], ft[4], ft[5], ft[6], ft[4]
        self.tt("dve", kx[:], k_[:], kk_bc, ALU.mult, [k_, prow], [kx])
        self.act(sq[:], kx[:], AF.Square, [kx], [sq])
        self.red(sm[0][:], sq[:].rearrange("p (h j) -> p h j", h=8), [sq], [sm[0]])
        self.act(sm[1][:], sm[0][:], AF.Ln, [sm[0]], [sm[1]], bias=1e-12)
        self.act(sm[1][:], sm[1][:], AF.Exp, [sm[1]], [sm[1]], scale=-0.5)
        self.tt("dve", kk[:].rearrange("p (h j) -> p h j", h=8), kx[:].rearrange("p (h j) -> p h j", h=8),
                sm[1][:].unsqueeze(2).broadcast_to([128, 8, 64]), ALU.mult, [kx, sm[1]], [kk])
        self.tt("dve", t1[:], a_[:], ka_bc, ALU.mult, [a_, prow], [t1])
        self.tt("dve", t1[:], t1[:], C["omka"][:], ALU.add, [t1, C["omka"]], [t1])
        self.tt("dve", kp[:], k_[:], t1[:], ALU.mult, [k_, t1], [kp])
        self.stt(nb[:], kk[:], -1.0, a_[:], ALU.mult, ALU.mult, [kk, a_], [nb])
        self.tt("dve", t2[:], r_[:], kp[:], ALU.mult, [r_, kp], [t2])
        self.tt("dve", t2[:], t2[:], rk_bc, ALU.mult, [t2, prow], [t2])
        self.red(sm[2][:], t2[:].rearrange("p (h j) -> p h j", h=8), [t2], [sm[2]])
        self.tt("dve", cxs[:, 512:1024].rearrange("p (h j) -> p h j", h=8), v_[:].rearrange("p (h j) -> p h j", h=8),
                sm[2][:].unsqueeze(2).broadcast_to([128, 8, 64]), ALU.mult, [v_, sm[2]], [cxs])
        self.cp("act", ops[:, 12 * 512:13 * 512], v_[:], [v_], [ops])
        for d in range(2):
            sg_ = (sgf, sgb)[d]
            pc = (LA, LB)[d]
            gi, ginv, tmp, ge = ft[7], ft[8], ft[2], ft[1]
            self.mm(pc[:], C["tri"][d][:], sg_[:], True, True, [C["tri"][d], sg_], [pc])
            self.act(gi[:], pc[:], AF.Exp, [pc], [gi], scale=-DECAY)
            self.act(ginv[:], pc[:], AF.Exp, [pc], [ginv], scale=DECAY)
            self.tt("dve", tmp[:], pc[:], sg_[:], ALU.subtract, [pc, sg_], [tmp])
            self.act(ge[:], tmp[:], AF.Exp, [tmp], [ge], scale=-DECAY)
            b0 = 6 * d * 512
            self.tt("dve", ops[:, b0:b0 + 512], r_[:], gi[:], ALU.mult, [r_, gi], [ops])
            self.tt("dve", ops[:, b0 + 512:b0 + 1024], kp[:], ginv[:], ALU.mult, [kp, ginv], [ops])
            self.tt("dve", ops[:, b0 + 1024:b0 + 1536], nb[:], ginv[:], ALU.mult, [nb, ginv], [ops])
            self.tt("dve", ops[:, b0 + 1536:b0 + 2048], kk[:], ge[:], ALU.mult, [kk, ge], [ops])
            for h in range(8):
                c = (d * 8 + h) * 2
                self.mm(LC[0:64, c:c + 2], sg_[:, h * 64:(h + 1) * 64], C["cind"][:], True, True, [sg_, C["cind"]], [LC])
        self.act(dcs[0:64, 0:32].rearrange("p (c x) -> p c x", c=2), LC[0:64, 0:32].rearrange("p (x c) -> p c x", c=2),
                 AF.Exp, [LC], [dcs], scale=-DECAY)
        self.dma("pool", R["GCs"][g].rearrange("c p x -> p c x"), dcs[0:64, 0:32].rearrange("p (c x) -> p c x", c=2),
                 [dcs], [self.bR["GCs"]])
        o3 = ops[:, :].rearrange("p (s c) -> p s c", s=14)
        OP3 = OPSd[t0:t0 + 128, :].rearrange("t (s c) -> t s c", s=14)
        self.dma("pool", OP3[:, 0:4, :], o3[:, 0:4, :], [ops], [self.bR["OPS"]])
        self.dma("pool", OP3[:, 6:10, :], o3[:, 6:10, :], [ops], [self.bR["OPS"]])
        self.dma("pool", OP3[:, 12:13, :], o3[:, 12:13, :], [ops], [self.bR["OPS"]])
        self.dma("pool", CXd[t0:t0 + 128, 0:1024], cxs[:, 0:1024], [cxs], [self.bR["CX"]])

    def scan(self):
        C, R, I, O = self.C, self.R, self.I, self.O
        ident, M1, M3, MI = C["ident"], C["M1"], C["M3"], C["MI"]
        OPSd = R["OPS"]
        with self.scope() as st:
            OP = [self.sb(st, f"OP{i}", [128, 8 * 512], BF16) for i in range(2)]
            gCt = [self.sb(st, f"gCt{i}", [128, 8], F32) for i in range(2)]
            eCt = [self.sb(st, f"eCt{i}", [128, 8], F32) for i in range(2)]
            XT = self.sb(st, "XT", [128, 8 * 256], BF16)
            GS = self.sb(st, "GS", [128, 8 * 320], BF16)
            XA = [self.sb(st, f"XA{i}", [128, 8 * 128], BF16) for i in range(2)]
            NN = [self.sb(st, f"NN{i}", [128, 8 * 128], BF16) for i in range(2)]
            GT = self.sb(st, "GT", [128, 512], BF16)
            Pb = self.sb(st, "Pb", [128, 512], BF16)
            T32 = self.sb(st, "T32", [128, 512], F32)
            Tb = [self.sb(st, f"Tb{i}", [128, 512], BF16) for i in range(2)]
            Ttmp = self.sb(st, "Ttmp", [128, 512], F32)
            Ysb = [self.sb(st, f"Ysb{i}", [128, 512], F32) for i in range(2)]
            HT = self.sb(st, "HT", [128, 1024], BF16)
            SCz = self.sb(st, "SCz", [128, 4 * 2 * 64], BF16)
            Osb = [self.sb(st, f"Osb{i}", [128, 512], F32) for i in range(2)]
            S32 = self.sb(st, "S32", [128, 1024], F32)
            Sb = [self.sb(st, f"Sb{i}", [128, 1024], BF16) for i in range(2)]
            Stmp = self.sb(st, "Stmp", [128, 512], F32)
            BT0 = self.ps(st, "BT0", [128, 1024], BF16)
            BT1 = self.ps(st, "BT1", [128, 1024], BF16)
            PX = self.ps(st, "PX", [128, 1024], F32)
            PN = self.ps(st, "PN", [128, 1024], F32)
            F4 = self.ps(st, "F4", [128, 512], F32)
            F5 = self.ps(st, "F5", [128, 512], F32)
            self.memset("pool", SCz[:], 0.0, [SCz])
            M1b = M1[:].unsqueeze(1).broadcast_to([128, 8, 128])
            M3b = M3[:].unsqueeze(1).broadcast_to([128, 8, 64])
            stepno = 0
            for si, (s0, sl, kind) in enumerate(self.seqs):
                nch = sl // 64
                if kind == 0:
                    self.memset("pool", T32[:], 0.0, [T32])
                    self.memset("pool", S32[:], 0.0, [S32])
                else:
                    for l in range(2):
                        self.dma("sp", T32[64 * l:64 * l + 64, :], I["st_r"][l], [], [T32])
                        self.dma("sp", S32[:, l * 512:(l + 1) * 512], I["st_h"][l], [], [S32])
                cur = 0
                self.cp("act", Tb[cur][:], T32[:], [T32], [Tb[cur]])
                self.cp("act", Sb[cur][:], S32[:], [S32], [Sb[cur]])
                for i in range(nch):
                    op, gc, ec = OP[stepno % 2], gCt[stepno % 2], eCt[stepno % 2]
                    ysb, osb = Ysb[stepno % 2], Osb[stepno % 2]
                    stepno += 1
                    tok = [s0 + 64 * i, s0 + 64 * (nch - 1 - i)]
                    o3 = op[:, :].rearrange("p (s c) -> p s c", s=8)
                    for l in range(2):
                        src = OPSd[tok[l]:tok[l] + 64, :].rearrange("t (s c) -> t s c", s=14)
                        self.dma("sp", o3[64 * l:64 * l + 64, 0:6, :], src[:, 6 * l:6 * l + 6, :], [self.bR["OPS"]], [op])
                        self.dma("sp", o3[64 * l:64 * l + 64, 6:8, :], src[:, 12:14, :], [self.bR["OPS"]], [op])
                        gg, cc = tok[l] // 128, (tok[l] % 128) // 64
                        self.dma("sp", gc[64 * l:64 * l + 64, :], R["GCs"][gg, cc][:, 8 * l:8 * l + 8], [self.bR["GCs"]], [gc])
                        self.dma("sp", ec[:, 4 * l:4 * l + 4], R["ECs"][gg, cc][:, 4 * l:4 * l + 4], [self.bR["ECs"]], [ec])
                    for h in range(8):
                        bt = BT0 if h < 4 else BT1
                        for X, slot in enumerate((3, 0, 2, 1)):
                            for l in range(2):
                                r0 = 64 * l
                                self.tr(bt[r0:r0 + 64, (h % 4) * 256 + X * 64:(h % 4) * 256 + X * 64 + 64],
                                        op[r0:r0 + 64, slot * 512 + h * 64:slot * 512 + h * 64 + 64],
                                        ident[r0:r0 + 64, r0:r0 + 64], [op, ident], [bt])
                    self.cp("dve", XT[:, 0:1024], BT0[:, :], [BT0], [XT])
                    self.cp("act", XT[:, 1024:2048], BT1[:, :], [BT1], [XT])
                    for h in range(8):
                        for l in range(2):
                            r0 = 64 * l
                            xb = h * 256
                            self.mm(PX[r0:r0 + 64, h * 128:h * 128 + 128], XT[r0:r0 + 64, xb + 128:xb + 192],
                                    XT[r0:r0 + 64, xb:xb + 128], True, True, [XT], [PX])
                            self.mm(PN[r0:r0 + 64, h * 128:h * 128 + 128], XT[r0:r0 + 64, xb + 192:xb + 256],
                                    XT[r0:r0 + 64, xb:xb + 128], True, True, [XT], [PN])
                            self.mm(F4[r0:r0 + 64, h * 64:h * 64 + 64], XT[r0:r0 + 64, xb:xb + 64],
                                    XT[r0:r0 + 64, xb + 128:xb + 192], True, True, [XT], [F4])
                    G3 = GS[:, :].rearrange("p (h c) -> p h c", h=8)
                    self.tt("dve", G3[:, :, 0:128], PX[:, :].rearrange("p (h c) -> p h c", h=8), M1b, ALU.mult, [PX, M1], [GS])
                    self.tt("dve", G3[:, :, 128:256], PN[:, :].rearrange("p (h c) -> p h c", h=8), M1b, ALU.mult, [PN, M1], [GS])
                    self.tt("dve", G3[:, :, 256:320], F4[:, :].rearrange("p (h c) -> p h c", h=8), M3b, ALU.mult, [F4, M3], [GS])
                    xa = XA[0]
                    xa3 = xa[:, :].rearrange("p (h c) -> p h c", h=8)
                    self.cp("dve", xa3[:, :, 0:64], op[:, 3 * 512:4 * 512].rearrange("p (h c) -> p h c", h=8), [op], [xa])
                    for h in range(8):
                        for l in range(2):
                            r0 = 64 * l
                            self.mm(F5[r0:r0 + 64, h * 64:h * 64 + 64], GS[r0:r0 + 64, h * 320 + 128:h * 320 + 192],
                                    op[r0:r0 + 64, 6 * 512 + h * 64:6 * 512 + h * 64 + 64], True, True, [GS, op], [F5])
                    self.cp("act", xa3[:, :, 64:128], F5[:, :].rearrange("p (h c) -> p h c", h=8), [F5], [xa])
                    xc = 0
                    for k in range(6):
                        xcur, xnext = XA[xc], XA[1 - xc]
                        if k == 0:
                            nsrc = GS
                            noff = lambda h: h * 320 + 256
                            ntoff = lambda h: h * 320
                        else:
                            nsrc = NN[(k - 1) % 2]
                            noff = lambda h: h * 128
                            ntoff = lambda h: h * 128 + 64
                        if k < 5:
                            nn = NN[k % 2]
                            for h in range(8):
                                for l in range(2):
                                    r0 = 64 * l
                                    self.mm(PN[r0:r0 + 64, h * 128:h * 128 + 64], nsrc[r0:r0 + 64, ntoff(h):ntoff(h) + 64],
                                            nsrc[r0:r0 + 64, noff(h):noff(h) + 64], True, True, [nsrc], [PN])
                                    self.mm(PN[r0:r0 + 64, h * 128 + 64:h * 128 + 128], nsrc[r0:r0 + 64, noff(h):noff(h) + 64],
                                            nsrc[r0:r0 + 64, ntoff(h):ntoff(h) + 64], True, True, [nsrc], [PN])
                        if k < 5:
                            self.cp("act", nn[:, :], PN[:, :], [PN], [nn])
                        for h in range(8):
                            for l in range(2):
                                r0 = 64 * l
                                self.mm(PX[r0:r0 + 64, h * 128:h * 128 + 128], nsrc[r0:r0 + 64, ntoff(h):ntoff(h) + 64],
                                        xcur[r0:r0 + 64, h * 128:h * 128 + 128], True, True, [nsrc, xcur], [PX])
                        self.tt("dve", xnext[:, :], PX[:, :], xcur[:, :], ALU.add, [PX, xcur], [xnext])
                        xc = 1 - xc
                    x6 = XA[xc]
                    x63 = x6[:, :].rearrange("p (h c) -> p h c", h=8)
                    for h in range(8):
                        for l in range(2):
                            r0 = 64 * l
                            self.tr(BT1[r0:r0 + 64, h * 64:h * 64 + 64], x6[r0:r0 + 64, h * 128:h * 128 + 64],
                                    ident[r0:r0 + 64, r0:r0 + 64], [x6, ident], [BT1])
                    self.cp("dve", GT[:, :], BT1[:, 0:512], [BT1], [GT])
                    tbc, tbn = Tb[cur], Tb[1 - cur]
                    for h in range(8):
                        for l in range(2):
                            r0 = 64 * l
                            self.mm(F4[r0:r0 + 64, h * 64:h * 64 + 64], GT[r0:r0 + 64, h * 64:h * 64 + 64],
                                    tbc[r0:r0 + 64, h * 64:h * 64 + 64], True, True, [GT, tbc], [F4])
                    self.tt("dve", Pb[:, :].rearrange("p (h c) -> p h c", h=8), F4[:, :].rearrange("p (h c) -> p h c", h=8),
                            x63[:, :, 64:128], ALU.add, [F4, x6], [Pb])
                    for h in range(8):
                        for l in range(2):
                            r0 = 64 * l
                            yo = PN[r0:r0 + 64, h * 64:h * 64 + 64]
                            self.mm(yo, XT[r0:r0 + 64, h * 256 + 64:h * 256 + 128], tbc[r0:r0 + 64, h * 64:h * 64 + 64],
                                    True, False, [XT, tbc], [PN])
                            self.mm(yo, GS[r0:r0 + 64, h * 320 + 192:h * 320 + 256],
                                    op[r0:r0 + 64, 6 * 512 + h * 64:6 * 512 + h * 64 + 64], False, False, [GS, op], [PN])
                            self.mm(yo, GS[r0:r0 + 64, h * 320 + 64:h * 320 + 128], Pb[r0:r0 + 64, h * 64:h * 64 + 64],
                                    False, True, [GS, Pb], [PN])
                    self.cp("act", ysb[:, :], PN[:, 0:512], [PN], [ysb])
                    for l in range(2):
                        self.dma("pool", R["YS"][l, tok[l]:tok[l] + 64, :], ysb[64 * l:64 * l + 64, :], [ysb], [self.bR["YS"]])
                    for h in range(8):
                        for l in range(2):
                            r0 = 64 * l
                            to = F5[r0:r0 + 64, h * 64:h * 64 + 64]
                            self.mm(to, op[r0:r0 + 64, 1 * 512 + h * 64:1 * 512 + h * 64 + 64],
                                    op[r0:r0 + 64, 6 * 512 + h * 64:6 * 512 + h * 64 + 64], True, False, [op], [F5])
                            self.mm(to, op[r0:r0 + 64, 2 * 512 + h * 64:2 * 512 + h * 64 + 64],
                                    Pb[r0:r0 + 64, h * 64:h * 64 + 64], False, True, [op, Pb], [F5])
                    self.tt("dve", Ttmp[:, :], F5[:, :], T32[:, :], ALU.add, [F5, T32], [Ttmp])
                    self.tt("dve", T32[:, :].rearrange("p (h c) -> p h c", h=8), Ttmp[:, :].rearrange("p (h c) -> p h c", h=8),
                            gc[:, :].unsqueeze(2).broadcast_to([128, 8, 64]), ALU.mult, [Ttmp, gc], [T32])
                    self.cp("act", tbn[:, :], T32[:, :], [T32], [tbn])
                    sbc, sbn = Sb[cur], Sb[1 - cur]
                    for X, slot in enumerate((4, 5)):
                        for h in range(4):
                            c0 = (X * 4 + h) * 128
                            self.tr(BT0[:, c0:c0 + 128], op[:, slot * 512 + h * 128:slot * 512 + h * 128 + 128], ident[:, :],
                                    [op, ident], [BT0])
                    self.cp("dve", HT[:, :], BT0[:, :], [BT0], [HT])
                    for h in range(4):
                        for l in range(2):
                            r0 = 64 * l
                            self.mm(PX[r0:r0 + 64, h * 64:h * 64 + 64], HT[:, (4 + h) * 128 + r0:(4 + h) * 128 + r0 + 64],
                                    HT[:, h * 128 + r0:h * 128 + r0 + 64], True, True, [HT], [PX])
                    for l in range(2):
                        r0 = 64 * l
                        sc4 = SCz[r0:r0 + 64, :].rearrange("p (h l t) -> p h l t", h=4, l=2)
                        self.tt("dve", sc4[:, :, l, :], PX[r0:r0 + 64, 0:256].rearrange("p (h t) -> p h t", h=4),
                                MI[r0:r0 + 64, :].unsqueeze(1).broadcast_to([64, 4, 64]), ALU.mult, [PX, MI], [SCz])
                    for h in range(4):
                        for l in range(2):
                            r0 = 64 * l
                            oo = PX[r0:r0 + 64, 512 + h * 128:512 + h * 128 + 128]
                            self.mm(oo, SCz[:, (h * 2 + l) * 64:(h * 2 + l) * 64 + 64], op[:, 7 * 512 + h * 128:7 * 512 + h * 128 + 128],
                                    True, False, [SCz, op], [PX])
                            self.mm(oo, HT[:, h * 128 + r0:h * 128 + r0 + 64], sbc[:, (l * 4 + h) * 128:(l * 4 + h) * 128 + 128],
                                    False, True, [HT, sbc], [PX])
                    self.cp("act", osb[:, :], PX[:, 512:1024], [PX], [osb])
                    for l in range(2):
                        self.dma("pool", R["OSC"][l, tok[l]:tok[l] + 64, :], osb[64 * l:64 * l + 64, :], [osb], [self.bR["OSC"]])
                    for l in range(2):
                        r0 = 64 * l
                        pu = PN if l == 0 else F4
                        pu0 = 512 if l == 0 else 0
                        for h in range(4):
                            self.mm(pu[:, pu0 + h * 128:pu0 + h * 128 + 128], op[r0:r0 + 64, 5 * 512 + h * 128:5 * 512 + h * 128 + 128],
                                    op[r0:r0 + 64, 7 * 512 + h * 128:7 * 512 + h * 128 + 128], True, True, [op], [pu])
                        self.tt("dve", Stmp[:, :], pu[:, pu0:pu0 + 512], S32[:, l * 512:(l + 1) * 512], ALU.add, [pu, S32], [Stmp])
                        self.tt("dve", S32[:, l * 512:(l + 1) * 512].rearrange("p (h c) -> p h c", h=4),
                                Stmp[:, :].rearrange("p (h c) -> p h c", h=4),
                                ec[:, 4 * l:4 * l + 4].unsqueeze(2).broadcast_to([128, 4, 128]), ALU.mult, [Stmp, ec], [S32])
                    self.cp("act", sbn[:, :], S32[:, :], [S32], [sbn])
                    cur = 1 - cur
                if kind == 0:
                    for l in range(2):
                        self.dma("pool", O["nsr"][si, l], T32[64 * l:64 * l + 64, :], [T32], [self.bO["nsr"]])
                        self.dma("pool", O["nsh"][si, l], S32[:, l * 512:(l + 1) * 512], [S32], [self.bO["nsh"]])

    def ctx_tiles(self):
        out = []
        per = max(1, 512 // self.L_ctx)
        si = 0
        while si < self.n_ctx:
            ns = min(per, self.n_ctx - si)
            out.append((si * self.L_ctx, ns * self.L_ctx))
            si += ns
        return out

    def phaseC1(self):
        C, R, I = self.C, self.R, self.I
        ident, prow, mods, selv = C["ident"], C["prow"], C["mods"], C["selv"]
        hnw_bc, lnw_bc, lnb_bc = prow[:, 0:512], prow[:, 2048:2560], prow[:, 2560:3072]
        xTv = I["xT"].rearrange("(k p) t -> p k t", p=128)
        xov = I["xTo"].rearrange("(k p) t -> p k t", p=128)
        x1v = R["X1T"].rearrange("(k p) t -> p k t", p=128)
        x1wv = R["X1W"].rearrange("(k p) t -> p k t", p=128)
        NCT = self.n_ctx * self.L_ctx
        QL, NQ = self.QL, self.NQ
        WL = QL + 128
        with self.scope() as st:
            WO = self.sb(st, "WO", [128, 8 * D], BF16)
            wv = I["w_out"].rearrange("(k p) c -> p k c", p=128)
            for k in range(KD):
                self.dma("pool", WO[:, k * D:(k + 1) * D], wv[:, k, :], [], [WO])
            yf = [self.sb(st, f"yf{i}", [128, 512], F32) for i in range(2)]
            yb = [self.sb(st, f"yb{i}", [128, 512], F32) for i in range(2)]
            of = [self.sb(st, f"of{i}", [128, 512], F32) for i in range(2)]
            ob = [self.sb(st, f"ob{i}", [128, 512], F32) for i in range(2)]
            cx = [self.sb(st, f"cx{i}", [128, 1536], BF16) for i in range(2)]
            cxa = self.sb(st, "cxa", [128, 1536], F32)
            ft = [self.sb(st, f"c1f{i}", [128, 512], F32) for i in range(4)]
            sm = [self.sb(st, f"c1s{i}", [128, 8], F32) for i in range(6)]
            MIX = self.sb(st, "MIX", [128, 1024], BF16)
            MIXT = self.sb(st, "MIXT", [128, 8 * 512], BF16)
            xt = self.sb(st, "c1xt", [128, 8 * 512], F32)
            x1 = self.sb(st, "c1x1", [128, 8 * 512], F32)
            PT = self.ps(st, "c1PT", [128, 1024], BF16)
            PO = [self.ps(st, f"c1PO{i}", [128, 512], F32) for i in range(2)]
            self._c1i = 0

            def post(y, o, cxt, gq):
                sq, osq = ft[1], ft[3]
                self.act(sq[:, :], y[:, :], AF.Square, [y], [sq])
                y3 = y[:, :].rearrange("p (h j) -> p h j", h=8)
                self.red(sm[0][:, :], y3, [y], [sm[0]])
                self.red(sm[1][:, :], sq[:, :].rearrange("p (h j) -> p h j", h=8), [sq], [sm[1]])
                self.ts("dve", sm[2][:, :], sm[0][:, :], 1.0 / 64, None, ALU.mult, ALU.bypass, [sm[0]], [sm[2]])
                self.tt("dve", sm[3][:, :], sm[2][:, :], sm[2][:, :], ALU.mult, [sm[2]], [sm[3]])
                self.stt(sm[4][:, :], sm[1][:, :], 1.0 / 64, sm[3][:, :], ALU.mult, ALU.subtract, [sm[1], sm[3]], [sm[4]])
                self.act(sm[5][:, :], sm[4][:, :], AF.Sqrt, [sm[4]], [sm[5]], bias=GN_EPS)
                self.recip(sm[5][:, :], sm[5][:, :], [sm[5]], [sm[5]])
                self.tt("dve", y3, y3, sm[2][:, :].unsqueeze(2).broadcast_to([128, 8, 64]), ALU.subtract, [y, sm[2]], [y])
                self.tt("dve", y3, y3, sm[5][:, :].unsqueeze(2).broadcast_to([128, 8, 64]), ALU.mult, [y, sm[5]], [y])
                self.tt("dve", y[:, :], y[:, :], lnw_bc, ALU.mult, [y, prow], [y])
                self.tt("dve", y[:, :], y[:, :], lnb_bc, ALU.add, [y, prow], [y])
                self.tt("dve", y[:, :], y[:, :], cxt[:, 512:1024], ALU.add, [y, cxt], [y])
                self.tt("dve", MIX[:, 512:1024], y[:, :], cxt[:, 0:512], ALU.mult, [y, cxt], [MIX])
                self.act(osq[:, :], o[:, :], AF.Square, [o], [osq])
                self.red(sm[0][:, 0:4], osq[:, :].rearrange("p (h j) -> p h j", h=4), [osq], [sm[0]])
                self.act(sm[1][:, 0:4], sm[0][:, 0:4], AF.Sqrt, [sm[0]], [sm[1]], scale=1.0 / 128, bias=RMS_EPS)
                self.recip(sm[1][:, 0:4], sm[1][:, 0:4], [sm[1]], [sm[1]])
                o3 = o[:, :].rearrange("p (h j) -> p h j", h=4)
                self.tt("dve", o3, o3, sm[1][:, 0:4].unsqueeze(2).broadcast_to([128, 4, 128]), ALU.mult, [o, sm[1]], [o])
                self.tt("dve", o[:, :], o[:, :], hnw_bc, ALU.mult, [o, prow], [o])
                self.tt("dve", MIX[:, 0:512], o[:, :], cxt[:, 1024:1536], ALU.mult, [o, cxt], [MIX])
                for k in range(KD):
                    self.tr(PT[:, k * 128:(k + 1) * 128], MIX[:, k * 128:(k + 1) * 128], ident[:, :], [MIX, ident], [PT])
                self.cp("act", MIXT[:, :].rearrange("p (k t) -> p k t", k=8)[:, :, gq * 128:(gq + 1) * 128],
                        PT[:, :].rearrange("p (k t) -> p k t", k=8), [PT], [MIXT])

            def dense(n, kind, dst):
                for m in range(KD):
                    po = PO[m % 2]
                    for k in range(KD):
                        self.mm(po[:, 0:n], WO[:, k * D + m * 128:k * D + (m + 1) * 128], MIXT[:, k * 512:k * 512 + n],
                                k == 0, k == KD - 1, [WO, MIXT], [po])
                    self.stt(x1[:, m * n:(m + 1) * n], po[:, 0:n], mods[:, 2, kind, m:m + 1], xt[:, m * n:(m + 1) * n],
                             ALU.mult, ALU.add, [po, mods, xt], [x1])
                self.dma("pool", dst, x1[:, 0:8 * n].rearrange("p (k t) -> p k t", k=8), [x1], [])

            gi = 0
            for (t0, n) in self.ctx_tiles():
                self.dma("sp", xt[:, 0:8 * n].rearrange("p (k t) -> p k t", k=8), xTv[:, :, t0:t0 + n], [], [xt])
                for gq in range(n // 128):
                    ta = t0 + gq * 128
                    b = gi % 2
                    gi += 1
                    self.dma("sp", yf[b][:, :], R["YS"][0, ta:ta + 128, :], [], [yf[b]])
                    self.dma("sp", yb[b][:, :], R["YS"][1, ta:ta + 128, :], [], [yb[b]])
                    self.dma("sp", of[b][:, :], R["OSC"][0, ta:ta + 128, :], [], [of[b]])
                    self.dma("sp", ob[b][:, :], R["OSC"][1, ta:ta + 128, :], [], [ob[b]])
                    self.dma("sp", cx[b][:, :], R["CX"][ta:ta + 128, :], [], [cx[b]])
                    y, o = ft[0], ft[2]
                    self.tt("dve", y[:, :], yf[b][:, :], yb[b][:, :], ALU.add, [yf[b], yb[b]], [y])
                    self.tt("dve", o[:, :], of[b][:, :], ob[b][:, :], ALU.add, [of[b], ob[b]], [o])
                    post(y, o, cx[b], gq)
                dense(n, 0, x1v[:, :, t0:t0 + n])
            w0 = 0
            while w0 < WL:
                n = min(512, WL - w0)
                self.dma("sp", xt[:, 0:8 * n].rearrange("p (k t) -> p k t", k=8), xov[:, :, w0:w0 + n], [], [xt])
                for gq in range(n // 128):
                    wj = w0 + gq * 128
                    y, o = ft[0], ft[2]
                    for q in range(NQ):
                        b = gi % 2
                        gi += 1
                        ta = NCT + q * QL - 64 + wj
                        lo, hi = max(ta, NCT), min(ta + 128, NCT + self.L_lat)
                        p0, pn = lo - ta, hi - lo
                        if pn < 128:
                            for tl in (yf[b], yb[b], of[b], ob[b], cx[b]):
                                self.memset("dve", tl[:, :], 0.0, [tl])
                        if pn > 0:
                            self.dma("sp", yf[b][p0:p0 + pn, :], R["YS"][0, lo:hi, :], [], [yf[b]])
                            self.dma("sp", yb[b][p0:p0 + pn, :], R["YS"][1, lo:hi, :], [], [yb[b]])
                            self.dma("sp", of[b][p0:p0 + pn, :], R["OSC"][0, lo:hi, :], [], [of[b]])
                            self.dma("sp", ob[b][p0:p0 + pn, :], R["OSC"][1, lo:hi, :], [], [ob[b]])
                            self.dma("sp", cx[b][p0:p0 + pn, :], R["CX"][lo:hi, :], [], [cx[b]])
                        sc = selv[:, q:q + 1]
                        t_y, t_o = ft[1], ft[3]
                        self.tt("dve", t_y[:, :], yf[b][:, :], yb[b][:, :], ALU.add, [yf[b], yb[b]], [t_y])
                        self.tt("dve", t_o[:, :], of[b][:, :], ob[b][:, :], ALU.add, [of[b], ob[b]], [t_o])
                        if q == 0:
                            self.ts("dve", y[:, :], t_y[:, :], sc, None, ALU.mult, ALU.bypass, [t_y, selv], [y])
                            self.ts("dve", o[:, :], t_o[:, :], sc, None, ALU.mult, ALU.bypass, [t_o, selv], [o])
                            self.ts("dve", cxa[:, :], cx[b][:, :], sc, None, ALU.mult, ALU.bypass, [cx[b], selv], [cxa])
                        else:
                            self.stt(y[:, :], t_y[:, :], sc, y[:, :], ALU.mult, ALU.add, [t_y, selv, y], [y])
                            self.stt(o[:, :], t_o[:, :], sc, o[:, :], ALU.mult, ALU.add, [t_o, selv, o], [o])
                            self.stt(cxa[:, :], cx[b][:, :], sc, cxa[:, :], ALU.mult, ALU.add, [cx[b], selv, cxa], [cxa])
                    post(y, o, cxa, gq)
                dense(n, 1, x1wv[:, :, w0:w0 + n])
                w0 += n

    def phaseC2(self):
        C, R, I, O = self.C, self.R, self.I, self.O
        mods, fnw, selv = C["mods"], C["fnw"], C["selv"]
        x1v = R["X1T"].rearrange("(k p) t -> p k t", p=128)
        x1wv = R["X1W"].rearrange("(k p) t -> p k t", p=128)
        yTv = O["yT"].rearrange("(k p) t -> p k t", p=128)
        NWM = 640
        NCT = self.n_ctx * self.L_ctx
        QL, NQ = self.QL, self.NQ
        with self.scope() as st:
            WG = self.sb(st, "WG", [128, 8 * DFF], BF16)
            WU = self.sb(st, "WU", [128, 8 * DFF], BF16)
            for (W, nm) in ((WG, "w_gate"), (WU, "w_up")):
                wv = I[nm].rearrange("(k p) c -> p k c", p=128)
                for k in range(KD):
                    self.dma("pool", W[:, k * DFF:(k + 1) * DFF], wv[:, k, :], [], [W])
            cw = self.sb(st, "cw", [128, NF * 9], F32)
            self.dma("sp", cw[:, :], I["convw"], [], [cw])
            cvb = self.sb(st, "cvb", [128, NF], F32)
            self.dma("sp", cvb[:, :], I["cvb"], [], [cvb])
            wd = [self.sb(st, f"wd{i}", [128, NF * 128], BF16) for i in range(2)]
            wdv = I["w_down"].rearrange("(f p) c -> p f c", p=128)
            x1w = self.sb(st, "x1w", [128, 8 * NWM], F32)
            tiles = {
                "sq": self.sb(st, "c2sq", [128, 8 * NWM], BF16),
                "rstd": self.sb(st, "c2rstd", [128, NWM], F32),
                "ntmp": [self.sb(st, f"c2ntmp{i}", [128, NWM], F32) for i in range(2)],
            }
            h2T = self.sb(st, "h2T", [128, 8 * NWM], BF16)
            HM = self.sb(st, "HM", [128, NF * 512], BF16)
            GP = [self.sb(st, f"GP{i}", [128, 10 * 66], F32) for i in range(2)]
            GPc = [self.sb(st, f"GPc{i}", [128, 2 * 258], F32) for i in range(2)]
            acc = [self.sb(st, f"c2acc{i}", [128, 512], F32) for i in range(2)]
            gl = [self.sb(st, f"c2gl{i}", [128, 512], F32) for i in range(2)]
            x2 = self.sb(st, "x2", [128, 8 * 512], F32)
            PG = [self.ps(st, f"c2PG{i}", [128, 1024], F32) for i in range(2)]
            PU = [self.ps(st, f"c2PU{i}", [128, 512], F32) for i in range(2)]
            PD = [self.ps(st, f"c2PD{i}", [128, 512], F32) for i in range(2)]
            fi = 0
            wdi = 0
            work = [(0, t0, n, None) for (t0, n) in self.ctx_tiles()]
            for r0 in range(0, QL // 64, 8):
                work.append((1, r0, min(8, QL // 64 - r0) * 64, None))
            for (kind, t0, n, _) in work:
                if kind == 1:
                    r0 = t0
                    nwv = n + 128
                    coff = 64
                    self.dma("sp", x1w[:, 0:8 * nwv].rearrange("p (k t) -> p k t", k=8),
                             x1wv[:, :, 64 * r0:64 * r0 + nwv], [], [x1w])
                else:
                    nwv, coff = n, 0
                    self.dma("sp", x1w[:, 0:8 * nwv].rearrange("p (k t) -> p k t", k=8), x1v[:, :, t0:t0 + n], [], [x1w])
                self.norm_mod(tiles, x1w, nwv, PU[0], mods[:, 3, kind, :], mods[:, 4, kind, :], h2T, mods)
                nrow_t = n // 64
                for gpz in (GP if kind == 1 else GPc):
                    self.memset("dve", gpz[:, :], 0.0, [gpz])
                for f in range(NF):
                    pg, pu = PG[fi % 2], PU[fi % 2]
                    ac, g_ = acc[fi % 2], gl[fi % 2]
                    c0 = 0
                    while c0 < nwv:
                        cn = min(512, nwv - c0)
                        for k in range(KD):
                            self.mm(pg[:, c0:c0 + cn], WG[:, k * DFF + f * 128:k * DFF + (f + 1) * 128],
                                    h2T[:, k * nwv + c0:k * nwv + c0 + cn], k == 0, k == KD - 1, [WG, h2T], [pg])
                        c0 += cn
                    for k in range(KD):
                        self.mm(pu[:, 0:n], WU[:, k * DFF + f * 128:k * DFF + (f + 1) * 128],
                                h2T[:, k * nwv + coff:k * nwv + coff + n], k == 0, k == KD - 1, [WU, h2T], [pu])
                    if kind == 1:
                        gp = GP[fi % 2]
                        gp3 = gp[:, :].rearrange("p (r c) -> p r c", c=66)
                        nrw = nwv // 64
                        self.cp("act", gp3[:, 0:nrw, 1:65], pg[:, 0:nwv].rearrange("p (r c) -> p r c", c=64), [pg], [gp])
                        if r0 == 0:
                            self.ts("dve", gp3[:, 0, 1:65], gp3[:, 0, 1:65], selv[:, NQ:NQ + 1], None, ALU.mult, ALU.bypass,
                                    [gp, selv], [gp])
                        if r0 + nrow_t == QL // 64:
                            self.ts("dve", gp3[:, nrw - 1, 1:65], gp3[:, nrw - 1, 1:65], selv[:, NQ + 1:NQ + 2], None,
                                    ALU.mult, ALU.bypass, [gp, selv], [gp])
                        a3 = ac[:, 0:n].rearrange("p (r c) -> p r c", c=64)
                        first = True
                        for dy in range(3):
                            for dx in range(3):
                                src = gp3[:, dy:dy + nrow_t, dx:dx + 64]
                                wcol = cw[:, f * 9 + dy * 3 + dx:f * 9 + dy * 3 + dx + 1]
                                if first:
                                    self.ts("dve", a3, src, wcol, cvb[:, f:f + 1], ALU.mult, ALU.add, [gp, cw, cvb], [ac])
                                    first = False
                                else:
                                    self.stt(a3, src, wcol, a3, ALU.mult, ALU.add, [gp, cw, ac], [ac])
                    else:
                        gp = GPc[fi % 2]
                        nsq = n // self.L_ctx
                        Lc = self.L_ctx
                        gp3 = gp[:, 0:nsq * (Lc + 2)].rearrange("p (r c) -> p r c", c=Lc + 2)
                        self.cp("act", gp3[:, :, 1:Lc + 1], pg[:, 0:n].rearrange("p (r c) -> p r c", c=Lc), [pg], [gp])
                        a3 = ac[:, 0:n].rearrange("p (r c) -> p r c", c=Lc)
                        for dx in range(3):
                            src = gp3[:, :, dx:dx + Lc]
                            wcol = cw[:, f * 9 + 3 + dx:f * 9 + 3 + dx + 1]
                            if dx == 0:
                                self.ts("dve", a3, src, wcol, cvb[:, f:f + 1], ALU.mult, ALU.add, [gp, cw, cvb], [ac])
                            else:
                                self.stt(a3, src, wcol, a3, ALU.mult, ALU.add, [gp, cw, ac], [ac])
                    self.act(g_[:, 0:n], ac[:, 0:n], AF.Gelu_apprx_tanh, [ac], [g_])
                    self.tt("dve", HM[:, f * 512:f * 512 + n], pu[:, 0:n], g_[:, 0:n], ALU.mult, [pu, g_], [HM])
                    fi += 1
                for m in range(KD):
                    w_ = wd[wdi % 2]
                    pd = PD[wdi % 2]
                    wdi += 1
                    self.dma("pool", w_[:, :].rearrange("p (f c) -> p f c", c=128), wdv[:, :, m * 128:(m + 1) * 128], [], [w_])
                    for f in range(NF):
                        self.mm(pd[:, 0:n], w_[:, f * 128:(f + 1) * 128], HM[:, f * 512:f * 512 + n], f == 0, f == NF - 1,
                                [w_, HM], [pd])
                    self.stt(x2[:, m * n:(m + 1) * n], pd[:, 0:n], mods[:, 5, kind, m:m + 1],
                             x1w[:, m * nwv + coff:m * nwv + coff + n], ALU.mult, ALU.add, [pd, mods, x1w], [x2])
                rstd = tiles["rstd"]
                self.rms_rstd(tiles, x2, n, PU[1], rstd)
                for k in range(KD):
                    self.stt(x2[:, k * n:(k + 1) * n], x2[:, k * n:(k + 1) * n], fnw[:, k:k + 1], rstd[:, 0:n],
                             ALU.mult, ALU.mult, [x2, fnw, rstd], [x2])
                od = NCT + 64 * t0 if kind == 1 else t0
                self.dma("pool", yTv[:, :, od:od + n], x2[:, 0:8 * n].rearrange("p (k t) -> p k t", k=8), [x2], [])

    def build(self, phases=("0", "A", "S", "C1", "C2")):
        self.declare_io()
        with contextlib.ExitStack() as gst:
            mst = contextlib.ExitStack()
            self.phase0(gst, mst)
            if "A" in phases:
                self.phaseA()
            if "S" in phases:
                self.scan()
            if "C1" in phases:
                self.phaseC1()
            self.S.barrier()
            mst.close()
            if "C2" in phases:
                self.phaseC2()
            self.S.barrier()
            fin = list(self.final)
            self.S.finalize(fin)
        return self.nc


def fm(v):
    return np.ascontiguousarray(np.asarray(v, np.float32).reshape(-1, 128).T)


def shared_inputs(inp):
    f = lambda a: np.ascontiguousarray(np.asarray(a, np.float32))
    S = {}
    S["ada_w"] = f(inp["ada_w"][0])
    S["adab"] = fm(inp["ada_b"][0])
    S["nmw"] = fm(inp["norm_mix_w"][0])
    S["nfw"] = fm(inp["norm_ffn_w"][0])
    S["fnw"] = fm(inp["final_norm_w"])
    S["w_in"] = f(inp["w_in"][0])
    S["rwkv_conv"] = f(inp["rwkv_conv"][0])
    S["hgrn_lb"] = f(inp["hgrn_lb"]).reshape(1, 2048)
    S["prow"] = np.concatenate([f(inp[k][0]).reshape(-1) for k in
                                ("hgrn_norm_w", "rwkv_k_k", "rwkv_k_a", "rwkv_r_k", "rwkv_ln_w", "rwkv_ln_b")]
                               + [f(inp["rwkv_w0"][0, 0]), f(inp["rwkv_w0"][0, 1]), f(inp["rwkv_a0"][0])]).reshape(1, 4608)
    w2x = np.zeros((32, 1536), np.float32)
    w2x[0:32, 0:512] = inp["rwkv_w2"][0, 0]
    w2x[0:32, 512:1024] = inp["rwkv_w2"][0, 1]
    w2x[0:32, 1024:1536] = inp["rwkv_a2"][0]
    S["w2x"] = w2x
    S["g2"] = f(inp["rwkv_g2"][0])
    S["w_out"] = f(inp["w_out"][0])
    S["w_gate"] = f(inp["ffn_w_gate"][0])
    S["w_up"] = f(inp["ffn_w_up"][0])
    S["w_down"] = f(inp["ffn_w_down"][0])
    cw = f(inp["ffn_conv"][0]).reshape(9, NF, 128)
    S["convw"] = np.ascontiguousarray(cw.transpose(2, 1, 0).reshape(128, NF * 9))
    S["cvb"] = fm(inp["ffn_conv_b"][0])
    return S


def core_inputs(inp, S, ctx_ids, lat_b, q):
    f = lambda a: np.ascontiguousarray(np.asarray(a, np.float32))
    L_lat = inp["x_sample"].shape[1]
    QL = min(1024, L_lat)
    NQ = L_lat // QL
    xl = f(inp["x_sample"][lat_b])
    xs = [f(inp["x_prompt"][i]) for i in ctx_ids] + [xl]
    x = np.concatenate(xs, axis=0)
    m = dict(S)
    m["xT"] = np.ascontiguousarray(x.T)
    xw = np.zeros((QL + 128, D), np.float32)
    lo, hi = q * QL - 64, q * QL + QL + 64
    a, b = max(lo, 0), min(hi, L_lat)
    xw[a - lo:b - lo] = xl[a:b]
    m["xTo"] = np.ascontiguousarray(xw.T)
    sel = np.zeros((128, NQ + 2), np.float32)
    sel[:, q] = 1.0
    sel[:, NQ] = 0.0 if q == 0 else 1.0
    sel[:, NQ + 1] = 0.0 if q == NQ - 1 else 1.0
    m["selv"] = sel
    cv = np.stack([f(inp["c_ctx"]), f(inp["c"][lat_b])], axis=1)
    m["cvec"] = np.ascontiguousarray(cv.reshape(8, 128, 2).transpose(1, 0, 2).reshape(128, 16))
    sh = f(inp["state_hgrn"][lat_b, 0])
    m["st_h"] = np.ascontiguousarray(sh.transpose(0, 2, 1, 3).reshape(2, 128, 512))
    sr = f(inp["state_rwkv"][lat_b, 0])
    m["st_r"] = np.ascontiguousarray(sr.transpose(0, 3, 1, 2).reshape(2, 64, 512))
    return m


_PROG_CACHE = {}


def get_prog(n_ctx, L_ctx, L_lat, debug=(), phases=("0", "A", "S", "C1", "C2")):
    key = (n_ctx, L_ctx, L_lat, tuple(sorted(debug)), tuple(phases))
    if key not in _PROG_CACHE:
        b = Builder(n_ctx, L_ctx, L_lat, debug)
        nc = b.build(phases)
        _PROG_CACHE[key] = (nc, b)
    return _PROG_CACHE[key]


def kernel(**inp):
    B, L_ctx = inp["x_prompt"].shape[0], inp["x_prompt"].shape[1]
    DB, L_lat = inp["x_sample"].shape[0], inp["x_sample"].shape[1]
    n_ctx = B // N_CORES
    QL = min(1024, L_lat)
    NQ = L_lat // QL
    assert DB * NQ == N_CORES
    nc, _ = get_prog(n_ctx, L_ctx, L_lat)
    S = shared_inputs(inp)
    in_maps = []
    for c in range(N_CORES):
        ctx_ids = list(range(c * n_ctx, (c + 1) * n_ctx))
        in_maps.append(core_inputs(inp, S, ctx_ids, c % DB, c // DB))
    res = run_bass_kernel_spmd(nc, in_maps, core_ids=list(range(N_CORES)))
    y_prompt = np.zeros((B, L_ctx, D), np.float32)
    y_sample = np.zeros((DB, L_lat, D), np.float32)
    nsh = np.zeros((B, 1, 2, 4, 128, 128), np.float32)
    nsr = np.zeros((B, 1, 2, 8, 64, 64), np.float32)
    for c in range(N_CORES):
        r = res.results[c]
        y = np.asarray(r["yT"]).T
        for i in range(n_ctx):
            y_prompt[c * n_ctx + i] = y[i * L_ctx:(i + 1) * L_ctx]
        b, q = c % DB, c // DB
        y_sample[b, q * QL:(q + 1) * QL] = y[n_ctx * L_ctx:]
        h = np.asarray(r["nsh"]).reshape(n_ctx, 2, 128, 4, 128).transpose(0, 1, 3, 2, 4)
        nsh[c * n_ctx:(c + 1) * n_ctx, 0] = h
        rr = np.asarray(r["nsr"]).reshape(n_ctx, 2, 64, 8, 64).transpose(0, 1, 3, 4, 2)
        nsr[c * n_ctx:(c + 1) * n_ctx, 0] = rr
    return (y_prompt, y_sample, nsh, nsr)
```

```python
import contextlib
import numpy as np
import concourse.bass as bass
import concourse.mybir as mybir
from concourse.bass_utils import run_bass_kernel_spmd

F32 = mybir.dt.float32
BF16 = mybir.dt.bfloat16
AF = mybir.ActivationFunctionType
ALU = mybir.AluOpType
AX = mybir.AxisListType

D = 1024
KD = 8
WA = 512
PA = 2560
PB = 1728
PIN = 4288
DFF = 2816
NF = 22
DECAY = 0.6065306597
RMS_EPS = 1e-6
GN_EPS = 64e-5
GRID_W = 64
N_CORES = 8

ENGS = ("pe", "act", "dve", "pool", "sp")


class Buf:
    __slots__ = ("name", "lw", "rd", "rdd")

    def __init__(self, name):
        self.name = name
        self.lw = None
        self.rd = {}
        self.rdd = []


class Ins:
    __slots__ = ("eng", "fn", "deps", "is_dma", "sig", "sigval", "dsem", "dval")

    def __init__(self, eng, fn, deps, is_dma):
        self.eng = eng
        self.fn = fn
        self.deps = deps
        self.is_dma = is_dma
        self.sig = False
        self.sigval = 0
        self.dsem = None
        self.dval = 0


class Sched:
    NDMA = 16

    def __init__(self, nc):
        self.nc = nc
        self.ins = []
        self.last = {}
        self.dmas = []

    def barrier(self):
        deps = sorted(set(self.last.values()) | set(self.dmas))
        if not deps:
            return
        for e in ENGS:
            self.ins.append(Ins(e, None, list(deps), False))
        self.dmas = []

    def add(self, eng, fn, reads=(), writes=(), is_dma=False):
        deps = set()
        for b in reads:
            if b.lw is not None:
                deps.add(b.lw)
        for b in writes:
            if b.lw is not None:
                deps.add(b.lw)
            deps.update(b.rd.values())
            deps.update(b.rdd)
        i = len(self.ins)
        self.ins.append(Ins(eng, fn, sorted(deps), is_dma))
        self.last[eng] = i
        if is_dma:
            self.dmas.append(i)
        for b in reads:
            if is_dma:
                b.rdd.append(i)
            else:
                b.rd[eng] = i
        for b in writes:
            b.lw = i
            b.rd = {}
            b.rdd = []
        return i

    def finalize(self, final_bufs):
        nc = self.nc
        ins = self.ins
        fdeps = set()
        for b in final_bufs:
            if b.lw is not None:
                fdeps.add(b.lw)
        ins.append(Ins("sp", None, sorted(fdeps), False))
        for it in ins:
            for d in it.deps:
                dd = ins[d]
                if dd.eng == "pe" and it.eng == "pe" and not dd.is_dma and not it.is_dma:
                    continue
                dd.sig = True
        st = contextlib.ExitStack()
        esem = {e: st.enter_context(nc.semaphore(f"s_{e}")) for e in ("pe", "act", "dve", "pool")}
        dq = [e for e in ENGS if any(it.is_dma and it.eng == e for it in ins)]
        per = max(2, self.NDMA // max(1, len(dq)))
        dsems = []
        dpool = {}
        for e in dq:
            dpool[e] = list(range(len(dsems), len(dsems) + per))
            dsems += [st.enter_context(nc.semaphore(f"s_dma_{e}{i}")) for i in range(per)]
        dcount = [0] * len(dsems)
        dlast = [None] * len(dsems)
        ecount = {e: 0 for e in esem}
        rr = {e: 0 for e in dq}
        for k, it in enumerate(ins):
            if it.is_dma:
                s = dpool[it.eng][rr[it.eng] % per]
                rr[it.eng] += 1
                if dlast[s] is not None:
                    it.deps = sorted(set(it.deps) | {dlast[s]})
                dcount[s] += 16
                it.dsem = s
                it.dval = dcount[s]
                dlast[s] = k
            elif it.sig:
                ecount[it.eng] += 1
                it.sigval = ecount[it.eng]
        progs = {e: [] for e in ENGS}
        waited = {e: {} for e in ENGS}
        for it in ins:
            w = {}
            for d in it.deps:
                dd = ins[d]
                if dd.is_dma:
                    key = ("d", dd.dsem)
                    val = dd.dval
                else:
                    if dd.eng == "pe" and it.eng == "pe" and not it.is_dma:
                        continue
                    key = ("e", dd.eng)
                    val = dd.sigval
                if w.get(key, 0) < val:
                    w[key] = val
            wl = []
            for key, val in w.items():
                if waited[it.eng].get(key, 0) >= val:
                    continue
                waited[it.eng][key] = val
                wl.append((dsems[key[1]] if key[0] == "d" else esem[key[1]], val))
            progs[it.eng].append((wl, it))
        self.counts = {e: len(progs[e]) for e in ENGS}
        with nc.Block() as block:
            def runner(name):
                def run(eng):
                    for wl, it in progs[name]:
                        for sem, val in wl:
                            eng.wait_ge(sem, val)
                        if it.fn is None:
                            continue
                        r = it.fn(eng)
                        if it.is_dma:
                            r.then_inc(dsems[it.dsem], 16)
                        elif it.sig:
                            r.then_inc(esem[it.eng], 1)
                return run
            block.tensor(runner("pe"))
            block.scalar(runner("act"))
            block.vector(runner("dve"))
            block.gpsimd(runner("pool"))
            block.sync(runner("sp"))
        st.close()


class Tl:
    __slots__ = ("t", "b")

    def __init__(self, t, name):
        self.t = t
        self.b = Buf(name)

    def __getitem__(self, k):
        return self.t[k]


def _bufs(xs):
    return [x.b if isinstance(x, Tl) else x for x in xs]


class Builder:
    def __init__(self, n_ctx, L_ctx, L_lat, debug=()):
        self.n_ctx, self.L_ctx, self.L_lat = n_ctx, L_ctx, L_lat
        self.NT = n_ctx * L_ctx + L_lat
        self.NG = self.NT // 128
        self.QL = min(1024, L_lat)
        self.NQ = L_lat // self.QL
        self.seqs = [(i * L_ctx, L_ctx, 0) for i in range(n_ctx)] + [(n_ctx * L_ctx, L_lat, 1)]
        self.debug = set(debug)
        self.nc = bass.Bass("TRN2", target_bir_lowering=False)
        self.S = Sched(self.nc)
        self.final = []
        self.uid = 0

    @contextlib.contextmanager
    def scope(self):
        with contextlib.ExitStack() as st:
            yield st
            self.S.barrier()

    def sb(self, st, name, shape, dt):
        self.uid += 1
        return Tl(st.enter_context(self.nc.sbuf_tensor(f"{name}_{self.uid}", shape, dt)), name)

    def ps(self, st, name, shape, dt):
        self.uid += 1
        return Tl(st.enter_context(self.nc.psum_tensor(f"{name}_{self.uid}", shape, dt)), name)

    def din(self, name, shape, dt=F32):
        return self.nc.dram_tensor(name, list(shape), dt, kind="ExternalInput").ap()

    def dout(self, name, shape, dt=F32):
        return self.nc.dram_tensor(name, list(shape), dt, kind="ExternalOutput").ap()

    def dscr(self, name, shape, dt):
        kind = "ExternalOutput" if name in self.debug else "Internal"
        return self.nc.dram_tensor(name, list(shape), dt, kind=kind).ap()

    def mm(self, out, lhsT, rhs, start, stop, reads, writes):
        self.S.add("pe", lambda e: e.matmul(out, lhsT=lhsT, rhs=rhs, start=start, stop=stop),
                   _bufs(reads), _bufs(writes))

    def tr(self, out, in_, ident, reads, writes):
        self.S.add("pe", lambda e: e.transpose(out=out, in_=in_, identity=ident), _bufs(reads), _bufs(writes))

    def act(self, out, in_, func, reads, writes, scale=1.0, bias=0.0):
        self.S.add("act", lambda e: e.activation(out=out, in_=in_, func=func, scale=scale, bias=bias),
                   _bufs(reads), _bufs(writes))

    def tt(self, eng, out, in0, in1, op, reads, writes):
        self.S.add(eng, lambda e: e.tensor_tensor(out=out, in0=in0, in1=in1, op=op), _bufs(reads), _bufs(writes))

    def ts(self, eng, out, in0, s1, s2, op0, op1, reads, writes):
        self.S.add(eng, lambda e: e.tensor_scalar(out=out, in0=in0, scalar1=s1, scalar2=s2, op0=op0, op1=op1),
                   _bufs(reads), _bufs(writes))

    def stt(self, out, in0, scalar, in1, op0, op1, reads, writes):
        self.S.add("dve", lambda e: e.scalar_tensor_tensor(out=out, in0=in0, scalar=scalar, in1=in1, op0=op0, op1=op1),
                   _bufs(reads), _bufs(writes))

    def cp(self, eng, out, in_, reads, writes):
        if eng == "act":
            self.act(out, in_, AF.Copy, reads, writes)
        else:
            self.S.add(eng, lambda e: e.tensor_copy(out=out, in_=in_), _bufs(reads), _bufs(writes))

    def red(self, out, in_, reads, writes):
        self.S.add("dve", lambda e: e.tensor_reduce(out=out, in_=in_, axis=AX.X, op=ALU.add), _bufs(reads), _bufs(writes))

    def recip(self, out, in_, reads, writes):
        self.S.add("dve", lambda e: e.reciprocal(out=out, in_=in_), _bufs(reads), _bufs(writes))

    def memset(self, eng, ap, val, writes):
        self.S.add(eng, lambda e: e.memset(ap, val), [], _bufs(writes))

    def asel(self, ap, pattern, cmp, fill, base, cm, tl):
        self.S.add("pool", lambda e: e.affine_select(out=ap, in_=ap, pattern=pattern, compare_op=cmp, fill=fill,
                                                     base=base, channel_multiplier=cm), [tl.b], [tl.b])

    def dma(self, eng, out, in_, reads, writes):
        nt = getattr(self, "_untracked", None)
        if nt is None:
            nt = self._untracked = set(id(b) for b in list(self.bR.values()) + list(self.bO.values()))
        writes = [w for w in _bufs(writes) if id(w) not in nt]
        reads = [r for r in _bufs(reads) if id(r) not in nt]
        self.S.add(eng, lambda e: e.dma_start(out=out, in_=in_), reads, writes, is_dma=True)

    def dbg(self, name, ap, shape, rd):
        if name in self.debug:
            d = self.dout(name, shape)
            b = Buf(name)
            self.dma("pool", d, ap, [rd], [b])
            self.final.append(b)

    def declare_io(self):
        NT, n_ctx = self.NT, self.n_ctx
        I = {}
        I["xT"] = self.din("xT", [D, NT])
        I["cvec"] = self.din("cvec", [128, 16])
        I["ada_w"] = self.din("ada_w", [D, 6 * D])
        I["adab"] = self.din("adab", [128, 48])
        I["nmw"] = self.din("nmw", [128, 8])
        I["nfw"] = self.din("nfw", [128, 8])
        I["fnw"] = self.din("fnw", [128, 8])
        I["w_in"] = self.din("w_in", [D, PIN])
        I["rwkv_conv"] = self.din("rwkv_conv", [3, PB])
        I["hgrn_lb"] = self.din("hgrn_lb", [1, 2048])
        I["prow"] = self.din("prow", [1, 9 * 512])
        I["w2x"] = self.din("w2x", [32, 3 * 512])
        I["g2"] = self.din("g2", [96, 512])
        I["w_out"] = self.din("w_out", [D, D])
        I["w_gate"] = self.din("w_gate", [D, DFF])
        I["w_up"] = self.din("w_up", [D, DFF])
        I["w_down"] = self.din("w_down", [DFF, D])
        I["convw"] = self.din("convw", [128, NF * 9])
        I["cvb"] = self.din("cvb", [128, NF])
        I["xTo"] = self.din("xTo", [D, self.QL + 128])
        I["selv"] = self.din("selv", [128, self.NQ + 2])
        I["st_h"] = self.din("st_h", [2, 128, 4 * 128])
        I["st_r"] = self.din("st_r", [2, 64, 8 * 64])
        self.I = I
        O = {}
        O["yT"] = self.dout("yT", [D, n_ctx * self.L_ctx + self.QL])
        O["nsh"] = self.dout("nsh", [n_ctx, 2, 128, 512])
        O["nsr"] = self.dout("nsr", [n_ctx, 2, 64, 512])
        self.O = O
        self.bO = {k: Buf("out_" + k) for k in O}
        R = {}
        R["OPS"] = self.dscr("OPS", [NT, 14 * 512], BF16)
        R["CX"] = self.dscr("CX", [NT, 3 * 512], BF16)
        R["GCs"] = self.dscr("GCs", [self.NG, 2, 64, 16], F32)
        R["ECs"] = self.dscr("ECs", [self.NG, 2, 128, 8], F32)
        R["YS"] = self.dscr("YS", [2, NT, 512], F32)
        R["OSC"] = self.dscr("OSC", [2, NT, 512], F32)
        R["X1T"] = self.dscr("X1T", [D, n_ctx * self.L_ctx], F32)
        R["X1W"] = self.dscr("X1W", [D, self.QL + 128], F32)
        self.R = R
        self.bR = {k: Buf("scr_" + k) for k in R}

    def phase0(self, gst0, gst):
        nc, I = self.nc, self.I
        C = {}
        C["ident"] = self.sb(gst0, "ident", [128, 128], BF16)
        C["onesD"] = self.sb(gst0, "onesD", [128, 128], BF16)
        mods = self.sb(gst0, "mods", [128, 6, 2, 8], F32)
        fnw = self.sb(gst0, "fnw", [128, 8], F32)
        C["selv"] = self.sb(gst0, "selv", [128, self.NQ + 2], F32)
        self.dma("sp", C["selv"][:], I["selv"], [], [C["selv"]])
        identf = self.sb(gst, "identf", [128, 128], F32)
        self.memset("pool", identf[:], 0.0, [identf])
        self.asel(identf[:], [[-1, 128]], ALU.not_equal, 1.0, 0, 1, identf)
        self.cp("dve", C["ident"][:], identf[:], [identf], [C["ident"]])
        self.memset("pool", C["onesD"][:], 1.0 / D, [C["onesD"]])
        trif = self.sb(gst, "trif", [128, 128], F32)
        self.memset("pool", trif[:], 1.0, [trif])
        self.asel(trif[:], [[1, 128]], ALU.is_ge, 0.0, 0, -1, trif)
        self.asel(trif[:, 64:128], [[0, 64]], ALU.is_ge, 0.0, -64, 1, trif)
        trib = self.sb(gst, "trib", [128, 128], F32)
        self.memset("pool", trib[:], 1.0, [trib])
        self.asel(trib[:], [[-1, 128]], ALU.is_ge, 0.0, 0, 1, trib)
        self.asel(trib[:, 0:64], [[0, 64]], ALU.is_ge, 0.0, 63, -1, trib)
        C["tri"] = [trif, trib]
        t16f = self.sb(gst, "tri16f", [128, 128], BF16)
        self.memset("pool", t16f[:], 1.0, [t16f])
        self.asel(t16f[:], [[1, 128]], ALU.is_ge, 0.0, 0, -1, t16f)
        self.asel(t16f[:, 64:128], [[0, 64]], ALU.is_ge, 0.0, -64, 1, t16f)
        t16b = self.sb(gst, "tri16b", [128, 128], BF16)
        self.memset("pool", t16b[:], 1.0, [t16b])
        self.asel(t16b[:], [[-1, 128]], ALU.is_ge, 0.0, 0, 1, t16b)
        self.asel(t16b[:, 0:64], [[0, 64]], ALU.is_ge, 0.0, 63, -1, t16b)
        C["tri16"] = [t16f, t16b]

        C["identf"] = identf
        cind = self.sb(gst, "cind", [128, 2], F32)
        self.memset("pool", cind[:], 1.0, [cind])
        self.asel(cind[:, 0:1], [[0, 1]], ALU.is_ge, 0.0, 63, -1, cind)
        self.asel(cind[:, 1:2], [[0, 1]], ALU.is_ge, 0.0, -64, 1, cind)
        C["cind"] = cind
        M1 = self.sb(gst, "M1", [128, 128], F32)
        M3 = self.sb(gst, "M3", [128, 64], F32)
        MI = self.sb(gst, "MI", [128, 64], F32)
        for m in (M1, M3, MI):
            self.memset("pool", m[:], 1.0, [m])
        self.asel(M1[0:64, 0:64], [[1, 64]], ALU.is_gt, 0.0, 0, -1, M1)
        self.asel(M1[0:64, 64:128], [[1, 64]], ALU.is_ge, 0.0, 0, -1, M1)
        self.asel(M1[64:128, 0:64], [[-1, 64]], ALU.is_gt, 0.0, 0, 1, M1)
        self.asel(M1[64:128, 64:128], [[-1, 64]], ALU.is_ge, 0.0, 0, 1, M1)
        self.asel(M3[0:64, :], [[-1, 64]], ALU.is_gt, 0.0, 0, 1, M3)
        self.asel(M3[64:128, :], [[1, 64]], ALU.is_gt, 0.0, 0, -1, M3)
        self.asel(MI[0:64, :], [[1, 64]], ALU.is_ge, 0.0, 0, -1, MI)
        self.asel(MI[64:128, :], [[-1, 64]], ALU.is_ge, 0.0, 0, 1, MI)
        C["M1"], C["M3"], C["MI"] = M1, M3, MI
        self.dbg("dbg_M1", M1[:], [128, 128], M1)
        self.dbg("dbg_trif", trif[:], [128, 128], trif)
        self.dbg("dbg_trib", trib[:], [128, 128], trib)

        modT = self.sb(gst, "modT", [128, 48, 2], F32)
        lb = self.sb(gst, "lb", [128, 1024], F32)
        omlb = self.sb(gst, "omlb", [128, 1024], F32)
        prow = self.sb(gst, "prow", [128, 9 * 512], F32)
        omka = self.sb(gst, "omka", [128, 512], F32)
        with self.scope() as st:
            cv = self.sb(st, "cv", [128, 16], F32)
            self.dma("sp", cv[:], I["cvec"], [], [cv])
            scv = self.sb(st, "scv", [128, 16], F32)
            self.act(scv[:], cv[:], AF.Silu, [cv], [scv])
            adab = self.sb(st, "adab", [128, 48], F32)
            self.dma("sp", adab[:], I["adab"], [], [adab])
            acc = self.sb(st, "modacc", [128, 96], F32)
            abuf = [self.sb(st, f"adaw{i}", [128, 3 * D], F32) for i in range(2)]
            pm = self.ps(st, "pmod", [128, 512], F32)
            adv = I["ada_w"].rearrange("(k p) c -> k p c", p=128)
            it = 0
            for k in range(KD):
                for hf in range(2):
                    ab = abuf[it % 2]
                    it += 1
                    self.dma("sp", ab[:, :], adv[k][:, hf * 3072:(hf + 1) * 3072], [], [ab])
                    for mm_ in range(24):
                        m = hf * 24 + mm_
                        self.mm(pm[:, 2 * m:2 * m + 2], ab[:, mm_ * 128:(mm_ + 1) * 128], scv[:, 2 * k:2 * k + 2], True, True,
                                [ab, scv], [pm])
                if k == 0:
                    self.cp("dve", acc[:], pm[:, 0:96], [pm], [acc])
                else:
                    self.tt("dve", acc[:], pm[:, 0:96], acc[:], ALU.add, [pm, acc], [acc])
            self.tt("dve", modT[:], acc[:].rearrange("p (m s) -> p m s", s=2),
                    adab[:].unsqueeze(2).broadcast_to([128, 48, 2]), ALU.add, [acc, adab], [modT])
            self.dbg("dbg_modT", modT[:].rearrange("p m s -> p (m s)"), [128, 96], modT)
            nmw = self.sb(st, "nmw", [128, 8], F32)
            nfw = self.sb(st, "nfw", [128, 8], F32)
            self.dma("sp", nmw[:], I["nmw"], [], [nmw])
            self.dma("sp", nfw[:], I["nfw"], [], [nfw])
            for s in range(2):
                for (wi, off, nw) in ((0, 8, nmw), (3, 32, nfw)):
                    self.stt(mods[:, wi, s, :], modT[:, off:off + 8, s], 1.0, nw[:], ALU.add, ALU.mult,
                             [modT, nw], [mods])
                for (wi, off) in ((1, 0), (2, 16), (4, 24), (5, 40)):
                    self.cp("dve", mods[:, wi, s, :], modT[:, off:off + 8, s], [modT], [mods])
            C["mods"] = mods
            self.dma("sp", fnw[:], I["fnw"], [], [fnw])
            C["fnw"] = fnw
            lbr = self.sb(st, "lbraw", [128, 2048], F32)
            self.dma("sp", lbr[:], I["hgrn_lb"].partition_broadcast(128), [], [lbr])
            lbd = self.sb(st, "lbd", [128, 1024], F32)
            self.tt("dve", lbd[:], lbr[:, 0:1024], lbr[:, 1024:2048], ALU.subtract, [lbr], [lbd])
            self.act(lb[:], lbd[:], AF.Sigmoid, [lbd], [lb])
            self.ts("dve", omlb[:], lb[:], -0.5, 0.5, ALU.mult, ALU.add, [lb], [omlb])
            self.ts("dve", lb[:], lb[:], 0.5, 0.5, ALU.mult, ALU.add, [lb], [lb])
            C["lb"], C["omlb"] = lb, omlb
            self.dma("sp", prow[:], I["prow"].partition_broadcast(128), [], [prow])
            C["prow"] = prow
            self.ts("dve", omka[:], prow[:, 1024:1536], -1.0, 1.0, ALU.mult, ALU.add, [prow], [omka])
            C["omka"] = omka
        self.C = C

    def rms_rstd(self, st_tiles, src, n, pbank, rstd, lnexp=True):
        sq = st_tiles["sq"]
        self.tt("dve", sq[:, 0:8 * n], src[:, 0:8 * n], src[:, 0:8 * n], ALU.mult, [src], [sq])
        c0 = 0
        while c0 < n:
            cn = min(512, n - c0)
            for k in range(KD):
                self.mm(pbank[:, 0:cn], self.C["onesD"][:], sq[:, k * n + c0:k * n + c0 + cn], k == 0, k == KD - 1,
                        [self.C["onesD"], sq], [pbank])
            if lnexp:
                self.act(rstd[:, c0:c0 + cn], pbank[:, 0:cn], AF.Ln, [pbank], [rstd], bias=RMS_EPS)
            else:
                self.act(rstd[:, c0:c0 + cn], pbank[:, 0:cn], AF.Sqrt, [pbank], [rstd], bias=RMS_EPS)
            c0 += cn
        if lnexp:
            self.act(rstd[:, 0:n], rstd[:, 0:n], AF.Exp, [rstd], [rstd], scale=-0.5)
        else:
            self.recip(rstd[:, 0:n], rstd[:, 0:n], [rstd], [rstd])

    def norm_mod(self, tiles, xt, n, pbank, A, B, hT, modtl, lnexp=True):
        rstd = tiles["rstd"]
        self.rms_rstd(tiles, xt, n, pbank, rstd, lnexp)
        for k in range(KD):
            tmp = tiles["ntmp"][k % 2]
            self.tt("dve", tmp[:, 0:n], xt[:, k * n:(k + 1) * n], rstd[:, 0:n], ALU.mult, [xt, rstd], [tmp])
            self.ts("dve", hT[:, k * n:(k + 1) * n], tmp[:, 0:n], A[:, k:k + 1], B[:, k:k + 1], ALU.mult, ALU.add,
                    [tmp, modtl], [hT])

    def phaseA(self):
        nc, I, C, R = self.nc, self.I, self.C, self.R
        mods = C["mods"]
        xTv = I["xT"].rearrange("(k p) t -> p k t", p=128)
        import os as _os
        _rnew = "1"
        for part in (("H",) if _rnew == "0" else ("H", "R")):
            halo = 0 if part == "H" else 1
            nw = 128 + 2 * halo
            with self.scope() as st:
                X = {"part": part, "nw": nw, "halo": halo}
                if part == "H":
                    X["lf"] = [self.sb(st, f"lf{i}", [128, 512], F32) for i in range(2)]
                else:
                    X["sg"] = [self.sb(st, f"sg{i}", [128, 512], F32) for i in range(2)]
                    X["lt"] = [self.sb(st, f"lt{i}", [128, 192], F32) for i in range(2)]
                if part == "H":
                    WH = self.sb(st, "WH", [128, 8 * PA], BF16)
                    wv = I["w_in"].rearrange("(k p) c -> p k c", p=128)
                    for k in range(KD):
                        self.dma("pool", WH[:, k * PA:(k + 1) * PA], wv[:, k, 0:PA], [], [WH])
                    X["WH"] = WH
                else:
                    WR = [self.sb(st, f"WR{t}", [128, 8 * PB], BF16) for t in range(3)]
                    with self.scope() as st2:
                        cvbc = self.sb(st2, "cvbc", [128, 3 * PB], F32)
                        self.dma("sp", cvbc[:], I["rwkv_conv"].rearrange("(o a) c -> o (a c)", o=1).partition_broadcast(128),
                                 [], [cvbc])
                        stg = [self.sb(st2, f"wstg{i}", [128, PB], F32) for i in range(2)]
                        wv = I["w_in"].rearrange("(k p) c -> p k c", p=128)
                        for k in range(KD):
                            sg = stg[k % 2]
                            self.dma("sp", sg[:], wv[:, k, PA:PIN], [], [sg])
                            for t in range(3):
                                self.tt("dve" if t != 1 else "pool", WR[t][:, k * PB:(k + 1) * PB], sg[:],
                                        cvbc[:, t * PB:(t + 1) * PB], ALU.mult, [sg, cvbc], [WR[t]])
                    X["WR"] = WR
                    X["W2x"] = self.sb(st, "W2x", [32, 1536], BF16)
                    self.dma("pool", X["W2x"][:], I["w2x"], [], [X["W2x"]])
                    X["G2"] = self.sb(st, "G2", [96, 512], BF16)
                    self.dma("pool", X["G2"][:], I["g2"], [], [X["G2"]])
                    X["lw"] = self.sb(st, "lw", [32, 384], BF16)
                    X["lg"] = self.sb(st, "lg", [96, 128], BF16)
                X["tiles"] = {
                    "sq": self.sb(st, "sq", [128, 8 * nw], BF16),
                    "rstd": self.sb(st, "rstd", [128, nw], F32),
                    "ntmp": [self.sb(st, f"ntmp{i}", [128, nw], F32) for i in range(2)],
                }
                X["xts"] = [self.sb(st, f"xt{i}", [128, 8 * nw], F32) for i in range(2)]
                X["hTs"] = [self.sb(st, f"hT{i}", [128, 8 * nw], BF16) for i in range(2)]
                X["OPst"] = [self.sb(st, f"OPst{i}", [128, 14 * 512], BF16) for i in range(2)]
                X["CXst"] = [self.sb(st, f"CXst{i}", [128, 3 * 512], BF16) for i in range(2)]
                X["dcs"] = [self.sb(st, f"dcs{i}", [128, 32], F32) for i in range(2)]
                X["sm"] = [self.sb(st, f"sm{i}", [128, 8], F32) for i in range(4)]
                if part == "H":
                    X["qs"] = [self.sb(st, f"qs{i}", [128, 512], F32) for i in range(2)]
                    X["sgz"] = [[self.sb(st, f"sgz{i}{d}", [128, 512], F32) for d in range(2)] for i in range(2)]
                    X["ft"] = [self.sb(st, f"ft{i}", [128, 512], F32) for i in range(7)]
                    X["hilo"] = [[self.sb(st, f"hilo{d}{i}", [128, 512], BF16) for i in range(2)] for d in range(2)]
                    X["pss"] = self.ps(st, "pss", [128, 512], F32)
                    X["pp"] = [self.ps(st, f"pp{i}", [128, 512], F32) for i in range(5)]
                    X["LA"] = self.ps(st, "LA", [128, 512], F32)
                    X["LB"] = self.ps(st, "LB", [128, 512], F32)
                else:
                    X["rkv"] = [[self.sb(st, f"rkv{i}{j}", [128, 512], F32) for j in range(3)] for i in range(2)]
                    X["ft"] = [self.sb(st, f"ft{i}", [128, 512], F32) for i in range(9)]
                    X["pss"] = self.ps(st, "pss", [128, 512], F32)
                    X["pp"] = [self.ps(st, f"pp{i}", [128, 512], F32) for i in range(4)]
                    X["LA"] = self.ps(st, "LA", [128, 512], F32)
                    X["LB"] = self.ps(st, "LB", [128, 512], F32)
                    X["LC"] = self.ps(st, "LC", [128, 512], F32)
                for _ in self.A_normproj(X, 0):
                    pass
                for g in range(self.NG):
                    self.A_early(X, g)
                    nxt = self.A_normproj(X, g + 1) if g + 1 < self.NG else None
                    if part == "H":
                        self._interleave(nxt, self.A_late_H(X, g), "abaaabaab")
                    else:
                        self._interleave(nxt, self.A_late_R(X, g), "aaaaabbbbbb")

    def A_normproj(self, X, g):
        I, C = self.I, self.C
        mods = C["mods"]
        xTv = I["xT"].rearrange("(k p) t -> p k t", p=128)
        halo, nw = X["halo"], X["nw"]
        t0 = g * 128
        s0, sl, kind = [s for s in self.seqs if s[0] <= t0 < s[0] + s[1]][0]
        s1 = s0 + sl
        lo, hi = max(t0 - halo, s0), min(t0 + 128 + halo, s1)
        off = lo - (t0 - halo)
        n = hi - lo
        xt, hT = X["xts"][g % 2], X["hTs"][g % 2]
        x3 = xt[:, :].rearrange("p (k t) -> p k t", k=8)
        if n < nw:
            self.memset("dve", xt[:, :], 0.0, [xt])
        self.dma("sp", x3[:, :, off:off + n], xTv[:, :, lo:hi], [], [xt])
        self.norm_mod(X["tiles"], xt, nw, X["pss"], mods[:, 0, kind, :], mods[:, 1, kind, :], hT, mods)
        h3 = hT[:, :].rearrange("p (k t) -> p k t", k=8)
        if off > 0:
            self.memset("dve", h3[:, :, 0:off], 0.0, [hT])
        if off + n < nw:
            self.memset("dve", h3[:, :, off + n:nw], 0.0, [hT])
        yield
        pp = X["pp"]
        if X["part"] == "H":
            WH = X["WH"]
            for cg in range(5):
                for k in range(KD):
                    self.mm(pp[cg][:, :], hT[:, k * nw:k * nw + 128], WH[:, k * PA + cg * 512:k * PA + (cg + 1) * 512],
                            k == 0, k == KD - 1, [hT, WH], [pp[cg]])
                yield
        else:
            WR = X["WR"]
            for cg in range(4):
                c0 = cg * 512
                cn = 512 if cg < 3 else 192
                for t in range(3):
                    for k in range(KD):
                        self.mm(pp[cg][:, 0:cn], hT[:, k * nw + t:k * nw + t + 128], WR[t][:, k * PB + c0:k * PB + c0 + cn],
                                t == 0 and k == 0, t == 2 and k == KD - 1, [hT, WR[t]], [pp[cg]])
                yield

    @staticmethod
    def _interleave(a, b, pattern):
        for ch in pattern:
            gen = a if ch == "a" else b
            if gen is not None:
                next(gen, None)
        for gen in (a, b):
            if gen is not None:
                for _ in gen:
                    pass

    def A_early(self, X, g):
        pp = X["pp"]
        b = g % 2
        ops, cxs = X["OPst"][b], X["CXst"][b]
        if X["part"] == "H":
            self.act(X["qs"][b][:], pp[0][:], AF.Silu, [pp[0]], [X["qs"][b]])
            self.act(cxs[:, 1024:1536], pp[4][:], AF.Silu, [pp[4]], [cxs])
            self.act(ops[:, 13 * 512:14 * 512], pp[1][:], AF.Copy, [pp[1]], [ops])
            self.act(X["sgz"][b][0][:], pp[2][:], AF.Tanh, [pp[2]], [X["sgz"][b][0]], scale=0.5)
            self.act(X["sgz"][b][1][:], pp[3][:], AF.Tanh, [pp[3]], [X["sgz"][b][1]], scale=0.5)
        else:
            lt = X["lt"][b]
            self.act(lt[:, 0:64], pp[3][:, 0:64], AF.Tanh, [pp[3]], [lt])
            self.act(lt[:, 64:96], pp[3][:, 64:96], AF.Copy, [pp[3]], [lt])
            self.act(lt[:, 96:192], pp[3][:, 96:192], AF.Sigmoid, [pp[3]], [lt])
            r_, k_, v_ = X["rkv"][b]
            self.cp("dve", r_[:], pp[0][:], [pp[0]], [r_])
            self.cp("act", k_[:], pp[1][:], [pp[1]], [k_])
            self.cp("dve", v_[:], pp[2][:], [pp[2]], [v_])

    def A_late_H(self, X, g):
        C, R = self.C, self.R
        OPSd, CXd = R["OPS"], R["CX"]
        b = g % 2
        t0 = g * 128
        ops, cxs, dcs = X["OPst"][b], X["CXst"][b], X["dcs"][b]
        qs = X["qs"][b]
        ft = X["ft"]
        LAB = (X["LA"], X["LB"])
        pe = X["pss"]
        for d in range(2):
            sg_ = X["sgz"][b][d]
            lf = X["lf"][d]
            f_, kd = ft[d], ft[2 + d]
            hi, lo = X["hilo"][d]
            self.tt("dve", f_[:], sg_[:], C["omlb"][:, d * 512:(d + 1) * 512], ALU.mult, [sg_, C["omlb"]], [f_])
            self.tt("pool", f_[:], f_[:], C["lb"][:, d * 512:(d + 1) * 512], ALU.add, [f_, C["lb"]], [f_])
            self.act(lf[:], f_[:], AF.Ln, [f_], [lf])
            self.ts("pool", kd[:], f_[:], -1.0, 1.0, ALU.mult, ALU.add, [f_], [kd])
            self.cp("dve", hi[:], lf[:], [lf], [hi])
            self.tt("dve", lo[:], lf[:], hi[:], ALU.subtract, [lf, hi], [lo])
        yield
        for d in range(2):
            lf = X["lf"][d]
            hi, lo = X["hilo"][d]
            pc = LAB[d]
            self.mm(pc[:], C["tri16"][d][:], hi[:], True, False, [C["tri16"][d], hi], [pc])
            self.mm(pc[:], C["tri16"][d][:], lo[:], False, True, [C["tri16"][d], lo], [pc])
            for h in range(4):
                c = (d * 4 + h) * 2
                self.mm(pe[:, c:c + 2], lf[:, h * 128:(h + 1) * 128], C["cind"][:], True, True, [lf, C["cind"]], [pe])
        yield
        for d in range(2):
            kd = ft[2 + d]
            csc, e1, e2 = ft[4], ft[5], ft[6]
            pc = LAB[d]
            self.ts("dve", csc[:], pc[:], -80.0, None, ALU.max, ALU.bypass, [pc], [csc])
            self.act(e1[:], csc[:], AF.Exp, [csc], [e1])
            self.act(e2[:], csc[:], AF.Exp, [csc], [e2], scale=-1.0)
            self.tt("dve", ops[:, (6 * d + 4) * 512:(6 * d + 5) * 512], qs[:], e1[:], ALU.mult, [qs, e1], [ops])
            self.tt("dve", ops[:, (6 * d + 5) * 512:(6 * d + 6) * 512], kd[:], e2[:], ALU.mult, [kd, e2], [ops])
        self.act(dcs[:, 0:16].rearrange("p (c x) -> p c x", c=2), pe[:, 0:16].rearrange("p (x c) -> p c x", c=2),
                 AF.Exp, [pe], [dcs])
        self.dma("pool", R["ECs"][g].rearrange("c p x -> p c x"), dcs[:, 0:16].rearrange("p (c x) -> p c x", c=2),
                 [dcs], [])
        o3 = ops[:, :].rearrange("p (s c) -> p s c", s=14)
        OP3 = OPSd[t0:t0 + 128, :].rearrange("t (s c) -> t s c", s=14)
        self.dma("pool", OP3[:, 4:6, :], o3[:, 4:6, :], [ops], [])
        self.dma("pool", OP3[:, 10:12, :], o3[:, 10:12, :], [ops], [])
        self.dma("pool", OP3[:, 13:14, :], o3[:, 13:14, :], [ops], [])
        self.dma("pool", CXd[t0:t0 + 128, 1024:1536], cxs[:, 1024:1536], [cxs], [])
        yield

    def A_late_R(self, X, g):
        C, R = self.C, self.R
        OPSd, CXd = R["OPS"], R["CX"]
        prow = C["prow"]
        kk_bc, ka_bc, rk_bc = prow[:, 512:1024], prow[:, 1024:1536], prow[:, 1536:2048]
        b = g % 2
        t0 = g * 128
        ops, cxs, dcs = X["OPst"][b], X["CXst"][b], X["dcs"][b]
        r_, k_, v_ = X["rkv"][b]
        lt, lw, lg, W2x, G2 = X["lt"][b], X["lw"], X["lg"], X["W2x"], X["G2"]
        ft, sm = X["ft"], X["sm"]
        LA, LB, LC = X["LA"], X["LB"], X["LC"]
        identf = C["identf"]
        for i in range(3):
            self.tr(LC[0:32, i * 128:(i + 1) * 128], lt[:, i * 32:(i + 1) * 32], identf[:], [lt, identf], [LC])
        self.tr(LC[0:96, 384:512], lt[:, 96:192], identf[:], [lt, identf], [LC])
        yield
        self.cp("dve", lw[0:32, :], LC[0:32, 0:384], [LC], [lw])
        self.cp("dve", lg[:, :], LC[0:96, 384:512], [LC], [lg])
        a_, sgf, sgb = ft[0], X["sg"][0], X["sg"][1]
        self.mm(LA[:], lw[0:32, 256:384], W2x[0:32, 1024:1536], True, True, [lw, W2x], [LA])
        self.mm(LB[:], lw[0:32, 128:256], W2x[0:32, 512:1024], True, True, [lw, W2x], [LB])
        self.mm(LC[:], lg[:, :], G2[:, :], True, True, [lg, G2], [LC])
        yield
        self.tt("dve", a_[:], LA[:], prow[:, 4096:4608], ALU.add, [LA, prow], [a_])
        self.act(a_[:], a_[:], AF.Sigmoid, [a_], [a_])
        self.mm(LA[:], lw[0:32, 0:128], W2x[0:32, 0:512], True, True, [lw, W2x], [LA])
        self.tt("dve", sgb[:], LB[:], prow[:, 3584:4096], ALU.add, [LB, prow], [sgb])
        self.act(sgb[:], sgb[:], AF.Sigmoid, [sgb], [sgb])
        self.act(cxs[:, 0:512], LC[:], AF.Copy, [LC], [cxs])
        self.tt("dve", sgf[:], LA[:], prow[:, 3072:3584], ALU.add, [LA, prow], [sgf])
        self.act(sgf[:], sgf[:], AF.Sigmoid, [sgf], [sgf])
        yield
        kx, sq, kk, t1, kp, nb, t2 = ft[1], ft[2], ft[3], ft[4], ft[5], ft[6], ft[4]
        self.tt("dve", kx[:], k_[:], kk_bc, ALU.mult, [k_, prow], [kx])
        self.act(sq[:], kx[:], AF.Square, [kx], [sq])
        self.red(sm[0][:], sq[:].rearrange("p (h j) -> p h j", h=8), [sq], [sm[0]])
        self.act(sm[1][:], sm[0][:], AF.Ln, [sm[0]], [sm[1]], bias=1e-12)
        self.act(sm[1][:], sm[1][:], AF.Exp, [sm[1]], [sm[1]], scale=-0.5)
        self.tt("dve", kk[:].rearrange("p (h j) -> p h j", h=8), kx[:].rearrange("p (h j) -> p h j", h=8),
                sm[1][:].unsqueeze(2).broadcast_to([128, 8, 64]), ALU.mult, [kx, sm[1]], [kk])
        self.tt("dve", t1[:], a_[:], ka_bc, ALU.mult, [a_, prow], [t1])
        self.tt("dve", t1[:], t1[:], C["omka"][:], ALU.add, [t1, C["omka"]], [t1])
        self.tt("dve", kp[:], k_[:], t1[:], ALU.mult, [k_, t1], [kp])
        self.stt(nb[:], kk[:], -1.0, a_[:], ALU.mult, ALU.mult, [kk, a_], [nb])
        self.tt("dve", t2[:], r_[:], kp[:], ALU.mult, [r_, kp], [t2])
        self.tt("dve", t2[:], t2[:], rk_bc, ALU.mult, [t2, prow], [t2])
        self.red(sm[2][:], t2[:].rearrange("p (h j) -> p h j", h=8), [t2], [sm[2]])
        self.tt("dve", cxs[:, 512:1024].rearrange("p (h j) -> p h j", h=8), v_[:].rearrange("p (h j) -> p h j", h=8),
                sm[2][:].unsqueeze(2).broadcast_to([128, 8, 64]), ALU.mult, [v_, sm[2]], [cxs])
        self.cp("act", ops[:, 12 * 512:13 * 512], v_[:], [v_], [ops])
        yield
        for d in range(2):
            sg_ = (sgf, sgb)[d]
            pc = (LA, LB)[d]
            self.mm(pc[:], C["tri"][d][:], sg_[:], True, True, [C["tri"][d], sg_], [pc])
            for h in range(8):
                c = (d * 8 + h) * 2
                self.mm(LC[0:64, c:c + 2], sg_[:, h * 64:(h + 1) * 64], C["cind"][:], True, True, [sg_, C["cind"]], [LC])
        yield
        for d in range(2):
            sg_ = (sgf, sgb)[d]
            pc = (LA, LB)[d]
            gi, ginv, tmp, ge = ft[7], ft[8], ft[2], ft[1]
            self.act(gi[:], pc[:], AF.Exp, [pc], [gi], scale=-DECAY)
            self.act(ginv[:], pc[:], AF.Exp, [pc], [ginv], scale=DECAY)
            self.tt("dve", tmp[:], pc[:], sg_[:], ALU.subtract, [pc, sg_], [tmp])
            self.act(ge[:], tmp[:], AF.Exp, [tmp], [ge], scale=-DECAY)
            b0 = 6 * d * 512
            self.tt("dve", ops[:, b0:b0 + 512], r_[:], gi[:], ALU.mult, [r_, gi], [ops])
            self.tt("dve", ops[:, b0 + 512:b0 + 1024], kp[:], ginv[:], ALU.mult, [kp, ginv], [ops])
            self.tt("dve", ops[:, b0 + 1024:b0 + 1536], nb[:], ginv[:], ALU.mult, [nb, ginv], [ops])
            self.tt("dve", ops[:, b0 + 1536:b0 + 2048], kk[:], ge[:], ALU.mult, [kk, ge], [ops])
        self.act(dcs[0:64, 0:32].rearrange("p (c x) -> p c x", c=2), LC[0:64, 0:32].rearrange("p (x c) -> p c x", c=2),
                 AF.Exp, [LC], [dcs], scale=-DECAY)
        self.dma("pool", R["GCs"][g].rearrange("c p x -> p c x"), dcs[0:64, 0:32].rearrange("p (c x) -> p c x", c=2),
                 [dcs], [self.bR["GCs"]])
        o3 = ops[:, :].rearrange("p (s c) -> p s c", s=14)
        OP3 = OPSd[t0:t0 + 128, :].rearrange("t (s c) -> t s c", s=14)
        self.dma("pool", OP3[:, 0:4, :], o3[:, 0:4, :], [ops], [self.bR["OPS"]])
        self.dma("pool", OP3[:, 6:10, :], o3[:, 6:10, :], [ops], [self.bR["OPS"]])
        self.dma("pool", OP3[:, 12:13, :], o3[:, 12:13, :], [ops], [self.bR["OPS"]])
        self.dma("pool", CXd[t0:t0 + 128, 0:1024], cxs[:, 0:1024], [cxs], [self.bR["CX"]])
        yield

    def scan(self):
        C, R, I, O = self.C, self.R, self.I, self.O
        ident, M1, M3, MI = C["ident"], C["M1"], C["M3"], C["MI"]
        OPSd = R["OPS"]
        with self.scope() as st:
            OP = [self.sb(st, f"OP{i}", [128, 8 * 512], BF16) for i in range(2)]
            gCt = [self.sb(st, f"gCt{i}", [128, 8], F32) for i in range(2)]
            eCt = [self.sb(st, f"eCt{i}", [128, 8], F32) for i in range(2)]
            XT = self.sb(st, "XT", [128, 8 * 256], BF16)
            GS = self.sb(st, "GS", [128, 8 * 320], BF16)
            XA = [self.sb(st, f"XA{i}", [128, 8 * 128], BF16) for i in range(2)]
            NN = [self.sb(st, f"NN{i}", [128, 8 * 128], BF16) for i in range(2)]
            GT = self.sb(st, "GT", [128, 512], BF16)
            Pb = self.sb(st, "Pb", [128, 512], BF16)
            T32 = self.sb(st, "T32", [128, 512], F32)
            Tb = [self.sb(st, f"Tb{i}", [128, 512], BF16) for i in range(2)]
            Ttmp = self.sb(st, "Ttmp", [128, 512], F32)
            Ysb = [self.sb(st, f"Ysb{i}", [128, 512], F32) for i in range(2)]
            HT = self.sb(st, "HT", [128, 1024], BF16)
            SCz = self.sb(st, "SCz", [128, 4 * 2 * 64], BF16)
            Osb = [self.sb(st, f"Osb{i}", [128, 512], F32) for i in range(2)]
            S32 = self.sb(st, "S32", [128, 1024], F32)
            Sb = [self.sb(st, f"Sb{i}", [128, 1024], BF16) for i in range(2)]
            Stmp = self.sb(st, "Stmp", [128, 512], F32)
            BT0 = self.ps(st, "BT0", [128, 1024], BF16)
            BT1 = self.ps(st, "BT1", [128, 1024], BF16)
            PX = self.ps(st, "PX", [128, 1024], F32)
            PN = self.ps(st, "PN", [128, 1024], F32)
            F4 = self.ps(st, "F4", [128, 512], F32)
            F5 = self.ps(st, "F5", [128, 512], F32)
            self.memset("pool", SCz[:], 0.0, [SCz])
            M1b = M1[:].unsqueeze(1).broadcast_to([128, 8, 128])
            M3b = M3[:].unsqueeze(1).broadcast_to([128, 8, 64])
            stepno = 0
            for si, (s0, sl, kind) in enumerate(self.seqs):
                nch = sl // 64
                if kind == 0:
                    self.memset("pool", T32[:], 0.0, [T32])
                    self.memset("pool", S32[:], 0.0, [S32])
                else:
                    for l in range(2):
                        self.dma("sp", T32[64 * l:64 * l + 64, :], I["st_r"][l], [], [T32])
                        self.dma("sp", S32[:, l * 512:(l + 1) * 512], I["st_h"][l], [], [S32])
                cur = 0
                self.cp("act", Tb[cur][:], T32[:], [T32], [Tb[cur]])
                self.cp("act", Sb[cur][:], S32[:], [S32], [Sb[cur]])
                for i in range(nch):
                    op, gc, ec = OP[stepno % 2], gCt[stepno % 2], eCt[stepno % 2]
                    ysb, osb = Ysb[stepno % 2], Osb[stepno % 2]
                    stepno += 1
                    tok = [s0 + 64 * i, s0 + 64 * (nch - 1 - i)]
                    o3 = op[:, :].rearrange("p (s c) -> p s c", s=8)
                    for l in range(2):
                        src = OPSd[tok[l]:tok[l] + 64, :].rearrange("t (s c) -> t s c", s=14)
                        self.dma("sp", o3[64 * l:64 * l + 64, 0:6, :], src[:, 6 * l:6 * l + 6, :], [self.bR["OPS"]], [op])
                        self.dma("sp", o3[64 * l:64 * l + 64, 6:8, :], src[:, 12:14, :], [self.bR["OPS"]], [op])
                        gg, cc = tok[l] // 128, (tok[l] % 128) // 64
                        self.dma("sp", gc[64 * l:64 * l + 64, :], R["GCs"][gg, cc][:, 8 * l:8 * l + 8], [self.bR["GCs"]], [gc])
                        self.dma("sp", ec[:, 4 * l:4 * l + 4], R["ECs"][gg, cc][:, 4 * l:4 * l + 4], [self.bR["ECs"]], [ec])
                    for h in range(8):
                        bt = BT0 if h < 4 else BT1
                        for X, slot in enumerate((3, 0, 2, 1)):
                            for l in range(2):
                                r0 = 64 * l
                                self.tr(bt[r0:r0 + 64, (h % 4) * 256 + X * 64:(h % 4) * 256 + X * 64 + 64],
                                        op[r0:r0 + 64, slot * 512 + h * 64:slot * 512 + h * 64 + 64],
                                        ident[r0:r0 + 64, r0:r0 + 64], [op, ident], [bt])
                    self.cp("dve", XT[:, 0:1024], BT0[:, :], [BT0], [XT])
                    self.cp("act", XT[:, 1024:2048], BT1[:, :], [BT1], [XT])
                    for h in range(8):
                        for l in range(2):
                            r0 = 64 * l
                            xb = h * 256
                            self.mm(PX[r0:r0 + 64, h * 128:h * 128 + 128], XT[r0:r0 + 64, xb + 128:xb + 192],
                                    XT[r0:r0 + 64, xb:xb + 128], True, True, [XT], [PX])
                            self.mm(PN[r0:r0 + 64, h * 128:h * 128 + 128], XT[r0:r0 + 64, xb + 192:xb + 256],
                                    XT[r0:r0 + 64, xb:xb + 128], True, True, [XT], [PN])
                            self.mm(F4[r0:r0 + 64, h * 64:h * 64 + 64], XT[r0:r0 + 64, xb:xb + 64],
                                    XT[r0:r0 + 64, xb + 128:xb + 192], True, True, [XT], [F4])
                    G3 = GS[:, :].rearrange("p (h c) -> p h c", h=8)
                    self.tt("dve", G3[:, :, 0:128], PX[:, :].rearrange("p (h c) -> p h c", h=8), M1b, ALU.mult, [PX, M1], [GS])
                    self.tt("dve", G3[:, :, 128:256], PN[:, :].rearrange("p (h c) -> p h c", h=8), M1b, ALU.mult, [PN, M1], [GS])
                    self.tt("dve", G3[:, :, 256:320], F4[:, :].rearrange("p (h c) -> p h c", h=8), M3b, ALU.mult, [F4, M3], [GS])
                    xa = XA[0]
                    xa3 = xa[:, :].rearrange("p (h c) -> p h c", h=8)
                    self.cp("dve", xa3[:, :, 0:64], op[:, 3 * 512:4 * 512].rearrange("p (h c) -> p h c", h=8), [op], [xa])
                    for h in range(8):
                        for l in range(2):
                            r0 = 64 * l
                            self.mm(F5[r0:r0 + 64, h * 64:h * 64 + 64], GS[r0:r0 + 64, h * 320 + 128:h * 320 + 192],
                                    op[r0:r0 + 64, 6 * 512 + h * 64:6 * 512 + h * 64 + 64], True, True, [GS, op], [F5])
                    self.cp("act", xa3[:, :, 64:128], F5[:, :].rearrange("p (h c) -> p h c", h=8), [F5], [xa])
                    xc = 0
                    for k in range(6):
                        xcur, xnext = XA[xc], XA[1 - xc]
                        if k == 0:
                            nsrc = GS
                            noff = lambda h: h * 320 + 256
                            ntoff = lambda h: h * 320
                        else:
                            nsrc = NN[(k - 1) % 2]
                            noff = lambda h: h * 128
                            ntoff = lambda h: h * 128 + 64
                        if k < 5:
                            nn = NN[k % 2]
                            for h in range(8):
                                for l in range(2):
                                    r0 = 64 * l
                                    self.mm(PN[r0:r0 + 64, h * 128:h * 128 + 64], nsrc[r0:r0 + 64, ntoff(h):ntoff(h) + 64],
                                            nsrc[r0:r0 + 64, noff(h):noff(h) + 64], True, True, [nsrc], [PN])
                                    self.mm(PN[r0:r0 + 64, h * 128 + 64:h * 128 + 128], nsrc[r0:r0 + 64, noff(h):noff(h) + 64],
                                            nsrc[r0:r0 + 64, ntoff(h):ntoff(h) + 64], True, True, [nsrc], [PN])
                        if k < 5:
                            self.cp("act", nn[:, :], PN[:, :], [PN], [nn])
                        for h in range(8):
                            for l in range(2):
                                r0 = 64 * l
                                self.mm(PX[r0:r0 + 64, h * 128:h * 128 + 128], nsrc[r0:r0 + 64, ntoff(h):ntoff(h) + 64],
                                        xcur[r0:r0 + 64, h * 128:h * 128 + 128], True, True, [nsrc, xcur], [PX])
                        self.tt("dve", xnext[:, :], PX[:, :], xcur[:, :], ALU.add, [PX, xcur], [xnext])
                        xc = 1 - xc
                    x6 = XA[xc]
                    x63 = x6[:, :].rearrange("p (h c) -> p h c", h=8)
                    for h in range(8):
                        for l in range(2):
                            r0 = 64 * l
                            self.tr(BT1[r0:r0 + 64, h * 64:h * 64 + 64], x6[r0:r0 + 64, h * 128:h * 128 + 64],
                                    ident[r0:r0 + 64, r0:r0 + 64], [x6, ident], [BT1])
                    self.cp("dve", GT[:, :], BT1[:, 0:512], [BT1], [GT])
                    tbc, tbn = Tb[cur], Tb[1 - cur]
                    for h in range(8):
                        for l in range(2):
                            r0 = 64 * l
                            self.mm(F4[r0:r0 + 64, h * 64:h * 64 + 64], GT[r0:r0 + 64, h * 64:h * 64 + 64],
                                    tbc[r0:r0 + 64, h * 64:h * 64 + 64], True, True, [GT, tbc], [F4])
                    self.tt("dve", Pb[:, :].rearrange("p (h c) -> p h c", h=8), F4[:, :].rearrange("p (h c) -> p h c", h=8),
                            x63[:, :, 64:128], ALU.add, [F4, x6], [Pb])
                    for h in range(8):
                        for l in range(2):
                            r0 = 64 * l
                            yo = PN[r0:r0 + 64, h * 64:h * 64 + 64]
                            self.mm(yo, XT[r0:r0 + 64, h * 256 + 64:h * 256 + 128], tbc[r0:r0 + 64, h * 64:h * 64 + 64],
                                    True, False, [XT, tbc], [PN])
                            self.mm(yo, GS[r0:r0 + 64, h * 320 + 192:h * 320 + 256],
                                    op[r0:r0 + 64, 6 * 512 + h * 64:6 * 512 + h * 64 + 64], False, False, [GS, op], [PN])
                            self.mm(yo, GS[r0:r0 + 64, h * 320 + 64:h * 320 + 128], Pb[r0:r0 + 64, h * 64:h * 64 + 64],
                                    False, True, [GS, Pb], [PN])
                    self.cp("act", ysb[:, :], PN[:, 0:512], [PN], [ysb])
                    for l in range(2):
                        self.dma("pool", R["YS"][l, tok[l]:tok[l] + 64, :], ysb[64 * l:64 * l + 64, :], [ysb], [self.bR["YS"]])
                    for h in range(8):
                        for l in range(2):
                            r0 = 64 * l
                            to = F5[r0:r0 + 64, h * 64:h * 64 + 64]
                            self.mm(to, op[r0:r0 + 64, 1 * 512 + h * 64:1 * 512 + h * 64 + 64],
                                    op[r0:r0 + 64, 6 * 512 + h * 64:6 * 512 + h * 64 + 64], True, False, [op], [F5])
                            self.mm(to, op[r0:r0 + 64, 2 * 512 + h * 64:2 * 512 + h * 64 + 64],
                                    Pb[r0:r0 + 64, h * 64:h * 64 + 64], False, True, [op, Pb], [F5])
                    self.tt("dve", Ttmp[:, :], F5[:, :], T32[:, :], ALU.add, [F5, T32], [Ttmp])
                    self.tt("dve", T32[:, :].rearrange("p (h c) -> p h c", h=8), Ttmp[:, :].rearrange("p (h c) -> p h c", h=8),
                            gc[:, :].unsqueeze(2).broadcast_to([128, 8, 64]), ALU.mult, [Ttmp, gc], [T32])
                    self.cp("act", tbn[:, :], T32[:, :], [T32], [tbn])
                    sbc, sbn = Sb[cur], Sb[1 - cur]
                    for X, slot in enumerate((4, 5)):
                        for h in range(4):
                            c0 = (X * 4 + h) * 128
                            self.tr(BT0[:, c0:c0 + 128], op[:, slot * 512 + h * 128:slot * 512 + h * 128 + 128], ident[:, :],
                                    [op, ident], [BT0])
                    self.cp("dve", HT[:, :], BT0[:, :], [BT0], [HT])
                    for h in range(4):
                        for l in range(2):
                            r0 = 64 * l
                            self.mm(PX[r0:r0 + 64, h * 64:h * 64 + 64], HT[:, (4 + h) * 128 + r0:(4 + h) * 128 + r0 + 64],
                                    HT[:, h * 128 + r0:h * 128 + r0 + 64], True, True, [HT], [PX])
                    for l in range(2):
                        r0 = 64 * l
                        sc4 = SCz[r0:r0 + 64, :].rearrange("p (h l t) -> p h l t", h=4, l=2)
                        self.tt("dve", sc4[:, :, l, :], PX[r0:r0 + 64, 0:256].rearrange("p (h t) -> p h t", h=4),
                                MI[r0:r0 + 64, :].unsqueeze(1).broadcast_to([64, 4, 64]), ALU.mult, [PX, MI], [SCz])
                    for h in range(4):
                        for l in range(2):
                            r0 = 64 * l
                            oo = PX[r0:r0 + 64, 512 + h * 128:512 + h * 128 + 128]
                            self.mm(oo, SCz[:, (h * 2 + l) * 64:(h * 2 + l) * 64 + 64], op[:, 7 * 512 + h * 128:7 * 512 + h * 128 + 128],
                                    True, False, [SCz, op], [PX])
                            self.mm(oo, HT[:, h * 128 + r0:h * 128 + r0 + 64], sbc[:, (l * 4 + h) * 128:(l * 4 + h) * 128 + 128],
                                    False, True, [HT, sbc], [PX])
                    self.cp("act", osb[:, :], PX[:, 512:1024], [PX], [osb])
                    for l in range(2):
                        self.dma("pool", R["OSC"][l, tok[l]:tok[l] + 64, :], osb[64 * l:64 * l + 64, :], [osb], [self.bR["OSC"]])
                    for l in range(2):
                        r0 = 64 * l
                        pu = PN if l == 0 else F4
                        pu0 = 512 if l == 0 else 0
                        for h in range(4):
                            self.mm(pu[:, pu0 + h * 128:pu0 + h * 128 + 128], op[r0:r0 + 64, 5 * 512 + h * 128:5 * 512 + h * 128 + 128],
                                    op[r0:r0 + 64, 7 * 512 + h * 128:7 * 512 + h * 128 + 128], True, True, [op], [pu])
                        self.tt("dve", Stmp[:, :], pu[:, pu0:pu0 + 512], S32[:, l * 512:(l + 1) * 512], ALU.add, [pu, S32], [Stmp])
                        self.tt("dve", S32[:, l * 512:(l + 1) * 512].rearrange("p (h c) -> p h c", h=4),
                                Stmp[:, :].rearrange("p (h c) -> p h c", h=4),
                                ec[:, 4 * l:4 * l + 4].unsqueeze(2).broadcast_to([128, 4, 128]), ALU.mult, [Stmp, ec], [S32])
                    self.cp("act", sbn[:, :], S32[:, :], [S32], [sbn])
                    cur = 1 - cur
                if kind == 0:
                    for l in range(2):
                        self.dma("pool", O["nsr"][si, l], T32[64 * l:64 * l + 64, :], [T32], [self.bO["nsr"]])
                        self.dma("pool", O["nsh"][si, l], S32[:, l * 512:(l + 1) * 512], [S32], [self.bO["nsh"]])

    def ctx_tiles(self):
        out = []
        per = max(1, 512 // self.L_ctx)
        si = 0
        while si < self.n_ctx:
            ns = min(per, self.n_ctx - si)
            out.append((si * self.L_ctx, ns * self.L_ctx))
            si += ns
        return out

    def phaseC1(self):
        C, R, I = self.C, self.R, self.I
        ident, prow, mods, selv = C["ident"], C["prow"], C["mods"], C["selv"]
        hnw_bc, lnw_bc, lnb_bc = prow[:, 0:512], prow[:, 2048:2560], prow[:, 2560:3072]
        xTv = I["xT"].rearrange("(k p) t -> p k t", p=128)
        xov = I["xTo"].rearrange("(k p) t -> p k t", p=128)
        x1v = R["X1T"].rearrange("(k p) t -> p k t", p=128)
        x1wv = R["X1W"].rearrange("(k p) t -> p k t", p=128)
        NCT = self.n_ctx * self.L_ctx
        QL, NQ = self.QL, self.NQ
        WL = QL + 128
        with self.scope() as st:
            WO = self.sb(st, "WO", [128, 8 * D], BF16)
            wv = I["w_out"].rearrange("(k p) c -> p k c", p=128)
            for k in range(KD):
                self.dma("pool", WO[:, k * D:(k + 1) * D], wv[:, k, :], [], [WO])
            yf = [self.sb(st, f"yf{i}", [128, 512], F32) for i in range(2)]
            yb = [self.sb(st, f"yb{i}", [128, 512], F32) for i in range(2)]
            of = [self.sb(st, f"of{i}", [128, 512], F32) for i in range(2)]
            ob = [self.sb(st, f"ob{i}", [128, 512], F32) for i in range(2)]
            cx = [self.sb(st, f"cx{i}", [128, 1536], BF16) for i in range(2)]
            cxa = self.sb(st, "cxa", [128, 1536], F32)
            ft = [self.sb(st, f"c1f{i}", [128, 512], F32) for i in range(4)]
            sm = [self.sb(st, f"c1s{i}", [128, 8], F32) for i in range(6)]
            MIX = self.sb(st, "MIX", [128, 1024], BF16)
            MIXT = self.sb(st, "MIXT", [128, 8 * 512], BF16)
            xt = self.sb(st, "c1xt", [128, 8 * 512], F32)
            x1 = self.sb(st, "c1x1", [128, 8 * 512], F32)
            PT = self.ps(st, "c1PT", [128, 1024], BF16)
            PO = [self.ps(st, f"c1PO{i}", [128, 512], F32) for i in range(2)]
            self._c1i = 0

            def post(y, o, cxt, gq):
                sq, osq = ft[1], ft[3]
                self.act(sq[:, :], y[:, :], AF.Square, [y], [sq])
                y3 = y[:, :].rearrange("p (h j) -> p h j", h=8)
                self.red(sm[0][:, :], y3, [y], [sm[0]])
                self.red(sm[1][:, :], sq[:, :].rearrange("p (h j) -> p h j", h=8), [sq], [sm[1]])
                self.ts("dve", sm[2][:, :], sm[0][:, :], 1.0 / 64, None, ALU.mult, ALU.bypass, [sm[0]], [sm[2]])
                self.tt("dve", sm[3][:, :], sm[2][:, :], sm[2][:, :], ALU.mult, [sm[2]], [sm[3]])
                self.stt(sm[4][:, :], sm[1][:, :], 1.0 / 64, sm[3][:, :], ALU.mult, ALU.subtract, [sm[1], sm[3]], [sm[4]])
                self.act(sm[5][:, :], sm[4][:, :], AF.Sqrt, [sm[4]], [sm[5]], bias=GN_EPS)
                self.recip(sm[5][:, :], sm[5][:, :], [sm[5]], [sm[5]])
                self.tt("dve", y3, y3, sm[2][:, :].unsqueeze(2).broadcast_to([128, 8, 64]), ALU.subtract, [y, sm[2]], [y])
                self.tt("dve", y3, y3, sm[5][:, :].unsqueeze(2).broadcast_to([128, 8, 64]), ALU.mult, [y, sm[5]], [y])
                self.tt("dve", y[:, :], y[:, :], lnw_bc, ALU.mult, [y, prow], [y])
                self.tt("dve", y[:, :], y[:, :], lnb_bc, ALU.add, [y, prow], [y])
                self.tt("dve", y[:, :], y[:, :], cxt[:, 512:1024], ALU.add, [y, cxt], [y])
                self.tt("dve", MIX[:, 512:1024], y[:, :], cxt[:, 0:512], ALU.mult, [y, cxt], [MIX])
                self.act(osq[:, :], o[:, :], AF.Square, [o], [osq])
                self.red(sm[0][:, 0:4], osq[:, :].rearrange("p (h j) -> p h j", h=4), [osq], [sm[0]])
                self.act(sm[1][:, 0:4], sm[0][:, 0:4], AF.Sqrt, [sm[0]], [sm[1]], scale=1.0 / 128, bias=RMS_EPS)
                self.recip(sm[1][:, 0:4], sm[1][:, 0:4], [sm[1]], [sm[1]])
                o3 = o[:, :].rearrange("p (h j) -> p h j", h=4)
                self.tt("dve", o3, o3, sm[1][:, 0:4].unsqueeze(2).broadcast_to([128, 4, 128]), ALU.mult, [o, sm[1]], [o])
                self.tt("dve", o[:, :], o[:, :], hnw_bc, ALU.mult, [o, prow], [o])
                self.tt("dve", MIX[:, 0:512], o[:, :], cxt[:, 1024:1536], ALU.mult, [o, cxt], [MIX])
                for k in range(KD):
                    self.tr(PT[:, k * 128:(k + 1) * 128], MIX[:, k * 128:(k + 1) * 128], ident[:, :], [MIX, ident], [PT])
                self.cp("act", MIXT[:, :].rearrange("p (k t) -> p k t", k=8)[:, :, gq * 128:(gq + 1) * 128],
                        PT[:, :].rearrange("p (k t) -> p k t", k=8), [PT], [MIXT])

            def dense(n, kind, dst):
                for m in range(KD):
                    po = PO[m % 2]
                    for k in range(KD):
                        self.mm(po[:, 0:n], WO[:, k * D + m * 128:k * D + (m + 1) * 128], MIXT[:, k * 512:k * 512 + n],
                                k == 0, k == KD - 1, [WO, MIXT], [po])
                    self.stt(x1[:, m * n:(m + 1) * n], po[:, 0:n], mods[:, 2, kind, m:m + 1], xt[:, m * n:(m + 1) * n],
                             ALU.mult, ALU.add, [po, mods, xt], [x1])
                self.dma("pool", dst, x1[:, 0:8 * n].rearrange("p (k t) -> p k t", k=8), [x1], [])

            gi = 0
            for (t0, n) in self.ctx_tiles():
                self.dma("sp", xt[:, 0:8 * n].rearrange("p (k t) -> p k t", k=8), xTv[:, :, t0:t0 + n], [], [xt])
                for gq in range(n // 128):
                    ta = t0 + gq * 128
                    b = gi % 2
                    gi += 1
                    self.dma("sp", yf[b][:, :], R["YS"][0, ta:ta + 128, :], [], [yf[b]])
                    self.dma("sp", yb[b][:, :], R["YS"][1, ta:ta + 128, :], [], [yb[b]])
                    self.dma("sp", of[b][:, :], R["OSC"][0, ta:ta + 128, :], [], [of[b]])
                    self.dma("sp", ob[b][:, :], R["OSC"][1, ta:ta + 128, :], [], [ob[b]])
                    self.dma("sp", cx[b][:, :], R["CX"][ta:ta + 128, :], [], [cx[b]])
                    y, o = ft[0], ft[2]
                    self.tt("dve", y[:, :], yf[b][:, :], yb[b][:, :], ALU.add, [yf[b], yb[b]], [y])
                    self.tt("dve", o[:, :], of[b][:, :], ob[b][:, :], ALU.add, [of[b], ob[b]], [o])
                    post(y, o, cx[b], gq)
                dense(n, 0, x1v[:, :, t0:t0 + n])
            w0 = 0
            while w0 < WL:
                n = min(512, WL - w0)
                self.dma("sp", xt[:, 0:8 * n].rearrange("p (k t) -> p k t", k=8), xov[:, :, w0:w0 + n], [], [xt])
                for gq in range(n // 128):
                    wj = w0 + gq * 128
                    y, o = ft[0], ft[2]
                    for q in range(NQ):
                        b = gi % 2
                        gi += 1
                        ta = NCT + q * QL - 64 + wj
                        lo, hi = max(ta, NCT), min(ta + 128, NCT + self.L_lat)
                        p0, pn = lo - ta, hi - lo
                        if pn < 128:
                            for tl in (yf[b], yb[b], of[b], ob[b], cx[b]):
                                self.memset("dve", tl[:, :], 0.0, [tl])
                        if pn > 0:
                            self.dma("sp", yf[b][p0:p0 + pn, :], R["YS"][0, lo:hi, :], [], [yf[b]])
                            self.dma("sp", yb[b][p0:p0 + pn, :], R["YS"][1, lo:hi, :], [], [yb[b]])
                            self.dma("sp", of[b][p0:p0 + pn, :], R["OSC"][0, lo:hi, :], [], [of[b]])
                            self.dma("sp", ob[b][p0:p0 + pn, :], R["OSC"][1, lo:hi, :], [], [ob[b]])
                            self.dma("sp", cx[b][p0:p0 + pn, :], R["CX"][lo:hi, :], [], [cx[b]])
                        sc = selv[:, q:q + 1]
                        t_y, t_o = ft[1], ft[3]
                        self.tt("dve", t_y[:, :], yf[b][:, :], yb[b][:, :], ALU.add, [yf[b], yb[b]], [t_y])
                        self.tt("dve", t_o[:, :], of[b][:, :], ob[b][:, :], ALU.add, [of[b], ob[b]], [t_o])
                        if q == 0:
                            self.ts("dve", y[:, :], t_y[:, :], sc, None, ALU.mult, ALU.bypass, [t_y, selv], [y])
                            self.ts("dve", o[:, :], t_o[:, :], sc, None, ALU.mult, ALU.bypass, [t_o, selv], [o])
                            self.ts("dve", cxa[:, :], cx[b][:, :], sc, None, ALU.mult, ALU.bypass, [cx[b], selv], [cxa])
                        else:
                            self.stt(y[:, :], t_y[:, :], sc, y[:, :], ALU.mult, ALU.add, [t_y, selv, y], [y])
                            self.stt(o[:, :], t_o[:, :], sc, o[:, :], ALU.mult, ALU.add, [t_o, selv, o], [o])
                            self.stt(cxa[:, :], cx[b][:, :], sc, cxa[:, :], ALU.mult, ALU.add, [cx[b], selv, cxa], [cxa])
                    post(y, o, cxa, gq)
                dense(n, 1, x1wv[:, :, w0:w0 + n])
                w0 += n

    def phaseC2(self):
        C, R, I, O = self.C, self.R, self.I, self.O
        mods, fnw, selv = C["mods"], C["fnw"], C["selv"]
        x1v = R["X1T"].rearrange("(k p) t -> p k t", p=128)
        x1wv = R["X1W"].rearrange("(k p) t -> p k t", p=128)
        yTv = O["yT"].rearrange("(k p) t -> p k t", p=128)
        NWM = 640
        NCT = self.n_ctx * self.L_ctx
        QL, NQ = self.QL, self.NQ
        with self.scope() as st:
            WG = self.sb(st, "WG", [128, 8 * DFF], BF16)
            WU = self.sb(st, "WU", [128, 8 * DFF], BF16)
            for (W, nm) in ((WG, "w_gate"), (WU, "w_up")):
                wv = I[nm].rearrange("(k p) c -> p k c", p=128)
                for k in range(KD):
                    self.dma("pool", W[:, k * DFF:(k + 1) * DFF], wv[:, k, :], [], [W])
            cw = self.sb(st, "cw", [128, NF * 9], F32)
            self.dma("sp", cw[:, :], I["convw"], [], [cw])
            cvb = self.sb(st, "cvb", [128, NF], F32)
            self.dma("sp", cvb[:, :], I["cvb"], [], [cvb])
            wd = [self.sb(st, f"wd{i}", [128, NF * 128], BF16) for i in range(2)]
            wdv = I["w_down"].rearrange("(f p) c -> p f c", p=128)
            x1w = self.sb(st, "x1w", [128, 8 * NWM], F32)
            tiles = {
                "sq": self.sb(st, "c2sq", [128, 8 * NWM], BF16),
                "rstd": self.sb(st, "c2rstd", [128, NWM], F32),
                "ntmp": [self.sb(st, f"c2ntmp{i}", [128, NWM], F32) for i in range(2)],
            }
            h2T = self.sb(st, "h2T", [128, 8 * NWM], BF16)
            HM = self.sb(st, "HM", [128, NF * 512], BF16)
            GP = [self.sb(st, f"GP{i}", [128, 10 * 66], F32) for i in range(2)]
            GPc = [self.sb(st, f"GPc{i}", [128, 2 * 258], F32) for i in range(2)]
            acc = [self.sb(st, f"c2acc{i}", [128, 512], F32) for i in range(2)]
            gl = [self.sb(st, f"c2gl{i}", [128, 512], F32) for i in range(2)]
            x2 = self.sb(st, "x2", [128, 8 * 512], F32)
            PG = [self.ps(st, f"c2PG{i}", [128, 1024], F32) for i in range(2)]
            PU = [self.ps(st, f"c2PU{i}", [128, 512], F32) for i in range(2)]
            PD = [self.ps(st, f"c2PD{i}", [128, 512], F32) for i in range(2)]
            fi = 0
            wdi = 0
            work = [(0, t0, n, None) for (t0, n) in self.ctx_tiles()]
            for r0 in range(0, QL // 64, 8):
                work.append((1, r0, min(8, QL // 64 - r0) * 64, None))
            for (kind, t0, n, _) in work:
                if kind == 1:
                    r0 = t0
                    nwv = n + 128
                    coff = 64
                    self.dma("sp", x1w[:, 0:8 * nwv].rearrange("p (k t) -> p k t", k=8),
                             x1wv[:, :, 64 * r0:64 * r0 + nwv], [], [x1w])
                else:
                    nwv, coff = n, 0
                    self.dma("sp", x1w[:, 0:8 * nwv].rearrange("p (k t) -> p k t", k=8), x1v[:, :, t0:t0 + n], [], [x1w])
                self.norm_mod(tiles, x1w, nwv, PU[0], mods[:, 3, kind, :], mods[:, 4, kind, :], h2T, mods)
                nrow_t = n // 64
                for gpz in (GP if kind == 1 else GPc):
                    self.memset("dve", gpz[:, :], 0.0, [gpz])
                for f in range(NF):
                    pg, pu = PG[fi % 2], PU[fi % 2]
                    ac, g_ = acc[fi % 2], gl[fi % 2]
                    c0 = 0
                    while c0 < nwv:
                        cn = min(512, nwv - c0)
                        for k in range(KD):
                            self.mm(pg[:, c0:c0 + cn], WG[:, k * DFF + f * 128:k * DFF + (f + 1) * 128],
                                    h2T[:, k * nwv + c0:k * nwv + c0 + cn], k == 0, k == KD - 1, [WG, h2T], [pg])
                        c0 += cn
                    for k in range(KD):
                        self.mm(pu[:, 0:n], WU[:, k * DFF + f * 128:k * DFF + (f + 1) * 128],
                                h2T[:, k * nwv + coff:k * nwv + coff + n], k == 0, k == KD - 1, [WU, h2T], [pu])
                    if kind == 1:
                        gp = GP[fi % 2]
                        gp3 = gp[:, :].rearrange("p (r c) -> p r c", c=66)
                        nrw = nwv // 64
                        self.cp("act", gp3[:, 0:nrw, 1:65], pg[:, 0:nwv].rearrange("p (r c) -> p r c", c=64), [pg], [gp])
                        if r0 == 0:
                            self.ts("dve", gp3[:, 0, 1:65], gp3[:, 0, 1:65], selv[:, NQ:NQ + 1], None, ALU.mult, ALU.bypass,
                                    [gp, selv], [gp])
                        if r0 + nrow_t == QL // 64:
                            self.ts("dve", gp3[:, nrw - 1, 1:65], gp3[:, nrw - 1, 1:65], selv[:, NQ + 1:NQ + 2], None,
                                    ALU.mult, ALU.bypass, [gp, selv], [gp])
                        a3 = ac[:, 0:n].rearrange("p (r c) -> p r c", c=64)
                        first = True
                        for dy in range(3):
                            for dx in range(3):
                                src = gp3[:, dy:dy + nrow_t, dx:dx + 64]
                                wcol = cw[:, f * 9 + dy * 3 + dx:f * 9 + dy * 3 + dx + 1]
                                if first:
                                    self.ts("dve", a3, src, wcol, cvb[:, f:f + 1], ALU.mult, ALU.add, [gp, cw, cvb], [ac])
                                    first = False
                                else:
                                    self.stt(a3, src, wcol, a3, ALU.mult, ALU.add, [gp, cw, ac], [ac])
                    else:
                        gp = GPc[fi % 2]
                        nsq = n // self.L_ctx
                        Lc = self.L_ctx
                        gp3 = gp[:, 0:nsq * (Lc + 2)].rearrange("p (r c) -> p r c", c=Lc + 2)
                        self.cp("act", gp3[:, :, 1:Lc + 1], pg[:, 0:n].rearrange("p (r c) -> p r c", c=Lc), [pg], [gp])
                        a3 = ac[:, 0:n].rearrange("p (r c) -> p r c", c=Lc)
                        for dx in range(3):
                            src = gp3[:, :, dx:dx + Lc]
                            wcol = cw[:, f * 9 + 3 + dx:f * 9 + 3 + dx + 1]
                            if dx == 0:
                                self.ts("dve", a3, src, wcol, cvb[:, f:f + 1], ALU.mult, ALU.add, [gp, cw, cvb], [ac])
                            else:
                                self.stt(a3, src, wcol, a3, ALU.mult, ALU.add, [gp, cw, ac], [ac])
                    self.act(g_[:, 0:n], ac[:, 0:n], AF.Gelu_apprx_tanh, [ac], [g_])
                    self.tt("dve", HM[:, f * 512:f * 512 + n], pu[:, 0:n], g_[:, 0:n], ALU.mult, [pu, g_], [HM])
                    fi += 1
                for m in range(KD):
                    w_ = wd[wdi % 2]
                    pd = PD[wdi % 2]
                    wdi += 1
                    self.dma("pool", w_[:, :].rearrange("p (f c) -> p f c", c=128), wdv[:, :, m * 128:(m + 1) * 128], [], [w_])
                    for f in range(NF):
                        self.mm(pd[:, 0:n], w_[:, f * 128:(f + 1) * 128], HM[:, f * 512:f * 512 + n], f == 0, f == NF - 1,
                                [w_, HM], [pd])
                    self.stt(x2[:, m * n:(m + 1) * n], pd[:, 0:n], mods[:, 5, kind, m:m + 1],
                             x1w[:, m * nwv + coff:m * nwv + coff + n], ALU.mult, ALU.add, [pd, mods, x1w], [x2])
                rstd = tiles["rstd"]
                self.rms_rstd(tiles, x2, n, PU[1], rstd)
                for k in range(KD):
                    self.stt(x2[:, k * n:(k + 1) * n], x2[:, k * n:(k + 1) * n], fnw[:, k:k + 1], rstd[:, 0:n],
                             ALU.mult, ALU.mult, [x2, fnw, rstd], [x2])
                od = NCT + 64 * t0 if kind == 1 else t0
                self.dma("pool", yTv[:, :, od:od + n], x2[:, 0:8 * n].rearrange("p (k t) -> p k t", k=8), [x2], [])

    def build(self, phases=("0", "A", "S", "C1", "C2")):
        self.declare_io()
        with contextlib.ExitStack() as gst:
            mst = contextlib.ExitStack()
            self.phase0(gst, mst)
            if "A" in phases:
                self.phaseA()
            if "S" in phases:
                self.scan()
            if "C1" in phases:
                self.phaseC1()
            self.S.barrier()
            mst.close()
            if "C2" in phases:
                self.phaseC2()
            self.S.barrier()
            fin = list(self.final)
            self.S.finalize(fin)
        return self.nc


def fm(v):
    return np.ascontiguousarray(np.asarray(v, np.float32).reshape(-1, 128).T)


def shared_inputs(inp):
    f = lambda a: np.ascontiguousarray(np.asarray(a, np.float32))
    S = {}
    S["ada_w"] = f(inp["ada_w"][0])
    S["adab"] = fm(inp["ada_b"][0])
    S["nmw"] = fm(inp["norm_mix_w"][0])
    S["nfw"] = fm(inp["norm_ffn_w"][0])
    S["fnw"] = fm(inp["final_norm_w"])
    S["w_in"] = f(inp["w_in"][0])
    S["rwkv_conv"] = f(inp["rwkv_conv"][0])
    S["hgrn_lb"] = f(inp["hgrn_lb"]).reshape(1, 2048)
    S["prow"] = np.concatenate([f(inp[k][0]).reshape(-1) for k in
                                ("hgrn_norm_w", "rwkv_k_k", "rwkv_k_a", "rwkv_r_k", "rwkv_ln_w", "rwkv_ln_b")]
                               + [f(inp["rwkv_w0"][0, 0]), f(inp["rwkv_w0"][0, 1]), f(inp["rwkv_a0"][0])]).reshape(1, 4608)
    w2x = np.zeros((32, 1536), np.float32)
    w2x[0:32, 0:512] = inp["rwkv_w2"][0, 0]
    w2x[0:32, 512:1024] = inp["rwkv_w2"][0, 1]
    w2x[0:32, 1024:1536] = inp["rwkv_a2"][0]
    S["w2x"] = w2x
    S["g2"] = f(inp["rwkv_g2"][0])
    S["w_out"] = f(inp["w_out"][0])
    S["w_gate"] = f(inp["ffn_w_gate"][0])
    S["w_up"] = f(inp["ffn_w_up"][0])
    S["w_down"] = f(inp["ffn_w_down"][0])
    cw = f(inp["ffn_conv"][0]).reshape(9, NF, 128)
    S["convw"] = np.ascontiguousarray(cw.transpose(2, 1, 0).reshape(128, NF * 9))
    S["cvb"] = fm(inp["ffn_conv_b"][0])
    return S


def core_inputs(inp, S, ctx_ids, lat_b, q):
    f = lambda a: np.ascontiguousarray(np.asarray(a, np.float32))
    L_lat = inp["x_sample"].shape[1]
    QL = min(1024, L_lat)
    NQ = L_lat // QL
    xl = f(inp["x_sample"][lat_b])
    xs = [f(inp["x_prompt"][i]) for i in ctx_ids] + [xl]
    x = np.concatenate(xs, axis=0)
    m = dict(S)
    m["xT"] = np.ascontiguousarray(x.T)
    xw = np.zeros((QL + 128, D), np.float32)
    lo, hi = q * QL - 64, q * QL + QL + 64
    a, b = max(lo, 0), min(hi, L_lat)
    xw[a - lo:b - lo] = xl[a:b]
    m["xTo"] = np.ascontiguousarray(xw.T)
    sel = np.zeros((128, NQ + 2), np.float32)
    sel[:, q] = 1.0
    sel[:, NQ] = 0.0 if q == 0 else 1.0
    sel[:, NQ + 1] = 0.0 if q == NQ - 1 else 1.0
    m["selv"] = sel
    cv = np.stack([f(inp["c_ctx"]), f(inp["c"][lat_b])], axis=1)
    m["cvec"] = np.ascontiguousarray(cv.reshape(8, 128, 2).transpose(1, 0, 2).reshape(128, 16))
    sh = f(inp["state_hgrn"][lat_b, 0])
    m["st_h"] = np.ascontiguousarray(sh.transpose(0, 2, 1, 3).reshape(2, 128, 512))
    sr = f(inp["state_rwkv"][lat_b, 0])
    m["st_r"] = np.ascontiguousarray(sr.transpose(0, 3, 1, 2).reshape(2, 64, 512))
    return m


_PROG_CACHE = {}


def get_prog(n_ctx, L_ctx, L_lat, debug=(), phases=("0", "A", "S", "C1", "C2")):
    key = (n_ctx, L_ctx, L_lat, tuple(sorted(debug)), tuple(phases))
    if key not in _PROG_CACHE:
        b = Builder(n_ctx, L_ctx, L_lat, debug)
        nc = b.build(phases)
        _PROG_CACHE[key] = (nc, b)
    return _PROG_CACHE[key]


def kernel(**inp):
    B, L_ctx = inp["x_prompt"].shape[0], inp["x_prompt"].shape[1]
    DB, L_lat = inp["x_sample"].shape[0], inp["x_sample"].shape[1]
    n_ctx = B // N_CORES
    QL = min(1024, L_lat)
    NQ = L_lat // QL
    assert DB * NQ == N_CORES
    nc, _ = get_prog(n_ctx, L_ctx, L_lat)
    S = shared_inputs(inp)
    in_maps = []
    for c in range(N_CORES):
        ctx_ids = list(range(c * n_ctx, (c + 1) * n_ctx))
        in_maps.append(core_inputs(inp, S, ctx_ids, c % DB, c // DB))
    res = run_bass_kernel_spmd(nc, in_maps, core_ids=list(range(N_CORES)))
    y_prompt = np.zeros((B, L_ctx, D), np.float32)
    y_sample = np.zeros((DB, L_lat, D), np.float32)
    nsh = np.zeros((B, 1, 2, 4, 128, 128), np.float32)
    nsr = np.zeros((B, 1, 2, 8, 64, 64), np.float32)
    for c in range(N_CORES):
        r = res.results[c]
        y = np.asarray(r["yT"]).T
        for i in range(n_ctx):
            y_prompt[c * n_ctx + i] = y[i * L_ctx:(i + 1) * L_ctx]
        b, q = c % DB, c // DB
        y_sample[b, q * QL:(q + 1) * QL] = y[n_ctx * L_ctx:]
        h = np.asarray(r["nsh"]).reshape(n_ctx, 2, 128, 4, 128).transpose(0, 1, 3, 2, 4)
        nsh[c * n_ctx:(c + 1) * n_ctx, 0] = h
        rr = np.asarray(r["nsr"]).reshape(n_ctx, 2, 64, 8, 64).transpose(0, 1, 3, 4, 2)
        nsr[c * n_ctx:(c + 1) * n_ctx, 0] = rr
    return (y_prompt, y_sample, nsh, nsr)
```

```python
import contextlib
import numpy as np
import concourse.bass as bass
import concourse.mybir as mybir
from concourse.bass_utils import run_bass_kernel_spmd

F32 = mybir.dt.float32
BF16 = mybir.dt.bfloat16
AF = mybir.ActivationFunctionType
ALU = mybir.AluOpType
AX = mybir.AxisListType

D = 1024
KD = 8
WA = 512
PA = 2560
PB = 1728
PIN = 4288
DFF = 2816
NF = 22
DECAY = 0.6065306597
RMS_EPS = 1e-6
GN_EPS = 64e-5
GRID_W = 64
N_CORES = 8

ENGS = ("pe", "act", "dve", "pool", "sp")


class Buf:
    __slots__ = ("name", "lw", "rd", "rdd")

    def __init__(self, name):
        self.name = name
        self.lw = None
        self.rd = {}
        self.rdd = []


class Ins:
    __slots__ = ("eng", "fn", "deps", "is_dma", "sig", "sigval", "dsem", "dval")

    def __init__(self, eng, fn, deps, is_dma):
        self.eng = eng
        self.fn = fn
        self.deps = deps
        self.is_dma = is_dma
        self.sig = False
        self.sigval = 0
        self.dsem = None
        self.dval = 0


class Sched:
    NDMA = 16

    def __init__(self, nc):
        self.nc = nc
        self.ins = []
        self.last = {}
        self.dmas = []

    def barrier(self):
        deps = sorted(set(self.last.values()) | set(self.dmas))
        if not deps:
            return
        for e in ENGS:
            self.ins.append(Ins(e, None, list(deps), False))
        self.dmas = []

    def add(self, eng, fn, reads=(), writes=(), is_dma=False):
        deps = set()
        for b in reads:
            if b.lw is not None:
                deps.add(b.lw)
        for b in writes:
            if b.lw is not None:
                deps.add(b.lw)
            deps.update(b.rd.values())
            deps.update(b.rdd)
        i = len(self.ins)
        self.ins.append(Ins(eng, fn, sorted(deps), is_dma))
        self.last[eng] = i
        if is_dma:
            self.dmas.append(i)
        for b in reads:
            if is_dma:
                b.rdd.append(i)
            else:
                b.rd[eng] = i
        for b in writes:
            b.lw = i
            b.rd = {}
            b.rdd = []
        return i

    def finalize(self, final_bufs):
        nc = self.nc
        ins = self.ins
        fdeps = set()
        for b in final_bufs:
            if b.lw is not None:
                fdeps.add(b.lw)
        ins.append(Ins("sp", None, sorted(fdeps), False))
        for it in ins:
            for d in it.deps:
                dd = ins[d]
                if dd.eng == "pe" and it.eng == "pe" and not dd.is_dma and not it.is_dma:
                    continue
                dd.sig = True
        st = contextlib.ExitStack()
        esem = {e: st.enter_context(nc.semaphore(f"s_{e}")) for e in ("pe", "act", "dve", "pool")}
        dq = [e for e in ENGS if any(it.is_dma and it.eng == e for it in ins)]
        per = max(2, self.NDMA // max(1, len(dq)))
        dsems = []
        dpool = {}
        for e in dq:
            dpool[e] = list(range(len(dsems), len(dsems) + per))
            dsems += [st.enter_context(nc.semaphore(f"s_dma_{e}{i}")) for i in range(per)]
        dcount = [0] * len(dsems)
        dlast = [None] * len(dsems)
        ecount = {e: 0 for e in esem}
        rr = {e: 0 for e in dq}
        for k, it in enumerate(ins):
            if it.is_dma:
                s = dpool[it.eng][rr[it.eng] % per]
                rr[it.eng] += 1
                if dlast[s] is not None:
                    it.deps = sorted(set(it.deps) | {dlast[s]})
                dcount[s] += 16
                it.dsem = s
                it.dval = dcount[s]
                dlast[s] = k
            elif it.sig:
                ecount[it.eng] += 1
                it.sigval = ecount[it.eng]
        progs = {e: [] for e in ENGS}
        waited = {e: {} for e in ENGS}
        for it in ins:
            w = {}
            for d in it.deps:
                dd = ins[d]
                if dd.is_dma:
                    key = ("d", dd.dsem)
                    val = dd.dval
                else:
                    if dd.eng == "pe" and it.eng == "pe" and not it.is_dma:
                        continue
                    key = ("e", dd.eng)
                    val = dd.sigval
                if w.get(key, 0) < val:
                    w[key] = val
            wl = []
            for key, val in w.items():
                if waited[it.eng].get(key, 0) >= val:
                    continue
                waited[it.eng][key] = val
                wl.append((dsems[key[1]] if key[0] == "d" else esem[key[1]], val))
            progs[it.eng].append((wl, it))
        self.counts = {e: len(progs[e]) for e in ENGS}
        with nc.Block() as block:
            def runner(name):
                def run(eng):
                    for wl, it in progs[name]:
                        for sem, val in wl:
                            eng.wait_ge(sem, val)
                        if it.fn is None:
                            continue
                        r = it.fn(eng)
                        if it.is_dma:
                            r.then_inc(dsems[it.dsem], 16)
                        elif it.sig:
                            r.then_inc(esem[it.eng], 1)
                return run
            block.tensor(runner("pe"))
            block.scalar(runner("act"))
            block.vector(runner("dve"))
            block.gpsimd(runner("pool"))
            block.sync(runner("sp"))
        st.close()


class Tl:
    __slots__ = ("t", "b")

    def __init__(self, t, name):
        self.t = t
        self.b = Buf(name)

    def __getitem__(self, k):
        return self.t[k]


def _bufs(xs):
    return [x.b if isinstance(x, Tl) else x for x in xs]


class Builder:
    def __init__(self, n_ctx, L_ctx, L_lat, debug=()):
        self.n_ctx, self.L_ctx, self.L_lat = n_ctx, L_ctx, L_lat
        self.NT = n_ctx * L_ctx + L_lat
        self.NG = self.NT // 128
        self.QL = min(1024, L_lat)
        self.NQ = L_lat // self.QL
        self.seqs = [(i * L_ctx, L_ctx, 0) for i in range(n_ctx)] + [(n_ctx * L_ctx, L_lat, 1)]
        self.debug = set(debug)
        self.nc = bass.Bass("TRN2", target_bir_lowering=False)
        self.S = Sched(self.nc)
        self.final = []
        self.uid = 0

    @contextlib.contextmanager
    def scope(self):
        with contextlib.ExitStack() as st:
            yield st
            self.S.barrier()

    def sb(self, st, name, shape, dt):
        self.uid += 1
        return Tl(st.enter_context(self.nc.sbuf_tensor(f"{name}_{self.uid}", shape, dt)), name)

    def ps(self, st, name, shape, dt):
        self.uid += 1
        return Tl(st.enter_context(self.nc.psum_tensor(f"{name}_{self.uid}", shape, dt)), name)

    def din(self, name, shape, dt=F32):
        return self.nc.dram_tensor(name, list(shape), dt, kind="ExternalInput").ap()

    def dout(self, name, shape, dt=F32):
        return self.nc.dram_tensor(name, list(shape), dt, kind="ExternalOutput").ap()

    def dscr(self, name, shape, dt):
        kind = "ExternalOutput" if name in self.debug else "Internal"
        return self.nc.dram_tensor(name, list(shape), dt, kind=kind).ap()

    def mm(self, out, lhsT, rhs, start, stop, reads, writes):
        self.S.add("pe", lambda e: e.matmul(out, lhsT=lhsT, rhs=rhs, start=start, stop=stop),
                   _bufs(reads), _bufs(writes))

    def tr(self, out, in_, ident, reads, writes):
        self.S.add("pe", lambda e: e.transpose(out=out, in_=in_, identity=ident), _bufs(reads), _bufs(writes))

    def act(self, out, in_, func, reads, writes, scale=1.0, bias=0.0):
        self.S.add("act", lambda e: e.activation(out=out, in_=in_, func=func, scale=scale, bias=bias),
                   _bufs(reads), _bufs(writes))

    def tt(self, eng, out, in0, in1, op, reads, writes):
        self.S.add(eng, lambda e: e.tensor_tensor(out=out, in0=in0, in1=in1, op=op), _bufs(reads), _bufs(writes))

    def ts(self, eng, out, in0, s1, s2, op0, op1, reads, writes):
        self.S.add(eng, lambda e: e.tensor_scalar(out=out, in0=in0, scalar1=s1, scalar2=s2, op0=op0, op1=op1),
                   _bufs(reads), _bufs(writes))

    def stt(self, out, in0, scalar, in1, op0, op1, reads, writes):
        self.S.add("dve", lambda e: e.scalar_tensor_tensor(out=out, in0=in0, scalar=scalar, in1=in1, op0=op0, op1=op1),
                   _bufs(reads), _bufs(writes))

    def cp(self, eng, out, in_, reads, writes):
        if eng == "act":
            self.act(out, in_, AF.Copy, reads, writes)
        else:
            self.S.add(eng, lambda e: e.tensor_copy(out=out, in_=in_), _bufs(reads), _bufs(writes))

    def red(self, out, in_, reads, writes):
        self.S.add("dve", lambda e: e.tensor_reduce(out=out, in_=in_, axis=AX.X, op=ALU.add), _bufs(reads), _bufs(writes))

    def recip(self, out, in_, reads, writes):
        self.S.add("dve", lambda e: e.reciprocal(out=out, in_=in_), _bufs(reads), _bufs(writes))

    def memset(self, eng, ap, val, writes):
        self.S.add(eng, lambda e: e.memset(ap, val), [], _bufs(writes))

    def asel(self, ap, pattern, cmp, fill, base, cm, tl):
        self.S.add("pool", lambda e: e.affine_select(out=ap, in_=ap, pattern=pattern, compare_op=cmp, fill=fill,
                                                     base=base, channel_multiplier=cm), [tl.b], [tl.b])

    def dma(self, eng, out, in_, reads, writes):
        nt = getattr(self, "_untracked", None)
        if nt is None:
            nt = self._untracked = set(id(b) for b in list(self.bR.values()) + list(self.bO.values()))
        writes = [w for w in _bufs(writes) if id(w) not in nt]
        reads = [r for r in _bufs(reads) if id(r) not in nt]
        self.S.add(eng, lambda e: e.dma_start(out=out, in_=in_), reads, writes, is_dma=True)

    def dbg(self, name, ap, shape, rd):
        if name in self.debug:
            d = self.dout(name, shape)
            b = Buf(name)
            self.dma("pool", d, ap, [rd], [b])
            self.final.append(b)

    def declare_io(self):
        NT, n_ctx = self.NT, self.n_ctx
        I = {}
        I["xT"] = self.din("xT", [D, NT])
        I["cvec"] = self.din("cvec", [128, 16])
        I["ada_w"] = self.din("ada_w", [D, 6 * D])
        I["adab"] = self.din("adab", [128, 48])
        I["nmw"] = self.din("nmw", [128, 8])
        I["nfw"] = self.din("nfw", [128, 8])
        I["fnw"] = self.din("fnw", [128, 8])
        I["w_in"] = self.din("w_in", [D, PIN])
        I["rwkv_conv"] = self.din("rwkv_conv", [3, PB])
        I["hgrn_lb"] = self.din("hgrn_lb", [1, 2048])
        I["prow"] = self.din("prow", [1, 9 * 512])
        I["w2x"] = self.din("w2x", [32, 3 * 512])
        I["g2"] = self.din("g2", [96, 512])
        I["w_out"] = self.din("w_out", [D, D])
        I["w_gate"] = self.din("w_gate", [D, DFF])
        I["w_up"] = self.din("w_up", [D, DFF])
        I["w_down"] = self.din("w_down", [DFF, D])
        I["convw"] = self.din("convw", [128, NF * 9])
        I["cvb"] = self.din("cvb", [128, NF])
        I["xTo"] = self.din("xTo", [D, self.QL + 128])
        I["selv"] = self.din("selv", [128, self.NQ + 2])
        I["st_h"] = self.din("st_h", [2, 128, 4 * 128])
        I["st_r"] = self.din("st_r", [2, 64, 8 * 64])
        self.I = I
        O = {}
        O["yT"] = self.dout("yT", [D, n_ctx * self.L_ctx + self.QL])
        O["nsh"] = self.dout("nsh", [n_ctx, 2, 128, 512])
        O["nsr"] = self.dout("nsr", [n_ctx, 2, 64, 512])
        self.O = O
        self.bO = {k: Buf("out_" + k) for k in O}
        R = {}
        R["OPS"] = self.dscr("OPS", [NT, 14 * 512], BF16)
        R["CX"] = self.dscr("CX", [NT, 3 * 512], BF16)
        R["GCs"] = self.dscr("GCs", [self.NG, 2, 64, 16], F32)
        R["ECs"] = self.dscr("ECs", [self.NG, 2, 128, 8], F32)
        R["YS"] = self.dscr("YS", [2, NT, 512], F32)
        R["OSC"] = self.dscr("OSC", [2, NT, 512], F32)
        R["X1T"] = self.dscr("X1T", [D, n_ctx * self.L_ctx], F32)
        R["X1W"] = self.dscr("X1W", [D, self.QL + 128], F32)
        self.R = R
        self.bR = {k: Buf("scr_" + k) for k in R}

    def phase0(self, gst0, gst):
        nc, I = self.nc, self.I
        C = {}
        C["ident"] = self.sb(gst0, "ident", [128, 128], BF16)
        C["onesD"] = self.sb(gst0, "onesD", [128, 128], BF16)
        mods = self.sb(gst0, "mods", [128, 6, 2, 8], F32)
        fnw = self.sb(gst0, "fnw", [128, 8], F32)
        C["selv"] = self.sb(gst0, "selv", [128, self.NQ + 2], F32)
        self.dma("sp", C["selv"][:], I["selv"], [], [C["selv"]])
        identf = self.sb(gst, "identf", [128, 128], F32)
        self.memset("pool", identf[:], 0.0, [identf])
        self.asel(identf[:], [[-1, 128]], ALU.not_equal, 1.0, 0, 1, identf)
        self.cp("dve", C["ident"][:], identf[:], [identf], [C["ident"]])
        self.memset("pool", C["onesD"][:], 1.0 / D, [C["onesD"]])
        trif = self.sb(gst, "trif", [128, 128], F32)
        self.memset("pool", trif[:], 1.0, [trif])
        self.asel(trif[:], [[1, 128]], ALU.is_ge, 0.0, 0, -1, trif)
        self.asel(trif[:, 64:128], [[0, 64]], ALU.is_ge, 0.0, -64, 1, trif)
        trib = self.sb(gst, "trib", [128, 128], F32)
        self.memset("pool", trib[:], 1.0, [trib])
        self.asel(trib[:], [[-1, 128]], ALU.is_ge, 0.0, 0, 1, trib)
        self.asel(trib[:, 0:64], [[0, 64]], ALU.is_ge, 0.0, 63, -1, trib)
        C["tri"] = [trif, trib]
        t16f = self.sb(gst, "tri16f", [128, 128], BF16)
        self.memset("pool", t16f[:], 1.0, [t16f])
        self.asel(t16f[:], [[1, 128]], ALU.is_ge, 0.0, 0, -1, t16f)
        self.asel(t16f[:, 64:128], [[0, 64]], ALU.is_ge, 0.0, -64, 1, t16f)
        t16b = self.sb(gst, "tri16b", [128, 128], BF16)
        self.memset("pool", t16b[:], 1.0, [t16b])
        self.asel(t16b[:], [[-1, 128]], ALU.is_ge, 0.0, 0, 1, t16b)
        self.asel(t16b[:, 0:64], [[0, 64]], ALU.is_ge, 0.0, 63, -1, t16b)
        C["tri16"] = [t16f, t16b]

        C["identf"] = identf
        cind = self.sb(gst, "cind", [128, 2], F32)
        self.memset("pool", cind[:], 1.0, [cind])
        self.asel(cind[:, 0:1], [[0, 1]], ALU.is_ge, 0.0, 63, -1, cind)
        self.asel(cind[:, 1:2], [[0, 1]], ALU.is_ge, 0.0, -64, 1, cind)
        C["cind"] = cind
        M1 = self.sb(gst, "M1", [128, 128], F32)
        M3 = self.sb(gst, "M3", [128, 64], F32)
        MI = self.sb(gst, "MI", [128, 64], F32)
        for m in (M1, M3, MI):
            self.memset("pool", m[:], 1.0, [m])
        self.asel(M1[0:64, 0:64], [[1, 64]], ALU.is_gt, 0.0, 0, -1, M1)
        self.asel(M1[0:64, 64:128], [[1, 64]], ALU.is_ge, 0.0, 0, -1, M1)
        self.asel(M1[64:128, 0:64], [[-1, 64]], ALU.is_gt, 0.0, 0, 1, M1)
        self.asel(M1[64:128, 64:128], [[-1, 64]], ALU.is_ge, 0.0, 0, 1, M1)
        self.asel(M3[0:64, :], [[-1, 64]], ALU.is_gt, 0.0, 0, 1, M3)
        self.asel(M3[64:128, :], [[1, 64]], ALU.is_gt, 0.0, 0, -1, M3)
        self.asel(MI[0:64, :], [[1, 64]], ALU.is_ge, 0.0, 0, -1, MI)
        self.asel(MI[64:128, :], [[-1, 64]], ALU.is_ge, 0.0, 0, 1, MI)
        C["M1"], C["M3"], C["MI"] = M1, M3, MI
        self.dbg("dbg_M1", M1[:], [128, 128], M1)
        self.dbg("dbg_trif", trif[:], [128, 128], trif)
        self.dbg("dbg_trib", trib[:], [128, 128], trib)

        modT = self.sb(gst, "modT", [128, 48, 2], F32)
        lb = self.sb(gst, "lb", [128, 1024], F32)
        omlb = self.sb(gst, "omlb", [128, 1024], F32)
        prow = self.sb(gst, "prow", [128, 9 * 512], F32)
        omka = self.sb(gst, "omka", [128, 512], F32)
        with self.scope() as st:
            cv = self.sb(st, "cv", [128, 16], F32)
            self.dma("sp", cv[:], I["cvec"], [], [cv])
            scv = self.sb(st, "scv", [128, 16], F32)
            self.act(scv[:], cv[:], AF.Silu, [cv], [scv])
            adab = self.sb(st, "adab", [128, 48], F32)
            self.dma("sp", adab[:], I["adab"], [], [adab])
            acc = self.sb(st, "modacc", [128, 96], F32)
            abuf = [self.sb(st, f"adaw{i}", [128, 3 * D], F32) for i in range(2)]
            pm = self.ps(st, "pmod", [128, 512], F32)
            adv = I["ada_w"].rearrange("(k p) c -> k p c", p=128)
            it = 0
            for k in range(KD):
                for hf in range(2):
                    ab = abuf[it % 2]
                    it += 1
                    self.dma("sp", ab[:, :], adv[k][:, hf * 3072:(hf + 1) * 3072], [], [ab])
                    for mm_ in range(24):
                        m = hf * 24 + mm_
                        self.mm(pm[:, 2 * m:2 * m + 2], ab[:, mm_ * 128:(mm_ + 1) * 128], scv[:, 2 * k:2 * k + 2], True, True,
                                [ab, scv], [pm])
                if k == 0:
                    self.cp("dve", acc[:], pm[:, 0:96], [pm], [acc])
                else:
                    self.tt("dve", acc[:], pm[:, 0:96], acc[:], ALU.add, [pm, acc], [acc])
            self.tt("dve", modT[:], acc[:].rearrange("p (m s) -> p m s", s=2),
                    adab[:].unsqueeze(2).broadcast_to([128, 48, 2]), ALU.add, [acc, adab], [modT])
            self.dbg("dbg_modT", modT[:].rearrange("p m s -> p (m s)"), [128, 96], modT)
            nmw = self.sb(st, "nmw", [128, 8], F32)
            nfw = self.sb(st, "nfw", [128, 8], F32)
            self.dma("sp", nmw[:], I["nmw"], [], [nmw])
            self.dma("sp", nfw[:], I["nfw"], [], [nfw])
            for s in range(2):
                for (wi, off, nw) in ((0, 8, nmw), (3, 32, nfw)):
                    self.stt(mods[:, wi, s, :], modT[:, off:off + 8, s], 1.0, nw[:], ALU.add, ALU.mult,
                             [modT, nw], [mods])
                for (wi, off) in ((1, 0), (2, 16), (4, 24), (5, 40)):
                    self.cp("dve", mods[:, wi, s, :], modT[:, off:off + 8, s], [modT], [mods])
            C["mods"] = mods
            self.dma("sp", fnw[:], I["fnw"], [], [fnw])
            C["fnw"] = fnw
            lbr = self.sb(st, "lbraw", [128, 2048], F32)
            self.dma("sp", lbr[:], I["hgrn_lb"].partition_broadcast(128), [], [lbr])
            lbd = self.sb(st, "lbd", [128, 1024], F32)
            self.tt("dve", lbd[:], lbr[:, 0:1024], lbr[:, 1024:2048], ALU.subtract, [lbr], [lbd])
            self.act(lb[:], lbd[:], AF.Sigmoid, [lbd], [lb])
            self.ts("dve", omlb[:], lb[:], -0.5, 0.5, ALU.mult, ALU.add, [lb], [omlb])
            self.ts("dve", lb[:], lb[:], 0.5, 0.5, ALU.mult, ALU.add, [lb], [lb])
            C["lb"], C["omlb"] = lb, omlb
            self.dma("sp", prow[:], I["prow"].partition_broadcast(128), [], [prow])
            C["prow"] = prow
            self.ts("dve", omka[:], prow[:, 1024:1536], -1.0, 1.0, ALU.mult, ALU.add, [prow], [omka])
            C["omka"] = omka
        self.C = C

    def rms_rstd(self, st_tiles, src, n, pbank, rstd, lnexp=True):
        sq = st_tiles["sq"]
        self.tt("dve", sq[:, 0:8 * n], src[:, 0:8 * n], src[:, 0:8 * n], ALU.mult, [src], [sq])
        c0 = 0
        while c0 < n:
            cn = min(512, n - c0)
            for k in range(KD):
                self.mm(pbank[:, 0:cn], self.C["onesD"][:], sq[:, k * n + c0:k * n + c0 + cn], k == 0, k == KD - 1,
                        [self.C["onesD"], sq], [pbank])
            if lnexp:
                self.act(rstd[:, c0:c0 + cn], pbank[:, 0:cn], AF.Ln, [pbank], [rstd], bias=RMS_EPS)
            else:
                self.act(rstd[:, c0:c0 + cn], pbank[:, 0:cn], AF.Sqrt, [pbank], [rstd], bias=RMS_EPS)
            c0 += cn
        if lnexp:
            self.act(rstd[:, 0:n], rstd[:, 0:n], AF.Exp, [rstd], [rstd], scale=-0.5)
        else:
            self.recip(rstd[:, 0:n], rstd[:, 0:n], [rstd], [rstd])

    def norm_mod(self, tiles, xt, n, pbank, A, B, hT, modtl, lnexp=True):
        rstd = tiles["rstd"]
        self.rms_rstd(tiles, xt, n, pbank, rstd, lnexp)
        for k in range(KD):
            tmp = tiles["ntmp"][k % 2]
            self.tt("dve", tmp[:, 0:n], xt[:, k * n:(k + 1) * n], rstd[:, 0:n], ALU.mult, [xt, rstd], [tmp])
            self.ts("dve", hT[:, k * n:(k + 1) * n], tmp[:, 0:n], A[:, k:k + 1], B[:, k:k + 1], ALU.mult, ALU.add,
                    [tmp, modtl], [hT])

    def phaseA(self):
        nc, I, C, R = self.nc, self.I, self.C, self.R
        mods = C["mods"]
        xTv = I["xT"].rearrange("(k p) t -> p k t", p=128)
        import os as _os
        _rnew = "1"
        for part in (("H",) if _rnew == "0" else ("H", "R")):
            halo = 0 if part == "H" else 1
            nw = 128 + 2 * halo
            with self.scope() as st:
                X = {"part": part, "nw": nw, "halo": halo}
                if part == "H":
                    X["lf"] = [self.sb(st, f"lf{i}", [128, 512], F32) for i in range(2)]
                else:
                    X["sg"] = [self.sb(st, f"sg{i}", [128, 512], F32) for i in range(2)]
                    X["lt"] = [self.sb(st, f"lt{i}", [128, 192], F32) for i in range(2)]
                if part == "H":
                    WH = self.sb(st, "WH", [128, 8 * PA], BF16)
                    wv = I["w_in"].rearrange("(k p) c -> p k c", p=128)
                    for k in range(KD):
                        self.dma("pool", WH[:, k * PA:(k + 1) * PA], wv[:, k, 0:PA], [], [WH])
                    X["WH"] = WH
                else:
                    WR = [self.sb(st, f"WR{t}", [128, 8 * PB], BF16) for t in range(3)]
                    with self.scope() as st2:
                        cvbc = self.sb(st2, "cvbc", [128, 3 * PB], F32)
                        self.dma("sp", cvbc[:], I["rwkv_conv"].rearrange("(o a) c -> o (a c)", o=1).partition_broadcast(128),
                                 [], [cvbc])
                        stg = [self.sb(st2, f"wstg{i}", [128, PB], F32) for i in range(2)]
                        wv = I["w_in"].rearrange("(k p) c -> p k c", p=128)
                        for k in range(KD):
                            sg = stg[k % 2]
                            self.dma("sp", sg[:], wv[:, k, PA:PIN], [], [sg])
                            for t in range(3):
                                self.tt("dve" if t != 1 else "pool", WR[t][:, k * PB:(k + 1) * PB], sg[:],
                                        cvbc[:, t * PB:(t + 1) * PB], ALU.mult, [sg, cvbc], [WR[t]])
                    X["WR"] = WR
                    X["W2x"] = self.sb(st, "W2x", [32, 1536], BF16)
                    self.dma("pool", X["W2x"][:], I["w2x"], [], [X["W2x"]])
                    X["G2"] = self.sb(st, "G2", [96, 512], BF16)
                    self.dma("pool", X["G2"][:], I["g2"], [], [X["G2"]])
                    X["lw"] = self.sb(st, "lw", [32, 384], BF16)
                    X["lg"] = self.sb(st, "lg", [96, 128], BF16)
                X["tiles"] = {
                    "sq": self.sb(st, "sq", [128, 8 * nw], BF16),
                    "rstd": self.sb(st, "rstd", [128, nw], F32),
                    "ntmp": [self.sb(st, f"ntmp{i}", [128, nw], F32) for i in range(2)],
                }
                X["xts"] = [self.sb(st, f"xt{i}", [128, 8 * nw], F32) for i in range(2)]
                X["hTs"] = [self.sb(st, f"hT{i}", [128, 8 * nw], BF16) for i in range(2)]
                X["OPst"] = [self.sb(st, f"OPst{i}", [128, 14 * 512], BF16) for i in range(2)]
                X["CXst"] = [self.sb(st, f"CXst{i}", [128, 3 * 512], BF16) for i in range(2)]
                X["dcs"] = [self.sb(st, f"dcs{i}", [128, 32], F32) for i in range(2)]
                X["sm"] = [self.sb(st, f"sm{i}", [128, 8], F32) for i in range(4)]
                if part == "H":
                    X["qs"] = [self.sb(st, f"qs{i}", [128, 512], F32) for i in range(2)]
                    X["sgz"] = [[self.sb(st, f"sgz{i}{d}", [128, 512], F32) for d in range(2)] for i in range(2)]
                    X["ft"] = [self.sb(st, f"ft{i}", [128, 512], F32) for i in range(7)]
                    X["hilo"] = [[self.sb(st, f"hilo{d}{i}", [128, 512], BF16) for i in range(2)] for d in range(2)]
                    X["pss"] = self.ps(st, "pss", [128, 512], F32)
                    X["pp"] = [self.ps(st, f"pp{i}", [128, 512], F32) for i in range(5)]
                    X["LA"] = self.ps(st, "LA", [128, 512], F32)
                    X["LB"] = self.ps(st, "LB", [128, 512], F32)
                else:
                    X["rkv"] = [[self.sb(st, f"rkv{i}{j}", [128, 512], F32) for j in range(3)] for i in range(2)]
                    X["ft"] = [self.sb(st, f"ft{i}", [128, 512], F32) for i in range(9)]
                    X["pss"] = self.ps(st, "pss", [128, 512], F32)
                    X["pp"] = [self.ps(st, f"pp{i}", [128, 512], F32) for i in range(4)]
                    X["LA"] = self.ps(st, "LA", [128, 512], F32)
                    X["LB"] = self.ps(st, "LB", [128, 512], F32)
                    X["LC"] = self.ps(st, "LC", [128, 512], F32)
                for _ in self.A_normproj(X, 0):
                    pass
                for g in range(self.NG):
                    self.A_early(X, g)
                    nxt = self.A_normproj(X, g + 1) if g + 1 < self.NG else None
                    if part == "H":
                        self._interleave(nxt, self.A_late_H(X, g), "abaaabaab")
                    else:
                        self._interleave(nxt, self.A_late_R(X, g), "bbbbaabaaba")

    def A_normproj(self, X, g):
        I, C = self.I, self.C
        mods = C["mods"]
        xTv = I["xT"].rearrange("(k p) t -> p k t", p=128)
        halo, nw = X["halo"], X["nw"]
        t0 = g * 128
        s0, sl, kind = [s for s in self.seqs if s[0] <= t0 < s[0] + s[1]][0]
        s1 = s0 + sl
        lo, hi = max(t0 - halo, s0), min(t0 + 128 + halo, s1)
        off = lo - (t0 - halo)
        n = hi - lo
        xt, hT = X["xts"][g % 2], X["hTs"][g % 2]
        x3 = xt[:, :].rearrange("p (k t) -> p k t", k=8)
        if n < nw:
            self.memset("dve", xt[:, :], 0.0, [xt])
        self.dma("sp", x3[:, :, off:off + n], xTv[:, :, lo:hi], [], [xt])
        self.norm_mod(X["tiles"], xt, nw, X["pss"], mods[:, 0, kind, :], mods[:, 1, kind, :], hT, mods)
        h3 = hT[:, :].rearrange("p (k t) -> p k t", k=8)
        if off > 0:
            self.memset("dve", h3[:, :, 0:off], 0.0, [hT])
        if off + n < nw:
            self.memset("dve", h3[:, :, off + n:nw], 0.0, [hT])
        yield
        pp = X["pp"]
        if X["part"] == "H":
            WH = X["WH"]
            for cg in range(5):
                for k in range(KD):
                    self.mm(pp[cg][:, :], hT[:, k * nw:k * nw + 128], WH[:, k * PA + cg * 512:k * PA + (cg + 1) * 512],
                            k == 0, k == KD - 1, [hT, WH], [pp[cg]])
                yield
        else:
            WR = X["WR"]
            for cg in range(4):
                c0 = cg * 512
                cn = 512 if cg < 3 else 192
                for t in range(3):
                    for k in range(KD):
                        self.mm(pp[cg][:, 0:cn], hT[:, k * nw + t:k * nw + t + 128], WR[t][:, k * PB + c0:k * PB + c0 + cn],
                                t == 0 and k == 0, t == 2 and k == KD - 1, [hT, WR[t]], [pp[cg]])
                yield

    @staticmethod
    def _interleave(a, b, pattern):
        for ch in pattern:
            gen = a if ch == "a" else b
            if gen is not None:
                next(gen, None)
        for gen in (a, b):
            if gen is not None:
                for _ in gen:
                    pass

    def A_early(self, X, g):
        pp = X["pp"]
        b = g % 2
        ops, cxs = X["OPst"][b], X["CXst"][b]
        if X["part"] == "H":
            self.act(X["qs"][b][:], pp[0][:], AF.Silu, [pp[0]], [X["qs"][b]])
            self.act(cxs[:, 1024:1536], pp[4][:], AF.Silu, [pp[4]], [cxs])
            self.act(ops[:, 13 * 512:14 * 512], pp[1][:], AF.Copy, [pp[1]], [ops])
            self.act(X["sgz"][b][0][:], pp[2][:], AF.Tanh, [pp[2]], [X["sgz"][b][0]], scale=0.5)
            self.act(X["sgz"][b][1][:], pp[3][:], AF.Tanh, [pp[3]], [X["sgz"][b][1]], scale=0.5)
        else:
            lt = X["lt"][b]
            self.act(lt[:, 0:64], pp[3][:, 0:64], AF.Tanh, [pp[3]], [lt])
            self.act(lt[:, 64:96], pp[3][:, 64:96], AF.Copy, [pp[3]], [lt])
            self.act(lt[:, 96:192], pp[3][:, 96:192], AF.Sigmoid, [pp[3]], [lt])
            r_, k_, v_ = X["rkv"][b]
            self.cp("dve", r_[:], pp[0][:], [pp[0]], [r_])
            self.cp("act", k_[:], pp[1][:], [pp[1]], [k_])
            self.cp("dve", v_[:], pp[2][:], [pp[2]], [v_])

    def A_late_H(self, X, g):
        C, R = self.C, self.R
        OPSd, CXd = R["OPS"], R["CX"]
        b = g % 2
        t0 = g * 128
        ops, cxs, dcs = X["OPst"][b], X["CXst"][b], X["dcs"][b]
        qs = X["qs"][b]
        ft = X["ft"]
        LAB = (X["LA"], X["LB"])
        pe = X["pss"]
        for d in range(2):
            sg_ = X["sgz"][b][d]
            lf = X["lf"][d]
            f_, kd = ft[d], ft[2 + d]
            hi, lo = X["hilo"][d]
            self.tt("dve", f_[:], sg_[:], C["omlb"][:, d * 512:(d + 1) * 512], ALU.mult, [sg_, C["omlb"]], [f_])
            self.tt("pool", f_[:], f_[:], C["lb"][:, d * 512:(d + 1) * 512], ALU.add, [f_, C["lb"]], [f_])
            self.act(lf[:], f_[:], AF.Ln, [f_], [lf])
            self.ts("pool", kd[:], f_[:], -1.0, 1.0, ALU.mult, ALU.add, [f_], [kd])
            self.cp("dve", hi[:], lf[:], [lf], [hi])
            self.tt("dve", lo[:], lf[:], hi[:], ALU.subtract, [lf, hi], [lo])
        yield
        for d in range(2):
            lf = X["lf"][d]
            hi, lo = X["hilo"][d]
            pc = LAB[d]
            self.mm(pc[:], C["tri16"][d][:], hi[:], True, False, [C["tri16"][d], hi], [pc])
            self.mm(pc[:], C["tri16"][d][:], lo[:], False, True, [C["tri16"][d], lo], [pc])
            for h in range(4):
                c = (d * 4 + h) * 2
                self.mm(pe[:, c:c + 2], lf[:, h * 128:(h + 1) * 128], C["cind"][:], True, True, [lf, C["cind"]], [pe])
        yield
        for d in range(2):
            kd = ft[2 + d]
            csc, e1, e2 = ft[4], ft[5], ft[6]
            pc = LAB[d]
            self.ts("dve", csc[:], pc[:], -80.0, None, ALU.max, ALU.bypass, [pc], [csc])
            self.act(e1[:], csc[:], AF.Exp, [csc], [e1])
            self.act(e2[:], csc[:], AF.Exp, [csc], [e2], scale=-1.0)
            self.tt("dve", ops[:, (6 * d + 4) * 512:(6 * d + 5) * 512], qs[:], e1[:], ALU.mult, [qs, e1], [ops])
            self.tt("dve", ops[:, (6 * d + 5) * 512:(6 * d + 6) * 512], kd[:], e2[:], ALU.mult, [kd, e2], [ops])
        self.act(dcs[:, 0:16].rearrange("p (c x) -> p c x", c=2), pe[:, 0:16].rearrange("p (x c) -> p c x", c=2),
                 AF.Exp, [pe], [dcs])
        self.dma("pool", R["ECs"][g].rearrange("c p x -> p c x"), dcs[:, 0:16].rearrange("p (c x) -> p c x", c=2),
                 [dcs], [])
        o3 = ops[:, :].rearrange("p (s c) -> p s c", s=14)
        OP3 = OPSd[t0:t0 + 128, :].rearrange("t (s c) -> t s c", s=14)
        self.dma("pool", OP3[:, 4:6, :], o3[:, 4:6, :], [ops], [])
        self.dma("pool", OP3[:, 10:12, :], o3[:, 10:12, :], [ops], [])
        self.dma("pool", OP3[:, 13:14, :], o3[:, 13:14, :], [ops], [])
        self.dma("pool", CXd[t0:t0 + 128, 1024:1536], cxs[:, 1024:1536], [cxs], [])
        yield

    def A_late_R(self, X, g):
        C, R = self.C, self.R
        OPSd, CXd = R["OPS"], R["CX"]
        prow = C["prow"]
        kk_bc, ka_bc, rk_bc = prow[:, 512:1024], prow[:, 1024:1536], prow[:, 1536:2048]
        b = g % 2
        t0 = g * 128
        ops, cxs, dcs = X["OPst"][b], X["CXst"][b], X["dcs"][b]
        r_, k_, v_ = X["rkv"][b]
        lt, lw, lg, W2x, G2 = X["lt"][b], X["lw"], X["lg"], X["W2x"], X["G2"]
        ft, sm = X["ft"], X["sm"]
        LA, LB, LC = X["LA"], X["LB"], X["LC"]
        identf = C["identf"]
        for i in range(3):
            self.tr(LC[0:32, i * 128:(i + 1) * 128], lt[:, i * 32:(i + 1) * 32], identf[:], [lt, identf], [LC])
        self.tr(LC[0:96, 384:512], lt[:, 96:192], identf[:], [lt, identf], [LC])
        yield
        self.cp("dve", lw[0:32, :], LC[0:32, 0:384], [LC], [lw])
        self.cp("dve", lg[:, :], LC[0:96, 384:512], [LC], [lg])
        a_, sgf, sgb = ft[0], X["sg"][0], X["sg"][1]
        self.mm(LA[:], lw[0:32, 256:384], W2x[0:32, 1024:1536], True, True, [lw, W2x], [LA])
        self.mm(LB[:], lw[0:32, 128:256], W2x[0:32, 512:1024], True, True, [lw, W2x], [LB])
        self.mm(LC[:], lg[:, :], G2[:, :], True, True, [lg, G2], [LC])
        yield
        self.tt("dve", a_[:], LA[:], prow[:, 4096:4608], ALU.add, [LA, prow], [a_])
        self.act(a_[:], a_[:], AF.Sigmoid, [a_], [a_])
        self.mm(LA[:], lw[0:32, 0:128], W2x[0:32, 0:512], True, True, [lw, W2x], [LA])
        self.tt("dve", sgb[:], LB[:], prow[:, 3584:4096], ALU.add, [LB, prow], [sgb])
        self.act(sgb[:], sgb[:], AF.Sigmoid, [sgb], [sgb])
        self.act(cxs[:, 0:512], LC[:], AF.Copy, [LC], [cxs])
        self.tt("dve", sgf[:], LA[:], prow[:, 3072:3584], ALU.add, [LA, prow], [sgf])
        self.act(sgf[:], sgf[:], AF.Sigmoid, [sgf], [sgf])
        yield
        for d in range(2):
            sg_ = (sgf, sgb)[d]
            pc = (LA, LB)[d]
            self.mm(pc[:], C["tri"][d][:], sg_[:], True, True, [C["tri"][d], sg_], [pc])
            for h in range(8):
                c = (d * 8 + h) * 2
                self.mm(LC[0:64, c:c + 2], sg_[:, h * 64:(h + 1) * 64], C["cind"][:], True, True, [sg_, C["cind"]], [LC])
        yield
        kx, sq, kk, t1, kp, nb, t2 = ft[1], ft[2], ft[3], ft[4], ft[5], ft[6], ft[4]
        self.tt("dve", kx[:], k_[:], kk_bc, ALU.mult, [k_, prow], [kx])
        self.act(sq[:], kx[:], AF.Square, [kx], [sq])
        self.red(sm[0][:], sq[:].rearrange("p (h j) -> p h j", h=8), [sq], [sm[0]])
        self.act(sm[1][:], sm[0][:], AF.Ln, [sm[0]], [sm[1]], bias=1e-12)
        self.act(sm[1][:], sm[1][:], AF.Exp, [sm[1]], [sm[1]], scale=-0.5)
        self.tt("dve", kk[:].rearrange("p (h j) -> p h j", h=8), kx[:].rearrange("p (h j) -> p h j", h=8),
                sm[1][:].unsqueeze(2).broadcast_to([128, 8, 64]), ALU.mult, [kx, sm[1]], [kk])
        self.tt("dve", t1[:], a_[:], ka_bc, ALU.mult, [a_, prow], [t1])
        self.tt("dve", t1[:], t1[:], C["omka"][:], ALU.add, [t1, C["omka"]], [t1])
        self.tt("dve", kp[:], k_[:], t1[:], ALU.mult, [k_, t1], [kp])
        self.stt(nb[:], kk[:], -1.0, a_[:], ALU.mult, ALU.mult, [kk, a_], [nb])
        self.tt("dve", t2[:], r_[:], kp[:], ALU.mult, [r_, kp], [t2])
        self.tt("dve", t2[:], t2[:], rk_bc, ALU.mult, [t2, prow], [t2])
        self.red(sm[2][:], t2[:].rearrange("p (h j) -> p h j", h=8), [t2], [sm[2]])
        self.tt("dve", cxs[:, 512:1024].rearrange("p (h j) -> p h j", h=8), v_[:].rearrange("p (h j) -> p h j", h=8),
                sm[2][:].unsqueeze(2).broadcast_to([128, 8, 64]), ALU.mult, [v_, sm[2]], [cxs])
        self.cp("act", ops[:, 12 * 512:13 * 512], v_[:], [v_], [ops])
        yield
        for d in range(2):
            sg_ = (sgf, sgb)[d]
            pc = (LA, LB)[d]
            gi, ginv, tmp, ge = ft[7], ft[8], ft[2], ft[1]
            self.act(gi[:], pc[:], AF.Exp, [pc], [gi], scale=-DECAY)
            self.act(ginv[:], pc[:], AF.Exp, [pc], [ginv], scale=DECAY)
            self.tt("dve", tmp[:], pc[:], sg_[:], ALU.subtract, [pc, sg_], [tmp])
            self.act(ge[:], tmp[:], AF.Exp, [tmp], [ge], scale=-DECAY)
            b0 = 6 * d * 512
            self.tt("dve", ops[:, b0:b0 + 512], r_[:], gi[:], ALU.mult, [r_, gi], [ops])
            self.tt("dve", ops[:, b0 + 512:b0 + 1024], kp[:], ginv[:], ALU.mult, [kp, ginv], [ops])
            self.tt("dve", ops[:, b0 + 1024:b0 + 1536], nb[:], ginv[:], ALU.mult, [nb, ginv], [ops])
            self.tt("dve", ops[:, b0 + 1536:b0 + 2048], kk[:], ge[:], ALU.mult, [kk, ge], [ops])
        self.act(dcs[0:64, 0:32].rearrange("p (c x) -> p c x", c=2), LC[0:64, 0:32].rearrange("p (x c) -> p c x", c=2),
                 AF.Exp, [LC], [dcs], scale=-DECAY)
        self.dma("pool", R["GCs"][g].rearrange("c p x -> p c x"), dcs[0:64, 0:32].rearrange("p (c x) -> p c x", c=2),
                 [dcs], [self.bR["GCs"]])
        o3 = ops[:, :].rearrange("p (s c) -> p s c", s=14)
        OP3 = OPSd[t0:t0 + 128, :].rearrange("t (s c) -> t s c", s=14)
        self.dma("pool", OP3[:, 0:4, :], o3[:, 0:4, :], [ops], [self.bR["OPS"]])
        self.dma("pool", OP3[:, 6:10, :], o3[:, 6:10, :], [ops], [self.bR["OPS"]])
        self.dma("pool", OP3[:, 12:13, :], o3[:, 12:13, :], [ops], [self.bR["OPS"]])
        self.dma("pool", CXd[t0:t0 + 128, 0:1024], cxs[:, 0:1024], [cxs], [self.bR["CX"]])
        yield

    def scan(self):
        C, R, I, O = self.C, self.R, self.I, self.O
        ident, M1, M3, MI = C["ident"], C["M1"], C["M3"], C["MI"]
        OPSd = R["OPS"]
        with self.scope() as st:
            OP = [self.sb(st, f"OP{i}", [128, 8 * 512], BF16) for i in range(2)]
            gCt = [self.sb(st, f"gCt{i}", [128, 8], F32) for i in range(2)]
            eCt = [self.sb(st, f"eCt{i}", [128, 8], F32) for i in range(2)]
            XT = self.sb(st, "XT", [128, 8 * 256], BF16)
            GS = self.sb(st, "GS", [128, 8 * 320], BF16)
            XA = [self.sb(st, f"XA{i}", [128, 8 * 128], BF16) for i in range(2)]
            NN = [self.sb(st, f"NN{i}", [128, 8 * 128], BF16) for i in range(2)]
            GT = self.sb(st, "GT", [128, 512], BF16)
            Pb = self.sb(st, "Pb", [128, 512], BF16)
            T32 = self.sb(st, "T32", [128, 512], F32)
            Tb = [self.sb(st, f"Tb{i}", [128, 512], BF16) for i in range(2)]
            Ttmp = self.sb(st, "Ttmp", [128, 512], F32)
            Ysb = [self.sb(st, f"Ysb{i}", [128, 512], F32) for i in range(2)]
            HT = self.sb(st, "HT", [128, 1024], BF16)
            SCz = self.sb(st, "SCz", [128, 4 * 2 * 64], BF16)
            Osb = [self.sb(st, f"Osb{i}", [128, 512], F32) for i in range(2)]
            S32 = self.sb(st, "S32", [128, 1024], F32)
            Sb = [self.sb(st, f"Sb{i}", [128, 1024], BF16) for i in range(2)]
            Stmp = self.sb(st, "Stmp", [128, 512], F32)
            BT0 = self.ps(st, "BT0", [128, 1024], BF16)
            BT1 = self.ps(st, "BT1", [128, 1024], BF16)
            PX = self.ps(st, "PX", [128, 1024], F32)
            PN = self.ps(st, "PN", [128, 1024], F32)
            F4 = self.ps(st, "F4", [128, 512], F32)
            F5 = self.ps(st, "F5", [128, 512], F32)
            self.memset("pool", SCz[:], 0.0, [SCz])
            M1b = M1[:].unsqueeze(1).broadcast_to([128, 8, 128])
            M3b = M3[:].unsqueeze(1).broadcast_to([128, 8, 64])
            stepno = 0
            for si, (s0, sl, kind) in enumerate(self.seqs):
                nch = sl // 64
                if kind == 0:
                    self.memset("pool", T32[:], 0.0, [T32])
                    self.memset("pool", S32[:], 0.0, [S32])
                else:
                    for l in range(2):
                        self.dma("sp", T32[64 * l:64 * l + 64, :], I["st_r"][l], [], [T32])
                        self.dma("sp", S32[:, l * 512:(l + 1) * 512], I["st_h"][l], [], [S32])
                cur = 0
                self.cp("act", Tb[cur][:], T32[:], [T32], [Tb[cur]])
                self.cp("act", Sb[cur][:], S32[:], [S32], [Sb[cur]])
                for i in range(nch):
                    op, gc, ec = OP[stepno % 2], gCt[stepno % 2], eCt[stepno % 2]
                    ysb, osb = Ysb[stepno % 2], Osb[stepno % 2]
                    stepno += 1
                    tok = [s0 + 64 * i, s0 + 64 * (nch - 1 - i)]
                    o3 = op[:, :].rearrange("p (s c) -> p s c", s=8)
                    for l in range(2):
                        src = OPSd[tok[l]:tok[l] + 64, :].rearrange("t (s c) -> t s c", s=14)
                        self.dma("sp", o3[64 * l:64 * l + 64, 0:6, :], src[:, 6 * l:6 * l + 6, :], [self.bR["OPS"]], [op])
                        self.dma("sp", o3[64 * l:64 * l + 64, 6:8, :], src[:, 12:14, :], [self.bR["OPS"]], [op])
                        gg, cc = tok[l] // 128, (tok[l] % 128) // 64
                        self.dma("sp", gc[64 * l:64 * l + 64, :], R["GCs"][gg, cc][:, 8 * l:8 * l + 8], [self.bR["GCs"]], [gc])
                        self.dma("sp", ec[:, 4 * l:4 * l + 4], R["ECs"][gg, cc][:, 4 * l:4 * l + 4], [self.bR["ECs"]], [ec])
                    for h in range(8):
                        bt = BT0 if h < 4 else BT1
                        for X, slot in enumerate((3, 0, 2, 1)):
                            for l in range(2):
                                r0 = 64 * l
                                self.tr(bt[r0:r0 + 64, (h % 4) * 256 + X * 64:(h % 4) * 256 + X * 64 + 64],
                                        op[r0:r0 + 64, slot * 512 + h * 64:slot * 512 + h * 64 + 64],
                                        ident[r0:r0 + 64, r0:r0 + 64], [op, ident], [bt])
                    self.cp("dve", XT[:, 0:1024], BT0[:, :], [BT0], [XT])
                    self.cp("act", XT[:, 1024:2048], BT1[:, :], [BT1], [XT])
                    for h in range(8):
                        for l in range(2):
                            r0 = 64 * l
                            xb = h * 256
                            self.mm(PX[r0:r0 + 64, h * 128:h * 128 + 128], XT[r0:r0 + 64, xb + 128:xb + 192],
                                    XT[r0:r0 + 64, xb:xb + 128], True, True, [XT], [PX])
                            self.mm(PN[r0:r0 + 64, h * 128:h * 128 + 128], XT[r0:r0 + 64, xb + 192:xb + 256],
                                    XT[r0:r0 + 64, xb:xb + 128], True, True, [XT], [PN])
                            self.mm(F4[r0:r0 + 64, h * 64:h * 64 + 64], XT[r0:r0 + 64, xb:xb + 64],
                                    XT[r0:r0 + 64, xb + 128:xb + 192], True, True, [XT], [F4])
                    G3 = GS[:, :].rearrange("p (h c) -> p h c", h=8)
                    self.tt("dve", G3[:, :, 0:128], PX[:, :].rearrange("p (h c) -> p h c", h=8), M1b, ALU.mult, [PX, M1], [GS])
                    self.tt("dve", G3[:, :, 128:256], PN[:, :].rearrange("p (h c) -> p h c", h=8), M1b, ALU.mult, [PN, M1], [GS])
                    self.tt("dve", G3[:, :, 256:320], F4[:, :].rearrange("p (h c) -> p h c", h=8), M3b, ALU.mult, [F4, M3], [GS])
                    xa = XA[0]
                    xa3 = xa[:, :].rearrange("p (h c) -> p h c", h=8)
                    self.cp("dve", xa3[:, :, 0:64], op[:, 3 * 512:4 * 512].rearrange("p (h c) -> p h c", h=8), [op], [xa])
                    for h in range(8):
                        for l in range(2):
                            r0 = 64 * l
                            self.mm(F5[r0:r0 + 64, h * 64:h * 64 + 64], GS[r0:r0 + 64, h * 320 + 128:h * 320 + 192],
                                    op[r0:r0 + 64, 6 * 512 + h * 64:6 * 512 + h * 64 + 64], True, True, [GS, op], [F5])
                    self.cp("act", xa3[:, :, 64:128], F5[:, :].rearrange("p (h c) -> p h c", h=8), [F5], [xa])
                    xc = 0
                    for k in range(6):
                        xcur, xnext = XA[xc], XA[1 - xc]
                        if k == 0:
                            nsrc = GS
                            noff = lambda h: h * 320 + 256
                            ntoff = lambda h: h * 320
                        else:
                            nsrc = NN[(k - 1) % 2]
                            noff = lambda h: h * 128
                            ntoff = lambda h: h * 128 + 64
                        if k < 5:
                            nn = NN[k % 2]
                            for h in range(8):
                                for l in range(2):
                                    r0 = 64 * l
                                    self.mm(PN[r0:r0 + 64, h * 128:h * 128 + 64], nsrc[r0:r0 + 64, ntoff(h):ntoff(h) + 64],
                                            nsrc[r0:r0 + 64, noff(h):noff(h) + 64], True, True, [nsrc], [PN])
                                    self.mm(PN[r0:r0 + 64, h * 128 + 64:h * 128 + 128], nsrc[r0:r0 + 64, noff(h):noff(h) + 64],
                                            nsrc[r0:r0 + 64, ntoff(h):ntoff(h) + 64], True, True, [nsrc], [PN])
                        if k < 5:
                            self.cp("act", nn[:, :], PN[:, :], [PN], [nn])
                        for h in range(8):
                            for l in range(2):
                                r0 = 64 * l
                                self.mm(PX[r0:r0 + 64, h * 128:h * 128 + 128], nsrc[r0:r0 + 64, ntoff(h):ntoff(h) + 64],
                                        xcur[r0:r0 + 64, h * 128:h * 128 + 128], True, True, [nsrc, xcur], [PX])
                        self.tt("dve", xnext[:, :], PX[:, :], xcur[:, :], ALU.add, [PX, xcur], [xnext])
                        xc = 1 - xc
                    x6 = XA[xc]
                    x63 = x6[:, :].rearrange("p (h c) -> p h c", h=8)
                    for h in range(8):
                        for l in range(2):
                            r0 = 64 * l
                            self.tr(BT1[r0:r0 + 64, h * 64:h * 64 + 64], x6[r0:r0 + 64, h * 128:h * 128 + 64],
                                    ident[r0:r0 + 64, r0:r0 + 64], [x6, ident], [BT1])
                    self.cp("dve", GT[:, :], BT1[:, 0:512], [BT1], [GT])
                    tbc, tbn = Tb[cur], Tb[1 - cur]
                    for h in range(8):
                        for l in range(2):
                            r0 = 64 * l
                            self.mm(F4[r0:r0 + 64, h * 64:h * 64 + 64], GT[r0:r0 + 64, h * 64:h * 64 + 64],
                                    tbc[r0:r0 + 64, h * 64:h * 64 + 64], True, True, [GT, tbc], [F4])
                    self.tt("dve", Pb[:, :].rearrange("p (h c) -> p h c", h=8), F4[:, :].rearrange("p (h c) -> p h c", h=8),
                            x63[:, :, 64:128], ALU.add, [F4, x6], [Pb])
                    for h in range(8):
                        for l in range(2):
                            r0 = 64 * l
                            yo = PN[r0:r0 + 64, h * 64:h * 64 + 64]
                            self.mm(yo, XT[r0:r0 + 64, h * 256 + 64:h * 256 + 128], tbc[r0:r0 + 64, h * 64:h * 64 + 64],
                                    True, False, [XT, tbc], [PN])
                            self.mm(yo, GS[r0:r0 + 64, h * 320 + 192:h * 320 + 256],
                                    op[r0:r0 + 64, 6 * 512 + h * 64:6 * 512 + h * 64 + 64], False, False, [GS, op], [PN])
                            self.mm(yo, GS[r0:r0 + 64, h * 320 + 64:h * 320 + 128], Pb[r0:r0 + 64, h * 64:h * 64 + 64],
                                    False, True, [GS, Pb], [PN])
                    self.cp("act", ysb[:, :], PN[:, 0:512], [PN], [ysb])
                    for l in range(2):
                        self.dma("pool", R["YS"][l, tok[l]:tok[l] + 64, :], ysb[64 * l:64 * l + 64, :], [ysb], [self.bR["YS"]])
                    for h in range(8):
                        for l in range(2):
                            r0 = 64 * l
                            to = F5[r0:r0 + 64, h * 64:h * 64 + 64]
                            self.mm(to, op[r0:r0 + 64, 1 * 512 + h * 64:1 * 512 + h * 64 + 64],
                                    op[r0:r0 + 64, 6 * 512 + h * 64:6 * 512 + h * 64 + 64], True, False, [op], [F5])
                            self.mm(to, op[r0:r0 + 64, 2 * 512 + h * 64:2 * 512 + h * 64 + 64],
                                    Pb[r0:r0 + 64, h * 64:h * 64 + 64], False, True, [op, Pb], [F5])
                    self.tt("dve", Ttmp[:, :], F5[:, :], T32[:, :], ALU.add, [F5, T32], [Ttmp])
                    self.tt("dve", T32[:, :].rearrange("p (h c) -> p h c", h=8), Ttmp[:, :].rearrange("p (h c) -> p h c", h=8),
                            gc[:, :].unsqueeze(2).broadcast_to([128, 8, 64]), ALU.mult, [Ttmp, gc], [T32])
                    self.cp("act", tbn[:, :], T32[:, :], [T32], [tbn])
                    sbc, sbn = Sb[cur], Sb[1 - cur]
                    for X, slot in enumerate((4, 5)):
                        for h in range(4):
                            c0 = (X * 4 + h) * 128
                            self.tr(BT0[:, c0:c0 + 128], op[:, slot * 512 + h * 128:slot * 512 + h * 128 + 128], ident[:, :],
                                    [op, ident], [BT0])
                    self.cp("dve", HT[:, :], BT0[:, :], [BT0], [HT])
                    for h in range(4):
                        for l in range(2):
                            r0 = 64 * l
                            self.mm(PX[r0:r0 + 64, h * 64:h * 64 + 64], HT[:, (4 + h) * 128 + r0:(4 + h) * 128 + r0 + 64],
                                    HT[:, h * 128 + r0:h * 128 + r0 + 64], True, True, [HT], [PX])
                    for l in range(2):
                        r0 = 64 * l
                        sc4 = SCz[r0:r0 + 64, :].rearrange("p (h l t) -> p h l t", h=4, l=2)
                        self.tt("dve", sc4[:, :, l, :], PX[r0:r0 + 64, 0:256].rearrange("p (h t) -> p h t", h=4),
                                MI[r0:r0 + 64, :].unsqueeze(1).broadcast_to([64, 4, 64]), ALU.mult, [PX, MI], [SCz])
                    for h in range(4):
                        for l in range(2):
                            r0 = 64 * l
                            oo = PX[r0:r0 + 64, 512 + h * 128:512 + h * 128 + 128]
                            self.mm(oo, SCz[:, (h * 2 + l) * 64:(h * 2 + l) * 64 + 64], op[:, 7 * 512 + h * 128:7 * 512 + h * 128 + 128],
                                    True, False, [SCz, op], [PX])
                            self.mm(oo, HT[:, h * 128 + r0:h * 128 + r0 + 64], sbc[:, (l * 4 + h) * 128:(l * 4 + h) * 128 + 128],
                                    False, True, [HT, sbc], [PX])
                    self.cp("act", osb[:, :], PX[:, 512:1024], [PX], [osb])
                    for l in range(2):
                        self.dma("pool", R["OSC"][l, tok[l]:tok[l] + 64, :], osb[64 * l:64 * l + 64, :], [osb], [self.bR["OSC"]])
                    for l in range(2):
                        r0 = 64 * l
                        pu = PN if l == 0 else F4
                        pu0 = 512 if l == 0 else 0
                        for h in range(4):
                            self.mm(pu[:, pu0 + h * 128:pu0 + h * 128 + 128], op[r0:r0 + 64, 5 * 512 + h * 128:5 * 512 + h * 128 + 128],
                                    op[r0:r0 + 64, 7 * 512 + h * 128:7 * 512 + h * 128 + 128], True, True, [op], [pu])
                        self.tt("dve", Stmp[:, :], pu[:, pu0:pu0 + 512], S32[:, l * 512:(l + 1) * 512], ALU.add, [pu, S32], [Stmp])
                        self.tt("dve", S32[:, l * 512:(l + 1) * 512].rearrange("p (h c) -> p h c", h=4),
                                Stmp[:, :].rearrange("p (h c) -> p h c", h=4),
                                ec[:, 4 * l:4 * l + 4].unsqueeze(2).broadcast_to([128, 4, 128]), ALU.mult, [Stmp, ec], [S32])
                    self.cp("act", sbn[:, :], S32[:, :], [S32], [sbn])
                    cur = 1 - cur
                if kind == 0:
                    for l in range(2):
                        self.dma("pool", O["nsr"][si, l], T32[64 * l:64 * l + 64, :], [T32], [self.bO["nsr"]])
                        self.dma("pool", O["nsh"][si, l], S32[:, l * 512:(l + 1) * 512], [S32], [self.bO["nsh"]])

    def ctx_tiles(self):
        out = []
        per = max(1, 512 // self.L_ctx)
        si = 0
        while si < self.n_ctx:
            ns = min(per, self.n_ctx - si)
            out.append((si * self.L_ctx, ns * self.L_ctx))
            si += ns
        return out

    def phaseC1(self):
        C, R, I = self.C, self.R, self.I
        ident, prow, mods, selv = C["ident"], C["prow"], C["mods"], C["selv"]
        hnw_bc, lnw_bc, lnb_bc = prow[:, 0:512], prow[:, 2048:2560], prow[:, 2560:3072]
        xTv = I["xT"].rearrange("(k p) t -> p k t", p=128)
        xov = I["xTo"].rearrange("(k p) t -> p k t", p=128)
        x1v = R["X1T"].rearrange("(k p) t -> p k t", p=128)
        x1wv = R["X1W"].rearrange("(k p) t -> p k t", p=128)
        NCT = self.n_ctx * self.L_ctx
        QL, NQ = self.QL, self.NQ
        WL = QL + 128
        with self.scope() as st:
            WO = self.sb(st, "WO", [128, 8 * D], BF16)
            wv = I["w_out"].rearrange("(k p) c -> p k c", p=128)
            for k in range(KD):
                self.dma("pool", WO[:, k * D:(k + 1) * D], wv[:, k, :], [], [WO])
            yf = [self.sb(st, f"yf{i}", [128, 512], F32) for i in range(2)]
            yb = [self.sb(st, f"yb{i}", [128, 512], F32) for i in range(2)]
            of = [self.sb(st, f"of{i}", [128, 512], F32) for i in range(2)]
            ob = [self.sb(st, f"ob{i}", [128, 512], F32) for i in range(2)]
            cx = [self.sb(st, f"cx{i}", [128, 1536], BF16) for i in range(2)]
            cxa = self.sb(st, "cxa", [128, 1536], F32)
            ft = [self.sb(st, f"c1f{i}", [128, 512], F32) for i in range(4)]
            sm = [self.sb(st, f"c1s{i}", [128, 8], F32) for i in range(6)]
            MIX = self.sb(st, "MIX", [128, 1024], BF16)
            MIXT = self.sb(st, "MIXT", [128, 8 * 512], BF16)
            xt = self.sb(st, "c1xt", [128, 8 * 512], F32)
            x1 = self.sb(st, "c1x1", [128, 8 * 512], F32)
            PT = self.ps(st, "c1PT", [128, 1024], BF16)
            PO = [self.ps(st, f"c1PO{i}", [128, 512], F32) for i in range(2)]
            self._c1i = 0

            def post(y, o, cxt, gq):
                sq, osq = ft[1], ft[3]
                self.act(sq[:, :], y[:, :], AF.Square, [y], [sq])
                y3 = y[:, :].rearrange("p (h j) -> p h j", h=8)
                self.red(sm[0][:, :], y3, [y], [sm[0]])
                self.red(sm[1][:, :], sq[:, :].rearrange("p (h j) -> p h j", h=8), [sq], [sm[1]])
                self.ts("dve", sm[2][:, :], sm[0][:, :], 1.0 / 64, None, ALU.mult, ALU.bypass, [sm[0]], [sm[2]])
                self.tt("dve", sm[3][:, :], sm[2][:, :], sm[2][:, :], ALU.mult, [sm[2]], [sm[3]])
                self.stt(sm[4][:, :], sm[1][:, :], 1.0 / 64, sm[3][:, :], ALU.mult, ALU.subtract, [sm[1], sm[3]], [sm[4]])
                self.act(sm[5][:, :], sm[4][:, :], AF.Sqrt, [sm[4]], [sm[5]], bias=GN_EPS)
                self.recip(sm[5][:, :], sm[5][:, :], [sm[5]], [sm[5]])
                self.tt("dve", y3, y3, sm[2][:, :].unsqueeze(2).broadcast_to([128, 8, 64]), ALU.subtract, [y, sm[2]], [y])
                self.tt("dve", y3, y3, sm[5][:, :].unsqueeze(2).broadcast_to([128, 8, 64]), ALU.mult, [y, sm[5]], [y])
                self.tt("dve", y[:, :], y[:, :], lnw_bc, ALU.mult, [y, prow], [y])
                self.tt("dve", y[:, :], y[:, :], lnb_bc, ALU.add, [y, prow], [y])
                self.tt("dve", y[:, :], y[:, :], cxt[:, 512:1024], ALU.add, [y, cxt], [y])
                self.tt("dve", MIX[:, 512:1024], y[:, :], cxt[:, 0:512], ALU.mult, [y, cxt], [MIX])
                self.act(osq[:, :], o[:, :], AF.Square, [o], [osq])
                self.red(sm[0][:, 0:4], osq[:, :].rearrange("p (h j) -> p h j", h=4), [osq], [sm[0]])
                self.act(sm[1][:, 0:4], sm[0][:, 0:4], AF.Sqrt, [sm[0]], [sm[1]], scale=1.0 / 128, bias=RMS_EPS)
                self.recip(sm[1][:, 0:4], sm[1][:, 0:4], [sm[1]], [sm[1]])
                o3 = o[:, :].rearrange("p (h j) -> p h j", h=4)
                self.tt("dve", o3, o3, sm[1][:, 0:4].unsqueeze(2).broadcast_to([128, 4, 128]), ALU.mult, [o, sm[1]], [o])
                self.tt("dve", o[:, :], o[:, :], hnw_bc, ALU.mult, [o, prow], [o])
                self.tt("dve", MIX[:, 0:512], o[:, :], cxt[:, 1024:1536], ALU.mult, [o, cxt], [MIX])
                for k in range(KD):
                    self.tr(PT[:, k * 128:(k + 1) * 128], MIX[:, k * 128:(k + 1) * 128], ident[:, :], [MIX, ident], [PT])
                self.cp("act", MIXT[:, :].rearrange("p (k t) -> p k t", k=8)[:, :, gq * 128:(gq + 1) * 128],
                        PT[:, :].rearrange("p (k t) -> p k t", k=8), [PT], [MIXT])

            def dense(n, kind, dst):
                for m in range(KD):
                    po = PO[m % 2]
                    for k in range(KD):
                        self.mm(po[:, 0:n], WO[:, k * D + m * 128:k * D + (m + 1) * 128], MIXT[:, k * 512:k * 512 + n],
                                k == 0, k == KD - 1, [WO, MIXT], [po])
                    self.stt(x1[:, m * n:(m + 1) * n], po[:, 0:n], mods[:, 2, kind, m:m + 1], xt[:, m * n:(m + 1) * n],
                             ALU.mult, ALU.add, [po, mods, xt], [x1])
                self.dma("pool", dst, x1[:, 0:8 * n].rearrange("p (k t) -> p k t", k=8), [x1], [])

            gi = 0
            for (t0, n) in self.ctx_tiles():
                self.dma("sp", xt[:, 0:8 * n].rearrange("p (k t) -> p k t", k=8), xTv[:, :, t0:t0 + n], [], [xt])
                for gq in range(n // 128):
                    ta = t0 + gq * 128
                    b = gi % 2
                    gi += 1
                    self.dma("sp", yf[b][:, :], R["YS"][0, ta:ta + 128, :], [], [yf[b]])
                    self.dma("sp", yb[b][:, :], R["YS"][1, ta:ta + 128, :], [], [yb[b]])
                    self.dma("sp", of[b][:, :], R["OSC"][0, ta:ta + 128, :], [], [of[b]])
                    self.dma("sp", ob[b][:, :], R["OSC"][1, ta:ta + 128, :], [], [ob[b]])
                    self.dma("sp", cx[b][:, :], R["CX"][ta:ta + 128, :], [], [cx[b]])
                    y, o = ft[0], ft[2]
                    self.tt("dve", y[:, :], yf[b][:, :], yb[b][:, :], ALU.add, [yf[b], yb[b]], [y])
                    self.tt("dve", o[:, :], of[b][:, :], ob[b][:, :], ALU.add, [of[b], ob[b]], [o])
                    post(y, o, cx[b], gq)
                dense(n, 0, x1v[:, :, t0:t0 + n])
            w0 = 0
            while w0 < WL:
                n = min(512, WL - w0)
                self.dma("sp", xt[:, 0:8 * n].rearrange("p (k t) -> p k t", k=8), xov[:, :, w0:w0 + n], [], [xt])
                for gq in range(n // 128):
                    wj = w0 + gq * 128
                    y, o = ft[0], ft[2]
                    for q in range(NQ):
                        b = gi % 2
                        gi += 1
                        ta = NCT + q * QL - 64 + wj
                        lo, hi = max(ta, NCT), min(ta + 128, NCT + self.L_lat)
                        p0, pn = lo - ta, hi - lo
                        if pn < 128:
                            for tl in (yf[b], yb[b], of[b], ob[b], cx[b]):
                                self.memset("dve", tl[:, :], 0.0, [tl])
                        if pn > 0:
                            self.dma("sp", yf[b][p0:p0 + pn, :], R["YS"][0, lo:hi, :], [], [yf[b]])
                            self.dma("sp", yb[b][p0:p0 + pn, :], R["YS"][1, lo:hi, :], [], [yb[b]])
                            self.dma("sp", of[b][p0:p0 + pn, :], R["OSC"][0, lo:hi, :], [], [of[b]])
                            self.dma("sp", ob[b][p0:p0 + pn, :], R["OSC"][1, lo:hi, :], [], [ob[b]])
                            self.dma("sp", cx[b][p0:p0 + pn, :], R["CX"][lo:hi, :], [], [cx[b]])
                        sc = selv[:, q:q + 1]
                        t_y, t_o = ft[1], ft[3]
                        self.tt("dve", t_y[:, :], yf[b][:, :], yb[b][:, :], ALU.add, [yf[b], yb[b]], [t_y])
                        self.tt("dve", t_o[:, :], of[b][:, :], ob[b][:, :], ALU.add, [of[b], ob[b]], [t_o])
                        if q == 0:
                            self.ts("dve", y[:, :], t_y[:, :], sc, None, ALU.mult, ALU.bypass, [t_y, selv], [y])
                            self.ts("dve", o[:, :], t_o[:, :], sc, None, ALU.mult, ALU.bypass, [t_o, selv], [o])
                            self.ts("dve", cxa[:, :], cx[b][:, :], sc, None, ALU.mult, ALU.bypass, [cx[b], selv], [cxa])
                        else:
                            self.stt(y[:, :], t_y[:, :], sc, y[:, :], ALU.mult, ALU.add, [t_y, selv, y], [y])
                            self.stt(o[:, :], t_o[:, :], sc, o[:, :], ALU.mult, ALU.add, [t_o, selv, o], [o])
                            self.stt(cxa[:, :], cx[b][:, :], sc, cxa[:, :], ALU.mult, ALU.add, [cx[b], selv, cxa], [cxa])
                    post(y, o, cxa, gq)
                dense(n, 1, x1wv[:, :, w0:w0 + n])
                w0 += n

    def phaseC2(self):
        C, R, I, O = self.C, self.R, self.I, self.O
        mods, fnw, selv = C["mods"], C["fnw"], C["selv"]
        x1v = R["X1T"].rearrange("(k p) t -> p k t", p=128)
        x1wv = R["X1W"].rearrange("(k p) t -> p k t", p=128)
        yTv = O["yT"].rearrange("(k p) t -> p k t", p=128)
        NWM = 640
        NCT = self.n_ctx * self.L_ctx
        QL, NQ = self.QL, self.NQ
        with self.scope() as st:
            WG = self.sb(st, "WG", [128, 8 * DFF], BF16)
            WU = self.sb(st, "WU", [128, 8 * DFF], BF16)
            for (W, nm) in ((WG, "w_gate"), (WU, "w_up")):
                wv = I[nm].rearrange("(k p) c -> p k c", p=128)
                for k in range(KD):
                    self.dma("pool", W[:, k * DFF:(k + 1) * DFF], wv[:, k, :], [], [W])
            cw = self.sb(st, "cw", [128, NF * 9], F32)
            self.dma("sp", cw[:, :], I["convw"], [], [cw])
            cvb = self.sb(st, "cvb", [128, NF], F32)
            self.dma("sp", cvb[:, :], I["cvb"], [], [cvb])
            wd = [self.sb(st, f"wd{i}", [128, NF * 128], BF16) for i in range(2)]
            wdv = I["w_down"].rearrange("(f p) c -> p f c", p=128)
            x1w = self.sb(st, "x1w", [128, 8 * NWM], F32)
            tiles = {
                "sq": self.sb(st, "c2sq", [128, 8 * NWM], BF16),
                "rstd": self.sb(st, "c2rstd", [128, NWM], F32),
                "ntmp": [self.sb(st, f"c2ntmp{i}", [128, NWM], F32) for i in range(2)],
            }
            h2T = self.sb(st, "h2T", [128, 8 * NWM], BF16)
            HM = self.sb(st, "HM", [128, NF * 512], BF16)
            GP = [self.sb(st, f"GP{i}", [128, 10 * 66], F32) for i in range(2)]
            GPc = [self.sb(st, f"GPc{i}", [128, 2 * 258], F32) for i in range(2)]
            acc = [self.sb(st, f"c2acc{i}", [128, 512], F32) for i in range(2)]
            gl = [self.sb(st, f"c2gl{i}", [128, 512], F32) for i in range(2)]
            x2 = self.sb(st, "x2", [128, 8 * 512], F32)
            PG = [self.ps(st, f"c2PG{i}", [128, 1024], F32) for i in range(2)]
            PU = [self.ps(st, f"c2PU{i}", [128, 512], F32) for i in range(2)]
            PD = [self.ps(st, f"c2PD{i}", [128, 512], F32) for i in range(2)]
            fi = 0
            wdi = 0
            work = [(0, t0, n, None) for (t0, n) in self.ctx_tiles()]
            for r0 in range(0, QL // 64, 8):
                work.append((1, r0, min(8, QL // 64 - r0) * 64, None))
            for (kind, t0, n, _) in work:
                if kind == 1:
                    r0 = t0
                    nwv = n + 128
                    coff = 64
                    self.dma("sp", x1w[:, 0:8 * nwv].rearrange("p (k t) -> p k t", k=8),
                             x1wv[:, :, 64 * r0:64 * r0 + nwv], [], [x1w])
                else:
                    nwv, coff = n, 0
                    self.dma("sp", x1w[:, 0:8 * nwv].rearrange("p (k t) -> p k t", k=8), x1v[:, :, t0:t0 + n], [], [x1w])
                self.norm_mod(tiles, x1w, nwv, PU[0], mods[:, 3, kind, :], mods[:, 4, kind, :], h2T, mods)
                nrow_t = n // 64
                for gpz in (GP if kind == 1 else GPc):
                    self.memset("dve", gpz[:, :], 0.0, [gpz])
                for f in range(NF):
                    pg, pu = PG[fi % 2], PU[fi % 2]
                    ac, g_ = acc[fi % 2], gl[fi % 2]
                    c0 = 0
                    while c0 < nwv:
                        cn = min(512, nwv - c0)
                        for k in range(KD):
                            self.mm(pg[:, c0:c0 + cn], WG[:, k * DFF + f * 128:k * DFF + (f + 1) * 128],
                                    h2T[:, k * nwv + c0:k * nwv + c0 + cn], k == 0, k == KD - 1, [WG, h2T], [pg])
                        c0 += cn
                    for k in range(KD):
                        self.mm(pu[:, 0:n], WU[:, k * DFF + f * 128:k * DFF + (f + 1) * 128],
                                h2T[:, k * nwv + coff:k * nwv + coff + n], k == 0, k == KD - 1, [WU, h2T], [pu])
                    if kind == 1:
                        gp = GP[fi % 2]
                        gp3 = gp[:, :].rearrange("p (r c) -> p r c", c=66)
                        nrw = nwv // 64
                        self.cp("act", gp3[:, 0:nrw, 1:65], pg[:, 0:nwv].rearrange("p (r c) -> p r c", c=64), [pg], [gp])
                        if r0 == 0:
                            self.ts("dve", gp3[:, 0, 1:65], gp3[:, 0, 1:65], selv[:, NQ:NQ + 1], None, ALU.mult, ALU.bypass,
                                    [gp, selv], [gp])
                        if r0 + nrow_t == QL // 64:
                            self.ts("dve", gp3[:, nrw - 1, 1:65], gp3[:, nrw - 1, 1:65], selv[:, NQ + 1:NQ + 2], None,
                                    ALU.mult, ALU.bypass, [gp, selv], [gp])
                        a3 = ac[:, 0:n].rearrange("p (r c) -> p r c", c=64)
                        first = True
                        for dy in range(3):
                            for dx in range(3):
                                src = gp3[:, dy:dy + nrow_t, dx:dx + 64]
                                wcol = cw[:, f * 9 + dy * 3 + dx:f * 9 + dy * 3 + dx + 1]
                                if first:
                                    self.ts("dve", a3, src, wcol, cvb[:, f:f + 1], ALU.mult, ALU.add, [gp, cw, cvb], [ac])
                                    first = False
                                else:
                                    self.stt(a3, src, wcol, a3, ALU.mult, ALU.add, [gp, cw, ac], [ac])
                    else:
                        gp = GPc[fi % 2]
                        nsq = n // self.L_ctx
                        Lc = self.L_ctx
                        gp3 = gp[:, 0:nsq * (Lc + 2)].rearrange("p (r c) -> p r c", c=Lc + 2)
                        self.cp("act", gp3[:, :, 1:Lc + 1], pg[:, 0:n].rearrange("p (r c) -> p r c", c=Lc), [pg], [gp])
                        a3 = ac[:, 0:n].rearrange("p (r c) -> p r c", c=Lc)
                        for dx in range(3):
                            src = gp3[:, :, dx:dx + Lc]
                            wcol = cw[:, f * 9 + 3 + dx:f * 9 + 3 + dx + 1]
                            if dx == 0:
                                self.ts("dve", a3, src, wcol, cvb[:, f:f + 1], ALU.mult, ALU.add, [gp, cw, cvb], [ac])
                            else:
                                self.stt(a3, src, wcol, a3, ALU.mult, ALU.add, [gp, cw, ac], [ac])
                    self.act(g_[:, 0:n], ac[:, 0:n], AF.Gelu_apprx_tanh, [ac], [g_])
                    self.tt("dve", HM[:, f * 512:f * 512 + n], pu[:, 0:n], g_[:, 0:n], ALU.mult, [pu, g_], [HM])
                    fi += 1
                for m in range(KD):
                    w_ = wd[wdi % 2]
                    pd = PD[wdi % 2]
                    wdi += 1
                    self.dma("pool", w_[:, :].rearrange("p (f c) -> p f c", c=128), wdv[:, :, m * 128:(m + 1) * 128], [], [w_])
                    for f in range(NF):
                        self.mm(pd[:, 0:n], w_[:, f * 128:(f + 1) * 128], HM[:, f * 512:f * 512 + n], f == 0, f == NF - 1,
                                [w_, HM], [pd])
                    self.stt(x2[:, m * n:(m + 1) * n], pd[:, 0:n], mods[:, 5, kind, m:m + 1],
                             x1w[:, m * nwv + coff:m * nwv + coff + n], ALU.mult, ALU.add, [pd, mods, x1w], [x2])
                rstd = tiles["rstd"]
                self.rms_rstd(tiles, x2, n, PU[1], rstd)
                for k in range(KD):
                    self.stt(x2[:, k * n:(k + 1) * n], x2[:, k * n:(k + 1) * n], fnw[:, k:k + 1], rstd[:, 0:n],
                             ALU.mult, ALU.mult, [x2, fnw, rstd], [x2])
                od = NCT + 64 * t0 if kind == 1 else t0
                self.dma("pool", yTv[:, :, od:od + n], x2[:, 0:8 * n].rearrange("p (k t) -> p k t", k=8), [x2], [])

    def build(self, phases=("0", "A", "S", "C1", "C2")):
        self.declare_io()
        with contextlib.ExitStack() as gst:
            mst = contextlib.ExitStack()
            self.phase0(gst, mst)
            if "A" in phases:
                self.phaseA()
            if "S" in phases:
                self.scan()
            if "C1" in phases:
                self.phaseC1()
            self.S.barrier()
            mst.close()
            if "C2" in phases:
                self.phaseC2()
            self.S.barrier()
            fin = list(self.final)
            self.S.finalize(fin)
        return self.nc


def fm(v):
    return np.ascontiguousarray(np.asarray(v, np.float32).reshape(-1, 128).T)


def shared_inputs(inp):
    f = lambda a: np.ascontiguousarray(np.asarray(a, np.float32))
    S = {}
    S["ada_w"] = f(inp["ada_w"][0])
    S["adab"] = fm(inp["ada_b"][0])
    S["nmw"] = fm(inp["norm_mix_w"][0])
    S["nfw"] = fm(inp["norm_ffn_w"][0])
    S["fnw"] = fm(inp["final_norm_w"])
    S["w_in"] = f(inp["w_in"][0])
    S["rwkv_conv"] = f(inp["rwkv_conv"][0])
    S["hgrn_lb"] = f(inp["hgrn_lb"]).reshape(1, 2048)
    S["prow"] = np.concatenate([f(inp[k][0]).reshape(-1) for k in
                                ("hgrn_norm_w", "rwkv_k_k", "rwkv_k_a", "rwkv_r_k", "rwkv_ln_w", "rwkv_ln_b")]
                               + [f(inp["rwkv_w0"][0, 0]), f(inp["rwkv_w0"][0, 1]), f(inp["rwkv_a0"][0])]).reshape(1, 4608)
    w2x = np.zeros((32, 1536), np.float32)
    w2x[0:32, 0:512] = inp["rwkv_w2"][0, 0]
    w2x[0:32, 512:1024] = inp["rwkv_w2"][0, 1]
    w2x[0:32, 1024:1536] = inp["rwkv_a2"][0]
    S["w2x"] = w2x
    S["g2"] = f(inp["rwkv_g2"][0])
    S["w_out"] = f(inp["w_out"][0])
    S["w_gate"] = f(inp["ffn_w_gate"][0])
    S["w_up"] = f(inp["ffn_w_up"][0])
    S["w_down"] = f(inp["ffn_w_down"][0])
    cw = f(inp["ffn_conv"][0]).reshape(9, NF, 128)
    S["convw"] = np.ascontiguousarray(cw.transpose(2, 1, 0).reshape(128, NF * 9))
    S["cvb"] = fm(inp["ffn_conv_b"][0])
    return S


def core_inputs(inp, S, ctx_ids, lat_b, q):
    f = lambda a: np.ascontiguousarray(np.asarray(a, np.float32))
    L_lat = inp["x_sample"].shape[1]
    QL = min(1024, L_lat)
    NQ = L_lat // QL
    xl = f(inp["x_sample"][lat_b])
    xs = [f(inp["x_prompt"][i]) for i in ctx_ids] + [xl]
    x = np.concatenate(xs, axis=0)
    m = dict(S)
    m["xT"] = np.ascontiguousarray(x.T)
    xw = np.zeros((QL + 128, D), np.float32)
    lo, hi = q * QL - 64, q * QL + QL + 64
    a, b = max(lo, 0), min(hi, L_lat)
    xw[a - lo:b - lo] = xl[a:b]
    m["xTo"] = np.ascontiguousarray(xw.T)
    sel = np.zeros((128, NQ + 2), np.float32)
    sel[:, q] = 1.0
    sel[:, NQ] = 0.0 if q == 0 else 1.0
    sel[:, NQ + 1] = 0.0 if q == NQ - 1 else 1.0
    m["selv"] = sel
    cv = np.stack([f(inp["c_ctx"]), f(inp["c"][lat_b])], axis=1)
    m["cvec"] = np.ascontiguousarray(cv.reshape(8, 128, 2).transpose(1, 0, 2).reshape(128, 16))
    sh = f(inp["state_hgrn"][lat_b, 0])
    m["st_h"] = np.ascontiguousarray(sh.transpose(0, 2, 1, 3).reshape(2, 128, 512))
    sr = f(inp["state_rwkv"][lat_b, 0])
    m["st_r"] = np.ascontiguousarray(sr.transpose(0, 3, 1, 2).reshape(2, 64, 512))
    return m


_PROG_CACHE = {}


def get_prog(n_ctx, L_ctx, L_lat, debug=(), phases=("0", "A", "S", "C1", "C2")):
    key = (n_ctx, L_ctx, L_lat, tuple(sorted(debug)), tuple(phases))
    if key not in _PROG_CACHE:
        b = Builder(n_ctx, L_ctx, L_lat, debug)
        nc = b.build(phases)
        _PROG_CACHE[key] = (nc, b)
    return _PROG_CACHE[key]


def kernel(**inp):
    B, L_ctx = inp["x_prompt"].shape[0], inp["x_prompt"].shape[1]
    DB, L_lat = inp["x_sample"].shape[0], inp["x_sample"].shape[1]
    n_ctx = B // N_CORES
    QL = min(1024, L_lat)
    NQ = L_lat // QL
    assert DB * NQ == N_CORES
    nc, _ = get_prog(n_ctx, L_ctx, L_lat)
    S = shared_inputs(inp)
    in_maps = []
    for c in range(N_CORES):
        ctx_ids = list(range(c * n_ctx, (c + 1) * n_ctx))
        in_maps.append(core_inputs(inp, S, ctx_ids, c % DB, c // DB))
    res = run_bass_kernel_spmd(nc, in_maps, core_ids=list(range(N_CORES)))
    y_prompt = np.zeros((B, L_ctx, D), np.float32)
    y_sample = np.zeros((DB, L_lat, D), np.float32)
    nsh = np.zeros((B, 1, 2, 4, 128, 128), np.float32)
    nsr = np.zeros((B, 1, 2, 8, 64, 64), np.float32)
    for c in range(N_CORES):
        r = res.results[c]
        y = np.asarray(r["yT"]).T
        for i in range(n_ctx):
            y_prompt[c * n_ctx + i] = y[i * L_ctx:(i + 1) * L_ctx]
        b, q = c % DB, c // DB
        y_sample[b, q * QL:(q + 1) * QL] = y[n_ctx * L_ctx:]
        h = np.asarray(r["nsh"]).reshape(n_ctx, 2, 128, 4, 128).transpose(0, 1, 3, 2, 4)
        nsh[c * n_ctx:(c + 1) * n_ctx, 0] = h
        rr = np.asarray(r["nsr"]).reshape(n_ctx, 2, 64, 8, 64).transpose(0, 1, 3, 4, 2)
        nsr[c * n_ctx:(c + 1) * n_ctx, 0] = rr
    return (y_prompt, y_sample, nsh, nsr)
```

```python
import contextlib
import numpy as np
import concourse.bass as bass
import concourse.mybir as mybir
from concourse.bass_utils import run_bass_kernel_spmd

F32 = mybir.dt.float32
BF16 = mybir.dt.bfloat16
AF = mybir.ActivationFunctionType
ALU = mybir.AluOpType
AX = mybir.AxisListType

D = 1024
KD = 8
WA = 512
PA = 2560
PB = 1728
PIN = 4288
DFF = 2816
NF = 22
DECAY = 0.6065306597
RMS_EPS = 1e-6
GN_EPS = 64e-5
GRID_W = 64
N_CORES = 8

ENGS = ("pe", "act", "dve", "pool", "sp")


class Buf:
    __slots__ = ("name", "lw", "rd", "rdd")

    def __init__(self, name):
        self.name = name
        self.lw = None
        self.rd = {}
        self.rdd = []


class Ins:
    __slots__ = ("eng", "fn", "deps", "is_dma", "sig", "sigval", "dsem", "dval")

    def __init__(self, eng, fn, deps, is_dma):
        self.eng = eng
        self.fn = fn
        self.deps = deps
        self.is_dma = is_dma
        self.sig = False
        self.sigval = 0
        self.dsem = None
        self.dval = 0


class Sched:
    NDMA = 16

    def __init__(self, nc):
        self.nc = nc
        self.ins = []
        self.last = {}
        self.dmas = []

    def barrier(self):
        deps = sorted(set(self.last.values()) | set(self.dmas))
        if not deps:
            return
        for e in ENGS:
            self.ins.append(Ins(e, None, list(deps), False))
        self.dmas = []

    def add(self, eng, fn, reads=(), writes=(), is_dma=False):
        deps = set()
        for b in reads:
            if b.lw is not None:
                deps.add(b.lw)
        for b in writes:
            if b.lw is not None:
                deps.add(b.lw)
            deps.update(b.rd.values())
            deps.update(b.rdd)
        i = len(self.ins)
        self.ins.append(Ins(eng, fn, sorted(deps), is_dma))
        self.last[eng] = i
        if is_dma:
            self.dmas.append(i)
        for b in reads:
            if is_dma:
                b.rdd.append(i)
            else:
                b.rd[eng] = i
        for b in writes:
            b.lw = i
            b.rd = {}
            b.rdd = []
        return i

    def finalize(self, final_bufs):
        nc = self.nc
        ins = self.ins
        fdeps = set()
        for b in final_bufs:
            if b.lw is not None:
                fdeps.add(b.lw)
        ins.append(Ins("sp", None, sorted(fdeps), False))
        for it in ins:
            for d in it.deps:
                dd = ins[d]
                if dd.eng == "pe" and it.eng == "pe" and not dd.is_dma and not it.is_dma:
                    continue
                dd.sig = True
        st = contextlib.ExitStack()
        esem = {e: st.enter_context(nc.semaphore(f"s_{e}")) for e in ("pe", "act", "dve", "pool")}
        dq = [e for e in ENGS if any(it.is_dma and it.eng == e for it in ins)]
        per = max(2, self.NDMA // max(1, len(dq)))
        dsems = []
        dpool = {}
        for e in dq:
            dpool[e] = list(range(len(dsems), len(dsems) + per))
            dsems += [st.enter_context(nc.semaphore(f"s_dma_{e}{i}")) for i in range(per)]
        dcount = [0] * len(dsems)
        dlast = [None] * len(dsems)
        ecount = {e: 0 for e in esem}
        rr = {e: 0 for e in dq}
        for k, it in enumerate(ins):
            if it.is_dma:
                s = dpool[it.eng][rr[it.eng] % per]
                rr[it.eng] += 1
                if dlast[s] is not None:
                    it.deps = sorted(set(it.deps) | {dlast[s]})
                dcount[s] += 16
                it.dsem = s
                it.dval = dcount[s]
                dlast[s] = k
            elif it.sig:
                ecount[it.eng] += 1
                it.sigval = ecount[it.eng]
        progs = {e: [] for e in ENGS}
        waited = {e: {} for e in ENGS}
        for it in ins:
            w = {}
            for d in it.deps:
                dd = ins[d]
                if dd.is_dma:
                    key = ("d", dd.dsem)
                    val = dd.dval
                else:
                    if dd.eng == "pe" and it.eng == "pe" and not it.is_dma:
                        continue
                    key = ("e", dd.eng)
                    val = dd.sigval
                if w.get(key, 0) < val:
                    w[key] = val
            wl = []
            for key, val in w.items():
                if waited[it.eng].get(key, 0) >= val:
                    continue
                waited[it.eng][key] = val
                wl.append((dsems[key[1]] if key[0] == "d" else esem[key[1]], val))
            progs[it.eng].append((wl, it))
        self.counts = {e: len(progs[e]) for e in ENGS}
        with nc.Block() as block:
            def runner(name):
                def run(eng):
                    for wl, it in progs[name]:
                        for sem, val in wl:
                            eng.wait_ge(sem, val)
                        if it.fn is None:
                            continue
                        r = it.fn(eng)
                        if it.is_dma:
                            r.then_inc(dsems[it.dsem], 16)
                        elif it.sig:
                            r.then_inc(esem[it.eng], 1)
                return run
            block.tensor(runner("pe"))
            block.scalar(runner("act"))
            block.vector(runner("dve"))
            block.gpsimd(runner("pool"))
            block.sync(runner("sp"))
        st.close()


class Tl:
    __slots__ = ("t", "b")

    def __init__(self, t, name):
        self.t = t
        self.b = Buf(name)

    def __getitem__(self, k):
        return self.t[k]


def _bufs(xs):
    return [x.b if isinstance(x, Tl) else x for x in xs]


class Builder:
    def __init__(self, n_ctx, L_ctx, L_lat, debug=()):
        self.n_ctx, self.L_ctx, self.L_lat = n_ctx, L_ctx, L_lat
        self.NT = n_ctx * L_ctx + L_lat
        self.NG = self.NT // 128
        self.QL = min(1024, L_lat)
        self.NQ = L_lat // self.QL
        self.seqs = [(i * L_ctx, L_ctx, 0) for i in range(n_ctx)] + [(n_ctx * L_ctx, L_lat, 1)]
        self.debug = set(debug)
        self.nc = bass.Bass("TRN2", target_bir_lowering=False)
        self.S = Sched(self.nc)
        self.final = []
        self.uid = 0

    @contextlib.contextmanager
    def scope(self):
        with contextlib.ExitStack() as st:
            yield st
            self.S.barrier()

    def sb(self, st, name, shape, dt):
        self.uid += 1
        return Tl(st.enter_context(self.nc.sbuf_tensor(f"{name}_{self.uid}", shape, dt)), name)

    def ps(self, st, name, shape, dt):
        self.uid += 1
        return Tl(st.enter_context(self.nc.psum_tensor(f"{name}_{self.uid}", shape, dt)), name)

    def din(self, name, shape, dt=F32):
        return self.nc.dram_tensor(name, list(shape), dt, kind="ExternalInput").ap()

    def dout(self, name, shape, dt=F32):
        return self.nc.dram_tensor(name, list(shape), dt, kind="ExternalOutput").ap()

    def dscr(self, name, shape, dt):
        kind = "ExternalOutput" if name in self.debug else "Internal"
        return self.nc.dram_tensor(name, list(shape), dt, kind=kind).ap()

    def mm(self, out, lhsT, rhs, start, stop, reads, writes):
        self.S.add("pe", lambda e: e.matmul(out, lhsT=lhsT, rhs=rhs, start=start, stop=stop),
                   _bufs(reads), _bufs(writes))

    def tr(self, out, in_, ident, reads, writes):
        self.S.add("pe", lambda e: e.transpose(out=out, in_=in_, identity=ident), _bufs(reads), _bufs(writes))

    def act(self, out, in_, func, reads, writes, scale=1.0, bias=0.0):
        self.S.add("act", lambda e: e.activation(out=out, in_=in_, func=func, scale=scale, bias=bias),
                   _bufs(reads), _bufs(writes))

    def tt(self, eng, out, in0, in1, op, reads, writes):
        self.S.add(eng, lambda e: e.tensor_tensor(out=out, in0=in0, in1=in1, op=op), _bufs(reads), _bufs(writes))

    def ts(self, eng, out, in0, s1, s2, op0, op1, reads, writes):
        self.S.add(eng, lambda e: e.tensor_scalar(out=out, in0=in0, scalar1=s1, scalar2=s2, op0=op0, op1=op1),
                   _bufs(reads), _bufs(writes))

    def stt(self, out, in0, scalar, in1, op0, op1, reads, writes):
        self.S.add("dve", lambda e: e.scalar_tensor_tensor(out=out, in0=in0, scalar=scalar, in1=in1, op0=op0, op1=op1),
                   _bufs(reads), _bufs(writes))

    def cp(self, eng, out, in_, reads, writes):
        if eng == "act":
            self.act(out, in_, AF.Copy, reads, writes)
        else:
            self.S.add(eng, lambda e: e.tensor_copy(out=out, in_=in_), _bufs(reads), _bufs(writes))

    def red(self, out, in_, reads, writes):
        self.S.add("dve", lambda e: e.tensor_reduce(out=out, in_=in_, axis=AX.X, op=ALU.add), _bufs(reads), _bufs(writes))

    def recip(self, out, in_, reads, writes):
        self.S.add("dve", lambda e: e.reciprocal(out=out, in_=in_), _bufs(reads), _bufs(writes))

    def memset(self, eng, ap, val, writes):
        self.S.add(eng, lambda e: e.memset(ap, val), [], _bufs(writes))

    def asel(self, ap, pattern, cmp, fill, base, cm, tl):
        self.S.add("pool", lambda e: e.affine_select(out=ap, in_=ap, pattern=pattern, compare_op=cmp, fill=fill,
                                                     base=base, channel_multiplier=cm), [tl.b], [tl.b])

    def dma(self, eng, out, in_, reads, writes):
        nt = getattr(self, "_untracked", None)
        if nt is None:
            nt = self._untracked = set(id(b) for b in list(self.bR.values()) + list(self.bO.values()))
        writes = [w for w in _bufs(writes) if id(w) not in nt]
        reads = [r for r in _bufs(reads) if id(r) not in nt]
        self.S.add(eng, lambda e: e.dma_start(out=out, in_=in_), reads, writes, is_dma=True)

    def dbg(self, name, ap, shape, rd):
        if name in self.debug:
            d = self.dout(name, shape)
            b = Buf(name)
            self.dma("pool", d, ap, [rd], [b])
            self.final.append(b)

    def declare_io(self):
        NT, n_ctx = self.NT, self.n_ctx
        I = {}
        I["xT"] = self.din("xT", [D, NT])
        I["cvec"] = self.din("cvec", [128, 16])
        I["ada_w"] = self.din("ada_w", [D, 6 * D])
        I["adab"] = self.din("adab", [128, 48])
        I["nmw"] = self.din("nmw", [128, 8])
        I["nfw"] = self.din("nfw", [128, 8])
        I["fnw"] = self.din("fnw", [128, 8])
        I["w_in"] = self.din("w_in", [D, PIN])
        I["rwkv_conv"] = self.din("rwkv_conv", [3, PB])
        I["hgrn_lb"] = self.din("hgrn_lb", [1, 2048])
        I["prow"] = self.din("prow", [1, 9 * 512])
        I["w2x"] = self.din("w2x", [32, 3 * 512])
        I["g2"] = self.din("g2", [96, 512])
        I["w_out"] = self.din("w_out", [D, D])
        I["w_gate"] = self.din("w_gate", [D, DFF])
        I["w_up"] = self.din("w_up", [D, DFF])
        I["w_down"] = self.din("w_down", [DFF, D])
        I["convw"] = self.din("convw", [128, NF * 9])
        I["cvb"] = self.din("cvb", [128, NF])
        I["xTo"] = self.din("xTo", [D, self.QL + 128])
        I["selv"] = self.din("selv", [128, self.NQ + 2])
        I["st_h"] = self.din("st_h", [2, 128, 4 * 128])
        I["st_r"] = self.din("st_r", [2, 64, 8 * 64])
        self.I = I
        O = {}
        O["yT"] = self.dout("yT", [D, n_ctx * self.L_ctx + self.QL])
        O["nsh"] = self.dout("nsh", [n_ctx, 2, 128, 512])
        O["nsr"] = self.dout("nsr", [n_ctx, 2, 64, 512])
        self.O = O
        self.bO = {k: Buf("out_" + k) for k in O}
        R = {}
        R["OPS"] = self.dscr("OPS", [NT, 14 * 512], BF16)
        R["CX"] = self.dscr("CX", [NT, 3 * 512], BF16)
        R["GCs"] = self.dscr("GCs", [self.NG, 2, 64, 16], F32)
        R["ECs"] = self.dscr("ECs", [self.NG, 2, 128, 8], F32)
        R["YS"] = self.dscr("YS", [2, NT, 512], F32)
        R["OSC"] = self.dscr("OSC", [2, NT, 512], F32)
        R["X1T"] = self.dscr("X1T", [D, n_ctx * self.L_ctx], F32)
        R["X1W"] = self.dscr("X1W", [D, self.QL + 128], F32)
        self.R = R
        self.bR = {k: Buf("scr_" + k) for k in R}

    def phase0(self, gst0, gst):
        nc, I = self.nc, self.I
        C = {}
        C["ident"] = self.sb(gst0, "ident", [128, 128], BF16)
        C["onesD"] = self.sb(gst0, "onesD", [128, 128], BF16)
        mods = self.sb(gst0, "mods", [128, 6, 2, 8], F32)
        fnw = self.sb(gst0, "fnw", [128, 8], F32)
        C["selv"] = self.sb(gst0, "selv", [128, self.NQ + 2], F32)
        self.dma("sp", C["selv"][:], I["selv"], [], [C["selv"]])
        identf = self.sb(gst, "identf", [128, 128], F32)
        self.memset("pool", identf[:], 0.0, [identf])
        self.asel(identf[:], [[-1, 128]], ALU.not_equal, 1.0, 0, 1, identf)
        self.cp("dve", C["ident"][:], identf[:], [identf], [C["ident"]])
        self.memset("pool", C["onesD"][:], 1.0 / D, [C["onesD"]])
        trif = self.sb(gst, "trif", [128, 128], F32)
        self.memset("pool", trif[:], 1.0, [trif])
        self.asel(trif[:], [[1, 128]], ALU.is_ge, 0.0, 0, -1, trif)
        self.asel(trif[:, 64:128], [[0, 64]], ALU.is_ge, 0.0, -64, 1, trif)
        trib = self.sb(gst, "trib", [128, 128], F32)
        self.memset("pool", trib[:], 1.0, [trib])
        self.asel(trib[:], [[-1, 128]], ALU.is_ge, 0.0, 0, 1, trib)
        self.asel(trib[:, 0:64], [[0, 64]], ALU.is_ge, 0.0, 63, -1, trib)
        C["tri"] = [trif, trib]
        t16f = self.sb(gst, "tri16f", [128, 128], BF16)
        self.memset("pool", t16f[:], 1.0, [t16f])
        self.asel(t16f[:], [[1, 128]], ALU.is_ge, 0.0, 0, -1, t16f)
        self.asel(t16f[:, 64:128], [[0, 64]], ALU.is_ge, 0.0, -64, 1, t16f)
        t16b = self.sb(gst, "tri16b", [128, 128], BF16)
        self.memset("pool", t16b[:], 1.0, [t16b])
        self.asel(t16b[:], [[-1, 128]], ALU.is_ge, 0.0, 0, 1, t16b)
        self.asel(t16b[:, 0:64], [[0, 64]], ALU.is_ge, 0.0, 63, -1, t16b)
        C["tri16"] = [t16f, t16b]

        C["identf"] = identf
        cind = self.sb(gst, "cind", [128, 2], F32)
        self.memset("pool", cind[:], 1.0, [cind])
        self.asel(cind[:, 0:1], [[0, 1]], ALU.is_ge, 0.0, 63, -1, cind)
        self.asel(cind[:, 1:2], [[0, 1]], ALU.is_ge, 0.0, -64, 1, cind)
        C["cind"] = cind
        M1 = self.sb(gst, "M1", [128, 128], F32)
        M3 = self.sb(gst, "M3", [128, 64], F32)
        MI = self.sb(gst, "MI", [128, 64], F32)
        for m in (M1, M3, MI):
            self.memset("pool", m[:], 1.0, [m])
        self.asel(M1[0:64, 0:64], [[1, 64]], ALU.is_gt, 0.0, 0, -1, M1)
        self.asel(M1[0:64, 64:128], [[1, 64]], ALU.is_ge, 0.0, 0, -1, M1)
        self.asel(M1[64:128, 0:64], [[-1, 64]], ALU.is_gt, 0.0, 0, 1, M1)
        self.asel(M1[64:128, 64:128], [[-1, 64]], ALU.is_ge, 0.0, 0, 1, M1)
        self.asel(M3[0:64, :], [[-1, 64]], ALU.is_gt, 0.0, 0, 1, M3)
        self.asel(M3[64:128, :], [[1, 64]], ALU.is_gt, 0.0, 0, -1, M3)
        self.asel(MI[0:64, :], [[1, 64]], ALU.is_ge, 0.0, 0, -1, MI)
        self.asel(MI[64:128, :], [[-1, 64]], ALU.is_ge, 0.0, 0, 1, MI)
        C["M1"], C["M3"], C["MI"] = M1, M3, MI
        self.dbg("dbg_M1", M1[:], [128, 128], M1)
        self.dbg("dbg_trif", trif[:], [128, 128], trif)
        self.dbg("dbg_trib", trib[:], [128, 128], trib)

        modT = self.sb(gst, "modT", [128, 48, 2], F32)
        lb = self.sb(gst, "lb", [128, 1024], F32)
        omlb = self.sb(gst, "omlb", [128, 1024], F32)
        prow = self.sb(gst, "prow", [128, 9 * 512], F32)
        omka = self.sb(gst, "omka", [128, 512], F32)
        with self.scope() as st:
            cv = self.sb(st, "cv", [128, 16], F32)
            self.dma("sp", cv[:], I["cvec"], [], [cv])
            scv = self.sb(st, "scv", [128, 16], F32)
            self.act(scv[:], cv[:], AF.Silu, [cv], [scv])
            adab = self.sb(st, "adab", [128, 48], F32)
            self.dma("sp", adab[:], I["adab"], [], [adab])
            acc = self.sb(st, "modacc", [128, 96], F32)
            abuf = [self.sb(st, f"adaw{i}", [128, 3 * D], F32) for i in range(2)]
            pm = self.ps(st, "pmod", [128, 512], F32)
            adv = I["ada_w"].rearrange("(k p) c -> k p c", p=128)
            it = 0
            for k in range(KD):
                for hf in range(2):
                    ab = abuf[it % 2]
                    it += 1
                    self.dma("sp", ab[:, :], adv[k][:, hf * 3072:(hf + 1) * 3072], [], [ab])
                    for mm_ in range(24):
                        m = hf * 24 + mm_
                        self.mm(pm[:, 2 * m:2 * m + 2], ab[:, mm_ * 128:(mm_ + 1) * 128], scv[:, 2 * k:2 * k + 2], True, True,
                                [ab, scv], [pm])
                if k == 0:
                    self.cp("dve", acc[:], pm[:, 0:96], [pm], [acc])
                else:
                    self.tt("dve", acc[:], pm[:, 0:96], acc[:], ALU.add, [pm, acc], [acc])
            self.tt("dve", modT[:], acc[:].rearrange("p (m s) -> p m s", s=2),
                    adab[:].unsqueeze(2).broadcast_to([128, 48, 2]), ALU.add, [acc, adab], [modT])
            self.dbg("dbg_modT", modT[:].rearrange("p m s -> p (m s)"), [128, 96], modT)
            nmw = self.sb(st, "nmw", [128, 8], F32)
            nfw = self.sb(st, "nfw", [128, 8], F32)
            self.dma("sp", nmw[:], I["nmw"], [], [nmw])
            self.dma("sp", nfw[:], I["nfw"], [], [nfw])
            for s in range(2):
                for (wi, off, nw) in ((0, 8, nmw), (3, 32, nfw)):
                    self.stt(mods[:, wi, s, :], modT[:, off:off + 8, s], 1.0, nw[:], ALU.add, ALU.mult,
                             [modT, nw], [mods])
                for (wi, off) in ((1, 0), (2, 16), (4, 24), (5, 40)):
                    self.cp("dve", mods[:, wi, s, :], modT[:, off:off + 8, s], [modT], [mods])
            C["mods"] = mods
            self.dma("sp", fnw[:], I["fnw"], [], [fnw])
            C["fnw"] = fnw
            lbr = self.sb(st, "lbraw", [128, 2048], F32)
            self.dma("sp", lbr[:], I["hgrn_lb"].partition_broadcast(128), [], [lbr])
            lbd = self.sb(st, "lbd", [128, 1024], F32)
            self.tt("dve", lbd[:], lbr[:, 0:1024], lbr[:, 1024:2048], ALU.subtract, [lbr], [lbd])
            self.act(lb[:], lbd[:], AF.Sigmoid, [lbd], [lb])
            self.ts("dve", omlb[:], lb[:], -0.5, 0.5, ALU.mult, ALU.add, [lb], [omlb])
            self.ts("dve", lb[:], lb[:], 0.5, 0.5, ALU.mult, ALU.add, [lb], [lb])
            C["lb"], C["omlb"] = lb, omlb
            self.dma("sp", prow[:], I["prow"].partition_broadcast(128), [], [prow])
            C["prow"] = prow
            self.ts("dve", omka[:], prow[:, 1024:1536], -1.0, 1.0, ALU.mult, ALU.add, [prow], [omka])
            C["omka"] = omka
        self.C = C

    def rms_rstd(self, st_tiles, src, n, pbank, rstd, lnexp=True):
        sq = st_tiles["sq"]
        self.tt("dve", sq[:, 0:8 * n], src[:, 0:8 * n], src[:, 0:8 * n], ALU.mult, [src], [sq])
        c0 = 0
        while c0 < n:
            cn = min(512, n - c0)
            for k in range(KD):
                self.mm(pbank[:, 0:cn], self.C["onesD"][:], sq[:, k * n + c0:k * n + c0 + cn], k == 0, k == KD - 1,
                        [self.C["onesD"], sq], [pbank])
            if lnexp:
                self.act(rstd[:, c0:c0 + cn], pbank[:, 0:cn], AF.Ln, [pbank], [rstd], bias=RMS_EPS)
            else:
                self.act(rstd[:, c0:c0 + cn], pbank[:, 0:cn], AF.Sqrt, [pbank], [rstd], bias=RMS_EPS)
            c0 += cn
        if lnexp:
            self.act(rstd[:, 0:n], rstd[:, 0:n], AF.Exp, [rstd], [rstd], scale=-0.5)
        else:
            self.recip(rstd[:, 0:n], rstd[:, 0:n], [rstd], [rstd])

    def norm_mod(self, tiles, xt, n, pbank, A, B, hT, modtl, lnexp=True):
        rstd = tiles["rstd"]
        self.rms_rstd(tiles, xt, n, pbank, rstd, lnexp)
        for k in range(KD):
            tmp = tiles["ntmp"][k % 2]
            self.tt("dve", tmp[:, 0:n], xt[:, k * n:(k + 1) * n], rstd[:, 0:n], ALU.mult, [xt, rstd], [tmp])
            self.ts("dve", hT[:, k * n:(k + 1) * n], tmp[:, 0:n], A[:, k:k + 1], B[:, k:k + 1], ALU.mult, ALU.add,
                    [tmp, modtl], [hT])

    def phaseA(self):
        nc, I, C, R = self.nc, self.I, self.C, self.R
        mods = C["mods"]
        xTv = I["xT"].rearrange("(k p) t -> p k t", p=128)
        import os as _os
        _rnew = "1"
        for part in (("H",) if _rnew == "0" else ("H", "R")):
            halo = 0 if part == "H" else 1
            nw = 128 + 2 * halo
            with self.scope() as st:
                X = {"part": part, "nw": nw, "halo": halo}
                if part == "H":
                    X["lf"] = [self.sb(st, f"lf{i}", [128, 512], F32) for i in range(2)]
                else:
                    X["sg"] = [self.sb(st, f"sg{i}", [128, 512], F32) for i in range(2)]
                    X["lt"] = [self.sb(st, f"lt{i}", [128, 192], F32) for i in range(2)]
                if part == "H":
                    WH = self.sb(st, "WH", [128, 8 * PA], BF16)
                    wv = I["w_in"].rearrange("(k p) c -> p k c", p=128)
                    for k in range(KD):
                        self.dma("pool", WH[:, k * PA:(k + 1) * PA], wv[:, k, 0:PA], [], [WH])
                    X["WH"] = WH
                else:
                    WR = [self.sb(st, f"WR{t}", [128, 8 * PB], BF16) for t in range(3)]
                    with self.scope() as st2:
                        cvbc = self.sb(st2, "cvbc", [128, 3 * PB], F32)
                        self.dma("sp", cvbc[:], I["rwkv_conv"].rearrange("(o a) c -> o (a c)", o=1).partition_broadcast(128),
                                 [], [cvbc])
                        stg = [self.sb(st2, f"wstg{i}", [128, PB], F32) for i in range(2)]
                        wv = I["w_in"].rearrange("(k p) c -> p k c", p=128)
                        for k in range(KD):
                            sg = stg[k % 2]
                            self.dma("sp", sg[:], wv[:, k, PA:PIN], [], [sg])
                            for t in range(3):
                                self.tt("dve" if t != 1 else "pool", WR[t][:, k * PB:(k + 1) * PB], sg[:],
                                        cvbc[:, t * PB:(t + 1) * PB], ALU.mult, [sg, cvbc], [WR[t]])
                    X["WR"] = WR
                    X["W2x"] = self.sb(st, "W2x", [32, 1536], BF16)
                    self.dma("pool", X["W2x"][:], I["w2x"], [], [X["W2x"]])
                    X["G2"] = self.sb(st, "G2", [96, 512], BF16)
                    self.dma("pool", X["G2"][:], I["g2"], [], [X["G2"]])
                    X["lw"] = self.sb(st, "lw", [32, 384], BF16)
                    X["lg"] = self.sb(st, "lg", [96, 128], BF16)
                X["tiles"] = {
                    "sq": self.sb(st, "sq", [128, 8 * nw], BF16),
                    "rstd": self.sb(st, "rstd", [128, nw], F32),
                    "ntmp": [self.sb(st, f"ntmp{i}", [128, nw], F32) for i in range(2)],
                }
                X["xts"] = [self.sb(st, f"xt{i}", [128, 8 * nw], F32) for i in range(2)]
                X["hTs"] = [self.sb(st, f"hT{i}", [128, 8 * nw], BF16) for i in range(2)]
                X["OPst"] = [self.sb(st, f"OPst{i}", [128, 14 * 512], BF16) for i in range(2)]
                X["CXst"] = [self.sb(st, f"CXst{i}", [128, 3 * 512], BF16) for i in range(2)]
                X["dcs"] = [self.sb(st, f"dcs{i}", [128, 32], F32) for i in range(2)]
                X["sm"] = [self.sb(st, f"sm{i}", [128, 8], F32) for i in range(4)]
                if part == "H":
                    X["qs"] = [self.sb(st, f"qs{i}", [128, 512], F32) for i in range(2)]
                    X["sgz"] = [[self.sb(st, f"sgz{i}{d}", [128, 512], F32) for d in range(2)] for i in range(2)]
                    X["ft"] = [self.sb(st, f"ft{i}", [128, 512], F32) for i in range(7)]
                    X["hilo"] = [[self.sb(st, f"hilo{d}{i}", [128, 512], BF16) for i in range(2)] for d in range(2)]
                    X["pss"] = self.ps(st, "pss", [128, 512], F32)
                    X["pp"] = [self.ps(st, f"pp{i}", [128, 512], F32) for i in range(5)]
                    X["LA"] = self.ps(st, "LA", [128, 512], F32)
                    X["LB"] = self.ps(st, "LB", [128, 512], F32)
                else:
                    X["rkv"] = [[self.sb(st, f"rkv{i}{j}", [128, 512], F32) for j in range(3)] for i in range(2)]
                    _hl = [self.sb(st, f"hilo{i}", [128, 512], BF16) for i in range(2)]
                    X["hilo"] = [_hl, _hl]
                    X["ft"] = [self.sb(st, f"ft{i}", [128, 512], F32) for i in range(9)]
                    X["pss"] = self.ps(st, "pss", [128, 512], F32)
                    X["pp"] = [self.ps(st, f"pp{i}", [128, 512], F32) for i in range(4)]
                    X["LA"] = self.ps(st, "LA", [128, 512], F32)
                    X["LB"] = self.ps(st, "LB", [128, 512], F32)
                    X["LC"] = self.ps(st, "LC", [128, 512], F32)
                for _ in self.A_normproj(X, 0):
                    pass
                for g in range(self.NG):
                    self.A_early(X, g)
                    nxt = self.A_normproj(X, g + 1) if g + 1 < self.NG else None
                    if part == "H":
                        self._interleave(nxt, self.A_late_H(X, g), "abaaabaab")
                    else:
                        self._interleave(nxt, self.A_late_R(X, g), "bbbbaabaaba")

    def A_normproj(self, X, g):
        I, C = self.I, self.C
        mods = C["mods"]
        xTv = I["xT"].rearrange("(k p) t -> p k t", p=128)
        halo, nw = X["halo"], X["nw"]
        t0 = g * 128
        s0, sl, kind = [s for s in self.seqs if s[0] <= t0 < s[0] + s[1]][0]
        s1 = s0 + sl
        lo, hi = max(t0 - halo, s0), min(t0 + 128 + halo, s1)
        off = lo - (t0 - halo)
        n = hi - lo
        xt, hT = X["xts"][g % 2], X["hTs"][g % 2]
        x3 = xt[:, :].rearrange("p (k t) -> p k t", k=8)
        if n < nw:
            self.memset("dve", xt[:, :], 0.0, [xt])
        self.dma("sp", x3[:, :, off:off + n], xTv[:, :, lo:hi], [], [xt])
        self.norm_mod(X["tiles"], xt, nw, X["pss"], mods[:, 0, kind, :], mods[:, 1, kind, :], hT, mods)
        h3 = hT[:, :].rearrange("p (k t) -> p k t", k=8)
        if off > 0:
            self.memset("dve", h3[:, :, 0:off], 0.0, [hT])
        if off + n < nw:
            self.memset("dve", h3[:, :, off + n:nw], 0.0, [hT])
        yield
        pp = X["pp"]
        if X["part"] == "H":
            WH = X["WH"]
            for cg in range(5):
                for k in range(KD):
                    self.mm(pp[cg][:, :], hT[:, k * nw:k * nw + 128], WH[:, k * PA + cg * 512:k * PA + (cg + 1) * 512],
                            k == 0, k == KD - 1, [hT, WH], [pp[cg]])
                yield
        else:
            WR = X["WR"]
            for cg in range(4):
                c0 = cg * 512
                cn = 512 if cg < 3 else 192
                for t in range(3):
                    for k in range(KD):
                        self.mm(pp[cg][:, 0:cn], hT[:, k * nw + t:k * nw + t + 128], WR[t][:, k * PB + c0:k * PB + c0 + cn],
                                t == 0 and k == 0, t == 2 and k == KD - 1, [hT, WR[t]], [pp[cg]])
                yield

    @staticmethod
    def _interleave(a, b, pattern):
        for ch in pattern:
            gen = a if ch == "a" else b
            if gen is not None:
                next(gen, None)
        for gen in (a, b):
            if gen is not None:
                for _ in gen:
                    pass

    def A_early(self, X, g):
        pp = X["pp"]
        b = g % 2
        ops, cxs = X["OPst"][b], X["CXst"][b]
        if X["part"] == "H":
            self.act(X["qs"][b][:], pp[0][:], AF.Silu, [pp[0]], [X["qs"][b]])
            self.act(cxs[:, 1024:1536], pp[4][:], AF.Silu, [pp[4]], [cxs])
            self.act(ops[:, 13 * 512:14 * 512], pp[1][:], AF.Copy, [pp[1]], [ops])
            self.act(X["sgz"][b][0][:], pp[2][:], AF.Tanh, [pp[2]], [X["sgz"][b][0]], scale=0.5)
            self.act(X["sgz"][b][1][:], pp[3][:], AF.Tanh, [pp[3]], [X["sgz"][b][1]], scale=0.5)
        else:
            lt = X["lt"][b]
            self.act(lt[:, 0:64], pp[3][:, 0:64], AF.Tanh, [pp[3]], [lt])
            self.act(lt[:, 64:96], pp[3][:, 64:96], AF.Copy, [pp[3]], [lt])
            self.act(lt[:, 96:192], pp[3][:, 96:192], AF.Sigmoid, [pp[3]], [lt])
            r_, k_, v_ = X["rkv"][b]
            self.cp("dve", r_[:], pp[0][:], [pp[0]], [r_])
            self.cp("act", k_[:], pp[1][:], [pp[1]], [k_])
            self.cp("dve", v_[:], pp[2][:], [pp[2]], [v_])

    def A_late_H(self, X, g):
        C, R = self.C, self.R
        OPSd, CXd = R["OPS"], R["CX"]
        b = g % 2
        t0 = g * 128
        ops, cxs, dcs = X["OPst"][b], X["CXst"][b], X["dcs"][b]
        qs = X["qs"][b]
        ft = X["ft"]
        LAB = (X["LA"], X["LB"])
        pe = X["pss"]
        for d in range(2):
            sg_ = X["sgz"][b][d]
            lf = X["lf"][d]
            f_, kd = ft[d], ft[2 + d]
            hi, lo = X["hilo"][d]
            self.tt("dve", f_[:], sg_[:], C["omlb"][:, d * 512:(d + 1) * 512], ALU.mult, [sg_, C["omlb"]], [f_])
            self.tt("pool", f_[:], f_[:], C["lb"][:, d * 512:(d + 1) * 512], ALU.add, [f_, C["lb"]], [f_])
            self.act(lf[:], f_[:], AF.Ln, [f_], [lf])
            self.ts("pool", kd[:], f_[:], -1.0, 1.0, ALU.mult, ALU.add, [f_], [kd])
            self.cp("dve", hi[:], lf[:], [lf], [hi])
            self.tt("dve", lo[:], lf[:], hi[:], ALU.subtract, [lf, hi], [lo])
        yield
        for d in range(2):
            lf = X["lf"][d]
            hi, lo = X["hilo"][d]
            pc = LAB[d]
            self.mm(pc[:], C["tri16"][d][:], hi[:], True, False, [C["tri16"][d], hi], [pc])
            self.mm(pc[:], C["tri16"][d][:], lo[:], False, True, [C["tri16"][d], lo], [pc])
            for h in range(4):
                c = (d * 4 + h) * 2
                self.mm(pe[:, c:c + 2], lf[:, h * 128:(h + 1) * 128], C["cind"][:], True, True, [lf, C["cind"]], [pe])
        yield
        for d in range(2):
            kd = ft[2 + d]
            csc, e1, e2 = ft[4], ft[5], ft[6]
            pc = LAB[d]
            self.ts("dve", csc[:], pc[:], -80.0, None, ALU.max, ALU.bypass, [pc], [csc])
            self.act(e1[:], csc[:], AF.Exp, [csc], [e1])
            self.act(e2[:], csc[:], AF.Exp, [csc], [e2], scale=-1.0)
            self.tt("dve", ops[:, (6 * d + 4) * 512:(6 * d + 5) * 512], qs[:], e1[:], ALU.mult, [qs, e1], [ops])
            self.tt("dve", ops[:, (6 * d + 5) * 512:(6 * d + 6) * 512], kd[:], e2[:], ALU.mult, [kd, e2], [ops])
        self.act(dcs[:, 0:16].rearrange("p (c x) -> p c x", c=2), pe[:, 0:16].rearrange("p (x c) -> p c x", c=2),
                 AF.Exp, [pe], [dcs])
        self.dma("pool", R["ECs"][g].rearrange("c p x -> p c x"), dcs[:, 0:16].rearrange("p (c x) -> p c x", c=2),
                 [dcs], [])
        o3 = ops[:, :].rearrange("p (s c) -> p s c", s=14)
        OP3 = OPSd[t0:t0 + 128, :].rearrange("t (s c) -> t s c", s=14)
        self.dma("pool", OP3[:, 4:6, :], o3[:, 4:6, :], [ops], [])
        self.dma("pool", OP3[:, 10:12, :], o3[:, 10:12, :], [ops], [])
        self.dma("pool", OP3[:, 13:14, :], o3[:, 13:14, :], [ops], [])
        self.dma("pool", CXd[t0:t0 + 128, 1024:1536], cxs[:, 1024:1536], [cxs], [])
        yield

    def A_late_R(self, X, g):
        C, R = self.C, self.R
        OPSd, CXd = R["OPS"], R["CX"]
        prow = C["prow"]
        kk_bc, ka_bc, rk_bc = prow[:, 512:1024], prow[:, 1024:1536], prow[:, 1536:2048]
        b = g % 2
        t0 = g * 128
        ops, cxs, dcs = X["OPst"][b], X["CXst"][b], X["dcs"][b]
        r_, k_, v_ = X["rkv"][b]
        lt, lw, lg, W2x, G2 = X["lt"][b], X["lw"], X["lg"], X["W2x"], X["G2"]
        ft, sm = X["ft"], X["sm"]
        LA, LB, LC = X["LA"], X["LB"], X["LC"]
        identf = C["identf"]
        for i in range(3):
            self.tr(LC[0:32, i * 128:(i + 1) * 128], lt[:, i * 32:(i + 1) * 32], identf[:], [lt, identf], [LC])
        self.tr(LC[0:96, 384:512], lt[:, 96:192], identf[:], [lt, identf], [LC])
        yield
        self.cp("dve", lw[0:32, :], LC[0:32, 0:384], [LC], [lw])
        self.cp("dve", lg[:, :], LC[0:96, 384:512], [LC], [lg])
        a_, sgf, sgb = ft[0], X["sg"][0], X["sg"][1]
        self.mm(LA[:], lw[0:32, 256:384], W2x[0:32, 1024:1536], True, True, [lw, W2x], [LA])
        self.mm(LB[:], lw[0:32, 128:256], W2x[0:32, 512:1024], True, True, [lw, W2x], [LB])
        self.mm(LC[:], lg[:, :], G2[:, :], True, True, [lg, G2], [LC])
        yield
        self.tt("dve", a_[:], LA[:], prow[:, 4096:4608], ALU.add, [LA, prow], [a_])
        self.act(a_[:], a_[:], AF.Sigmoid, [a_], [a_])
        self.mm(LA[:], lw[0:32, 0:128], W2x[0:32, 0:512], True, True, [lw, W2x], [LA])
        self.tt("dve", sgb[:], LB[:], prow[:, 3584:4096], ALU.add, [LB, prow], [sgb])
        self.act(sgb[:], sgb[:], AF.Sigmoid, [sgb], [sgb])
        self.act(cxs[:, 0:512], LC[:], AF.Copy, [LC], [cxs])
        self.tt("dve", sgf[:], LA[:], prow[:, 3072:3584], ALU.add, [LA, prow], [sgf])
        self.act(sgf[:], sgf[:], AF.Sigmoid, [sgf], [sgf])
        yield
        for d in range(2):
            sg_ = (sgf, sgb)[d]
            pc = (LA, LB)[d]
            hi, lo = X["hilo"][d]
            self.cp("dve", hi[:], sg_[:], [sg_], [hi])
            self.tt("dve", lo[:], sg_[:], hi[:], ALU.subtract, [sg_, hi], [lo])
            for h in range(8):
                c = (d * 8 + h) * 2
                self.mm(LC[0:64, c:c + 2], sg_[:, h * 64:(h + 1) * 64], C["cind"][:], True, True, [sg_, C["cind"]], [LC])
            self.mm(pc[:], C["tri16"][d][:], hi[:], True, False, [C["tri16"][d], hi], [pc])
            self.mm(pc[:], C["tri16"][d][:], lo[:], False, True, [C["tri16"][d], lo], [pc])
        yield
        kx, sq, kk, t1, kp, nb, t2 = ft[1], ft[2], ft[3], ft[4], ft[5], ft[6], ft[4]
        self.tt("dve", kx[:], k_[:], kk_bc, ALU.mult, [k_, prow], [kx])
        self.act(sq[:], kx[:], AF.Square, [kx], [sq])
        self.red(sm[0][:], sq[:].rearrange("p (h j) -> p h j", h=8), [sq], [sm[0]])
        self.act(sm[1][:], sm[0][:], AF.Ln, [sm[0]], [sm[1]], bias=1e-12)
        self.act(sm[1][:], sm[1][:], AF.Exp, [sm[1]], [sm[1]], scale=-0.5)
        self.tt("dve", kk[:].rearrange("p (h j) -> p h j", h=8), kx[:].rearrange("p (h j) -> p h j", h=8),
                sm[1][:].unsqueeze(2).broadcast_to([128, 8, 64]), ALU.mult, [kx, sm[1]], [kk])
        self.tt("dve", t1[:], a_[:], ka_bc, ALU.mult, [a_, prow], [t1])
        self.tt("dve", t1[:], t1[:], C["omka"][:], ALU.add, [t1, C["omka"]], [t1])
        self.tt("dve", kp[:], k_[:], t1[:], ALU.mult, [k_, t1], [kp])
        self.stt(nb[:], kk[:], -1.0, a_[:], ALU.mult, ALU.mult, [kk, a_], [nb])
        self.tt("dve", t2[:], r_[:], kp[:], ALU.mult, [r_, kp], [t2])
        self.tt("dve", t2[:], t2[:], rk_bc, ALU.mult, [t2, prow], [t2])
        self.red(sm[2][:], t2[:].rearrange("p (h j) -> p h j", h=8), [t2], [sm[2]])
        self.tt("dve", cxs[:, 512:1024].rearrange("p (h j) -> p h j", h=8), v_[:].rearrange("p (h j) -> p h j", h=8),
                sm[2][:].unsqueeze(2).broadcast_to([128, 8, 64]), ALU.mult, [v_, sm[2]], [cxs])
        self.cp("act", ops[:, 12 * 512:13 * 512], v_[:], [v_], [ops])
        yield
        for d in range(2):
            sg_ = (sgf, sgb)[d]
            pc = (LA, LB)[d]
            gi, ginv, tmp, ge = ft[7], ft[8], ft[2], ft[1]
            self.act(gi[:], pc[:], AF.Exp, [pc], [gi], scale=-DECAY)
            self.act(ginv[:], pc[:], AF.Exp, [pc], [ginv], scale=DECAY)
            self.tt("dve", tmp[:], pc[:], sg_[:], ALU.subtract, [pc, sg_], [tmp])
            self.act(ge[:], tmp[:], AF.Exp, [tmp], [ge], scale=-DECAY)
            b0 = 6 * d * 512
            self.tt("dve", ops[:, b0:b0 + 512], r_[:], gi[:], ALU.mult, [r_, gi], [ops])
            self.tt("dve", ops[:, b0 + 512:b0 + 1024], kp[:], ginv[:], ALU.mult, [kp, ginv], [ops])
            self.tt("dve", ops[:, b0 + 1024:b0 + 1536], nb[:], ginv[:], ALU.mult, [nb, ginv], [ops])
            self.tt("dve", ops[:, b0 + 1536:b0 + 2048], kk[:], ge[:], ALU.mult, [kk, ge], [ops])
        self.act(dcs[0:64, 0:32].rearrange("p (c x) -> p c x", c=2), LC[0:64, 0:32].rearrange("p (x c) -> p c x", c=2),
                 AF.Exp, [LC], [dcs], scale=-DECAY)
        self.dma("pool", R["GCs"][g].rearrange("c p x -> p c x"), dcs[0:64, 0:32].rearrange("p (c x) -> p c x", c=2),
                 [dcs], [self.bR["GCs"]])
        o3 = ops[:, :].rearrange("p (s c) -> p s c", s=14)
        OP3 = OPSd[t0:t0 + 128, :].rearrange("t (s c) -> t s c", s=14)
        self.dma("pool", OP3[:, 0:4, :], o3[:, 0:4, :], [ops], [self.bR["OPS"]])
        self.dma("pool", OP3[:, 6:10, :], o3[:, 6:10, :], [ops], [self.bR["OPS"]])
        self.dma("pool", OP3[:, 12:13, :], o3[:, 12:13, :], [ops], [self.bR["OPS"]])
        self.dma("pool", CXd[t0:t0 + 128, 0:1024], cxs[:, 0:1024], [cxs], [self.bR["CX"]])
        yield

    def scan(self):
        C, R, I, O = self.C, self.R, self.I, self.O
        ident, M1, M3, MI = C["ident"], C["M1"], C["M3"], C["MI"]
        OPSd = R["OPS"]
        with self.scope() as st:
            OP = [self.sb(st, f"OP{i}", [128, 8 * 512], BF16) for i in range(2)]
            gCt = [self.sb(st, f"gCt{i}", [128, 8], F32) for i in range(2)]
            eCt = [self.sb(st, f"eCt{i}", [128, 8], F32) for i in range(2)]
            XT = self.sb(st, "XT", [128, 8 * 256], BF16)
            GS = self.sb(st, "GS", [128, 8 * 320], BF16)
            XA = [self.sb(st, f"XA{i}", [128, 8 * 128], BF16) for i in range(2)]
            NN = [self.sb(st, f"NN{i}", [128, 8 * 128], BF16) for i in range(2)]
            GT = self.sb(st, "GT", [128, 512], BF16)
            Pb = self.sb(st, "Pb", [128, 512], BF16)
            T32 = self.sb(st, "T32", [128, 512], F32)
            Tb = [self.sb(st, f"Tb{i}", [128, 512], BF16) for i in range(2)]
            Ttmp = self.sb(st, "Ttmp", [128, 512], F32)
            Ysb = [self.sb(st, f"Ysb{i}", [128, 512], F32) for i in range(2)]
            HT = self.sb(st, "HT", [128, 1024], BF16)
            SCz = self.sb(st, "SCz", [128, 4 * 2 * 64], BF16)
            Osb = [self.sb(st, f"Osb{i}", [128, 512], F32) for i in range(2)]
            S32 = self.sb(st, "S32", [128, 1024], F32)
            Sb = [self.sb(st, f"Sb{i}", [128, 1024], BF16) for i in range(2)]
            Stmp = self.sb(st, "Stmp", [128, 512], F32)
            BT0 = self.ps(st, "BT0", [128, 1024], BF16)
            BT1 = self.ps(st, "BT1", [128, 1024], BF16)
            PX = self.ps(st, "PX", [128, 1024], F32)
            PN = self.ps(st, "PN", [128, 1024], F32)
            F4 = self.ps(st, "F4", [128, 512], F32)
            F5 = self.ps(st, "F5", [128, 512], F32)
            self.memset("pool", SCz[:], 0.0, [SCz])
            M1b = M1[:].unsqueeze(1).broadcast_to([128, 8, 128])
            M3b = M3[:].unsqueeze(1).broadcast_to([128, 8, 64])
            stepno = 0
            for si, (s0, sl, kind) in enumerate(self.seqs):
                nch = sl // 64
                if kind == 0:
                    self.memset("pool", T32[:], 0.0, [T32])
                    self.memset("pool", S32[:], 0.0, [S32])
                else:
                    for l in range(2):
                        self.dma("sp", T32[64 * l:64 * l + 64, :], I["st_r"][l], [], [T32])
                        self.dma("sp", S32[:, l * 512:(l + 1) * 512], I["st_h"][l], [], [S32])
                cur = 0
                self.cp("act", Tb[cur][:], T32[:], [T32], [Tb[cur]])
                self.cp("act", Sb[cur][:], S32[:], [S32], [Sb[cur]])
                for i in range(nch):
                    op, gc, ec = OP[stepno % 2], gCt[stepno % 2], eCt[stepno % 2]
                    ysb, osb = Ysb[stepno % 2], Osb[stepno % 2]
                    stepno += 1
                    tok = [s0 + 64 * i, s0 + 64 * (nch - 1 - i)]
                    o3 = op[:, :].rearrange("p (s c) -> p s c", s=8)
                    for l in range(2):
                        src = OPSd[tok[l]:tok[l] + 64, :].rearrange("t (s c) -> t s c", s=14)
                        self.dma("sp", o3[64 * l:64 * l + 64, 0:6, :], src[:, 6 * l:6 * l + 6, :], [self.bR["OPS"]], [op])
                        self.dma("sp", o3[64 * l:64 * l + 64, 6:8, :], src[:, 12:14, :], [self.bR["OPS"]], [op])
                        gg, cc = tok[l] // 128, (tok[l] % 128) // 64
                        self.dma("sp", gc[64 * l:64 * l + 64, :], R["GCs"][gg, cc][:, 8 * l:8 * l + 8], [self.bR["GCs"]], [gc])
                        self.dma("sp", ec[:, 4 * l:4 * l + 4], R["ECs"][gg, cc][:, 4 * l:4 * l + 4], [self.bR["ECs"]], [ec])
                    for h in range(8):
                        bt = BT0 if h < 4 else BT1
                        for X, slot in enumerate((3, 0, 2, 1)):
                            for l in range(2):
                                r0 = 64 * l
                                self.tr(bt[r0:r0 + 64, (h % 4) * 256 + X * 64:(h % 4) * 256 + X * 64 + 64],
                                        op[r0:r0 + 64, slot * 512 + h * 64:slot * 512 + h * 64 + 64],
                                        ident[r0:r0 + 64, r0:r0 + 64], [op, ident], [bt])
                    self.cp("dve", XT[:, 0:1024], BT0[:, :], [BT0], [XT])
                    self.cp("act", XT[:, 1024:2048], BT1[:, :], [BT1], [XT])
                    for h in range(8):
                        for l in range(2):
                            r0 = 64 * l
                            xb = h * 256
                            self.mm(PX[r0:r0 + 64, h * 128:h * 128 + 128], XT[r0:r0 + 64, xb + 128:xb + 192],
                                    XT[r0:r0 + 64, xb:xb + 128], True, True, [XT], [PX])
                            self.mm(PN[r0:r0 + 64, h * 128:h * 128 + 128], XT[r0:r0 + 64, xb + 192:xb + 256],
                                    XT[r0:r0 + 64, xb:xb + 128], True, True, [XT], [PN])
                            self.mm(F4[r0:r0 + 64, h * 64:h * 64 + 64], XT[r0:r0 + 64, xb:xb + 64],
                                    XT[r0:r0 + 64, xb + 128:xb + 192], True, True, [XT], [F4])
                    G3 = GS[:, :].rearrange("p (h c) -> p h c", h=8)
                    self.tt("dve", G3[:, :, 0:128], PX[:, :].rearrange("p (h c) -> p h c", h=8), M1b, ALU.mult, [PX, M1], [GS])
                    self.tt("dve", G3[:, :, 128:256], PN[:, :].rearrange("p (h c) -> p h c", h=8), M1b, ALU.mult, [PN, M1], [GS])
                    self.tt("dve", G3[:, :, 256:320], F4[:, :].rearrange("p (h c) -> p h c", h=8), M3b, ALU.mult, [F4, M3], [GS])
                    xa = XA[0]
                    xa3 = xa[:, :].rearrange("p (h c) -> p h c", h=8)
                    self.cp("dve", xa3[:, :, 0:64], op[:, 3 * 512:4 * 512].rearrange("p (h c) -> p h c", h=8), [op], [xa])
                    for h in range(8):
                        for l in range(2):
                            r0 = 64 * l
                            self.mm(F5[r0:r0 + 64, h * 64:h * 64 + 64], GS[r0:r0 + 64, h * 320 + 128:h * 320 + 192],
                                    op[r0:r0 + 64, 6 * 512 + h * 64:6 * 512 + h * 64 + 64], True, True, [GS, op], [F5])
                    self.cp("act", xa3[:, :, 64:128], F5[:, :].rearrange("p (h c) -> p h c", h=8), [F5], [xa])
                    xc = 0
                    for k in range(6):
                        xcur, xnext = XA[xc], XA[1 - xc]
                        if k == 0:
                            nsrc = GS
                            noff = lambda h: h * 320 + 256
                            ntoff = lambda h: h * 320
                        else:
                            nsrc = NN[(k - 1) % 2]
                            noff = lambda h: h * 128
                            ntoff = lambda h: h * 128 + 64
                        if k < 5:
                            nn = NN[k % 2]
                            for h in range(8):
                                for l in range(2):
                                    r0 = 64 * l
                                    self.mm(PN[r0:r0 + 64, h * 128:h * 128 + 64], nsrc[r0:r0 + 64, ntoff(h):ntoff(h) + 64],
                                            nsrc[r0:r0 + 64, noff(h):noff(h) + 64], True, True, [nsrc], [PN])
                                    self.mm(PN[r0:r0 + 64, h * 128 + 64:h * 128 + 128], nsrc[r0:r0 + 64, noff(h):noff(h) + 64],
                                            nsrc[r0:r0 + 64, ntoff(h):ntoff(h) + 64], True, True, [nsrc], [PN])
                        if k < 5:
                            self.cp("act", nn[:, :], PN[:, :], [PN], [nn])
                        for h in range(8):
                            for l in range(2):
                                r0 = 64 * l
                                self.mm(PX[r0:r0 + 64, h * 128:h * 128 + 128], nsrc[r0:r0 + 64, ntoff(h):ntoff(h) + 64],
                                        xcur[r0:r0 + 64, h * 128:h * 128 + 128], True, True, [nsrc, xcur], [PX])
                        self.tt("dve", xnext[:, :], PX[:, :], xcur[:, :], ALU.add, [PX, xcur], [xnext])
                        xc = 1 - xc
                    x6 = XA[xc]
                    x63 = x6[:, :].rearrange("p (h c) -> p h c", h=8)
                    for h in range(8):
                        for l in range(2):
                            r0 = 64 * l
                            self.tr(BT1[r0:r0 + 64, h * 64:h * 64 + 64], x6[r0:r0 + 64, h * 128:h * 128 + 64],
                                    ident[r0:r0 + 64, r0:r0 + 64], [x6, ident], [BT1])
                    self.cp("dve", GT[:, :], BT1[:, 0:512], [BT1], [GT])
                    tbc, tbn = Tb[cur], Tb[1 - cur]
                    for h in range(8):
                        for l in range(2):
                            r0 = 64 * l
                            self.mm(F4[r0:r0 + 64, h * 64:h * 64 + 64], GT[r0:r0 + 64, h * 64:h * 64 + 64],
                                    tbc[r0:r0 + 64, h * 64:h * 64 + 64], True, True, [GT, tbc], [F4])
                    self.tt("dve", Pb[:, :].rearrange("p (h c) -> p h c", h=8), F4[:, :].rearrange("p (h c) -> p h c", h=8),
                            x63[:, :, 64:128], ALU.add, [F4, x6], [Pb])
                    for h in range(8):
                        for l in range(2):
                            r0 = 64 * l
                            yo = PN[r0:r0 + 64, h * 64:h * 64 + 64]
                            self.mm(yo, XT[r0:r0 + 64, h * 256 + 64:h * 256 + 128], tbc[r0:r0 + 64, h * 64:h * 64 + 64],
                                    True, False, [XT, tbc], [PN])
                            self.mm(yo, GS[r0:r0 + 64, h * 320 + 192:h * 320 + 256],
                                    op[r0:r0 + 64, 6 * 512 + h * 64:6 * 512 + h * 64 + 64], False, False, [GS, op], [PN])
                            self.mm(yo, GS[r0:r0 + 64, h * 320 + 64:h * 320 + 128], Pb[r0:r0 + 64, h * 64:h * 64 + 64],
                                    False, True, [GS, Pb], [PN])
                    self.cp("act", ysb[:, :], PN[:, 0:512], [PN], [ysb])
                    for l in range(2):
                        self.dma("pool", R["YS"][l, tok[l]:tok[l] + 64, :], ysb[64 * l:64 * l + 64, :], [ysb], [self.bR["YS"]])
                    for h in range(8):
                        for l in range(2):
                            r0 = 64 * l
                            to = F5[r0:r0 + 64, h * 64:h * 64 + 64]
                            self.mm(to, op[r0:r0 + 64, 1 * 512 + h * 64:1 * 512 + h * 64 + 64],
                                    op[r0:r0 + 64, 6 * 512 + h * 64:6 * 512 + h * 64 + 64], True, False, [op], [F5])
                            self.mm(to, op[r0:r0 + 64, 2 * 512 + h * 64:2 * 512 + h * 64 + 64],
                                    Pb[r0:r0 + 64, h * 64:h * 64 + 64], False, True, [op, Pb], [F5])
                    self.tt("dve", Ttmp[:, :], F5[:, :], T32[:, :], ALU.add, [F5, T32], [Ttmp])
                    self.tt("dve", T32[:, :].rearrange("p (h c) -> p h c", h=8), Ttmp[:, :].rearrange("p (h c) -> p h c", h=8),
                            gc[:, :].unsqueeze(2).broadcast_to([128, 8, 64]), ALU.mult, [Ttmp, gc], [T32])
                    self.cp("act", tbn[:, :], T32[:, :], [T32], [tbn])
                    sbc, sbn = Sb[cur], Sb[1 - cur]
                    for X, slot in enumerate((4, 5)):
                        for h in range(4):
                            c0 = (X * 4 + h) * 128
                            self.tr(BT0[:, c0:c0 + 128], op[:, slot * 512 + h * 128:slot * 512 + h * 128 + 128], ident[:, :],
                                    [op, ident], [BT0])
                    self.cp("dve", HT[:, :], BT0[:, :], [BT0], [HT])
                    for h in range(4):
                        for l in range(2):
                            r0 = 64 * l
                            self.mm(PX[r0:r0 + 64, h * 64:h * 64 + 64], HT[:, (4 + h) * 128 + r0:(4 + h) * 128 + r0 + 64],
                                    HT[:, h * 128 + r0:h * 128 + r0 + 64], True, True, [HT], [PX])
                    for l in range(2):
                        r0 = 64 * l
                        sc4 = SCz[r0:r0 + 64, :].rearrange("p (h l t) -> p h l t", h=4, l=2)
                        self.tt("dve", sc4[:, :, l, :], PX[r0:r0 + 64, 0:256].rearrange("p (h t) -> p h t", h=4),
                                MI[r0:r0 + 64, :].unsqueeze(1).broadcast_to([64, 4, 64]), ALU.mult, [PX, MI], [SCz])
                    for h in range(4):
                        for l in range(2):
                            r0 = 64 * l
                            oo = PX[r0:r0 + 64, 512 + h * 128:512 + h * 128 + 128]
                            self.mm(oo, SCz[:, (h * 2 + l) * 64:(h * 2 + l) * 64 + 64], op[:, 7 * 512 + h * 128:7 * 512 + h * 128 + 128],
                                    True, False, [SCz, op], [PX])
                            self.mm(oo, HT[:, h * 128 + r0:h * 128 + r0 + 64], sbc[:, (l * 4 + h) * 128:(l * 4 + h) * 128 + 128],
                                    False, True, [HT, sbc], [PX])
                    self.cp("act", osb[:, :], PX[:, 512:1024], [PX], [osb])
                    for l in range(2):
                        self.dma("pool", R["OSC"][l, tok[l]:tok[l] + 64, :], osb[64 * l:64 * l + 64, :], [osb], [self.bR["OSC"]])
                    for l in range(2):
                        r0 = 64 * l
                        pu = PN if l == 0 else F4
                        pu0 = 512 if l == 0 else 0
                        for h in range(4):
                            self.mm(pu[:, pu0 + h * 128:pu0 + h * 128 + 128], op[r0:r0 + 64, 5 * 512 + h * 128:5 * 512 + h * 128 + 128],
                                    op[r0:r0 + 64, 7 * 512 + h * 128:7 * 512 + h * 128 + 128], True, True, [op], [pu])
                        self.tt("dve", Stmp[:, :], pu[:, pu0:pu0 + 512], S32[:, l * 512:(l + 1) * 512], ALU.add, [pu, S32], [Stmp])
                        self.tt("dve", S32[:, l * 512:(l + 1) * 512].rearrange("p (h c) -> p h c", h=4),
                                Stmp[:, :].rearrange("p (h c) -> p h c", h=4),
                                ec[:, 4 * l:4 * l + 4].unsqueeze(2).broadcast_to([128, 4, 128]), ALU.mult, [Stmp, ec], [S32])
                    self.cp("act", sbn[:, :], S32[:, :], [S32], [sbn])
                    cur = 1 - cur
                if kind == 0:
                    for l in range(2):
                        self.dma("pool", O["nsr"][si, l], T32[64 * l:64 * l + 64, :], [T32], [self.bO["nsr"]])
                        self.dma("pool", O["nsh"][si, l], S32[:, l * 512:(l + 1) * 512], [S32], [self.bO["nsh"]])

    def ctx_tiles(self):
        out = []
        per = max(1, 512 // self.L_ctx)
        si = 0
        while si < self.n_ctx:
            ns = min(per, self.n_ctx - si)
            out.append((si * self.L_ctx, ns * self.L_ctx))
            si += ns
        return out

    def phaseC1(self):
        C, R, I = self.C, self.R, self.I
        ident, prow, mods, selv = C["ident"], C["prow"], C["mods"], C["selv"]
        hnw_bc, lnw_bc, lnb_bc = prow[:, 0:512], prow[:, 2048:2560], prow[:, 2560:3072]
        xTv = I["xT"].rearrange("(k p) t -> p k t", p=128)
        xov = I["xTo"].rearrange("(k p) t -> p k t", p=128)
        x1v = R["X1T"].rearrange("(k p) t -> p k t", p=128)
        x1wv = R["X1W"].rearrange("(k p) t -> p k t", p=128)
        NCT = self.n_ctx * self.L_ctx
        QL, NQ = self.QL, self.NQ
        WL = QL + 128
        with self.scope() as st:
            WO = self.sb(st, "WO", [128, 8 * D], BF16)
            wv = I["w_out"].rearrange("(k p) c -> p k c", p=128)
            for k in range(KD):
                self.dma("pool", WO[:, k * D:(k + 1) * D], wv[:, k, :], [], [WO])
            yf = [self.sb(st, f"yf{i}", [128, 512], F32) for i in range(2)]
            yb = [self.sb(st, f"yb{i}", [128, 512], F32) for i in range(2)]
            of = [self.sb(st, f"of{i}", [128, 512], F32) for i in range(2)]
            ob = [self.sb(st, f"ob{i}", [128, 512], F32) for i in range(2)]
            cx = [self.sb(st, f"cx{i}", [128, 1536], BF16) for i in range(2)]
            cxa = self.sb(st, "cxa", [128, 1536], F32)
            ft = [self.sb(st, f"c1f{i}", [128, 512], F32) for i in range(4)]
            sm = [self.sb(st, f"c1s{i}", [128, 8], F32) for i in range(6)]
            MIX = self.sb(st, "MIX", [128, 1024], BF16)
            MIXT = self.sb(st, "MIXT", [128, 8 * 512], BF16)
            xt = self.sb(st, "c1xt", [128, 8 * 512], F32)
            x1 = self.sb(st, "c1x1", [128, 8 * 512], F32)
            PT = self.ps(st, "c1PT", [128, 1024], BF16)
            PO = [self.ps(st, f"c1PO{i}", [128, 512], F32) for i in range(2)]
            self._c1i = 0

            def post(y, o, cxt, gq):
                sq, osq = ft[1], ft[3]
                self.act(sq[:, :], y[:, :], AF.Square, [y], [sq])
                y3 = y[:, :].rearrange("p (h j) -> p h j", h=8)
                self.red(sm[0][:, :], y3, [y], [sm[0]])
                self.red(sm[1][:, :], sq[:, :].rearrange("p (h j) -> p h j", h=8), [sq], [sm[1]])
                self.ts("dve", sm[2][:, :], sm[0][:, :], 1.0 / 64, None, ALU.mult, ALU.bypass, [sm[0]], [sm[2]])
                self.tt("dve", sm[3][:, :], sm[2][:, :], sm[2][:, :], ALU.mult, [sm[2]], [sm[3]])
                self.stt(sm[4][:, :], sm[1][:, :], 1.0 / 64, sm[3][:, :], ALU.mult, ALU.subtract, [sm[1], sm[3]], [sm[4]])
                self.act(sm[5][:, :], sm[4][:, :], AF.Sqrt, [sm[4]], [sm[5]], bias=GN_EPS)
                self.recip(sm[5][:, :], sm[5][:, :], [sm[5]], [sm[5]])
                self.tt("dve", y3, y3, sm[2][:, :].unsqueeze(2).broadcast_to([128, 8, 64]), ALU.subtract, [y, sm[2]], [y])
                self.tt("dve", y3, y3, sm[5][:, :].unsqueeze(2).broadcast_to([128, 8, 64]), ALU.mult, [y, sm[5]], [y])
                self.tt("dve", y[:, :], y[:, :], lnw_bc, ALU.mult, [y, prow], [y])
                self.tt("dve", y[:, :], y[:, :], lnb_bc, ALU.add, [y, prow], [y])
                self.tt("dve", y[:, :], y[:, :], cxt[:, 512:1024], ALU.add, [y, cxt], [y])
                self.tt("dve", MIX[:, 512:1024], y[:, :], cxt[:, 0:512], ALU.mult, [y, cxt], [MIX])
                self.act(osq[:, :], o[:, :], AF.Square, [o], [osq])
                self.red(sm[0][:, 0:4], osq[:, :].rearrange("p (h j) -> p h j", h=4), [osq], [sm[0]])
                self.act(sm[1][:, 0:4], sm[0][:, 0:4], AF.Sqrt, [sm[0]], [sm[1]], scale=1.0 / 128, bias=RMS_EPS)
                self.recip(sm[1][:, 0:4], sm[1][:, 0:4], [sm[1]], [sm[1]])
                o3 = o[:, :].rearrange("p (h j) -> p h j", h=4)
                self.tt("dve", o3, o3, sm[1][:, 0:4].unsqueeze(2).broadcast_to([128, 4, 128]), ALU.mult, [o, sm[1]], [o])
                self.tt("dve", o[:, :], o[:, :], hnw_bc, ALU.mult, [o, prow], [o])
                self.tt("dve", MIX[:, 0:512], o[:, :], cxt[:, 1024:1536], ALU.mult, [o, cxt], [MIX])
                for k in range(KD):
                    self.tr(PT[:, k * 128:(k + 1) * 128], MIX[:, k * 128:(k + 1) * 128], ident[:, :], [MIX, ident], [PT])
                self.cp("act", MIXT[:, :].rearrange("p (k t) -> p k t", k=8)[:, :, gq * 128:(gq + 1) * 128],
                        PT[:, :].rearrange("p (k t) -> p k t", k=8), [PT], [MIXT])

            def dense(n, kind, dst):
                for m in range(KD):
                    po = PO[m % 2]
                    for k in range(KD):
                        self.mm(po[:, 0:n], WO[:, k * D + m * 128:k * D + (m + 1) * 128], MIXT[:, k * 512:k * 512 + n],
                                k == 0, k == KD - 1, [WO, MIXT], [po])
                    self.stt(x1[:, m * n:(m + 1) * n], po[:, 0:n], mods[:, 2, kind, m:m + 1], xt[:, m * n:(m + 1) * n],
                             ALU.mult, ALU.add, [po, mods, xt], [x1])
                self.dma("pool", dst, x1[:, 0:8 * n].rearrange("p (k t) -> p k t", k=8), [x1], [])

            gi = 0
            for (t0, n) in self.ctx_tiles():
                self.dma("sp", xt[:, 0:8 * n].rearrange("p (k t) -> p k t", k=8), xTv[:, :, t0:t0 + n], [], [xt])
                for gq in range(n // 128):
                    ta = t0 + gq * 128
                    b = gi % 2
                    gi += 1
                    self.dma("sp", yf[b][:, :], R["YS"][0, ta:ta + 128, :], [], [yf[b]])
                    self.dma("sp", yb[b][:, :], R["YS"][1, ta:ta + 128, :], [], [yb[b]])
                    self.dma("sp", of[b][:, :], R["OSC"][0, ta:ta + 128, :], [], [of[b]])
                    self.dma("sp", ob[b][:, :], R["OSC"][1, ta:ta + 128, :], [], [ob[b]])
                    self.dma("sp", cx[b][:, :], R["CX"][ta:ta + 128, :], [], [cx[b]])
                    y, o = ft[0], ft[2]
                    self.tt("dve", y[:, :], yf[b][:, :], yb[b][:, :], ALU.add, [yf[b], yb[b]], [y])
                    self.tt("dve", o[:, :], of[b][:, :], ob[b][:, :], ALU.add, [of[b], ob[b]], [o])
                    post(y, o, cx[b], gq)
                dense(n, 0, x1v[:, :, t0:t0 + n])
            w0 = 0
            while w0 < WL:
                n = min(512, WL - w0)
                self.dma("sp", xt[:, 0:8 * n].rearrange("p (k t) -> p k t", k=8), xov[:, :, w0:w0 + n], [], [xt])
                for gq in range(n // 128):
                    wj = w0 + gq * 128
                    y, o = ft[0], ft[2]
                    for q in range(NQ):
                        b = gi % 2
                        gi += 1
                        ta = NCT + q * QL - 64 + wj
                        lo, hi = max(ta, NCT), min(ta + 128, NCT + self.L_lat)
                        p0, pn = lo - ta, hi - lo
                        if pn < 128:
                            for tl in (yf[b], yb[b], of[b], ob[b], cx[b]):
                                self.memset("dve", tl[:, :], 0.0, [tl])
                        if pn > 0:
                            self.dma("sp", yf[b][p0:p0 + pn, :], R["YS"][0, lo:hi, :], [], [yf[b]])
                            self.dma("sp", yb[b][p0:p0 + pn, :], R["YS"][1, lo:hi, :], [], [yb[b]])
                            self.dma("sp", of[b][p0:p0 + pn, :], R["OSC"][0, lo:hi, :], [], [of[b]])
                            self.dma("sp", ob[b][p0:p0 + pn, :], R["OSC"][1, lo:hi, :], [], [ob[b]])
                            self.dma("sp", cx[b][p0:p0 + pn, :], R["CX"][lo:hi, :], [], [cx[b]])
                        sc = selv[:, q:q + 1]
                        t_y, t_o = ft[1], ft[3]
                        self.tt("dve", t_y[:, :], yf[b][:, :], yb[b][:, :], ALU.add, [yf[b], yb[b]], [t_y])
                        self.tt("dve", t_o[:, :], of[b][:, :], ob[b][:, :], ALU.add, [of[b], ob[b]], [t_o])
                        if q == 0:
                            self.ts("dve", y[:, :], t_y[:, :], sc, None, ALU.mult, ALU.bypass, [t_y, selv], [y])
                            self.ts("dve", o[:, :], t_o[:, :], sc, None, ALU.mult, ALU.bypass, [t_o, selv], [o])
                            self.ts("dve", cxa[:, :], cx[b][:, :], sc, None, ALU.mult, ALU.bypass, [cx[b], selv], [cxa])
                        else:
                            self.stt(y[:, :], t_y[:, :], sc, y[:, :], ALU.mult, ALU.add, [t_y, selv, y], [y])
                            self.stt(o[:, :], t_o[:, :], sc, o[:, :], ALU.mult, ALU.add, [t_o, selv, o], [o])
                            self.stt(cxa[:, :], cx[b][:, :], sc, cxa[:, :], ALU.mult, ALU.add, [cx[b], selv, cxa], [cxa])
                    post(y, o, cxa, gq)
                dense(n, 1, x1wv[:, :, w0:w0 + n])
                w0 += n

    def phaseC2(self):
        C, R, I, O = self.C, self.R, self.I, self.O
        mods, fnw, selv = C["mods"], C["fnw"], C["selv"]
        x1v = R["X1T"].rearrange("(k p) t -> p k t", p=128)
        x1wv = R["X1W"].rearrange("(k p) t -> p k t", p=128)
        yTv = O["yT"].rearrange("(k p) t -> p k t", p=128)
        NWM = 640
        NCT = self.n_ctx * self.L_ctx
        QL, NQ = self.QL, self.NQ
        with self.scope() as st:
            WG = self.sb(st, "WG", [128, 8 * DFF], BF16)
            WU = self.sb(st, "WU", [128, 8 * DFF], BF16)
            for (W, nm) in ((WG, "w_gate"), (WU, "w_up")):
                wv = I[nm].rearrange("(k p) c -> p k c", p=128)
                for k in range(KD):
                    self.dma("pool", W[:, k * DFF:(k + 1) * DFF], wv[:, k, :], [], [W])
            cw = self.sb(st, "cw", [128, NF * 9], F32)
            self.dma("sp", cw[:, :], I["convw"], [], [cw])
            cvb = self.sb(st, "cvb", [128, NF], F32)
            self.dma("sp", cvb[:, :], I["cvb"], [], [cvb])
            wd = [self.sb(st, f"wd{i}", [128, NF * 128], BF16) for i in range(2)]
            wdv = I["w_down"].rearrange("(f p) c -> p f c", p=128)
            x1w = self.sb(st, "x1w", [128, 8 * NWM], F32)
            tiles = {
                "sq": self.sb(st, "c2sq", [128, 8 * NWM], BF16),
                "rstd": self.sb(st, "c2rstd", [128, NWM], F32),
                "ntmp": [self.sb(st, f"c2ntmp{i}", [128, NWM], F32) for i in range(2)],
            }
            h2T = self.sb(st, "h2T", [128, 8 * NWM], BF16)
            HM = self.sb(st, "HM", [128, NF * 512], BF16)
            GP = [self.sb(st, f"GP{i}", [128, 10 * 66], F32) for i in range(2)]
            GPc = [self.sb(st, f"GPc{i}", [128, 2 * 258], F32) for i in range(2)]
            acc = [self.sb(st, f"c2acc{i}", [128, 512], F32) for i in range(2)]
            gl = [self.sb(st, f"c2gl{i}", [128, 512], F32) for i in range(2)]
            x2 = self.sb(st, "x2", [128, 8 * 512], F32)
            PG = [self.ps(st, f"c2PG{i}", [128, 1024], F32) for i in range(2)]
            PU = [self.ps(st, f"c2PU{i}", [128, 512], F32) for i in range(2)]
            PD = [self.ps(st, f"c2PD{i}", [128, 512], F32) for i in range(2)]
            fi = 0
            wdi = 0
            work = [(0, t0, n, None) for (t0, n) in self.ctx_tiles()]
            for r0 in range(0, QL // 64, 8):
                work.append((1, r0, min(8, QL // 64 - r0) * 64, None))
            for (kind, t0, n, _) in work:
                if kind == 1:
                    r0 = t0
                    nwv = n + 128
                    coff = 64
                    self.dma("sp", x1w[:, 0:8 * nwv].rearrange("p (k t) -> p k t", k=8),
                             x1wv[:, :, 64 * r0:64 * r0 + nwv], [], [x1w])
                else:
                    nwv, coff = n, 0
                    self.dma("sp", x1w[:, 0:8 * nwv].rearrange("p (k t) -> p k t", k=8), x1v[:, :, t0:t0 + n], [], [x1w])
                self.norm_mod(tiles, x1w, nwv, PU[0], mods[:, 3, kind, :], mods[:, 4, kind, :], h2T, mods)
                nrow_t = n // 64
                for gpz in (GP if kind == 1 else GPc):
                    self.memset("dve", gpz[:, :], 0.0, [gpz])
                for f in range(NF):
                    pg, pu = PG[fi % 2], PU[fi % 2]
                    ac, g_ = acc[fi % 2], gl[fi % 2]
                    c0 = 0
                    while c0 < nwv:
                        cn = min(512, nwv - c0)
                        for k in range(KD):
                            self.mm(pg[:, c0:c0 + cn], WG[:, k * DFF + f * 128:k * DFF + (f + 1) * 128],
                                    h2T[:, k * nwv + c0:k * nwv + c0 + cn], k == 0, k == KD - 1, [WG, h2T], [pg])
                        c0 += cn
                    for k in range(KD):
                        self.mm(pu[:, 0:n], WU[:, k * DFF + f * 128:k * DFF + (f + 1) * 128],
                                h2T[:, k * nwv + coff:k * nwv + coff + n], k == 0, k == KD - 1, [WU, h2T], [pu])
                    if kind == 1:
                        gp = GP[fi % 2]
                        gp3 = gp[:, :].rearrange("p (r c) -> p r c", c=66)
                        nrw = nwv // 64
                        self.cp("act", gp3[:, 0:nrw, 1:65], pg[:, 0:nwv].rearrange("p (r c) -> p r c", c=64), [pg], [gp])
                        if r0 == 0:
                            self.ts("dve", gp3[:, 0, 1:65], gp3[:, 0, 1:65], selv[:, NQ:NQ + 1], None, ALU.mult, ALU.bypass,
                                    [gp, selv], [gp])
                        if r0 + nrow_t == QL // 64:
                            self.ts("dve", gp3[:, nrw - 1, 1:65], gp3[:, nrw - 1, 1:65], selv[:, NQ + 1:NQ + 2], None,
                                    ALU.mult, ALU.bypass, [gp, selv], [gp])
                        a3 = ac[:, 0:n].rearrange("p (r c) -> p r c", c=64)
                        first = True
                        for dy in range(3):
                            for dx in range(3):
                                src = gp3[:, dy:dy + nrow_t, dx:dx + 64]
                                wcol = cw[:, f * 9 + dy * 3 + dx:f * 9 + dy * 3 + dx + 1]
                                if first:
                                    self.ts("dve", a3, src, wcol, cvb[:, f:f + 1], ALU.mult, ALU.add, [gp, cw, cvb], [ac])
                                    first = False
                                else:
                                    self.stt(a3, src, wcol, a3, ALU.mult, ALU.add, [gp, cw, ac], [ac])
                    else:
                        gp = GPc[fi % 2]
                        nsq = n // self.L_ctx
                        Lc = self.L_ctx
                        gp3 = gp[:, 0:nsq * (Lc + 2)].rearrange("p (r c) -> p r c", c=Lc + 2)
                        self.cp("act", gp3[:, :, 1:Lc + 1], pg[:, 0:n].rearrange("p (r c) -> p r c", c=Lc), [pg], [gp])
                        a3 = ac[:, 0:n].rearrange("p (r c) -> p r c", c=Lc)
                        for dx in range(3):
                            src = gp3[:, :, dx:dx + Lc]
                            wcol = cw[:, f * 9 + 3 + dx:f * 9 + 3 + dx + 1]
                            if dx == 0:
                                self.ts("dve", a3, src, wcol, cvb[:, f:f + 1], ALU.mult, ALU.add, [gp, cw, cvb], [ac])
                            else:
                                self.stt(a3, src, wcol, a3, ALU.mult, ALU.add, [gp, cw, ac], [ac])
                    self.act(g_[:, 0:n], ac[:, 0:n], AF.Gelu_apprx_tanh, [ac], [g_])
                    self.tt("dve", HM[:, f * 512:f * 512 + n], pu[:, 0:n], g_[:, 0:n], ALU.mult, [pu, g_], [HM])
                    fi += 1
                for m in range(KD):
                    w_ = wd[wdi % 2]
                    pd = PD[wdi % 2]
                    wdi += 1
                    self.dma("pool", w_[:, :].rearrange("p (f c) -> p f c", c=128), wdv[:, :, m * 128:(m + 1) * 128], [], [w_])
                    for f in range(NF):
                        self.mm(pd[:, 0:n], w_[:, f * 128:(f + 1) * 128], HM[:, f * 512:f * 512 + n], f == 0, f == NF - 1,
                                [w_, HM], [pd])
                    self.stt(x2[:, m * n:(m + 1) * n], pd[:, 0:n], mods[:, 5, kind, m:m + 1],
                             x1w[:, m * nwv + coff:m * nwv + coff + n], ALU.mult, ALU.add, [pd, mods, x1w], [x2])
                rstd = tiles["rstd"]
                self.rms_rstd(tiles, x2, n, PU[1], rstd)
                for k in range(KD):
                    self.stt(x2[:, k * n:(k + 1) * n], x2[:, k * n:(k + 1) * n], fnw[:, k:k + 1], rstd[:, 0:n],
                             ALU.mult, ALU.mult, [x2, fnw, rstd], [x2])
                od = NCT + 64 * t0 if kind == 1 else t0
                self.dma("pool", yTv[:, :, od:od + n], x2[:, 0:8 * n].rearrange("p (k t) -> p k t", k=8), [x2], [])

    def build(self, phases=("0", "A", "S", "C1", "C2")):
        self.declare_io()
        with contextlib.ExitStack() as gst:
            mst = contextlib.ExitStack()
            self.phase0(gst, mst)
            if "A" in phases:
                self.phaseA()
            if "S" in phases:
                self.scan()
            if "C1" in phases:
                self.phaseC1()
            self.S.barrier()
            mst.close()
            if "C2" in phases:
                self.phaseC2()
            self.S.barrier()
            fin = list(self.final)
            self.S.finalize(fin)
        return self.nc


def fm(v):
    return np.ascontiguousarray(np.asarray(v, np.float32).reshape(-1, 128).T)


def shared_inputs(inp):
    f = lambda a: np.ascontiguousarray(np.asarray(a, np.float32))
    S = {}
    S["ada_w"] = f(inp["ada_w"][0])
    S["adab"] = fm(inp["ada_b"][0])
    S["nmw"] = fm(inp["norm_mix_w"][0])
    S["nfw"] = fm(inp["norm_ffn_w"][0])
    S["fnw"] = fm(inp["final_norm_w"])
    S["w_in"] = f(inp["w_in"][0])
    S["rwkv_conv"] = f(inp["rwkv_conv"][0])
    S["hgrn_lb"] = f(inp["hgrn_lb"]).reshape(1, 2048)
    S["prow"] = np.concatenate([f(inp[k][0]).reshape(-1) for k in
                                ("hgrn_norm_w", "rwkv_k_k", "rwkv_k_a", "rwkv_r_k", "rwkv_ln_w", "rwkv_ln_b")]
                               + [f(inp["rwkv_w0"][0, 0]), f(inp["rwkv_w0"][0, 1]), f(inp["rwkv_a0"][0])]).reshape(1, 4608)
    w2x = np.zeros((32, 1536), np.float32)
    w2x[0:32, 0:512] = inp["rwkv_w2"][0, 0]
    w2x[0:32, 512:1024] = inp["rwkv_w2"][0, 1]
    w2x[0:32, 1024:1536] = inp["rwkv_a2"][0]
    S["w2x"] = w2x
    S["g2"] = f(inp["rwkv_g2"][0])
    S["w_out"] = f(inp["w_out"][0])
    S["w_gate"] = f(inp["ffn_w_gate"][0])
    S["w_up"] = f(inp["ffn_w_up"][0])
    S["w_down"] = f(inp["ffn_w_down"][0])
    cw = f(inp["ffn_conv"][0]).reshape(9, NF, 128)
    S["convw"] = np.ascontiguousarray(cw.transpose(2, 1, 0).reshape(128, NF * 9))
    S["cvb"] = fm(inp["ffn_conv_b"][0])
    return S


def core_inputs(inp, S, ctx_ids, lat_b, q):
    f = lambda a: np.ascontiguousarray(np.asarray(a, np.float32))
    L_lat = inp["x_sample"].shape[1]
    QL = min(1024, L_lat)
    NQ = L_lat // QL
    xl = f(inp["x_sample"][lat_b])
    xs = [f(inp["x_prompt"][i]) for i in ctx_ids] + [xl]
    x = np.concatenate(xs, axis=0)
    m = dict(S)
    m["xT"] = np.ascontiguousarray(x.T)
    xw = np.zeros((QL + 128, D), np.float32)
    lo, hi = q * QL - 64, q * QL + QL + 64
    a, b = max(lo, 0), min(hi, L_lat)
    xw[a - lo:b - lo] = xl[a:b]
    m["xTo"] = np.ascontiguousarray(xw.T)
    sel = np.zeros((128, NQ + 2), np.float32)
    sel[:, q] = 1.0
    sel[:, NQ] = 0.0 if q == 0 else 1.0
    sel[:, NQ + 1] = 0.0 if q == NQ - 1 else 1.0
    m["selv"] = sel
    cv = np.stack([f(inp["c_ctx"]), f(inp["c"][lat_b])], axis=1)
    m["cvec"] = np.ascontiguousarray(cv.reshape(8, 128, 2).transpose(1, 0, 2).reshape(128, 16))
    sh = f(inp["state_hgrn"][lat_b, 0])
    m["st_h"] = np.ascontiguousarray(sh.transpose(0, 2, 1, 3).reshape(2, 128, 512))
    sr = f(inp["state_rwkv"][lat_b, 0])
    m["st_r"] = np.ascontiguousarray(sr.transpose(0, 3, 1, 2).reshape(2, 64, 512))
    return m


_PROG_CACHE = {}


def get_prog(n_ctx, L_ctx, L_lat, debug=(), phases=("0", "A", "S", "C1", "C2")):
    key = (n_ctx, L_ctx, L_lat, tuple(sorted(debug)), tuple(phases))
    if key not in _PROG_CACHE:
        b = Builder(n_ctx, L_ctx, L_lat, debug)
        nc = b.build(phases)
        _PROG_CACHE[key] = (nc, b)
    return _PROG_CACHE[key]


def kernel(**inp):
    B, L_ctx = inp["x_prompt"].shape[0], inp["x_prompt"].shape[1]
    DB, L_lat = inp["x_sample"].shape[0], inp["x_sample"].shape[1]
    n_ctx = B // N_CORES
    QL = min(1024, L_lat)
    NQ = L_lat // QL
    assert DB * NQ == N_CORES
    nc, _ = get_prog(n_ctx, L_ctx, L_lat)
    S = shared_inputs(inp)
    in_maps = []
    for c in range(N_CORES):
        ctx_ids = list(range(c * n_ctx, (c + 1) * n_ctx))
        in_maps.append(core_inputs(inp, S, ctx_ids, c % DB, c // DB))
    res = run_bass_kernel_spmd(nc, in_maps, core_ids=list(range(N_CORES)))
    y_prompt = np.zeros((B, L_ctx, D), np.float32)
    y_sample = np.zeros((DB, L_lat, D), np.float32)
    nsh = np.zeros((B, 1, 2, 4, 128, 128), np.float32)
    nsr = np.zeros((B, 1, 2, 8, 64, 64), np.float32)
    for c in range(N_CORES):
        r = res.results[c]
        y = np.asarray(r["yT"]).T
        for i in range(n_ctx):
            y_prompt[c * n_ctx + i] = y[i * L_ctx:(i + 1) * L_ctx]
        b, q = c % DB, c // DB
        y_sample[b, q * QL:(q + 1) * QL] = y[n_ctx * L_ctx:]
        h = np.asarray(r["nsh"]).reshape(n_ctx, 2, 128, 4, 128).transpose(0, 1, 3, 2, 4)
        nsh[c * n_ctx:(c + 1) * n_ctx, 0] = h
        rr = np.asarray(r["nsr"]).reshape(n_ctx, 2, 64, 8, 64).transpose(0, 1, 3, 4, 2)
        nsr[c * n_ctx:(c + 1) * n_ctx, 0] = rr
    return (y_prompt, y_sample, nsh, nsr)
```
